# Optimizing a Trainium2 kernel written in Bass

```python
import math
import jax, jax.numpy as jnp
from jax import lax
import numpy as np

D_MODEL = 1024
BATCH = 32
SEQ = 2048
DEPTH = 4

HEAD_DIM = 64
A_HEADS = D_MODEL // 256
A_V_DIM = 2 * HEAD_DIM
B_HEADS = D_MODEL // 128
ROT_DIM = HEAD_DIM // 4
ROPE_THETA = 500000.0
GRID_W = 64
WIN_R = 8
WIN_C = 16
MLA_HEADS = D_MODEL // 64
MLA_NOPE = 64
MLA_ROPE = 32
MLA_V = 64
Q_LORA = D_MODEL // 4
KV_LORA = D_MODEL // 8
D_FF = 4 * D_MODEL
QBLK = 128
EPS = 1e-5
N_EVEN = (DEPTH + 1) // 2
N_ODD = DEPTH // 2
A_QK = A_HEADS * 2 * HEAD_DIM
A_V = A_HEADS * A_V_DIM
B_W = B_HEADS * HEAD_DIM
EVEN_IN = 2 * A_QK + A_V + 3 * B_W
EVEN_OUT = A_V + B_W
MLA_IN = Q_LORA + KV_LORA + MLA_ROPE

kernel_name = "hybrid_diff_natten_mla_encoder"


def rmsnorm(x, g):
    xf = x.astype(jnp.float32)
    y = xf * lax.rsqrt(jnp.mean(jnp.square(xf), axis=-1, keepdims=True) + EPS)
    return (y * g.astype(jnp.float32)).astype(x.dtype)


def rotary_tables(S, rot_dim):
    pos = jnp.arange(S, dtype=jnp.float32)
    inv = ROPE_THETA ** (-jnp.arange(0, rot_dim, 2, dtype=jnp.float32) / rot_dim)
    ang = pos[:, None] * inv[None, :]
    return jnp.cos(ang)[:, None, :], jnp.sin(ang)[:, None, :]


def apply_rope(x, cos, sin):
    rot = 2 * cos.shape[-1]
    xf = x[..., :rot].astype(jnp.float32)
    x1, x2 = xf[..., : rot // 2], xf[..., rot // 2:]
    xr = jnp.concatenate([x1 * cos - x2 * sin, x2 * cos + x1 * sin], axis=-1).astype(x.dtype)
    return jnp.concatenate([xr, x[..., rot:]], axis=-1)


def diff_attention(q, k, v, lam):
    _, B, H, S, d = q.shape
    nb = S // QBLK
    qb = q.reshape(2, B, H, nb, QBLK, d).transpose(3, 0, 1, 2, 4, 5)
    scale = d ** -0.5

    def block(qi):
        s = jnp.einsum('nbhqd,nbhkd->nbhqk', qi, k).astype(jnp.float32) * scale
        p = jax.nn.softmax(s, axis=-1)
        a = p[0] - lam * p[1]
        return jnp.einsum('bhqk,bhkd->bhqd', a.astype(v.dtype), v)

    o = lax.map(block, qb)
    return o.transpose(1, 2, 0, 3, 4).reshape(B, H, S, -1)


def neighborhood_attention(q, k, v, rpb):
    B, H, S, d = q.shape
    rows = S // GRID_W
    wr = min(WIN_R, rows)
    wc = min(WIN_C, GRID_W)
    qg = q.reshape(B, H, rows, GRID_W, d).transpose(2, 0, 1, 3, 4)
    kg = k.reshape(B, H, rows, GRID_W, d)
    vg = v.reshape(B, H, rows, GRID_W, d)
    c = np.arange(GRID_W)
    cs = np.clip(c - wc // 2, 0, GRID_W - wc)
    col_mask = (c[None, :] >= cs[:, None]) & (c[None, :] < cs[:, None] + wc)
    col_idx = np.clip(c[None, :] - c[:, None], -(WIN_C - 1), WIN_C - 1) + (WIN_C - 1)
    mask = jnp.asarray(col_mask)[:, None, :]
    scale = d ** -0.5

    def row_block(args):
        qr, r = args
        r0 = jnp.clip(r - wr // 2, 0, rows - wr)
        kb = lax.dynamic_slice_in_dim(kg, r0, wr, axis=2)
        vb = lax.dynamic_slice_in_dim(vg, r0, wr, axis=2)
        s = jnp.einsum('bhqd,bhiwd->bhqiw', qr, kb).astype(jnp.float32) * scale
        row_idx = r0 + jnp.arange(wr) - r + (WIN_R - 1)
        bias = rpb[:, row_idx[None, :, None], col_idx[:, None, :]].astype(jnp.float32)
        s = jnp.where(mask, s + bias, -1e30)
        p = jax.nn.softmax(s.reshape(B, H, GRID_W, wr * GRID_W), axis=-1).reshape(s.shape)
        return jnp.einsum('bhqiw,bhiwd->bhqd', p.astype(vb.dtype), vb)

    o = lax.map(row_block, (qg, jnp.arange(rows, dtype=jnp.int32)))
    return o.transpose(1, 2, 0, 3, 4).reshape(B, H, S, d)


def mla_attention(q_nope, q_rope, k_nope, k_rope, v):
    B, H, S, dn = q_nope.shape
    dr = q_rope.shape[-1]
    nb = S // QBLK
    qn = q_nope.reshape(B, H, nb, QBLK, dn).transpose(2, 0, 1, 3, 4)
    qr = q_rope.reshape(B, H, nb, QBLK, dr).transpose(2, 0, 1, 3, 4)
    scale = (dn + dr) ** -0.5

    def block(args):
        qni, qri = args
        s = (jnp.einsum('bhqd,bhkd->bhqk', qni, k_nope)
             + jnp.einsum('bhqr,bkr->bhqk', qri, k_rope)).astype(jnp.float32) * scale
        p = jax.nn.softmax(s, axis=-1)
        return jnp.einsum('bhqk,bhkd->bhqd', p.astype(v.dtype), v)

    o = lax.map(block, (qn, qr))
    return o.transpose(1, 2, 0, 3, 4).reshape(B, H, S, -1)


def even_mixer(h, w_in, w_out, lq1, lk1, lq2, lk2, subln_g, rpb, lam_init):
    B, S, _ = h.shape
    proj = h @ w_in
    splits = np.cumsum([A_QK, A_QK, A_V, B_W, B_W])
    a_q, a_k, a_v, b_q, b_k, b_v = jnp.split(proj, splits, axis=-1)
    cos, sin = rotary_tables(S, ROT_DIM)
    a_q = apply_rope(a_q.reshape(B, S, 2 * A_HEADS, HEAD_DIM), cos, sin)
    a_k = apply_rope(a_k.reshape(B, S, 2 * A_HEADS, HEAD_DIM), cos, sin)
    a_q = a_q.reshape(B, S, A_HEADS, 2, HEAD_DIM).transpose(3, 0, 2, 1, 4)
    a_k = a_k.reshape(B, S, A_HEADS, 2, HEAD_DIM).transpose(3, 0, 2, 1, 4)
    a_v = a_v.reshape(B, S, A_HEADS, A_V_DIM).transpose(0, 2, 1, 3)
    f32 = jnp.float32
    lam = (jnp.exp(jnp.sum(lq1.astype(f32) * lk1.astype(f32)))
           - jnp.exp(jnp.sum(lq2.astype(f32) * lk2.astype(f32))) + lam_init)
    o_a = diff_attention(a_q, a_k, a_v, lam)
    o_a = rmsnorm(o_a, subln_g) * (1.0 - lam_init)
    o_a = o_a.transpose(0, 2, 1, 3).reshape(B, S, A_V)
    to_heads = lambda t: t.reshape(B, S, B_HEADS, HEAD_DIM).transpose(0, 2, 1, 3)
    o_b = neighborhood_attention(to_heads(b_q), to_heads(b_k), to_heads(b_v), rpb)
    o_b = o_b.transpose(0, 2, 1, 3).reshape(B, S, B_W)
    return jnp.concatenate([o_a, o_b], axis=-1) @ w_out


def odd_mixer(h, w_in, q_norm_g, kv_norm_g, w_uq, w_ukv, w_out):
    B, S, _ = h.shape
    proj = h @ w_in
    c_q, c_kv, k_r = jnp.split(proj, [Q_LORA, Q_LORA + KV_LORA], axis=-1)
    cos, sin = rotary_tables(S, MLA_ROPE)
    q = (rmsnorm(c_q, q_norm_g) @ w_uq).reshape(B, S, MLA_HEADS, MLA_NOPE + MLA_ROPE)
    q_nope = q[..., :MLA_NOPE]
    q_rope = apply_rope(q[..., MLA_NOPE:], cos, sin)
    kv = (rmsnorm(c_kv, kv_norm_g) @ w_ukv).reshape(B, S, MLA_HEADS, MLA_NOPE + MLA_V)
    k_nope, v = kv[..., :MLA_NOPE], kv[..., MLA_NOPE:]
    k_rope = apply_rope(k_r[:, :, None, :], cos, sin)[:, :, 0, :]
    th = lambda t: t.transpose(0, 2, 1, 3)
    o = mla_attention(th(q_nope), th(q_rope), th(k_nope), k_rope, th(v))
    return o.transpose(0, 2, 1, 3).reshape(B, S, MLA_HEADS * MLA_V) @ w_out


def setup_inputs(seed: int = 0) -> dict:
    key = jax.random.key(seed)
    ks = jax.random.split(key, 24)
    f32 = jnp.float32
    nrm = lambda k, shape, s: jax.random.normal(k, shape, f32) * s
    gain = lambda k, shape: 1.0 + 0.05 * jax.random.normal(k, shape, f32)
    return {
        "x": nrm(ks[0], (BATCH, SEQ, D_MODEL), 1.0),
        "ln_mix_g": gain(ks[1], (DEPTH, D_MODEL)),
        "ln_mlp_g": gain(ks[2], (DEPTH, D_MODEL)),
        "w_up": nrm(ks[3], (DEPTH, D_MODEL, D_FF), D_MODEL ** -0.5),
        "w_down": nrm(ks[4], (DEPTH, D_FF, D_MODEL), D_FF ** -0.5),
        "ln_f_g": gain(ks[5], (D_MODEL,)),
        "ev_w_in": nrm(ks[6], (N_EVEN, D_MODEL, EVEN_IN), D_MODEL ** -0.5),
        "ev_w_out": nrm(ks[7], (N_EVEN, EVEN_OUT, D_MODEL), EVEN_OUT ** -0.5),
        "ev_lambda_q1": nrm(ks[8], (N_EVEN, HEAD_DIM), 0.1),
        "ev_lambda_k1": nrm(ks[9], (N_EVEN, HEAD_DIM), 0.1),
        "ev_lambda_q2": nrm(ks[10], (N_EVEN, HEAD_DIM), 0.1),
        "ev_lambda_k2": nrm(ks[11], (N_EVEN, HEAD_DIM), 0.1),
        "ev_subln_g": gain(ks[12], (N_EVEN, A_V_DIM)),
        "ev_rpb": nrm(ks[13], (N_EVEN, B_HEADS, 2 * WIN_R - 1, 2 * WIN_C - 1), 0.02),
        "od_w_in": nrm(ks[14], (N_ODD, D_MODEL, MLA_IN), D_MODEL ** -0.5),
        "od_q_norm_g": gain(ks[15], (N_ODD, Q_LORA)),
        "od_kv_norm_g": gain(ks[16], (N_ODD, KV_LORA)),
        "od_w_uq": nrm(ks[17], (N_ODD, Q_LORA, MLA_HEADS * (MLA_NOPE + MLA_ROPE)), Q_LORA ** -0.5),
        "od_w_ukv": nrm(ks[18], (N_ODD, KV_LORA, MLA_HEADS * (MLA_NOPE + MLA_V)), KV_LORA ** -0.5),
        "od_w_out": nrm(ks[19], (N_ODD, MLA_HEADS * MLA_V, D_MODEL), (MLA_HEADS * MLA_V) ** -0.5),
    }


def reference(x, ln_mix_g, ln_mlp_g, w_up, w_down, ln_f_g,
              ev_w_in, ev_w_out, ev_lambda_q1, ev_lambda_k1, ev_lambda_q2, ev_lambda_k2,
              ev_subln_g, ev_rpb,
              od_w_in, od_q_norm_g, od_kv_norm_g, od_w_uq, od_w_ukv, od_w_out):
    for l in range(DEPTH):
        i = l // 2
        h = rmsnorm(x, ln_mix_g[l])
        if l % 2 == 0:
            lam_init = 0.8 - 0.6 * math.exp(-0.3 * l)
            x = x + even_mixer(h, ev_w_in[i], ev_w_out[i], ev_lambda_q1[i], ev_lambda_k1[i],
                               ev_lambda_q2[i], ev_lambda_k2[i], ev_subln_g[i], ev_rpb[i], lam_init)
        else:
            x = x + odd_mixer(h, od_w_in[i], od_q_norm_g[i], od_kv_norm_g[i],
                              od_w_uq[i], od_w_ukv[i], od_w_out[i])
        h = rmsnorm(x, ln_mlp_g[l])
        x = x + jnp.square(jax.nn.relu(h @ w_up[l])) @ w_down[l]
    return rmsnorm(x, ln_f_g)
```

```python
import math
from contextlib import ExitStack

import numpy as np
import ml_dtypes
import concourse.bass as bass
import concourse.mybir as mybir
from concourse.bass_utils import run_bass_kernel_spmd

F32 = mybir.dt.float32
BF16 = mybir.dt.bfloat16
AF = mybir.ActivationFunctionType
ALU = mybir.AluOpType
AX = mybir.AxisListType

S = 2048
D = 1024
NCH = 8
NTG = 4
TG = 512
EPS = 1e-5
NCORES = 8
ENGS = ("pe", "act", "dve", "pool", "sp")


class Op:
    __slots__ = ("eng", "fn", "deps", "dma_ch", "sig", "idx", "has_consumer", "waits")

    def __init__(self, eng, fn, dma_ch=None):
        self.eng = eng
        self.fn = fn
        self.deps = set()
        self.dma_ch = dma_ch
        self.sig = None
        self.has_consumer = False
        self.waits = []


class Prog:
    def __init__(self, nc, es):
        self.nc = nc
        self.es = es
        self.sems = {}
        self.cnt = {e: 0 for e in ENGS}
        self.dcnt = {}
        self.reset_phase()
        self.nphase = 0

    def reset_phase(self):
        self.ops = []
        self.last_writer = {}
        self.readers = {}

    def sem(self, key):
        s = self.sems.get(key)
        if s is None:
            name = "s_" + "_".join(str(k) for k in key)
            s = self.es.enter_context(self.nc.semaphore(name))
            self.sems[key] = s
        return s

    def op(self, eng, fn, reads=(), writes=(), dma_ch=None):
        o = Op(eng, fn, dma_ch)
        o.idx = len(self.ops)
        deps = set()
        for r in reads:
            w = self.last_writer.get(r)
            if w is not None:
                deps.add(w)
        for r in writes:
            w = self.last_writer.get(r)
            if w is not None:
                deps.add(w)
            for rd in self.readers.get(r, ()):
                deps.add(rd)
        o.deps = deps
        self.ops.append(o)
        for r in reads:
            self.readers.setdefault(r, []).append(o.idx)
        for r in writes:
            self.last_writer[r] = o.idx
            self.readers[r] = []
        return o

    def emit_phase(self, final=False):
        ops = self.ops
        if not ops:
            return
        nc = self.nc
        for o in ops:
            best = {}
            for d in o.deps:
                p = ops[d]
                if p.dma_ch is not None:
                    key = ("dma", p.dma_ch)
                elif p.eng != o.eng or o.eng in ("act", "dve", "pool"):
                    key = ("eng", p.eng)
                else:
                    continue
                if d > best.get(key, -1):
                    best[key] = d
            keep = set(best.values())
            o.deps = keep
            for d in keep:
                ops[d].has_consumer = True
        last = {}
        for o in ops:
            if o.dma_ch is None:
                last[o.eng] = o
        for o in last.values():
            o.has_consumer = True
        prev_cnt = dict(self.cnt)
        prev_dcnt = dict(self.dcnt)
        for o in ops:
            if o.dma_ch is not None:
                self.dcnt[o.dma_ch] = self.dcnt.get(o.dma_ch, 0) + 16
                o.sig = ("dma", o.dma_ch, self.dcnt[o.dma_ch])
            elif o.has_consumer:
                self.cnt[o.eng] += 1
                o.sig = ("eng", o.eng, self.cnt[o.eng])
        waited = {e: {} for e in ENGS}
        for e in ENGS:
            for e2 in ENGS:
                if prev_cnt[e2] > 0:
                    waited[e][("eng", e2)] = prev_cnt[e2]
            for ch, v in prev_dcnt.items():
                waited[e][("dma", ch)] = v
        for o in ops:
            need = {}
            for d in o.deps:
                s = ops[d].sig
                key = (s[0], s[1])
                if s[2] > need.get(key, 0):
                    need[key] = s[2]
            for key, v in need.items():
                if waited[o.eng].get(key, 0) >= v:
                    continue
                waited[o.eng][key] = v
                o.waits.append((key, v))
        by_eng = {e: [o for o in ops if o.eng == e] for e in ENGS}
        sems = self.sem
        final_d = dict(self.dcnt)
        final_c = dict(self.cnt)

        def run(eng_name, eng):
            for e2 in ENGS:
                if e2 != eng_name and prev_cnt[e2] > 0:
                    eng.wait_ge(sems(("eng", e2)), prev_cnt[e2])
            for ch, v in prev_dcnt.items():
                eng.wait_ge(sems(("dma", ch)), v)
            for o in by_eng[eng_name]:
                for key, v in o.waits:
                    eng.wait_ge(sems(key), v)
                ins = o.fn(eng)
                if o.dma_ch is not None:
                    ins.then_inc(sems(("dma", o.dma_ch)), 16)
                elif o.sig is not None:
                    ins.then_inc(sems(("eng", eng_name)), 1)
            if final:
                for ch, v in final_d.items():
                    eng.wait_ge(sems(("dma", ch)), v)
                for e2 in ENGS:
                    if e2 != eng_name and final_c[e2] > 0:
                        eng.wait_ge(sems(("eng", e2)), final_c[e2])

        for e in ENGS:
            sems(("eng", e))
        for ch in self.dcnt:
            sems(("dma", ch))
        with nc.Block() as block:
            @block.sync
            def _(e):
                run("sp", e)

            @block.tensor
            def _(e):
                run("pe", e)

            @block.scalar
            def _(e):
                run("act", e)

            @block.vector
            def _(e):
                run("dve", e)

            @block.gpsimd
            def _(e):
                run("pool", e)
        self.nphase += 1
        self.reset_phase()


class Pipe:
    def __init__(self, look):
        self.look = look
        self.q = []

    def step(self, qk, pv):
        qk()
        self.q.append(pv)
        if len(self.q) > self.look:
            self.q.pop(0)()

    def flush(self):
        while self.q:
            self.q.pop(0)()


def _rot_tables(rot_dim, theta=500000.0):
    pos = np.arange(S, dtype=np.float32)
    inv = (np.float32(theta) ** (-np.arange(0, rot_dim, 2, dtype=np.float32) / np.float32(rot_dim))).astype(np.float32)
    ang = (pos[:, None] * inv[None, :]).astype(np.float32)
    return np.cos(ang).astype(np.float32), np.sin(ang).astype(np.float32)


def _rope_tables_A():
    cos, sin = _rot_tables(16)
    C = np.ones((128, S), np.float32)
    Sg = np.zeros((128, S), np.float32)
    for base in (0, 64):
        for d in range(8):
            C[base + d] = cos[:, d]
            Sg[base + d] = -sin[:, d]
            C[base + 8 + d] = cos[:, d]
            Sg[base + 8 + d] = sin[:, d]
    return C, Sg


def _rope_tables_M():
    cos, sin = _rot_tables(32)
    C = np.ones((128, S), np.float32)
    Sg = np.zeros((128, S), np.float32)
    for d in range(16):
        C[64 + d] = cos[:, d]
        Sg[64 + d] = -sin[:, d]
        C[64 + 16 + d] = cos[:, d]
        Sg[64 + 16 + d] = sin[:, d]
    return C, Sg


def _swap_perm(n, head, rot_off, half):
    idx = np.arange(n)
    for h0 in range(0, n, head):
        for d in range(half):
            idx[h0 + rot_off + d] = h0 + rot_off + half + d
            idx[h0 + rot_off + half + d] = h0 + rot_off + d
    return idx


def _nbr_plan():
    rows, W, wr, wc = 32, 64, 8, 16
    gids = {}
    plan = []
    for m in range(16):
        need = {}
        for b in range(2):
            qr = 2 * m + b
            r0 = min(max(qr - wr // 2, 0), rows - wr)
            for kr in range(r0, r0 + wr):
                t, a = kr // 2, kr % 2
                need.setdefault(t, set()).add((a, b))
        lst = []
        for t in sorted(need):
            key = (2 * t - 2 * m, tuple(sorted(need[t])))
            if key not in gids:
                gids[key] = len(gids)
            lst.append((t, gids[key]))
        plan.append(lst)
    ng = len(gids)
    ridx = np.zeros((ng, 128, 128), np.int64)
    cidx = np.zeros((ng, 128, 128), np.int64)
    valid = np.zeros((ng, 128, 128), bool)
    c = np.arange(W)
    cs = np.clip(c - wc // 2, 0, W - wc)
    colmask = (c[None, :] >= cs[:, None]) & (c[None, :] < cs[:, None] + wc)
    for (delta, pat), g in gids.items():
        for a in range(2):
            for b in range(2):
                dr = delta + a - b + 7
                ok = (a, b) in pat
                for kc in range(W):
                    k = a * 64 + kc
                    q = b * 64 + c
                    ridx[g, k, q] = min(max(dr, 0), 14)
                    cidx[g, k, q] = np.clip(kc - c, -15, 15) + 15
                    valid[g, k, q] = ok & colmask[c, kc]
    return plan, ng, ridx, cidx, valid


_PLAN, _NG, _RIDX, _CIDX, _VALID = _nbr_plan()
MASKV = -30000.0


def build(nseq=4, layers=(0, 1, 2, 3), do_mlp=True, do_mix=True, parts="AB", dbg=False):
    nc = bass.Bass("TRN2", target_bir_lowering=False)
    dt = nc.dram_tensor
    xT_d = dt("xT", [nseq, D, S], F32, kind="ExternalInput").ap()
    out_d = dt("outT", [nseq, D, S], F32, kind="ExternalOutput").ap()
    vecs_d = dt("vecs", [128, 80], F32, kind="ExternalInput").ap()
    lam_d = dt("lamv", [128, 2 * 4 * 64], F32, kind="ExternalInput").ap()
    w_up_d = dt("w_up", [4, D, 4096], F32, kind="ExternalInput").ap()
    w_dn_d = dt("w_down", [4, 4096, D], F32, kind="ExternalInput").ap()
    ev_in_d = dt("ev_w_in", [2, D, 3072], F32, kind="ExternalInput").ap()
    ev_sw_d = dt("ev_w_sw", [2, D, 1024], F32, kind="ExternalInput").ap()
    ev_out_d = dt("ev_w_out", [2, D, D], F32, kind="ExternalInput").ap()
    od_in_d = dt("od_w_in", [2, D, 416], F32, kind="ExternalInput").ap()
    od_insw_d = dt("od_w_in_sw", [2, D, 96], F32, kind="ExternalInput").ap()
    od_uq_d = dt("od_w_uq", [2, 256, 1536], F32, kind="ExternalInput").ap()
    od_uqsw_d = dt("od_w_uq_sw", [2, 256, 1536], F32, kind="ExternalInput").ap()
    od_ukv_d = dt("od_w_ukv", [2, 128, 2048], F32, kind="ExternalInput").ap()
    od_out_d = dt("od_w_out", [2, D, D], F32, kind="ExternalInput").ap()
    ropeA_d = dt("ropeA", [2, 128, S], BF16, kind="ExternalInput").ap()
    ropeM_d = dt("ropeM", [2, 128, S], BF16, kind="ExternalInput").ap()
    gb_d = dt("gb", [2, 128, 8 * _NG * 128], F32, kind="ExternalInput").ap()
    gmask_d = dt("gmask", [128, _NG * 128], BF16, kind="ExternalInput").ap()
    ident_d = dt("ident", [128, 128], BF16, kind="ExternalInput").ap()

    if dbg:
        dbg_d = {nm: dt("dbg_" + nm, [128, S], BF16, kind="ExternalOutput").ap() for nm in ("q", "k", "v", "o")}
    es = ExitStack()
    with es:
        uniq = [0]

        def sb(name, shape, dty, stack=es):
            uniq[0] += 1
            return stack.enter_context(nc.sbuf_tensor("%s_u%d" % (name, uniq[0]), shape, dty))

        P = Prog(nc, es)
        xT = sb("xT_s", [128, NCH * S], F32)
        hT = sb("hT_s", [128, NCH * S], BF16)
        vecs = sb("vecs_s", [128, 80], F32)
        lams = sb("lams", [128, 16], F32)
        ones = sb("ones", [128, 128], BF16)
        ident = sb("ident_s", [128, 128], BF16)
        NSTG = 2
        STGW = 1536
        stage = [sb("stage%d" % i, [128, STGW], F32) for i in range(NSTG)]
        NTMP = 5
        tmp = [sb("tmp%d" % i, [128, TG], F32) for i in range(NTMP)]
        sq = [sb("sq%d" % i, [128, TG], BF16) for i in range(2)]
        psS = es.enter_context(nc.psum_tensor("psS", [128, 2048], F32))
        psX0 = es.enter_context(nc.psum_tensor("psX0", [128, TG], F32))
        psX1 = es.enter_context(nc.psum_tensor("psX1", [128, TG], F32))
        psO = es.enter_context(nc.psum_tensor("psO", [128, TG], F32))
        psD = es.enter_context(nc.psum_tensor("psD", [128, TG], F32))
        psX = [psX0[:, :], psX1[:, :], psD[:, :], psO[:, :]] + [psS[:, k * TG:(k + 1) * TG] for k in range(4)]

        st = {"stg": 0, "tmp": 0, "sq": 0, "px": 0, "npx": 2}

        def xs(c, g):
            return xT[:, c * S + g * TG: c * S + (g + 1) * TG]

        def hs(c, g):
            return hT[:, c * S + g * TG: c * S + (g + 1) * TG]

        def next_tmp():
            i = st["tmp"]
            st["tmp"] = (i + 1) % NTMP
            return i

        def next_px():
            i = st["px"] % st["npx"]
            st["px"] = (i + 1) % st["npx"]
            return i

        def next_sq():
            i = st["sq"]
            st["sq"] = (i + 1) % 2
            return i

        def wload(dst, dst_off, src3, kc, n, dst_res):
            per = max(1, STGW // n)
            k0 = 0
            while k0 < kc:
                k1 = min(kc, k0 + per)
                tot = (k1 - k0) * n
                slot = st["stg"]
                st["stg"] = (slot + 1) % NSTG
                stg = stage[slot]
                src = src3[:, k0:k1, :]
                dview = stg[:, 0:tot].rearrange("p (k n) -> p k n", k=k1 - k0)
                P.op("sp", lambda e, dview=dview, src=src: e.dma_start(out=dview, in_=src),
                     writes=[("stg", slot)], dma_ch=("stg", slot))
                o0 = dst_off + k0 * n
                P.op("pool", lambda e, stg=stg, o0=o0, tot=tot: e.tensor_copy(out=dst[:, o0:o0 + tot], in_=stg[:, 0:tot]),
                     reads=[("stg", slot)], writes=[dst_res])
                k0 = k1

        def wview(w2d, r0, nk, c0, n):
            return w2d[r0:r0 + nk * 128, c0:c0 + n].rearrange("(k p) n -> p k n", p=128)

        def rmsnorm_x(gcol0, dst_fn, dst_res_fn, out_f32=False):
            for g in range(NTG):
                px = next_px()
                ps = psX[px]
                for c in range(NCH):
                    si = next_sq()
                    if c % 2 == 0:
                        P.op("act", lambda e, si=si, c=c, g=g: e.activation(out=sq[si][:], in_=xs(c, g), func=AF.Square),
                             reads=[("x", c, g)], writes=[("sq", si)])
                    else:
                        P.op("dve", lambda e, si=si, c=c, g=g: e.tensor_tensor(out=sq[si][:], in0=xs(c, g), in1=xs(c, g), op=ALU.mult),
                             reads=[("x", c, g)], writes=[("sq", si)])
                    P.op("pe", lambda e, si=si, c=c, ps=ps: e.matmul(ps[:], lhsT=ones[:], rhs=sq[si][:], start=(c == 0), stop=(c == NCH - 1)),
                         reads=[("sq", si), "ones"], writes=[("px", px)])
                ti = next_tmp()
                P.op("act", lambda e, ti=ti, ps=ps: e.activation(out=tmp[ti][:], in_=ps[:], func=AF.Sqrt, scale=1.0 / D, bias=EPS),
                     reads=[("px", px)], writes=[("tmp", ti)])
                P.op("dve", lambda e, ti=ti: e.reciprocal(out=tmp[ti][:], in_=tmp[ti][:]),
                     reads=[("tmp", ti)], writes=[("tmp", ti)])
                for c in range(NCH):
                    P.op("dve", lambda e, ti=ti, c=c, g=g: e.scalar_tensor_tensor(
                        out=dst_fn(c, g), in0=xs(c, g), scalar=vecs[:, gcol0 + c:gcol0 + c + 1], in1=tmp[ti][:],
                        op0=ALU.mult, op1=ALU.mult),
                        reads=[("x", c, g), ("tmp", ti), "vecs"], writes=[dst_res_fn(c, g)])

        init_es = ExitStack()
        lamt = sb("lamt", [128, 512], F32, init_es)
        lamw = sb("lamw", [128, 512], F32, init_es)
        P.op("sp", lambda e: e.dma_start(out=vecs[:], in_=vecs_d[:, :]), writes=["vecs"], dma_ch="vecs")
        P.op("sp", lambda e: e.dma_start(out=lamt[:], in_=lam_d[:, :]), writes=["lamt"], dma_ch="lamt")
        P.op("sp", lambda e: e.dma_start(out=ident[:], in_=ident_d[:, :]), writes=["ident"], dma_ch="ident")
        P.op("dve", lambda e: e.memset(ones[:], 1.0), writes=["ones"])
        for i in range(2):
            b = i * 256
            for j in range(2):
                P.op("dve", lambda e, b=b, j=j: e.tensor_tensor(out=lamw[:, b + 64 * j: b + 64 * j + 64], in0=lamt[:, b + 128 * j: b + 128 * j + 64],
                                                             in1=lamt[:, b + 128 * j + 64: b + 128 * j + 128], op=ALU.mult),
                     reads=["lamt"], writes=[("lamw", i, j)])
                P.op("dve", lambda e, b=b, j=j, i=i: e.reduce_sum(out=lams[:, 4 * i + j: 4 * i + j + 1], in_=lamw[:, b + 64 * j: b + 64 * j + 64], axis=AX.X),
                     reads=[("lamw", i, j)], writes=[("lams", i, j)])
                P.op("act", lambda e, j=j, i=i: e.activation(out=lams[:, 4 * i + j: 4 * i + j + 1], in_=lams[:, 4 * i + j: 4 * i + j + 1], func=AF.Exp),
                     reads=[("lams", i, j)], writes=[("lams", i, j)])
            lam_init = 0.8 - 0.6 * math.exp(-0.3 * (2 * i))
            P.op("dve", lambda e, i=i, lam_init=lam_init: e.scalar_tensor_tensor(
                out=lams[:, 4 * i + 2: 4 * i + 3], in0=lams[:, 4 * i + 1: 4 * i + 2], scalar=-lam_init, in1=lams[:, 4 * i: 4 * i + 1],
                op0=ALU.add, op1=ALU.subtract),
                reads=[("lams", i, 0), ("lams", i, 1)], writes=[("lams", i, 2)])
        P.emit_phase()
        init_es.close()

        def mlp_phase(l):
            with ExitStack() as ps_:
                st["npx"] = 8
                wm = [sb("wmlp%d" % i, [128, 8192], BF16, ps_) for i in range(2)]
                uT = sb("uT", [128, 4 * S], BF16, ps_)
                rmsnorm_x(32 + 8 * l, hs, lambda c, g: ("h", c, g))
                for part in range(8):
                    wb = part % 2
                    w = wm[wb]
                    wload(w, 0, wview(w_up_d[l], 0, 8, part * 512, 512), 8, 512, ("wm", wb, 0))
                    wload(w, 4096, wview(w_dn_d[l], part * 512, 4, 0, 1024), 4, 1024, ("wm", wb, 1))
                    for g in range(NTG):
                        for j in range(4):
                            px = next_px()
                            ps = psX[px]
                            for c in range(NCH):
                                P.op("pe", lambda e, w=w, c=c, j=j, g=g, ps=ps: e.matmul(
                                    ps[:], lhsT=w[:, c * 512 + j * 128: c * 512 + j * 128 + 128], rhs=hs(c, g),
                                    start=(c == 0), stop=(c == NCH - 1)),
                                    reads=[("wm", wb, 0), ("h", c, g)], writes=[("px", px)])
                            ti = next_tmp()
                            P.op("act", lambda e, ti=ti, ps=ps: e.activation(out=tmp[ti][:], in_=ps[:], func=AF.Relu),
                                 reads=[("px", px)], writes=[("tmp", ti)])
                            P.op("pool", lambda e, ti=ti, j=j, g=g: e.tensor_tensor(
                                out=uT[:, j * S + g * TG: j * S + (g + 1) * TG], in0=tmp[ti][:], in1=tmp[ti][:], op=ALU.mult),
                                reads=[("tmp", ti)], writes=[("u", j, g)])
                    for g in range(NTG):
                        for c in range(NCH):
                            px = next_px()
                            ps = psX[px]
                            for j in range(4):
                                P.op("pe", lambda e, w=w, c=c, j=j, g=g, ps=ps: e.matmul(
                                    ps[:], lhsT=w[:, 4096 + j * 1024 + c * 128: 4096 + j * 1024 + c * 128 + 128],
                                    rhs=uT[:, j * S + g * TG: j * S + (g + 1) * TG], start=(j == 0), stop=(j == 3)),
                                    reads=[("wm", wb, 1), ("u", j, g)], writes=[("px", px)])
                            P.op("dve", lambda e, c=c, g=g, ps=ps: e.tensor_tensor(out=xs(c, g), in0=ps[:], in1=xs(c, g), op=ALU.add),
                                 reads=[("px", px), ("x", c, g)], writes=[("x", c, g)])
                P.emit_phase()

        def proj_fm(ps, wt, woff, wstride, m, nk, rhs_fn, g, wres, rres_fn, pxres, prow0=0):
            for k in range(nk):
                P.op("pe", lambda e, k=k: e.matmul(ps[prow0:prow0 + m, :], lhsT=wt[:, woff + k * wstride: woff + k * wstride + m],
                                                  rhs=rhs_fn(k, g), start=(k == 0), stop=(k == nk - 1)),
                     reads=[wres, rres_fn(k, g)], writes=[pxres])

        def outproj_chunk(wout_d2, kc, oH, ores, watt, wb, wo=5120, load=True, compute=True):
            if load:
                wload(watt[wb], wo, wview(wout_d2, kc * 128, 1, 0, 1024), 1, 1024, ("wa", wb, 5))
            if not compute:
                return
            for g in range(NTG):
                for c in range(NCH):
                    px = next_px()
                    ps = psX[px]
                    P.op("pe", lambda e, c=c, g=g, ps=ps: e.matmul(ps[:], lhsT=watt[wb][:, wo + c * 128: wo + c * 128 + 128],
                                                                   rhs=oH[:, g * TG:(g + 1) * TG], start=True, stop=True),
                         reads=[("wa", wb, 5), (ores, g)], writes=[("px", px)])
                    P.op("dve", lambda e, c=c, g=g, ps=ps: e.tensor_tensor(out=xs(c, g), in0=ps[:], in1=xs(c, g), op=ALU.add),
                         reads=[("px", px), ("x", c, g)], writes=[("x", c, g)])

        def even_phase(l, part, do_norm):
            i = l // 2
            lam_init = 0.8 - 0.6 * math.exp(-0.3 * l)
            isA = part == "A"
            with ExitStack() as ps_:
                st["npx"] = 2
                st["px"] = 0
                watt = [sb("watt%d" % k, [128, 6144], BF16, ps_) for k in range(2)]
                qTm = [sb("qTm%d" % k, [128, S], BF16, ps_) for k in range(2)]
                kT = sb("kT", [128, S], BF16, ps_)
                vS = sb("vS", [128, 16 * 128], BF16, ps_)
                oH = [sb("oH%d" % k, [128, S], BF16, ps_) for k in range(2)]
                NPT = 6
                PT = [sb("PT%d" % k, [128, 512], BF16, ps_) for k in range(NPT)]
                unit_b = [0]
                for k_ in range(2):
                    P.op("pool", lambda e, k_=k_: e.memset(qTm[k_][:], 0.0), writes=[("qT", k_, g) for g in range(NTG)])
                pt_i = [0]
                sbuf_i = [0]
                if isA:
                    ropeC = sb("ropeC", [128, S], BF16, ps_)
                    ropeS = sb("ropeS", [128, S], BF16, ps_)
                    NAT = 8
                    atmp = [sb("atmp%d" % k, [128, TG], F32, ps_) for k in range(NAT)]
                    at_i = [0]
                    P.op("sp", lambda e: e.dma_start(out=ropeC[:], in_=ropeA_d[0]), writes=["ropeC"], dma_ch="ropeC")
                    P.op("sp", lambda e: e.dma_start(out=ropeS[:], in_=ropeA_d[1]), writes=["ropeS"], dma_ch="ropeS")
                else:
                    G = sb("G", [128, 8 * _NG * 128], BF16, ps_)
                    gmask = sb("gmask", [128, _NG * 128], BF16, ps_)
                    P.op("sp", lambda e: e.dma_start(out=gmask[:], in_=gmask_d[:, :]), writes=["gmask"], dma_ch="gmask")
                    GW = _NG * 128
                    for h in range(8):
                        slot = st["stg"]
                        st["stg"] = (slot + 1) % NSTG
                        stg = stage[slot]
                        P.op("sp", lambda e, stg=stg, h=h: e.dma_start(out=stg[:, 0:GW], in_=gb_d[i][:, h * GW:(h + 1) * GW]),
                             writes=[("stg", slot)], dma_ch=("stg", slot))
                        P.op("dve", lambda e, stg=stg, h=h: e.scalar_tensor_tensor(
                            out=G[:, h * GW:(h + 1) * GW], in0=stg[:, 0:GW], scalar=8.0, in1=gmask[:], op0=ALU.mult, op1=ALU.add),
                            reads=[("stg", slot), "gmask"], writes=[("G", h)])

                if do_norm:
                    rmsnorm_x(8 * l, hs, lambda c, g: ("h", c, g))

                def rope_evac(ps_a, ps_b, pxa, pxb, dst, g, dres, rows=128):
                    t1 = next_tmp()
                    t2 = next_tmp()
                    P.op("dve", lambda e: e.tensor_tensor(out=tmp[t1][0:rows, :], in0=ps_a[0:rows, :], in1=ropeC[0:rows, g * TG:(g + 1) * TG], op=ALU.mult),
                         reads=[("px", pxa), "ropeC"], writes=[("tmp", t1)])
                    P.op("dve", lambda e: e.tensor_tensor(out=tmp[t2][0:rows, :], in0=ps_b[0:rows, :], in1=ropeS[0:rows, g * TG:(g + 1) * TG], op=ALU.mult),
                         reads=[("px", pxb), "ropeS"], writes=[("tmp", t2)])
                    if dst is None:
                        for hf in range(2):
                            P.op("pool", lambda e, hf=hf: e.tensor_tensor(out=qTm[hf][64 * hf:64 * hf + 64, g * TG:(g + 1) * TG], in0=tmp[t1][64 * hf:64 * hf + 64, :],
                                                                      in1=tmp[t2][64 * hf:64 * hf + 64, :], op=ALU.add),
                                 reads=[("tmp", t1), ("tmp", t2)], writes=[("qT", hf, g)])
                    else:
                        P.op("pool", lambda e: e.tensor_tensor(out=dst[0:rows, g * TG:(g + 1) * TG], in0=tmp[t1][0:rows, :], in1=tmp[t2][0:rows, :], op=ALU.add),
                             reads=[("tmp", t1), ("tmp", t2)], writes=[(dres, g)])

                hfn = lambda k, g: hs(k, g)
                hres = lambda k, g: ("h", k, g)

                def v_tokmajor(w, woff, wres, ncols):
                    for t4 in range(4):
                        px = next_px()
                        ps = psX[px]
                        for tt in range(4):
                            tok = (t4 * 4 + tt) * 128
                            for k in range(NCH):
                                P.op("pe", lambda e, k=k, tt=tt, tok=tok, ps=ps: e.matmul(
                                    ps[:, tt * 128: tt * 128 + ncols], lhsT=hT[:, k * S + tok: k * S + tok + 128],
                                    rhs=w[:, woff + k * 128: woff + k * 128 + ncols], start=(k == 0), stop=(k == NCH - 1)),
                                    reads=[wres, ("h", k, tok // TG)], writes=[("px", px)])
                        P.op("act", lambda e, t4=t4, ps=ps: e.activation(out=vS[:, t4 * 512:(t4 + 1) * 512], in_=ps[:], func=AF.Copy),
                             reads=[("px", px)], writes=[("vS", t4)])

                pend_out = None
                for hd in (range(4) if isA else ()):
                    wb = hd % 2
                    w = watt[wb]
                    wload(w, 0, wview(ev_in_d[i], 0, 8, 128 * hd, 128), 8, 128, ("wa", wb, 0))
                    wload(w, 1024, wview(ev_sw_d[i], 0, 8, 128 * hd, 128), 8, 128, ("wa", wb, 1))
                    wload(w, 2048, wview(ev_in_d[i], 0, 8, 512 + 128 * hd, 128), 8, 128, ("wa", wb, 2))
                    wload(w, 3072, wview(ev_sw_d[i], 0, 8, 512 + 128 * hd, 128), 8, 128, ("wa", wb, 3))
                    wload(w, 4096, wview(ev_in_d[i], 0, 8, 1024 + 128 * hd, 128), 8, 128, ("wa", wb, 4))
                    for (dst, dres, o0) in ((None, "qT", 0), (kT, "kT", 2048)):
                        for g in range(NTG):
                            pa = next_px()
                            proj_fm(psX[pa], w, o0, 128, 128, 8, hfn, g, ("wa", wb, o0 // 1024), hres, ("px", pa))
                            pb = next_px()
                            proj_fm(psX[pb], w, o0 + 1024, 128, 128, 8, hfn, g, ("wa", wb, o0 // 1024 + 1), hres, ("px", pb))
                            rope_evac(psX[pa], psX[pb], pa, pb, dst, g, dres)
                    v_tokmajor(w, 4096, ("wa", wb, 4), 128)
                    if pend_out is not None:
                        outproj_chunk(*pend_out, load=False)
                        pend_out = None
                    ob = hd % 2
                    outproj_chunk(ev_out_d[i], hd, oH[ob], ("oH", ob), watt, wb, compute=False)
                    pipe = Pipe(3)
                    hold = {}

                    def fin_a(g, half, ob=ob, hold=hold):
                        tO = at_i[0]
                        tD = (tO + 1) % NAT
                        at_i[0] = (tO + 2) % NAT
                        P.op("act", lambda e: e.activation(out=atmp[tO][:], in_=psO[:], func=AF.Copy), reads=[("px", 3)], writes=[("at", tO)])
                        P.op("dve", lambda e: e.tensor_copy(out=atmp[tD][:], in_=psD[:]), reads=[("px", 2)], writes=[("at", tD)])
                        P.op("dve", lambda e: e.reciprocal(out=atmp[tD][:], in_=atmp[tD][:]), reads=[("at", tD)], writes=[("at", tD)])
                        P.op("dve", lambda e: e.tensor_tensor(out=atmp[tO][:], in0=atmp[tO][:], in1=atmp[tD][:], op=ALU.mult),
                             reads=[("at", tO), ("at", tD)], writes=[("at", tO)])
                        if half == 0:
                            hold[g] = tO
                            return
                        a = hold[g]
                        b = tO
                        P.op("dve", lambda e: e.scalar_tensor_tensor(
                            out=atmp[a][:], in0=atmp[b][:], scalar=lams[:, 4 * i + 2: 4 * i + 3], in1=atmp[a][:], op0=ALU.mult, op1=ALU.add),
                            reads=[("at", a), ("at", b), ("lams", i, 2)], writes=[("at", a)])
                        si = next_sq()
                        P.op("act", lambda e: e.activation(out=sq[si][:], in_=atmp[a][:], func=AF.Square), reads=[("at", a)], writes=[("sq", si)])
                        px = next_px()
                        P.op("pe", lambda e: e.matmul(psX[px][:], lhsT=ones[:], rhs=sq[si][:], start=True, stop=True),
                             reads=[("sq", si), "ones"], writes=[("px", px)])
                        k2 = 1.0 / ((1.0 - lam_init) ** 2)
                        P.op("act", lambda e: e.activation(out=atmp[tD][:], in_=psX[px][:], func=AF.Sqrt, scale=k2 / 128.0, bias=EPS * k2),
                             reads=[("px", px)], writes=[("at", tD)])
                        P.op("dve", lambda e: e.reciprocal(out=atmp[tD][:], in_=atmp[tD][:]), reads=[("at", tD)], writes=[("at", tD)])
                        P.op("dve", lambda e: e.scalar_tensor_tensor(
                            out=oH[ob][:, g * TG:(g + 1) * TG], in0=atmp[a][:], scalar=vecs[:, 78 + i: 79 + i], in1=atmp[tD][:], op0=ALU.mult, op1=ALU.mult),
                            reads=[("at", a), ("at", tD), "vecs"], writes=[(("oH", ob), g)])

                    def a_step(g, half, kt):
                        sbi = sbuf_i[0]
                        sbuf_i[0] = (sbi + 1) % 4
                        pti = pt_i[0]
                        pt_i[0] = (pti + 1) % NPT
                        sreg = psS[:, sbi * TG:(sbi + 1) * TG]

                        def qk():
                            P.op("pe", lambda e: e.matmul(sreg, lhsT=kT[:, kt * 128:(kt + 1) * 128], rhs=qTm[half][:, g * TG:(g + 1) * TG], start=True, stop=True),
                                 reads=[("kT", kt // 4), ("qT", half, g)], writes=[("psS", sbi)])
                            P.op("act", lambda e: e.activation(out=PT[pti][:, 0:TG], in_=sreg, func=AF.Exp, scale=0.125),
                                 reads=[("psS", sbi)], writes=[("PT", pti)])

                        def pv():
                            P.op("pe", lambda e: e.matmul(psO[:], lhsT=vS[:, kt * 128:(kt + 1) * 128], rhs=PT[pti][:, 0:TG], start=(kt == 0), stop=(kt == 15)),
                                 reads=[("PT", pti), ("vS", kt // 4)], writes=[("px", 3)])
                            P.op("pe", lambda e: e.matmul(psD[:], lhsT=ones[:], rhs=PT[pti][:, 0:TG], start=(kt == 0), stop=(kt == 15)),
                                 reads=[("PT", pti), "ones"], writes=[("px", 2)])
                            if kt == 15:
                                fin_a(g, half)
                        pipe.step(qk, pv)

                    for g in range(NTG):
                        for half in range(2):
                            for kt in range(16):
                                a_step(g, half, kt)
                    pipe.flush()
                    pend_out = (ev_out_d[i], hd, oH[ob], ("oH", ob), watt, wb)
                if pend_out is not None:
                    outproj_chunk(*pend_out, load=False)
                    pend_out = None

                for cpair in (() if isA else range(4)):
                    wb = cpair % 2
                    w = watt[wb]
                    wload(w, 0, wview(ev_in_d[i], 0, 8, 1536 + 128 * cpair, 128), 8, 128, ("wa", wb, 0))
                    wload(w, 2048, wview(ev_in_d[i], 0, 8, 2048 + 128 * cpair, 128), 8, 128, ("wa", wb, 2))
                    wload(w, 4096, wview(ev_in_d[i], 0, 8, 2560 + 128 * cpair, 128), 8, 128, ("wa", wb, 4))
                    for (dst, dres, o0) in ((None, "qT", 0), (kT, "kT", 2048)):
                        for g in range(NTG):
                            pa = next_px()
                            proj_fm(psX[pa], w, o0, 128, 128, 8, hfn, g, ("wa", wb, o0 // 1024), hres, ("px", pa))
                            if dst is None:
                                for hf in range(2):
                                    P.op("act", lambda e, pa=pa, hf=hf, g=g: e.activation(out=qTm[hf][64 * hf:64 * hf + 64, g * TG:(g + 1) * TG],
                                                                                      in_=psX[pa][64 * hf:64 * hf + 64, :], func=AF.Copy),
                                         reads=[("px", pa)], writes=[("qT", hf, g)])
                            else:
                                P.op("act", lambda e, pa=pa, dst=dst, g=g: e.activation(out=dst[:, g * TG:(g + 1) * TG], in_=psX[pa][:], func=AF.Copy),
                                     reads=[("px", pa)], writes=[(dres, g)])
                    v_tokmajor(w, 4096, ("wa", wb, 4), 128)
                    if pend_out is not None:
                        outproj_chunk(*pend_out, load=False)
                        pend_out = None
                    ob = cpair % 2
                    outproj_chunk(ev_out_d[i], 4 + cpair, oH[ob], ("oH", ob), watt, wb, compute=False)
                    accs_b = [(psO, ("px", 3), psD, ("px", 2)), (psX[0], ("px", 0), psX[1], ("px", 1))]
                    tl = []
                    for hh in range(2):
                        for m4 in range(4):
                            aset = accs_b[unit_b[0] % 2]
                            unit_b[0] += 1
                            for mm in range(4):
                                m = m4 * 4 + mm
                                tiles = _PLAN[m]
                                for ti_, (t, gid) in enumerate(tiles):
                                    tl.append(dict(hh=hh, h=2 * cpair + hh, m=m, mm=mm, m4=m4, t=t, gid=gid, first=(ti_ == 0),
                                                   last=(ti_ == len(tiles) - 1), aset=aset, fin=(mm == 3 and ti_ == len(tiles) - 1)))
                    pipe = Pipe(3)

                    def fin_b(d, ob=ob):
                        aO, nO, aD, nD = d["aset"]
                        r0 = 64 * d["hh"]
                        m4 = d["m4"]
                        tr = next_tmp()
                        P.op("dve", lambda e: e.reciprocal(out=tmp[tr][r0:r0 + 64, :], in_=aD[r0:r0 + 64, :]),
                             reads=[nD], writes=[("tmp", tr)])
                        P.op("dve", lambda e: e.tensor_tensor(
                            out=oH[ob][r0:r0 + 64, m4 * TG:(m4 + 1) * TG], in0=aO[r0:r0 + 64, :], in1=tmp[tr][r0:r0 + 64, :], op=ALU.mult),
                            reads=[nO, ("tmp", tr)], writes=[(("oH", ob), m4)])

                    def b_chunk(ch):
                        sbi = sbuf_i[0]
                        sbuf_i[0] = (sbi + 1) % 4
                        pti = pt_i[0]
                        pt_i[0] = (pti + 1) % NPT
                        n = len(ch)

                        def qk():
                            for j, d in enumerate(ch):
                                sreg = psS[:, sbi * TG + j * 128: sbi * TG + (j + 1) * 128]
                                goff = (d["h"] * _NG + d["gid"]) * 128
                                P.op("pe", lambda e, sreg=sreg, goff=goff: e.matmul(sreg, lhsT=ident[:], rhs=G[:, goff:goff + 128], start=True, stop=False),
                                     reads=[("G", d["h"]), "ident"], writes=[("psS", sbi)])
                                P.op("pe", lambda e, sreg=sreg, d=d: e.matmul(
                                    sreg, lhsT=kT[:, d["t"] * 128:(d["t"] + 1) * 128], rhs=qTm[d["hh"]][:, d["m"] * 128:(d["m"] + 1) * 128], start=False, stop=True),
                                    reads=[("kT", d["t"] // 4), ("qT", d["hh"], d["m"] // 4)], writes=[("psS", sbi)])
                            P.op("act", lambda e: e.activation(out=PT[pti][:, 0:n * 128], in_=psS[:, sbi * TG: sbi * TG + n * 128], func=AF.Exp, scale=0.125),
                                 reads=[("psS", sbi)], writes=[("PT", pti)])

                        def pv():
                            for j, d in enumerate(ch):
                                aO, nO, aD, nD = d["aset"]
                                mm = d["mm"]
                                P.op("pe", lambda e, j=j, d=d, aO=aO, mm=mm: e.matmul(
                                    aO[:, mm * 128:(mm + 1) * 128], lhsT=vS[:, d["t"] * 128:(d["t"] + 1) * 128], rhs=PT[pti][:, j * 128:(j + 1) * 128],
                                    start=d["first"], stop=d["last"]),
                                    reads=[("PT", pti), ("vS", d["t"] // 4)], writes=[nO])
                                P.op("pe", lambda e, j=j, d=d, aD=aD, mm=mm: e.matmul(
                                    aD[:, mm * 128:(mm + 1) * 128], lhsT=ones[:], rhs=PT[pti][:, j * 128:(j + 1) * 128],
                                    start=d["first"], stop=d["last"]),
                                    reads=[("PT", pti), "ones"], writes=[nD])
                                if d["fin"]:
                                    fin_b(d)
                        pipe.step(qk, pv)

                    for c0 in range(0, len(tl), 4):
                        b_chunk(tl[c0:c0 + 4])
                    pipe.flush()
                    pend_out = (ev_out_d[i], 4 + cpair, oH[ob], ("oH", ob), watt, wb)
                if pend_out is not None:
                    outproj_chunk(*pend_out, load=False)
                    pend_out = None
                P.emit_phase()

        def odd_phase(l):
            i = l // 2
            scale = 1.0 / math.sqrt(96.0)
            with ExitStack() as ps_:
                st["npx"] = 4
                st["px"] = 0
                watt = [sb("watt%d" % k, [128, 2048], BF16, ps_) for k in range(2)]
                win = sb("win", [128, 8 * 416], BF16, ps_)
                winsw = sb("winsw", [128, 8 * 96], BF16, ps_)
                qTt = [sb("qT%d" % k, [128, S], BF16, ps_) for k in range(2)]
                kTt = [sb("kT%d" % k, [128, S], BF16, ps_) for k in range(2)]
                vSt = [sb("vS%d" % k, [128, 16 * 128], BF16, ps_) for k in range(2)]
                oH = [sb("oH%d" % k, [128, S], BF16, ps_) for k in range(2)]
                NPT = 6
                PT = [sb("PT%d" % k, [128, TG], BF16, ps_) for k in range(NPT)]
                ropeC = sb("ropeC", [128, S], BF16, ps_)
                ropeS = sb("ropeS", [128, S], BF16, ps_)
                for k_ in range(2):
                    P.op("pool", lambda e, k_=k_: e.memset(qTt[k_][:], 0.0), writes=[(("qT", k_), g) for g in range(NTG)])
                for k_ in range(2):
                    P.op("pool", lambda e, k_=k_: e.memset(kTt[k_][:], 0.0),
                         writes=[(("kTn", k_), g) for g in range(NTG)] + [(("kTr", k_), g) for g in range(NTG)])
                accs = [(psX[3], ("px", 3)), (psX[2], ("px", 2))]
                unit = [0]
                cqn = sb("cqn", [128, 2 * S], BF16, ps_)
                ckvn = sb("ckvn", [128, S], BF16, ps_)
                pt_i = [0]
                sbuf_i = [0]
                P.op("sp", lambda e: e.dma_start(out=ropeC[:], in_=ropeM_d[0]), writes=["ropeC"], dma_ch="ropeC")
                P.op("sp", lambda e: e.dma_start(out=ropeS[:], in_=ropeM_d[1]), writes=["ropeS"], dma_ch="ropeS")
                for k in range(2):
                    P.op("dve", lambda e, k=k: e.memset(vSt[k][:], 1.0), writes=[("vS", k, 0), ("vS", k, 1)])
                rmsnorm_x(8 * l, hs, lambda c, g: ("h", c, g))
                wload(win, 0, wview(od_in_d[i], 0, 8, 0, 416), 8, 416, "win")
                wload(winsw, 0, wview(od_insw_d[i], 0, 8, 0, 96), 8, 96, "winsw")
                hfn = lambda k, g: hs(k, g)
                hres = lambda k, g: ("h", k, g)
                for g in range(NTG):
                    pq = [next_px(), next_px()]
                    for c2 in range(2):
                        proj_fm(psX[pq[c2]], win, 128 * c2, 416, 128, 8, hfn, g, "win", hres, ("px", pq[c2]))
                    pss = next_px()
                    for c2 in range(2):
                        si = next_sq()
                        P.op("act", lambda e, si=si, p_=pq[c2]: e.activation(out=sq[si][:], in_=psX[p_][:], func=AF.Square),
                             reads=[("px", pq[c2])], writes=[("sq", si)])
                        P.op("pe", lambda e, si=si, c2=c2, pss=pss: e.matmul(psX[pss][:], lhsT=ones[:], rhs=sq[si][:], start=(c2 == 0), stop=(c2 == 1)),
                             reads=[("sq", si), "ones"], writes=[("px", pss)])
                    tr = next_tmp()
                    P.op("act", lambda e, tr=tr, pss=pss: e.activation(out=tmp[tr][:], in_=psX[pss][:], func=AF.Sqrt, scale=1.0 / 256.0, bias=EPS),
                         reads=[("px", pss)], writes=[("tmp", tr)])
                    P.op("dve", lambda e, tr=tr: e.reciprocal(out=tmp[tr][:], in_=tmp[tr][:]), reads=[("tmp", tr)], writes=[("tmp", tr)])
                    for c2 in range(2):
                        P.op("dve", lambda e, tr=tr, c2=c2, g=g, p_=pq[c2]: e.scalar_tensor_tensor(
                            out=cqn[:, c2 * S + g * TG: c2 * S + (g + 1) * TG], in0=psX[p_][:], scalar=vecs[:, 72 + 2 * i + c2: 73 + 2 * i + c2],
                            in1=tmp[tr][:], op0=ALU.mult, op1=ALU.mult),
                            reads=[("px", pq[c2]), ("tmp", tr), "vecs"], writes=[("cqn", c2, g)])
                    pk = next_px()
                    proj_fm(psX[pk], win, 256, 416, 128, 8, hfn, g, "win", hres, ("px", pk))
                    pss = next_px()
                    si = next_sq()
                    P.op("act", lambda e, si=si, pk=pk: e.activation(out=sq[si][:], in_=psX[pk][:], func=AF.Square),
                         reads=[("px", pk)], writes=[("sq", si)])
                    P.op("pe", lambda e, si=si, pss=pss: e.matmul(psX[pss][:], lhsT=ones[:], rhs=sq[si][:], start=True, stop=True),
                         reads=[("sq", si), "ones"], writes=[("px", pss)])
                    tr = next_tmp()
                    P.op("act", lambda e, tr=tr, pss=pss: e.activation(out=tmp[tr][:], in_=psX[pss][:], func=AF.Sqrt, scale=1.0 / 128.0, bias=EPS),
                         reads=[("px", pss)], writes=[("tmp", tr)])
                    P.op("dve", lambda e, tr=tr: e.reciprocal(out=tmp[tr][:], in_=tmp[tr][:]), reads=[("tmp", tr)], writes=[("tmp", tr)])
                    P.op("dve", lambda e, tr=tr, g=g, pk=pk: e.scalar_tensor_tensor(
                        out=ckvn[:, g * TG:(g + 1) * TG], in0=psX[pk][:], scalar=vecs[:, 76 + i: 77 + i], in1=tmp[tr][:], op0=ALU.mult, op1=ALU.mult),
                        reads=[("px", pk), ("tmp", tr), "vecs"], writes=[("ckvn", g)])
                    pa = next_px()
                    proj_fm(psX[pa], win, 320, 416, 96, 8, hfn, g, "win", hres, ("px", pa))
                    pb = next_px()
                    proj_fm(psX[pb], winsw, 0, 96, 96, 8, hfn, g, "winsw", hres, ("px", pb))
                    t1 = next_tmp()
                    t2 = next_tmp()
                    P.op("dve", lambda e, t1=t1, pa=pa, g=g: e.tensor_tensor(out=tmp[t1][64:96, :], in0=psX[pa][64:96, :], in1=ropeC[64:96, g * TG:(g + 1) * TG], op=ALU.mult),
                         reads=[("px", pa), "ropeC"], writes=[("tmp", t1)])
                    P.op("dve", lambda e, t2=t2, pb=pb, g=g: e.tensor_tensor(out=tmp[t2][64:96, :], in0=psX[pb][64:96, :], in1=ropeS[64:96, g * TG:(g + 1) * TG], op=ALU.mult),
                         reads=[("px", pb), "ropeS"], writes=[("tmp", t2)])
                    for k in range(2):
                        P.op("pool", lambda e, t1=t1, t2=t2, g=g, k=k: e.tensor_tensor(
                            out=kTt[k][64:96, g * TG:(g + 1) * TG], in0=tmp[t1][64:96, :], in1=tmp[t2][64:96, :], op=ALU.add),
                            reads=[("tmp", t1), ("tmp", t2)], writes=[(("kTr", k), g)])

                cfn = lambda k, g: cqn[:, k * S + g * TG: k * S + (g + 1) * TG]
                cres = lambda k, g: ("cqn", k, g)
                st["npx"] = 2
                st["px"] = 0

                def proj_head(h):
                    wb = h % 2
                    w = watt[wb]
                    kT = kTt[wb]
                    vS = vSt[wb]
                    qT = qTt[wb]
                    vc0 = 0 if wb == 0 else 64
                    wload(w, 0, wview(od_uq_d[i], 0, 2, 96 * h, 96), 2, 96, ("wa", wb, 0))
                    wload(w, 256, wview(od_uqsw_d[i], 0, 2, 96 * h, 96), 2, 96, ("wa", wb, 1))
                    wload(w, 512, wview(od_ukv_d[i], 0, 1, 128 * h, 128), 1, 128, ("wa", wb, 2))
                    for g in range(NTG):
                        pa = next_px()
                        proj_fm(psX[pa], w, 0, 96, 128, 2, cfn, g, ("wa", wb, 0), cres, ("px", pa))
                        pb = next_px()
                        proj_fm(psX[pb], w, 256, 96, 128, 2, cfn, g, ("wa", wb, 1), cres, ("px", pb))
                        t1 = next_tmp()
                        t2 = next_tmp()
                        P.op("dve", lambda e, t1=t1, pa=pa, g=g: e.tensor_tensor(out=tmp[t1][0:96, :], in0=psX[pa][0:96, :], in1=ropeC[0:96, g * TG:(g + 1) * TG], op=ALU.mult),
                             reads=[("px", pa), "ropeC"], writes=[("tmp", t1)])
                        P.op("dve", lambda e, t2=t2, pb=pb, g=g: e.tensor_tensor(out=tmp[t2][0:96, :], in0=psX[pb][0:96, :], in1=ropeS[0:96, g * TG:(g + 1) * TG], op=ALU.mult),
                             reads=[("px", pb), "ropeS"], writes=[("tmp", t2)])
                        P.op("pool", lambda e, t1=t1, t2=t2, g=g: e.tensor_tensor(out=qT[0:96, g * TG:(g + 1) * TG], in0=tmp[t1][0:96, :], in1=tmp[t2][0:96, :], op=ALU.add),
                             reads=[("tmp", t1), ("tmp", t2)], writes=[(("qT", wb), g)])
                        pk = next_px()
                        P.op("pe", lambda e, pk=pk, g=g: e.matmul(psX[pk][:, :], lhsT=w[:, 512:640], rhs=ckvn[:, g * TG:(g + 1) * TG], start=True, stop=True),
                             reads=[("wa", wb, 2), ("ckvn", g)], writes=[("px", pk)])
                        P.op("act", lambda e, pk=pk, g=g: e.activation(out=kT[0:64, g * TG:(g + 1) * TG], in_=psX[pk][0:64, :], func=AF.Copy),
                             reads=[("px", pk)], writes=[(("kTn", wb), g)])
                    for t8 in range(2):
                        px = next_px()
                        ps = psX[px]
                        for tt in range(8):
                            tok = (t8 * 8 + tt) * 128
                            P.op("pe", lambda e, tt=tt, tok=tok, ps=ps: e.matmul(
                                ps[:, tt * 64:(tt + 1) * 64], lhsT=ckvn[:, tok:tok + 128], rhs=w[:, 576:640], start=True, stop=True),
                                reads=[("wa", wb, 2), ("ckvn", tok // TG)], writes=[("px", px)])
                        dstv = vS[:, t8 * 1024:(t8 + 1) * 1024].rearrange("p (t c) -> p t c", c=128)[:, :, vc0:vc0 + 64]
                        srcv = ps[:, :].rearrange("p (t c) -> p t c", c=64)
                        P.op("act", lambda e, dstv=dstv, srcv=srcv: e.activation(out=dstv, in_=srcv, func=AF.Copy),
                             reads=[("px", px)], writes=[("vS", wb, t8)])

                def attn_head(h):
                    wb = h % 2
                    kT = kTt[wb]
                    vS = vSt[wb]
                    qT = qTt[wb]
                    ob = (h // 2) % 2
                    orow = 64 * (h % 2)
                    drow = 64 - orow
                    pipe = Pipe(3)

                    def fin_m(g, aO, nO):
                        tr = next_tmp()
                        P.op("dve", lambda e: e.reciprocal(out=tmp[tr][drow:drow + 64, :], in_=aO[drow:drow + 64, :]),
                             reads=[nO], writes=[("tmp", tr)])
                        P.op("dve", lambda e: e.tensor_tensor(
                            out=oH[ob][orow:orow + 64, g * TG:(g + 1) * TG], in0=aO[orow:orow + 64, :], in1=tmp[tr][drow:drow + 64, :], op=ALU.mult),
                            reads=[nO, ("tmp", tr)], writes=[(("oH", ob), g)])

                    def m_step(g, kt, aO, nO):
                        sbi = sbuf_i[0]
                        sbuf_i[0] = (sbi + 1) % 4
                        pti = pt_i[0]
                        pt_i[0] = (pti + 1) % NPT
                        sreg = psS[:, sbi * TG:(sbi + 1) * TG]

                        def qk():
                            P.op("pe", lambda e: e.matmul(sreg, lhsT=kT[:, kt * 128:(kt + 1) * 128], rhs=qT[:, g * TG:(g + 1) * TG], start=True, stop=True),
                                 reads=[(("kTn", wb), kt // 4), (("kTr", wb), kt // 4), (("qT", wb), g)], writes=[("psS", sbi)])
                            P.op("act", lambda e: e.activation(out=PT[pti][:, 0:TG], in_=sreg, func=AF.Exp, scale=scale),
                                 reads=[("psS", sbi)], writes=[("PT", pti)])

                        def pv():
                            P.op("pe", lambda e: e.matmul(aO, lhsT=vS[:, kt * 128:(kt + 1) * 128], rhs=PT[pti][:, 0:TG], start=(kt == 0), stop=(kt == 15)),
                                 reads=[("PT", pti), ("vS", wb, kt // 8)], writes=[nO])
                            if kt == 15:
                                fin_m(g, aO, nO)
                        pipe.step(qk, pv)

                    for g in range(NTG):
                        aO, nO = accs[unit[0] % 2]
                        unit[0] += 1
                        for kt in range(16):
                            m_step(g, kt, aO, nO)
                    pipe.flush()
                    if h % 2 == 1:
                        outproj_chunk(od_out_d[i], h // 2, oH[ob], ("oH", ob), watt, wb, wo=1024)

                proj_head(0)
                for h in range(16):
                    if h + 1 < 16:
                        proj_head(h + 1)
                    attn_head(h)
                P.emit_phase()

        for s_ in range(nseq):
            for c in range(NCH):
                P.op("sp", lambda e, c=c, s_=s_: e.dma_start(out=xT[:, c * S:(c + 1) * S], in_=xT_d[s_][c * 128:(c + 1) * 128, :]),
                     writes=[("x", c, g) for g in range(NTG)], dma_ch=("xin", c))
            P.emit_phase()
            for l in layers:
                if do_mix:
                    if l % 2 == 0:
                        first = True
                        for part in parts:
                            even_phase(l, part, first)
                            first = False
                    else:
                        odd_phase(l)
                if do_mlp:
                    mlp_phase(l)
            with ExitStack() as ps_:
                st["npx"] = 2
                ob = [sb("outb%d" % k, [128, S], F32, ps_) for k in range(2)]
                rs = [sb("rs%d" % k, [128, TG], F32, ps_) for k in range(NTG)]
                for g in range(NTG):
                    px = next_px()
                    ps = psX[px]
                    for c in range(NCH):
                        si = next_sq()
                        P.op("act", lambda e, si=si, c=c, g=g: e.activation(out=sq[si][:], in_=xs(c, g), func=AF.Square),
                             reads=[("x", c, g)], writes=[("sq", si)])
                        P.op("pe", lambda e, si=si, c=c, ps=ps: e.matmul(ps[:], lhsT=ones[:], rhs=sq[si][:], start=(c == 0), stop=(c == NCH - 1)),
                             reads=[("sq", si), "ones"], writes=[("px", px)])
                    P.op("act", lambda e, g=g, ps=ps: e.activation(out=rs[g][:], in_=ps[:], func=AF.Sqrt, scale=1.0 / D, bias=EPS),
                         reads=[("px", px)], writes=[("rs", g)])
                    P.op("dve", lambda e, g=g: e.reciprocal(out=rs[g][:], in_=rs[g][:]), reads=[("rs", g)], writes=[("rs", g)])
                for c in range(NCH):
                    k = c % 2
                    for g in range(NTG):
                        P.op("dve", lambda e, c=c, g=g, k=k: e.scalar_tensor_tensor(
                            out=ob[k][:, g * TG:(g + 1) * TG], in0=xs(c, g), scalar=vecs[:, 64 + c:65 + c], in1=rs[g][:], op0=ALU.mult, op1=ALU.mult),
                            reads=[("x", c, g), ("rs", g), "vecs"], writes=[("ob", k)])
                    P.op("sp", lambda e, c=c, k=k, s_=s_: e.dma_start(out=out_d[s_][c * 128:(c + 1) * 128, :], in_=ob[k][:]),
                         reads=[("ob", k)], dma_ch=("out", k))
                P.emit_phase(final=(s_ == nseq - 1))
    return nc


def prep_shared(inp):
    f = lambda a: np.ascontiguousarray(np.asarray(a, dtype=np.float32))
    vecs = np.zeros((128, 80), np.float32)
    for l in range(4):
        vecs[:, 8 * l:8 * l + 8] = f(inp["ln_mix_g"])[l].reshape(8, 128).T
        vecs[:, 32 + 8 * l:32 + 8 * l + 8] = f(inp["ln_mlp_g"])[l].reshape(8, 128).T
    vecs[:, 64:72] = f(inp["ln_f_g"]).reshape(8, 128).T
    for i in range(2):
        vecs[:, 72 + 2 * i:74 + 2 * i] = f(inp["od_q_norm_g"])[i].reshape(2, 128).T
        vecs[:, 76 + i] = f(inp["od_kv_norm_g"])[i]
        vecs[:, 78 + i] = f(inp["ev_subln_g"])[i]
    lamv = np.zeros((128, 512), np.float32)
    for i in range(2):
        for j, nm in enumerate(("ev_lambda_q1", "ev_lambda_k1", "ev_lambda_q2", "ev_lambda_k2")):
            lamv[:, i * 256 + j * 64: i * 256 + (j + 1) * 64] = f(inp[nm])[i][None, :]
    ev_in = f(inp["ev_w_in"])
    permA = _swap_perm(1024, 64, 0, 8)
    ev_sw = np.ascontiguousarray(ev_in[:, :, :1024][:, :, permA])
    od_in = f(inp["od_w_in"])
    perm_kr = _swap_perm(32, 32, 0, 16)
    od_in_sw = np.ascontiguousarray(od_in[:, :, 320:416].copy())
    od_in_sw[:, :, 64:96] = od_in[:, :, 384:416][:, :, perm_kr]
    od_uq = f(inp["od_w_uq"])
    perm_q = _swap_perm(1536, 96, 64, 16)
    od_uq_sw = np.ascontiguousarray(od_uq[:, :, perm_q])
    CA, SA = _rope_tables_A()
    CM, SM = _rope_tables_M()
    rpb = f(inp["ev_rpb"])
    gb = rpb[:, :, _RIDX, _CIDX]
    gb = np.ascontiguousarray(gb.transpose(0, 3, 1, 2, 4)).reshape(2, 128, 8 * _NG * 128)
    gmask = np.where(_VALID, 0.0, MASKV).astype(np.float32)
    gmask = np.ascontiguousarray(gmask.transpose(1, 0, 2)).reshape(128, _NG * 128).astype(ml_dtypes.bfloat16)
    return {
        "vecs": vecs, "lamv": lamv,
        "w_up": f(inp["w_up"]), "w_down": f(inp["w_down"]),
        "ev_w_in": ev_in, "ev_w_sw": ev_sw, "ev_w_out": f(inp["ev_w_out"]),
        "od_w_in": od_in, "od_w_in_sw": od_in_sw, "od_w_uq": od_uq, "od_w_uq_sw": od_uq_sw,
        "od_w_ukv": f(inp["od_w_ukv"]), "od_w_out": f(inp["od_w_out"]),
        "ropeA": np.stack([CA, SA]).astype(ml_dtypes.bfloat16), "ropeM": np.stack([CM, SM]).astype(ml_dtypes.bfloat16),
        "gb": gb, "gmask": gmask,
        "ident": np.eye(128, dtype=np.float32).astype(ml_dtypes.bfloat16),
    }


_NC_CACHE = {}


def kernel(**inputs):
    x = np.asarray(inputs["x"], dtype=np.float32)
    B = x.shape[0]
    nseq = B // NCORES
    shared = prep_shared(inputs)
    key = nseq
    if key not in _NC_CACHE:
        _NC_CACHE[key] = build(nseq=nseq)
    nc = _NC_CACHE[key]
    in_maps = []
    for c in range(NCORES):
        xs_ = x[c * nseq:(c + 1) * nseq]
        m = dict(shared)
        m["xT"] = np.ascontiguousarray(xs_.transpose(0, 2, 1))
        in_maps.append(m)
    res = run_bass_kernel_spmd(nc, in_maps, core_ids=list(range(NCORES)))
    out = np.empty((B, S, D), np.float32)
    for c in range(NCORES):
        o = np.asarray(res.results[c]["outT"])
        out[c * nseq:(c + 1) * nseq] = o.transpose(0, 2, 1)
    return out
```

```python
import math
from contextlib import ExitStack

import numpy as np
import ml_dtypes
import concourse.bass as bass
import concourse.mybir as mybir
from concourse.bass_utils import run_bass_kernel_spmd

F32 = mybir.dt.float32
BF16 = mybir.dt.bfloat16
AF = mybir.ActivationFunctionType
ALU = mybir.AluOpType
AX = mybir.AxisListType

S = 2048
D = 1024
NCH = 8
NTG = 4
TG = 512
EPS = 1e-5
NCORES = 8
ENGS = ("pe", "act", "dve", "pool", "sp")


class Op:
    __slots__ = ("eng", "fn", "deps", "dma_ch", "sig", "idx", "has_consumer", "waits")

    def __init__(self, eng, fn, dma_ch=None):
        self.eng = eng
        self.fn = fn
        self.deps = set()
        self.dma_ch = dma_ch
        self.sig = None
        self.has_consumer = False
        self.waits = []


class Prog:
    def __init__(self, nc, es):
        self.nc = nc
        self.es = es
        self.sems = {}
        self.cnt = {e: 0 for e in ENGS}
        self.dcnt = {}
        self.reset_phase()
        self.nphase = 0

    def reset_phase(self):
        self.ops = []
        self.last_writer = {}
        self.readers = {}

    def sem(self, key):
        s = self.sems.get(key)
        if s is None:
            name = "s_" + "_".join(str(k) for k in key)
            s = self.es.enter_context(self.nc.semaphore(name))
            self.sems[key] = s
        return s

    def op(self, eng, fn, reads=(), writes=(), dma_ch=None):
        o = Op(eng, fn, dma_ch)
        o.idx = len(self.ops)
        deps = set()
        for r in reads:
            w = self.last_writer.get(r)
            if w is not None:
                deps.add(w)
        for r in writes:
            w = self.last_writer.get(r)
            if w is not None:
                deps.add(w)
            for rd in self.readers.get(r, ()):
                deps.add(rd)
        o.deps = deps
        self.ops.append(o)
        for r in reads:
            self.readers.setdefault(r, []).append(o.idx)
        for r in writes:
            self.last_writer[r] = o.idx
            self.readers[r] = []
        return o

    def emit_phase(self, final=False):
        ops = self.ops
        if not ops:
            return
        nc = self.nc
        for o in ops:
            best = {}
            for d in o.deps:
                p = ops[d]
                if p.dma_ch is not None:
                    key = ("dma", p.dma_ch)
                elif p.eng != o.eng or o.eng in ("act", "dve", "pool"):
                    key = ("eng", p.eng)
                else:
                    continue
                if d > best.get(key, -1):
                    best[key] = d
            keep = set(best.values())
            o.deps = keep
            for d in keep:
                ops[d].has_consumer = True
        last = {}
        for o in ops:
            if o.dma_ch is None:
                last[o.eng] = o
        for o in last.values():
            o.has_consumer = True
        prev_cnt = dict(self.cnt)
        prev_dcnt = dict(self.dcnt)
        for o in ops:
            if o.dma_ch is not None:
                self.dcnt[o.dma_ch] = self.dcnt.get(o.dma_ch, 0) + 16
                o.sig = ("dma", o.dma_ch, self.dcnt[o.dma_ch])
            elif o.has_consumer:
                self.cnt[o.eng] += 1
                o.sig = ("eng", o.eng, self.cnt[o.eng])
        waited = {e: {} for e in ENGS}
        for e in ENGS:
            for e2 in ENGS:
                if prev_cnt[e2] > 0:
                    waited[e][("eng", e2)] = prev_cnt[e2]
            for ch, v in prev_dcnt.items():
                waited[e][("dma", ch)] = v
        for o in ops:
            need = {}
            for d in o.deps:
                s = ops[d].sig
                key = (s[0], s[1])
                if s[2] > need.get(key, 0):
                    need[key] = s[2]
            for key, v in need.items():
                if waited[o.eng].get(key, 0) >= v:
                    continue
                waited[o.eng][key] = v
                o.waits.append((key, v))
        by_eng = {e: [o for o in ops if o.eng == e] for e in ENGS}
        sems = self.sem
        final_d = dict(self.dcnt)
        final_c = dict(self.cnt)

        def run(eng_name, eng):
            for e2 in ENGS:
                if e2 != eng_name and prev_cnt[e2] > 0:
                    eng.wait_ge(sems(("eng", e2)), prev_cnt[e2])
            for ch, v in prev_dcnt.items():
                eng.wait_ge(sems(("dma", ch)), v)
            for o in by_eng[eng_name]:
                for key, v in o.waits:
                    eng.wait_ge(sems(key), v)
                ins = o.fn(eng)
                if o.dma_ch is not None:
                    ins.then_inc(sems(("dma", o.dma_ch)), 16)
                elif o.sig is not None:
                    ins.then_inc(sems(("eng", eng_name)), 1)
            if final:
                for ch, v in final_d.items():
                    eng.wait_ge(sems(("dma", ch)), v)
                for e2 in ENGS:
                    if e2 != eng_name and final_c[e2] > 0:
                        eng.wait_ge(sems(("eng", e2)), final_c[e2])

        for e in ENGS:
            sems(("eng", e))
        for ch in self.dcnt:
            sems(("dma", ch))
        with nc.Block() as block:
            @block.sync
            def _(e):
                run("sp", e)

            @block.tensor
            def _(e):
                run("pe", e)

            @block.scalar
            def _(e):
                run("act", e)

            @block.vector
            def _(e):
                run("dve", e)

            @block.gpsimd
            def _(e):
                run("pool", e)
        self.nphase += 1
        self.reset_phase()


class Pipe:
    def __init__(self, look):
        self.look = look
        self.q = []

    def step(self, qk, pv):
        qk()
        self.q.append(pv)
        if len(self.q) > self.look:
            self.q.pop(0)()

    def flush(self):
        while self.q:
            self.q.pop(0)()


def _rot_tables(rot_dim, theta=500000.0):
    pos = np.arange(S, dtype=np.float32)
    inv = (np.float32(theta) ** (-np.arange(0, rot_dim, 2, dtype=np.float32) / np.float32(rot_dim))).astype(np.float32)
    ang = (pos[:, None] * inv[None, :]).astype(np.float32)
    return np.cos(ang).astype(np.float32), np.sin(ang).astype(np.float32)


def _rope_tables_A():
    cos, sin = _rot_tables(16)
    C = np.ones((128, S), np.float32)
    Sg = np.zeros((128, S), np.float32)
    for base in (0, 64):
        for d in range(8):
            C[base + d] = cos[:, d]
            Sg[base + d] = -sin[:, d]
            C[base + 8 + d] = cos[:, d]
            Sg[base + 8 + d] = sin[:, d]
    return C, Sg


def _rope_tables_M():
    cos, sin = _rot_tables(32)
    C = np.ones((128, S), np.float32)
    Sg = np.zeros((128, S), np.float32)
    for d in range(16):
        C[64 + d] = cos[:, d]
        Sg[64 + d] = -sin[:, d]
        C[64 + 16 + d] = cos[:, d]
        Sg[64 + 16 + d] = sin[:, d]
    return C, Sg


def _swap_perm(n, head, rot_off, half):
    idx = np.arange(n)
    for h0 in range(0, n, head):
        for d in range(half):
            idx[h0 + rot_off + d] = h0 + rot_off + half + d
            idx[h0 + rot_off + half + d] = h0 + rot_off + d
    return idx


def _nbr_plan():
    rows, W, wr, wc = 32, 64, 8, 16
    gids = {}
    plan = []
    for m in range(16):
        need = {}
        for b in range(2):
            qr = 2 * m + b
            r0 = min(max(qr - wr // 2, 0), rows - wr)
            for kr in range(r0, r0 + wr):
                t, a = kr // 2, kr % 2
                need.setdefault(t, set()).add((a, b))
        lst = []
        for t in sorted(need):
            key = (2 * t - 2 * m, tuple(sorted(need[t])))
            if key not in gids:
                gids[key] = len(gids)
            lst.append((t, gids[key]))
        plan.append(lst)
    ng = len(gids)
    ridx = np.zeros((ng, 128, 128), np.int64)
    cidx = np.zeros((ng, 128, 128), np.int64)
    valid = np.zeros((ng, 128, 128), bool)
    c = np.arange(W)
    cs = np.clip(c - wc // 2, 0, W - wc)
    colmask = (c[None, :] >= cs[:, None]) & (c[None, :] < cs[:, None] + wc)
    for (delta, pat), g in gids.items():
        for a in range(2):
            for b in range(2):
                dr = delta + a - b + 7
                ok = (a, b) in pat
                for kc in range(W):
                    k = a * 64 + kc
                    q = b * 64 + c
                    ridx[g, k, q] = min(max(dr, 0), 14)
                    cidx[g, k, q] = np.clip(kc - c, -15, 15) + 15
                    valid[g, k, q] = ok & colmask[c, kc]
    return plan, ng, ridx, cidx, valid


_PLAN, _NG, _RIDX, _CIDX, _VALID = _nbr_plan()
MASKV = -30000.0


def build(nseq=4, layers=(0, 1, 2, 3), do_mlp=True, do_mix=True, parts="AB", dbg=False):
    nc = bass.Bass("TRN2", target_bir_lowering=False)
    dt = nc.dram_tensor
    xT_d = dt("xT", [nseq, D, S], F32, kind="ExternalInput").ap()
    out_d = dt("outT", [nseq, D, S], F32, kind="ExternalOutput").ap()
    vecs_d = dt("vecs", [128, 80], F32, kind="ExternalInput").ap()
    lam_d = dt("lamv", [128, 2 * 4 * 64], F32, kind="ExternalInput").ap()
    w_up_d = dt("w_up", [4, D, 4096], F32, kind="ExternalInput").ap()
    w_dn_d = dt("w_down", [4, 4096, D], F32, kind="ExternalInput").ap()
    ev_in_d = dt("ev_w_in", [2, D, 3072], F32, kind="ExternalInput").ap()
    ev_sw_d = dt("ev_w_sw", [2, D, 1024], F32, kind="ExternalInput").ap()
    ev_out_d = dt("ev_w_out", [2, D, D], F32, kind="ExternalInput").ap()
    od_in_d = dt("od_w_in", [2, D, 416], F32, kind="ExternalInput").ap()
    od_insw_d = dt("od_w_in_sw", [2, D, 96], F32, kind="ExternalInput").ap()
    od_uq_d = dt("od_w_uq", [2, 256, 1536], F32, kind="ExternalInput").ap()
    od_uqsw_d = dt("od_w_uq_sw", [2, 256, 1536], F32, kind="ExternalInput").ap()
    od_ukv_d = dt("od_w_ukv", [2, 128, 2048], F32, kind="ExternalInput").ap()
    od_out_d = dt("od_w_out", [2, D, D], F32, kind="ExternalInput").ap()
    ropeA_d = dt("ropeA", [2, 128, S], BF16, kind="ExternalInput").ap()
    ropeM_d = dt("ropeM", [2, 128, S], BF16, kind="ExternalInput").ap()
    gb_d = dt("gb", [2, 128, 8 * _NG * 128], F32, kind="ExternalInput").ap()
    gmask_d = dt("gmask", [128, _NG * 128], BF16, kind="ExternalInput").ap()
    ident_d = dt("ident", [128, 128], BF16, kind="ExternalInput").ap()

    if dbg:
        dbg_d = {nm: dt("dbg_" + nm, [128, S], BF16, kind="ExternalOutput").ap() for nm in ("q", "k", "v", "o")}
    es = ExitStack()
    with es:
        uniq = [0]

        def sb(name, shape, dty, stack=es):
            uniq[0] += 1
            return stack.enter_context(nc.sbuf_tensor("%s_u%d" % (name, uniq[0]), shape, dty))

        P = Prog(nc, es)
        xT = sb("xT_s", [128, NCH * S], F32)
        hT = sb("hT_s", [128, NCH * S], BF16)
        vecs = sb("vecs_s", [128, 80], F32)
        lams = sb("lams", [128, 16], F32)
        ones = sb("ones", [128, 128], BF16)
        ident = sb("ident_s", [128, 128], BF16)
        NSTG = 2
        STGW = 1536
        stage = [sb("stage%d" % i, [128, STGW], F32) for i in range(NSTG)]
        NTMP = 5
        tmp = [sb("tmp%d" % i, [128, TG], F32) for i in range(NTMP)]
        sq = [sb("sq%d" % i, [128, TG], BF16) for i in range(2)]
        psS = es.enter_context(nc.psum_tensor("psS", [128, 2048], F32))
        psX0 = es.enter_context(nc.psum_tensor("psX0", [128, TG], F32))
        psX1 = es.enter_context(nc.psum_tensor("psX1", [128, TG], F32))
        psO = es.enter_context(nc.psum_tensor("psO", [128, TG], F32))
        psD = es.enter_context(nc.psum_tensor("psD", [128, TG], F32))
        psX = [psX0[:, :], psX1[:, :], psD[:, :], psO[:, :]] + [psS[:, k * TG:(k + 1) * TG] for k in range(4)]

        st = {"stg": 0, "tmp": 0, "sq": 0, "px": 0, "npx": 2}

        def xs(c, g):
            return xT[:, c * S + g * TG: c * S + (g + 1) * TG]

        def hs(c, g):
            return hT[:, c * S + g * TG: c * S + (g + 1) * TG]

        def next_tmp():
            i = st["tmp"]
            st["tmp"] = (i + 1) % NTMP
            return i

        def next_px():
            i = st["px"] % st["npx"]
            st["px"] = (i + 1) % st["npx"]
            return i

        def next_sq():
            i = st["sq"]
            st["sq"] = (i + 1) % 2
            return i

        def wload(dst, dst_off, src3, kc, n, dst_res):
            per = max(1, STGW // n)
            k0 = 0
            while k0 < kc:
                k1 = min(kc, k0 + per)
                tot = (k1 - k0) * n
                slot = st["stg"]
                st["stg"] = (slot + 1) % NSTG
                stg = stage[slot]
                src = src3[:, k0:k1, :]
                dview = stg[:, 0:tot].rearrange("p (k n) -> p k n", k=k1 - k0)
                P.op("sp", lambda e, dview=dview, src=src: e.dma_start(out=dview, in_=src),
                     writes=[("stg", slot)], dma_ch=("stg", slot))
                o0 = dst_off + k0 * n
                P.op("pool", lambda e, stg=stg, o0=o0, tot=tot: e.tensor_copy(out=dst[:, o0:o0 + tot], in_=stg[:, 0:tot]),
                     reads=[("stg", slot)], writes=[dst_res])
                k0 = k1

        def wview(w2d, r0, nk, c0, n):
            return w2d[r0:r0 + nk * 128, c0:c0 + n].rearrange("(k p) n -> p k n", p=128)

        def rmsnorm_x(gcol0, dst_fn, dst_res_fn, out_f32=False):
            for g in range(NTG):
                px = next_px()
                ps = psX[px]
                for c in range(NCH):
                    if c % 2 == 0:
                        P.op("act", lambda e, c=c, g=g: e.activation(out=hs(c, g), in_=xs(c, g), func=AF.Square),
                             reads=[("x", c, g)], writes=[("h", c, g)])
                    else:
                        P.op("dve", lambda e, c=c, g=g: e.tensor_tensor(out=hs(c, g), in0=xs(c, g), in1=xs(c, g), op=ALU.mult),
                             reads=[("x", c, g)], writes=[("h", c, g)])
                    P.op("pe", lambda e, c=c, g=g, ps=ps: e.matmul(ps[:], lhsT=ones[:], rhs=hs(c, g), start=(c == 0), stop=(c == NCH - 1)),
                         reads=[("h", c, g), "ones"], writes=[("px", px)])
                ti = next_tmp()
                P.op("act", lambda e, ti=ti, ps=ps: e.activation(out=tmp[ti][:], in_=ps[:], func=AF.Sqrt, scale=1.0 / D, bias=EPS),
                     reads=[("px", px)], writes=[("tmp", ti)])
                P.op("dve", lambda e, ti=ti: e.reciprocal(out=tmp[ti][:], in_=tmp[ti][:]),
                     reads=[("tmp", ti)], writes=[("tmp", ti)])
                for c in range(NCH):
                    P.op("dve", lambda e, ti=ti, c=c, g=g: e.scalar_tensor_tensor(
                        out=dst_fn(c, g), in0=xs(c, g), scalar=vecs[:, gcol0 + c:gcol0 + c + 1], in1=tmp[ti][:],
                        op0=ALU.mult, op1=ALU.mult),
                        reads=[("x", c, g), ("tmp", ti), "vecs"], writes=[dst_res_fn(c, g)])

        init_es = ExitStack()
        lamt = sb("lamt", [128, 512], F32, init_es)
        lamw = sb("lamw", [128, 512], F32, init_es)
        P.op("sp", lambda e: e.dma_start(out=vecs[:], in_=vecs_d[:, :]), writes=["vecs"], dma_ch="vecs")
        P.op("sp", lambda e: e.dma_start(out=lamt[:], in_=lam_d[:, :]), writes=["lamt"], dma_ch="lamt")
        P.op("sp", lambda e: e.dma_start(out=ident[:], in_=ident_d[:, :]), writes=["ident"], dma_ch="ident")
        P.op("dve", lambda e: e.memset(ones[:], 1.0), writes=["ones"])
        for i in range(2):
            b = i * 256
            for j in range(2):
                P.op("dve", lambda e, b=b, j=j: e.tensor_tensor(out=lamw[:, b + 64 * j: b + 64 * j + 64], in0=lamt[:, b + 128 * j: b + 128 * j + 64],
                                                             in1=lamt[:, b + 128 * j + 64: b + 128 * j + 128], op=ALU.mult),
                     reads=["lamt"], writes=[("lamw", i, j)])
                P.op("dve", lambda e, b=b, j=j, i=i: e.reduce_sum(out=lams[:, 4 * i + j: 4 * i + j + 1], in_=lamw[:, b + 64 * j: b + 64 * j + 64], axis=AX.X),
                     reads=[("lamw", i, j)], writes=[("lams", i, j)])
                P.op("act", lambda e, j=j, i=i: e.activation(out=lams[:, 4 * i + j: 4 * i + j + 1], in_=lams[:, 4 * i + j: 4 * i + j + 1], func=AF.Exp),
                     reads=[("lams", i, j)], writes=[("lams", i, j)])
            lam_init = 0.8 - 0.6 * math.exp(-0.3 * (2 * i))
            P.op("dve", lambda e, i=i, lam_init=lam_init: e.scalar_tensor_tensor(
                out=lams[:, 4 * i + 2: 4 * i + 3], in0=lams[:, 4 * i + 1: 4 * i + 2], scalar=-lam_init, in1=lams[:, 4 * i: 4 * i + 1],
                op0=ALU.add, op1=ALU.subtract),
                reads=[("lams", i, 0), ("lams", i, 1)], writes=[("lams", i, 2)])
        P.emit_phase()
        init_es.close()

        def mlp_phase(l):
            with ExitStack() as ps_:
                st["npx"] = 8
                wm = [sb("wmlp%d" % i, [128, 8192], BF16, ps_) for i in range(2)]
                uT = sb("uT", [128, 4 * S], BF16, ps_)
                rmsnorm_x(32 + 8 * l, hs, lambda c, g: ("h", c, g))
                for part in range(8):
                    wb = part % 2
                    w = wm[wb]
                    wload(w, 0, wview(w_up_d[l], 0, 8, part * 512, 512), 8, 512, ("wm", wb, 0))
                    wload(w, 4096, wview(w_dn_d[l], part * 512, 4, 0, 1024), 4, 1024, ("wm", wb, 1))
                    for g in range(NTG):
                        for j in range(4):
                            px = next_px()
                            ps = psX[px]
                            for c in range(NCH):
                                P.op("pe", lambda e, w=w, c=c, j=j, g=g, ps=ps: e.matmul(
                                    ps[:], lhsT=w[:, c * 512 + j * 128: c * 512 + j * 128 + 128], rhs=hs(c, g),
                                    start=(c == 0), stop=(c == NCH - 1)),
                                    reads=[("wm", wb, 0), ("h", c, g)], writes=[("px", px)])
                            ti = next_tmp()
                            P.op("act", lambda e, ti=ti, ps=ps: e.activation(out=tmp[ti][:], in_=ps[:], func=AF.Relu),
                                 reads=[("px", px)], writes=[("tmp", ti)])
                            P.op("pool", lambda e, ti=ti, j=j, g=g: e.tensor_tensor(
                                out=uT[:, j * S + g * TG: j * S + (g + 1) * TG], in0=tmp[ti][:], in1=tmp[ti][:], op=ALU.mult),
                                reads=[("tmp", ti)], writes=[("u", j, g)])
                    for g in range(NTG):
                        for c in range(NCH):
                            px = next_px()
                            ps = psX[px]
                            for j in range(4):
                                P.op("pe", lambda e, w=w, c=c, j=j, g=g, ps=ps: e.matmul(
                                    ps[:], lhsT=w[:, 4096 + j * 1024 + c * 128: 4096 + j * 1024 + c * 128 + 128],
                                    rhs=uT[:, j * S + g * TG: j * S + (g + 1) * TG], start=(j == 0), stop=(j == 3)),
                                    reads=[("wm", wb, 1), ("u", j, g)], writes=[("px", px)])
                            P.op("dve", lambda e, c=c, g=g, ps=ps: e.tensor_tensor(out=xs(c, g), in0=ps[:], in1=xs(c, g), op=ALU.add),
                                 reads=[("px", px), ("x", c, g)], writes=[("x", c, g)])
                P.emit_phase()

        def proj_fm(ps, wt, woff, wstride, m, nk, rhs_fn, g, wres, rres_fn, pxres, prow0=0):
            for k in range(nk):
                P.op("pe", lambda e, k=k: e.matmul(ps[prow0:prow0 + m, :], lhsT=wt[:, woff + k * wstride: woff + k * wstride + m],
                                                  rhs=rhs_fn(k, g), start=(k == 0), stop=(k == nk - 1)),
                     reads=[wres, rres_fn(k, g)], writes=[pxres])

        def outproj_chunk(wout_d2, kc, oH, ores, watt, wb, wo=5120, load=True, compute=True):
            if load:
                wload(watt[wb], wo, wview(wout_d2, kc * 128, 1, 0, 1024), 1, 1024, ("wa", wb, 5))
            if not compute:
                return
            for g in range(NTG):
                for c in range(NCH):
                    px = next_px()
                    ps = psX[px]
                    P.op("pe", lambda e, c=c, g=g, ps=ps: e.matmul(ps[:], lhsT=watt[wb][:, wo + c * 128: wo + c * 128 + 128],
                                                                   rhs=oH[:, g * TG:(g + 1) * TG], start=True, stop=True),
                         reads=[("wa", wb, 5), (ores, g)], writes=[("px", px)])
                    P.op("dve", lambda e, c=c, g=g, ps=ps: e.tensor_tensor(out=xs(c, g), in0=ps[:], in1=xs(c, g), op=ALU.add),
                         reads=[("px", px), ("x", c, g)], writes=[("x", c, g)])

        def even_phase(l, part, do_norm):
            i = l // 2
            lam_init = 0.8 - 0.6 * math.exp(-0.3 * l)
            isA = part == "A"
            with ExitStack() as ps_:
                st["npx"] = 2
                st["px"] = 0
                watt = [sb("watt%d" % k, [128, 6144], BF16, ps_) for k in range(2)]
                qTm = [sb("qTm%d" % k, [128, S], BF16, ps_) for k in range(2)]
                kT = sb("kT", [128, S], BF16, ps_)
                vS = sb("vS", [128, 16 * 128], BF16, ps_)
                oH = [sb("oH%d" % k, [128, S], BF16, ps_) for k in range(2)]
                NPT = 6
                PT = [sb("PT%d" % k, [128, 512], BF16, ps_) for k in range(NPT)]
                unit_b = [0]
                unit_a = [0]
                for k_ in range(2):
                    P.op("pool", lambda e, k_=k_: e.memset(qTm[k_][:], 0.0), writes=[("qT", k_, g) for g in range(NTG)])
                pt_i = [0]
                sbuf_i = [0]
                if isA:
                    ropeC = sb("ropeC", [128, S], BF16, ps_)
                    ropeS = sb("ropeS", [128, S], BF16, ps_)
                    NAT = 8
                    atmp = [sb("atmp%d" % k, [128, TG], F32, ps_) for k in range(NAT)]
                    at_i = [0]
                    P.op("sp", lambda e: e.dma_start(out=ropeC[:], in_=ropeA_d[0]), writes=["ropeC"], dma_ch="ropeC")
                    P.op("sp", lambda e: e.dma_start(out=ropeS[:], in_=ropeA_d[1]), writes=["ropeS"], dma_ch="ropeS")
                else:
                    G = sb("G", [128, 8 * _NG * 128], BF16, ps_)
                    gmask = sb("gmask", [128, _NG * 128], BF16, ps_)
                    P.op("sp", lambda e: e.dma_start(out=gmask[:], in_=gmask_d[:, :]), writes=["gmask"], dma_ch="gmask")
                    GW = _NG * 128
                    for h in range(8):
                        slot = st["stg"]
                        st["stg"] = (slot + 1) % NSTG
                        stg = stage[slot]
                        P.op("sp", lambda e, stg=stg, h=h: e.dma_start(out=stg[:, 0:GW], in_=gb_d[i][:, h * GW:(h + 1) * GW]),
                             writes=[("stg", slot)], dma_ch=("stg", slot))
                        P.op("dve", lambda e, stg=stg, h=h: e.scalar_tensor_tensor(
                            out=G[:, h * GW:(h + 1) * GW], in0=stg[:, 0:GW], scalar=8.0, in1=gmask[:], op0=ALU.mult, op1=ALU.add),
                            reads=[("stg", slot), "gmask"], writes=[("G", h)])

                if do_norm:
                    rmsnorm_x(8 * l, hs, lambda c, g: ("h", c, g))

                def rope_evac(ps_a, ps_b, pxa, pxb, dst, g, dres, rows=128):
                    t1 = next_tmp()
                    t2 = next_tmp()
                    P.op("dve", lambda e: e.tensor_tensor(out=tmp[t1][0:rows, :], in0=ps_a[0:rows, :], in1=ropeC[0:rows, g * TG:(g + 1) * TG], op=ALU.mult),
                         reads=[("px", pxa), "ropeC"], writes=[("tmp", t1)])
                    P.op("dve", lambda e: e.tensor_tensor(out=tmp[t2][0:rows, :], in0=ps_b[0:rows, :], in1=ropeS[0:rows, g * TG:(g + 1) * TG], op=ALU.mult),
                         reads=[("px", pxb), "ropeS"], writes=[("tmp", t2)])
                    if dst is None:
                        for hf in range(2):
                            P.op("pool", lambda e, hf=hf: e.tensor_tensor(out=qTm[hf][64 * hf:64 * hf + 64, g * TG:(g + 1) * TG], in0=tmp[t1][64 * hf:64 * hf + 64, :],
                                                                      in1=tmp[t2][64 * hf:64 * hf + 64, :], op=ALU.add),
                                 reads=[("tmp", t1), ("tmp", t2)], writes=[("qT", hf, g)])
                    else:
                        P.op("pool", lambda e: e.tensor_tensor(out=dst[0:rows, g * TG:(g + 1) * TG], in0=tmp[t1][0:rows, :], in1=tmp[t2][0:rows, :], op=ALU.add),
                             reads=[("tmp", t1), ("tmp", t2)], writes=[(dres, g)])

                hfn = lambda k, g: hs(k, g)
                hres = lambda k, g: ("h", k, g)

                def v_tokmajor(w, woff, wres, ncols):
                    for t4 in range(4):
                        px = next_px()
                        ps = psX[px]
                        for tt in range(4):
                            tok = (t4 * 4 + tt) * 128
                            for k in range(NCH):
                                P.op("pe", lambda e, k=k, tt=tt, tok=tok, ps=ps: e.matmul(
                                    ps[:, tt * 128: tt * 128 + ncols], lhsT=hT[:, k * S + tok: k * S + tok + 128],
                                    rhs=w[:, woff + k * 128: woff + k * 128 + ncols], start=(k == 0), stop=(k == NCH - 1)),
                                    reads=[wres, ("h", k, tok // TG)], writes=[("px", px)])
                        P.op("act", lambda e, t4=t4, ps=ps: e.activation(out=vS[:, t4 * 512:(t4 + 1) * 512], in_=ps[:], func=AF.Copy),
                             reads=[("px", px)], writes=[("vS", t4)])

                pend_out = None
                for hd in (range(4) if isA else ()):
                    wb = hd % 2
                    w = watt[wb]
                    wload(w, 0, wview(ev_in_d[i], 0, 8, 128 * hd, 128), 8, 128, ("wa", wb, 0))
                    wload(w, 1024, wview(ev_sw_d[i], 0, 8, 128 * hd, 128), 8, 128, ("wa", wb, 1))
                    wload(w, 2048, wview(ev_in_d[i], 0, 8, 512 + 128 * hd, 128), 8, 128, ("wa", wb, 2))
                    wload(w, 3072, wview(ev_sw_d[i], 0, 8, 512 + 128 * hd, 128), 8, 128, ("wa", wb, 3))
                    wload(w, 4096, wview(ev_in_d[i], 0, 8, 1024 + 128 * hd, 128), 8, 128, ("wa", wb, 4))
                    for (dst, dres, o0) in ((None, "qT", 0), (kT, "kT", 2048)):
                        for g in range(NTG):
                            pa = next_px()
                            proj_fm(psX[pa], w, o0, 128, 128, 8, hfn, g, ("wa", wb, o0 // 1024), hres, ("px", pa))
                            pb = next_px()
                            proj_fm(psX[pb], w, o0 + 1024, 128, 128, 8, hfn, g, ("wa", wb, o0 // 1024 + 1), hres, ("px", pb))
                            rope_evac(psX[pa], psX[pb], pa, pb, dst, g, dres)
                    v_tokmajor(w, 4096, ("wa", wb, 4), 128)
                    if pend_out is not None:
                        outproj_chunk(*pend_out, load=False)
                        pend_out = None
                    ob = hd % 2
                    outproj_chunk(ev_out_d[i], hd, oH[ob], ("oH", ob), watt, wb, compute=False)
                    pipe = Pipe(3)
                    hold = {}

                    def fin_a(g, half, aset, ob=ob, hold=hold):
                        aO, nO, aD, nD = aset
                        tO = at_i[0]
                        tD = (tO + 1) % NAT
                        at_i[0] = (tO + 2) % NAT
                        P.op("dve", lambda e: e.reciprocal(out=atmp[tD][:], in_=aD[:, :]), reads=[nD], writes=[("at", tD)])
                        P.op("dve", lambda e: e.tensor_tensor(out=atmp[tO][:], in0=aO[:, :], in1=atmp[tD][:], op=ALU.mult),
                             reads=[nO, ("at", tD)], writes=[("at", tO)])
                        if half == 0:
                            hold[g] = tO
                            return
                        a = hold[g]
                        b = tO
                        P.op("dve", lambda e: e.scalar_tensor_tensor(
                            out=atmp[a][:], in0=atmp[b][:], scalar=lams[:, 4 * i + 2: 4 * i + 3], in1=atmp[a][:], op0=ALU.mult, op1=ALU.add),
                            reads=[("at", a), ("at", b), ("lams", i, 2)], writes=[("at", a)])
                        si = next_sq()
                        P.op("act", lambda e: e.activation(out=sq[si][:], in_=atmp[a][:], func=AF.Square), reads=[("at", a)], writes=[("sq", si)])
                        sb2 = sbuf_i[0]
                        sbuf_i[0] = (sb2 + 1) % 4
                        sreg2 = psS[:, sb2 * TG:(sb2 + 1) * TG]
                        P.op("pe", lambda e: e.matmul(sreg2, lhsT=ones[:], rhs=sq[si][:], start=True, stop=True),
                             reads=[("sq", si), "ones"], writes=[("psS", sb2)])
                        k2 = 1.0 / ((1.0 - lam_init) ** 2)
                        P.op("act", lambda e: e.activation(out=atmp[tD][:], in_=sreg2, func=AF.Sqrt, scale=k2 / 128.0, bias=EPS * k2),
                             reads=[("psS", sb2)], writes=[("at", tD)])
                        P.op("dve", lambda e: e.reciprocal(out=atmp[tD][:], in_=atmp[tD][:]), reads=[("at", tD)], writes=[("at", tD)])
                        P.op("dve", lambda e: e.scalar_tensor_tensor(
                            out=oH[ob][:, g * TG:(g + 1) * TG], in0=atmp[a][:], scalar=vecs[:, 78 + i: 79 + i], in1=atmp[tD][:], op0=ALU.mult, op1=ALU.mult),
                            reads=[("at", a), ("at", tD), "vecs"], writes=[(("oH", ob), g)])

                    def a_step(g, half, kt, aset):
                        aO, nO, aD, nD = aset
                        sbi = sbuf_i[0]
                        sbuf_i[0] = (sbi + 1) % 4
                        pti = pt_i[0]
                        pt_i[0] = (pti + 1) % NPT
                        sreg = psS[:, sbi * TG:(sbi + 1) * TG]

                        def qk():
                            P.op("pe", lambda e: e.matmul(sreg, lhsT=kT[:, kt * 128:(kt + 1) * 128], rhs=qTm[half][:, g * TG:(g + 1) * TG], start=True, stop=True),
                                 reads=[("kT", kt // 4), ("qT", half, g)], writes=[("psS", sbi)])
                            P.op("act", lambda e: e.activation(out=PT[pti][:, 0:TG], in_=sreg, func=AF.Exp, scale=0.125),
                                 reads=[("psS", sbi)], writes=[("PT", pti)])

                        def pv():
                            P.op("pe", lambda e: e.matmul(aO[:, :], lhsT=vS[:, kt * 128:(kt + 1) * 128], rhs=PT[pti][:, 0:TG], start=(kt == 0), stop=(kt == 15)),
                                 reads=[("PT", pti), ("vS", kt // 4)], writes=[nO])
                            P.op("pe", lambda e: e.matmul(aD[:, :], lhsT=ones[:], rhs=PT[pti][:, 0:TG], start=(kt == 0), stop=(kt == 15)),
                                 reads=[("PT", pti), "ones"], writes=[nD])
                            if kt == 15:
                                fin_a(g, half, aset)
                        pipe.step(qk, pv)

                    accs_a = [(psO, ("px", 3), psD, ("px", 2)), (psX[0], ("px", 0), psX[1], ("px", 1))]
                    for g in range(NTG):
                        for half in range(2):
                            aset = accs_a[unit_a[0] % 2]
                            unit_a[0] += 1
                            for kt in range(16):
                                a_step(g, half, kt, aset)
                    pipe.flush()
                    pend_out = (ev_out_d[i], hd, oH[ob], ("oH", ob), watt, wb)
                if pend_out is not None:
                    outproj_chunk(*pend_out, load=False)
                    pend_out = None

                for cpair in (() if isA else range(4)):
                    wb = cpair % 2
                    w = watt[wb]
                    wload(w, 0, wview(ev_in_d[i], 0, 8, 1536 + 128 * cpair, 128), 8, 128, ("wa", wb, 0))
                    wload(w, 2048, wview(ev_in_d[i], 0, 8, 2048 + 128 * cpair, 128), 8, 128, ("wa", wb, 2))
                    wload(w, 4096, wview(ev_in_d[i], 0, 8, 2560 + 128 * cpair, 128), 8, 128, ("wa", wb, 4))
                    for (dst, dres, o0) in ((None, "qT", 0), (kT, "kT", 2048)):
                        for g in range(NTG):
                            pa = next_px()
                            proj_fm(psX[pa], w, o0, 128, 128, 8, hfn, g, ("wa", wb, o0 // 1024), hres, ("px", pa))
                            if dst is None:
                                for hf in range(2):
                                    P.op("act", lambda e, pa=pa, hf=hf, g=g: e.activation(out=qTm[hf][64 * hf:64 * hf + 64, g * TG:(g + 1) * TG],
                                                                                      in_=psX[pa][64 * hf:64 * hf + 64, :], func=AF.Copy),
                                         reads=[("px", pa)], writes=[("qT", hf, g)])
                            else:
                                P.op("act", lambda e, pa=pa, dst=dst, g=g: e.activation(out=dst[:, g * TG:(g + 1) * TG], in_=psX[pa][:], func=AF.Copy),
                                     reads=[("px", pa)], writes=[(dres, g)])
                    v_tokmajor(w, 4096, ("wa", wb, 4), 128)
                    if pend_out is not None:
                        outproj_chunk(*pend_out, load=False)
                        pend_out = None
                    ob = cpair % 2
                    outproj_chunk(ev_out_d[i], 4 + cpair, oH[ob], ("oH", ob), watt, wb, compute=False)
                    accs_b = [(psO, ("px", 3), psD, ("px", 2)), (psX[0], ("px", 0), psX[1], ("px", 1))]
                    tl = []
                    for hh in range(2):
                        for m4 in range(4):
                            aset = accs_b[unit_b[0] % 2]
                            unit_b[0] += 1
                            for mm in range(4):
                                m = m4 * 4 + mm
                                tiles = _PLAN[m]
                                for ti_, (t, gid) in enumerate(tiles):
                                    tl.append(dict(hh=hh, h=2 * cpair + hh, m=m, mm=mm, m4=m4, t=t, gid=gid, first=(ti_ == 0),
                                                   last=(ti_ == len(tiles) - 1), aset=aset, fin=(mm == 3 and ti_ == len(tiles) - 1)))
                    pipe = Pipe(3)

                    def fin_b(d, ob=ob):
                        aO, nO, aD, nD = d["aset"]
                        r0 = 64 * d["hh"]
                        m4 = d["m4"]
                        tr = next_tmp()
                        P.op("dve", lambda e: e.reciprocal(out=tmp[tr][r0:r0 + 64, :], in_=aD[r0:r0 + 64, :]),
                             reads=[nD], writes=[("tmp", tr)])
                        P.op("dve", lambda e: e.tensor_tensor(
                            out=oH[ob][r0:r0 + 64, m4 * TG:(m4 + 1) * TG], in0=aO[r0:r0 + 64, :], in1=tmp[tr][r0:r0 + 64, :], op=ALU.mult),
                            reads=[nO, ("tmp", tr)], writes=[(("oH", ob), m4)])

                    def b_chunk(ch):
                        sbi = sbuf_i[0]
                        sbuf_i[0] = (sbi + 1) % 4
                        pti = pt_i[0]
                        pt_i[0] = (pti + 1) % NPT
                        n = len(ch)

                        def qk():
                            for j, d in enumerate(ch):
                                sreg = psS[:, sbi * TG + j * 128: sbi * TG + (j + 1) * 128]
                                goff = (d["h"] * _NG + d["gid"]) * 128
                                P.op("pe", lambda e, sreg=sreg, goff=goff: e.matmul(sreg, lhsT=ident[:], rhs=G[:, goff:goff + 128], start=True, stop=False),
                                     reads=[("G", d["h"]), "ident"], writes=[("psS", sbi)])
                                P.op("pe", lambda e, sreg=sreg, d=d: e.matmul(
                                    sreg, lhsT=kT[:, d["t"] * 128:(d["t"] + 1) * 128], rhs=qTm[d["hh"]][:, d["m"] * 128:(d["m"] + 1) * 128], start=False, stop=True),
                                    reads=[("kT", d["t"] // 4), ("qT", d["hh"], d["m"] // 4)], writes=[("psS", sbi)])
                            P.op("act", lambda e: e.activation(out=PT[pti][:, 0:n * 128], in_=psS[:, sbi * TG: sbi * TG + n * 128], func=AF.Exp, scale=0.125),
                                 reads=[("psS", sbi)], writes=[("PT", pti)])

                        def pv():
                            for j, d in enumerate(ch):
                                aO, nO, aD, nD = d["aset"]
                                mm = d["mm"]
                                P.op("pe", lambda e, j=j, d=d, aO=aO, mm=mm: e.matmul(
                                    aO[:, mm * 128:(mm + 1) * 128], lhsT=vS[:, d["t"] * 128:(d["t"] + 1) * 128], rhs=PT[pti][:, j * 128:(j + 1) * 128],
                                    start=d["first"], stop=d["last"]),
                                    reads=[("PT", pti), ("vS", d["t"] // 4)], writes=[nO])
                                P.op("pe", lambda e, j=j, d=d, aD=aD, mm=mm: e.matmul(
                                    aD[:, mm * 128:(mm + 1) * 128], lhsT=ones[:], rhs=PT[pti][:, j * 128:(j + 1) * 128],
                                    start=d["first"], stop=d["last"]),
                                    reads=[("PT", pti), "ones"], writes=[nD])
                                if d["fin"]:
                                    fin_b(d)
                        pipe.step(qk, pv)

                    for c0 in range(0, len(tl), 4):
                        b_chunk(tl[c0:c0 + 4])
                    pipe.flush()
                    pend_out = (ev_out_d[i], 4 + cpair, oH[ob], ("oH", ob), watt, wb)
                if pend_out is not None:
                    outproj_chunk(*pend_out, load=False)
                    pend_out = None
                P.emit_phase()

        def odd_phase(l):
            i = l // 2
            scale = 1.0 / math.sqrt(96.0)
            with ExitStack() as ps_:
                st["npx"] = 4
                st["px"] = 0
                watt = [sb("watt%d" % k, [128, 2048], BF16, ps_) for k in range(2)]
                win = sb("win", [128, 8 * 416], BF16, ps_)
                winsw = sb("winsw", [128, 8 * 96], BF16, ps_)
                qTt = [sb("qT%d" % k, [128, S], BF16, ps_) for k in range(2)]
                kTt = [sb("kT%d" % k, [128, S], BF16, ps_) for k in range(2)]
                vSt = [sb("vS%d" % k, [128, 16 * 128], BF16, ps_) for k in range(2)]
                oH = [sb("oH%d" % k, [128, S], BF16, ps_) for k in range(2)]
                NPT = 6
                PT = [sb("PT%d" % k, [128, TG], BF16, ps_) for k in range(NPT)]
                ropeC = sb("ropeC", [128, S], BF16, ps_)
                ropeS = sb("ropeS", [128, S], BF16, ps_)
                for k_ in range(2):
                    P.op("pool", lambda e, k_=k_: e.memset(qTt[k_][:], 0.0), writes=[(("qT", k_), g) for g in range(NTG)])
                for k_ in range(2):
                    P.op("pool", lambda e, k_=k_: e.memset(kTt[k_][:], 0.0),
                         writes=[(("kTn", k_), g) for g in range(NTG)] + [(("kTr", k_), g) for g in range(NTG)])
                accs = [(psX[3], ("px", 3)), (psX[2], ("px", 2))]
                unit = [0]
                cqn = sb("cqn", [128, 2 * S], BF16, ps_)
                ckvn = sb("ckvn", [128, S], BF16, ps_)
                pt_i = [0]
                sbuf_i = [0]
                P.op("sp", lambda e: e.dma_start(out=ropeC[:], in_=ropeM_d[0]), writes=["ropeC"], dma_ch="ropeC")
                P.op("sp", lambda e: e.dma_start(out=ropeS[:], in_=ropeM_d[1]), writes=["ropeS"], dma_ch="ropeS")
                for k in range(2):
                    P.op("dve", lambda e, k=k: e.memset(vSt[k][:], 1.0), writes=[("vS", k, 0), ("vS", k, 1)])
                rmsnorm_x(8 * l, hs, lambda c, g: ("h", c, g))
                wload(win, 0, wview(od_in_d[i], 0, 8, 0, 416), 8, 416, "win")
                wload(winsw, 0, wview(od_insw_d[i], 0, 8, 0, 96), 8, 96, "winsw")
                hfn = lambda k, g: hs(k, g)
                hres = lambda k, g: ("h", k, g)
                for g in range(NTG):
                    pq = [next_px(), next_px()]
                    for c2 in range(2):
                        proj_fm(psX[pq[c2]], win, 128 * c2, 416, 128, 8, hfn, g, "win", hres, ("px", pq[c2]))
                    pss = next_px()
                    for c2 in range(2):
                        si = next_sq()
                        P.op("act", lambda e, si=si, p_=pq[c2]: e.activation(out=sq[si][:], in_=psX[p_][:], func=AF.Square),
                             reads=[("px", pq[c2])], writes=[("sq", si)])
                        P.op("pe", lambda e, si=si, c2=c2, pss=pss: e.matmul(psX[pss][:], lhsT=ones[:], rhs=sq[si][:], start=(c2 == 0), stop=(c2 == 1)),
                             reads=[("sq", si), "ones"], writes=[("px", pss)])
                    tr = next_tmp()
                    P.op("act", lambda e, tr=tr, pss=pss: e.activation(out=tmp[tr][:], in_=psX[pss][:], func=AF.Sqrt, scale=1.0 / 256.0, bias=EPS),
                         reads=[("px", pss)], writes=[("tmp", tr)])
                    P.op("dve", lambda e, tr=tr: e.reciprocal(out=tmp[tr][:], in_=tmp[tr][:]), reads=[("tmp", tr)], writes=[("tmp", tr)])
                    for c2 in range(2):
                        P.op("dve", lambda e, tr=tr, c2=c2, g=g, p_=pq[c2]: e.scalar_tensor_tensor(
                            out=cqn[:, c2 * S + g * TG: c2 * S + (g + 1) * TG], in0=psX[p_][:], scalar=vecs[:, 72 + 2 * i + c2: 73 + 2 * i + c2],
                            in1=tmp[tr][:], op0=ALU.mult, op1=ALU.mult),
                            reads=[("px", pq[c2]), ("tmp", tr), "vecs"], writes=[("cqn", c2, g)])
                    pk = next_px()
                    proj_fm(psX[pk], win, 256, 416, 128, 8, hfn, g, "win", hres, ("px", pk))
                    pss = next_px()
                    si = next_sq()
                    P.op("act", lambda e, si=si, pk=pk: e.activation(out=sq[si][:], in_=psX[pk][:], func=AF.Square),
                         reads=[("px", pk)], writes=[("sq", si)])
                    P.op("pe", lambda e, si=si, pss=pss: e.matmul(psX[pss][:], lhsT=ones[:], rhs=sq[si][:], start=True, stop=True),
                         reads=[("sq", si), "ones"], writes=[("px", pss)])
                    tr = next_tmp()
                    P.op("act", lambda e, tr=tr, pss=pss: e.activation(out=tmp[tr][:], in_=psX[pss][:], func=AF.Sqrt, scale=1.0 / 128.0, bias=EPS),
                         reads=[("px", pss)], writes=[("tmp", tr)])
                    P.op("dve", lambda e, tr=tr: e.reciprocal(out=tmp[tr][:], in_=tmp[tr][:]), reads=[("tmp", tr)], writes=[("tmp", tr)])
                    P.op("dve", lambda e, tr=tr, g=g, pk=pk: e.scalar_tensor_tensor(
                        out=ckvn[:, g * TG:(g + 1) * TG], in0=psX[pk][:], scalar=vecs[:, 76 + i: 77 + i], in1=tmp[tr][:], op0=ALU.mult, op1=ALU.mult),
                        reads=[("px", pk), ("tmp", tr), "vecs"], writes=[("ckvn", g)])
                    pa = next_px()
                    proj_fm(psX[pa], win, 320, 416, 96, 8, hfn, g, "win", hres, ("px", pa))
                    pb = next_px()
                    proj_fm(psX[pb], winsw, 0, 96, 96, 8, hfn, g, "winsw", hres, ("px", pb))
                    t1 = next_tmp()
                    t2 = next_tmp()
                    P.op("dve", lambda e, t1=t1, pa=pa, g=g: e.tensor_tensor(out=tmp[t1][64:96, :], in0=psX[pa][64:96, :], in1=ropeC[64:96, g * TG:(g + 1) * TG], op=ALU.mult),
                         reads=[("px", pa), "ropeC"], writes=[("tmp", t1)])
                    P.op("dve", lambda e, t2=t2, pb=pb, g=g: e.tensor_tensor(out=tmp[t2][64:96, :], in0=psX[pb][64:96, :], in1=ropeS[64:96, g * TG:(g + 1) * TG], op=ALU.mult),
                         reads=[("px", pb), "ropeS"], writes=[("tmp", t2)])
                    for k in range(2):
                        P.op("pool", lambda e, t1=t1, t2=t2, g=g, k=k: e.tensor_tensor(
                            out=kTt[k][64:96, g * TG:(g + 1) * TG], in0=tmp[t1][64:96, :], in1=tmp[t2][64:96, :], op=ALU.add),
                            reads=[("tmp", t1), ("tmp", t2)], writes=[(("kTr", k), g)])

                cfn = lambda k, g: cqn[:, k * S + g * TG: k * S + (g + 1) * TG]
                cres = lambda k, g: ("cqn", k, g)
                st["npx"] = 2
                st["px"] = 0

                def proj_head(h):
                    wb = h % 2
                    w = watt[wb]
                    kT = kTt[wb]
                    vS = vSt[wb]
                    qT = qTt[wb]
                    vc0 = 0 if wb == 0 else 64
                    wload(w, 0, wview(od_uq_d[i], 0, 2, 96 * h, 96), 2, 96, ("wa", wb, 0))
                    wload(w, 256, wview(od_uqsw_d[i], 0, 2, 96 * h, 96), 2, 96, ("wa", wb, 1))
                    wload(w, 512, wview(od_ukv_d[i], 0, 1, 128 * h, 128), 1, 128, ("wa", wb, 2))
                    for g in range(NTG):
                        pa = next_px()
                        proj_fm(psX[pa], w, 0, 96, 128, 2, cfn, g, ("wa", wb, 0), cres, ("px", pa))
                        pb = next_px()
                        proj_fm(psX[pb], w, 256, 96, 128, 2, cfn, g, ("wa", wb, 1), cres, ("px", pb))
                        t1 = next_tmp()
                        t2 = next_tmp()
                        P.op("dve", lambda e, t1=t1, pa=pa, g=g: e.tensor_tensor(out=tmp[t1][0:96, :], in0=psX[pa][0:96, :], in1=ropeC[0:96, g * TG:(g + 1) * TG], op=ALU.mult),
                             reads=[("px", pa), "ropeC"], writes=[("tmp", t1)])
                        P.op("dve", lambda e, t2=t2, pb=pb, g=g: e.tensor_tensor(out=tmp[t2][0:96, :], in0=psX[pb][0:96, :], in1=ropeS[0:96, g * TG:(g + 1) * TG], op=ALU.mult),
                             reads=[("px", pb), "ropeS"], writes=[("tmp", t2)])
                        P.op("pool", lambda e, t1=t1, t2=t2, g=g: e.tensor_tensor(out=qT[0:96, g * TG:(g + 1) * TG], in0=tmp[t1][0:96, :], in1=tmp[t2][0:96, :], op=ALU.add),
                             reads=[("tmp", t1), ("tmp", t2)], writes=[(("qT", wb), g)])
                        pk = next_px()
                        P.op("pe", lambda e, pk=pk, g=g: e.matmul(psX[pk][:, :], lhsT=w[:, 512:640], rhs=ckvn[:, g * TG:(g + 1) * TG], start=True, stop=True),
                             reads=[("wa", wb, 2), ("ckvn", g)], writes=[("px", pk)])
                        P.op("act", lambda e, pk=pk, g=g: e.activation(out=kT[0:64, g * TG:(g + 1) * TG], in_=psX[pk][0:64, :], func=AF.Copy),
                             reads=[("px", pk)], writes=[(("kTn", wb), g)])
                    for t8 in range(2):
                        px = next_px()
                        ps = psX[px]
                        for tt in range(8):
                            tok = (t8 * 8 + tt) * 128
                            P.op("pe", lambda e, tt=tt, tok=tok, ps=ps: e.matmul(
                                ps[:, tt * 64:(tt + 1) * 64], lhsT=ckvn[:, tok:tok + 128], rhs=w[:, 576:640], start=True, stop=True),
                                reads=[("wa", wb, 2), ("ckvn", tok // TG)], writes=[("px", px)])
                        dstv = vS[:, t8 * 1024:(t8 + 1) * 1024].rearrange("p (t c) -> p t c", c=128)[:, :, vc0:vc0 + 64]
                        srcv = ps[:, :].rearrange("p (t c) -> p t c", c=64)
                        P.op("act", lambda e, dstv=dstv, srcv=srcv: e.activation(out=dstv, in_=srcv, func=AF.Copy),
                             reads=[("px", px)], writes=[("vS", wb, t8)])

                def attn_head(h):
                    wb = h % 2
                    kT = kTt[wb]
                    vS = vSt[wb]
                    qT = qTt[wb]
                    ob = (h // 2) % 2
                    orow = 64 * (h % 2)
                    drow = 64 - orow
                    pipe = Pipe(3)

                    def fin_m(g, aO, nO):
                        tr = next_tmp()
                        P.op("dve", lambda e: e.reciprocal(out=tmp[tr][drow:drow + 64, :], in_=aO[drow:drow + 64, :]),
                             reads=[nO], writes=[("tmp", tr)])
                        P.op("dve", lambda e: e.tensor_tensor(
                            out=oH[ob][orow:orow + 64, g * TG:(g + 1) * TG], in0=aO[orow:orow + 64, :], in1=tmp[tr][drow:drow + 64, :], op=ALU.mult),
                            reads=[nO, ("tmp", tr)], writes=[(("oH", ob), g)])

                    def m_step(g, kt, aO, nO):
                        sbi = sbuf_i[0]
                        sbuf_i[0] = (sbi + 1) % 4
                        pti = pt_i[0]
                        pt_i[0] = (pti + 1) % NPT
                        sreg = psS[:, sbi * TG:(sbi + 1) * TG]

                        def qk():
                            P.op("pe", lambda e: e.matmul(sreg, lhsT=kT[:, kt * 128:(kt + 1) * 128], rhs=qT[:, g * TG:(g + 1) * TG], start=True, stop=True),
                                 reads=[(("kTn", wb), kt // 4), (("kTr", wb), kt // 4), (("qT", wb), g)], writes=[("psS", sbi)])
                            P.op("act", lambda e: e.activation(out=PT[pti][:, 0:TG], in_=sreg, func=AF.Exp, scale=scale),
                                 reads=[("psS", sbi)], writes=[("PT", pti)])

                        def pv():
                            P.op("pe", lambda e: e.matmul(aO, lhsT=vS[:, kt * 128:(kt + 1) * 128], rhs=PT[pti][:, 0:TG], start=(kt == 0), stop=(kt == 15)),
                                 reads=[("PT", pti), ("vS", wb, kt // 8)], writes=[nO])
                            if kt == 15:
                                fin_m(g, aO, nO)
                        pipe.step(qk, pv)

                    for g in range(NTG):
                        aO, nO = accs[unit[0] % 2]
                        unit[0] += 1
                        for kt in range(16):
                            m_step(g, kt, aO, nO)
                    pipe.flush()
                    if h % 2 == 1:
                        outproj_chunk(od_out_d[i], h // 2, oH[ob], ("oH", ob), watt, wb, wo=1024)

                proj_head(0)
                for h in range(16):
                    if h + 1 < 16:
                        proj_head(h + 1)
                    attn_head(h)
                P.emit_phase()

        for s_ in range(nseq):
            for c in range(NCH):
                P.op("sp", lambda e, c=c, s_=s_: e.dma_start(out=xT[:, c * S:(c + 1) * S], in_=xT_d[s_][c * 128:(c + 1) * 128, :]),
                     writes=[("x", c, g) for g in range(NTG)], dma_ch=("xin", c))
            P.emit_phase()
            for l in layers:
                if do_mix:
                    if l % 2 == 0:
                        first = True
                        for part in parts:
                            even_phase(l, part, first)
                            first = False
                    else:
                        odd_phase(l)
                if do_mlp:
                    mlp_phase(l)
            with ExitStack() as ps_:
                st["npx"] = 2
                ob = [sb("outb%d" % k, [128, S], F32, ps_) for k in range(2)]
                rs = [sb("rs%d" % k, [128, TG], F32, ps_) for k in range(NTG)]
                for g in range(NTG):
                    px = next_px()
                    ps = psX[px]
                    for c in range(NCH):
                        if c % 2 == 0:
                            P.op("act", lambda e, c=c, g=g: e.activation(out=hs(c, g), in_=xs(c, g), func=AF.Square),
                                 reads=[("x", c, g)], writes=[("h", c, g)])
                        else:
                            P.op("dve", lambda e, c=c, g=g: e.tensor_tensor(out=hs(c, g), in0=xs(c, g), in1=xs(c, g), op=ALU.mult),
                                 reads=[("x", c, g)], writes=[("h", c, g)])
                        P.op("pe", lambda e, c=c, g=g, ps=ps: e.matmul(ps[:], lhsT=ones[:], rhs=hs(c, g), start=(c == 0), stop=(c == NCH - 1)),
                             reads=[("h", c, g), "ones"], writes=[("px", px)])
                    P.op("act", lambda e, g=g, ps=ps: e.activation(out=rs[g][:], in_=ps[:], func=AF.Sqrt, scale=1.0 / D, bias=EPS),
                         reads=[("px", px)], writes=[("rs", g)])
                    P.op("dve", lambda e, g=g: e.reciprocal(out=rs[g][:], in_=rs[g][:]), reads=[("rs", g)], writes=[("rs", g)])
                for c in range(NCH):
                    k = c % 2
                    for g in range(NTG):
                        P.op("dve", lambda e, c=c, g=g, k=k: e.scalar_tensor_tensor(
                            out=ob[k][:, g * TG:(g + 1) * TG], in0=xs(c, g), scalar=vecs[:, 64 + c:65 + c], in1=rs[g][:], op0=ALU.mult, op1=ALU.mult),
                            reads=[("x", c, g), ("rs", g), "vecs"], writes=[("ob", k)])
                    P.op("sp", lambda e, c=c, k=k, s_=s_: e.dma_start(out=out_d[s_][c * 128:(c + 1) * 128, :], in_=ob[k][:]),
                         reads=[("ob", k)], dma_ch=("out", k))
                P.emit_phase(final=(s_ == nseq - 1))
    return nc


def prep_shared(inp):
    f = lambda a: np.ascontiguousarray(np.asarray(a, dtype=np.float32))
    vecs = np.zeros((128, 80), np.float32)
    for l in range(4):
        vecs[:, 8 * l:8 * l + 8] = f(inp["ln_mix_g"])[l].reshape(8, 128).T
        vecs[:, 32 + 8 * l:32 + 8 * l + 8] = f(inp["ln_mlp_g"])[l].reshape(8, 128).T
    vecs[:, 64:72] = f(inp["ln_f_g"]).reshape(8, 128).T
    for i in range(2):
        vecs[:, 72 + 2 * i:74 + 2 * i] = f(inp["od_q_norm_g"])[i].reshape(2, 128).T
        vecs[:, 76 + i] = f(inp["od_kv_norm_g"])[i]
        vecs[:, 78 + i] = f(inp["ev_subln_g"])[i]
    lamv = np.zeros((128, 512), np.float32)
    for i in range(2):
        for j, nm in enumerate(("ev_lambda_q1", "ev_lambda_k1", "ev_lambda_q2", "ev_lambda_k2")):
            lamv[:, i * 256 + j * 64: i * 256 + (j + 1) * 64] = f(inp[nm])[i][None, :]
    ev_in = f(inp["ev_w_in"])
    permA = _swap_perm(1024, 64, 0, 8)
    ev_sw = np.ascontiguousarray(ev_in[:, :, :1024][:, :, permA])
    od_in = f(inp["od_w_in"])
    perm_kr = _swap_perm(32, 32, 0, 16)
    od_in_sw = np.ascontiguousarray(od_in[:, :, 320:416].copy())
    od_in_sw[:, :, 64:96] = od_in[:, :, 384:416][:, :, perm_kr]
    od_uq = f(inp["od_w_uq"])
    perm_q = _swap_perm(1536, 96, 64, 16)
    od_uq_sw = np.ascontiguousarray(od_uq[:, :, perm_q])
    CA, SA = _rope_tables_A()
    CM, SM = _rope_tables_M()
    rpb = f(inp["ev_rpb"])
    gb = rpb[:, :, _RIDX, _CIDX]
    gb = np.ascontiguousarray(gb.transpose(0, 3, 1, 2, 4)).reshape(2, 128, 8 * _NG * 128)
    gmask = np.where(_VALID, 0.0, MASKV).astype(np.float32)
    gmask = np.ascontiguousarray(gmask.transpose(1, 0, 2)).reshape(128, _NG * 128).astype(ml_dtypes.bfloat16)
    return {
        "vecs": vecs, "lamv": lamv,
        "w_up": f(inp["w_up"]), "w_down": f(inp["w_down"]),
        "ev_w_in": ev_in, "ev_w_sw": ev_sw, "ev_w_out": f(inp["ev_w_out"]),
        "od_w_in": od_in, "od_w_in_sw": od_in_sw, "od_w_uq": od_uq, "od_w_uq_sw": od_uq_sw,
        "od_w_ukv": f(inp["od_w_ukv"]), "od_w_out": f(inp["od_w_out"]),
        "ropeA": np.stack([CA, SA]).astype(ml_dtypes.bfloat16), "ropeM": np.stack([CM, SM]).astype(ml_dtypes.bfloat16),
        "gb": gb, "gmask": gmask,
        "ident": np.eye(128, dtype=np.float32).astype(ml_dtypes.bfloat16),
    }


_NC_CACHE = {}


def kernel(**inputs):
    x = np.asarray(inputs["x"], dtype=np.float32)
    B = x.shape[0]
    nseq = B // NCORES
    shared = prep_shared(inputs)
    key = nseq
    if key not in _NC_CACHE:
        _NC_CACHE[key] = build(nseq=nseq)
    nc = _NC_CACHE[key]
    in_maps = []
    for c in range(NCORES):
        xs_ = x[c * nseq:(c + 1) * nseq]
        m = dict(shared)
        m["xT"] = np.ascontiguousarray(xs_.transpose(0, 2, 1))
        in_maps.append(m)
    res = run_bass_kernel_spmd(nc, in_maps, core_ids=list(range(NCORES)))
    out = np.empty((B, S, D), np.float32)
    for c in range(NCORES):
        o = np.asarray(res.results[c]["outT"])
        out[c * nseq:(c + 1) * nseq] = o.transpose(0, 2, 1)
    return out
```

```python
import math
from contextlib import ExitStack

import numpy as np
import ml_dtypes
import concourse.bass as bass
import concourse.mybir as mybir
from concourse.bass_utils import run_bass_kernel_spmd

F32 = mybir.dt.float32
BF16 = mybir.dt.bfloat16
AF = mybir.ActivationFunctionType
ALU = mybir.AluOpType
AX = mybir.AxisListType

S = 2048
D = 1024
NCH = 8
NTG = 4
TG = 512
EPS = 1e-5
NCORES = 8
ENGS = ("pe", "act", "dve", "pool", "sp")


class Op:
    __slots__ = ("eng", "fn", "deps", "dma_ch", "sig", "idx", "has_consumer", "waits")

    def __init__(self, eng, fn, dma_ch=None):
        self.eng = eng
        self.fn = fn
        self.deps = set()
        self.dma_ch = dma_ch
        self.sig = None
        self.has_consumer = False
        self.waits = []


class Prog:
    def __init__(self, nc, es):
        self.nc = nc
        self.es = es
        self.sems = {}
        self.cnt = {e: 0 for e in ENGS}
        self.dcnt = {}
        self.reset_phase()
        self.nphase = 0

    def reset_phase(self):
        self.ops = []
        self.last_writer = {}
        self.readers = {}

    def sem(self, key):
        s = self.sems.get(key)
        if s is None:
            name = "s_" + "_".join(str(k) for k in key)
            s = self.es.enter_context(self.nc.semaphore(name))
            self.sems[key] = s
        return s

    def op(self, eng, fn, reads=(), writes=(), dma_ch=None):
        o = Op(eng, fn, dma_ch)
        o.idx = len(self.ops)
        deps = set()
        for r in reads:
            w = self.last_writer.get(r)
            if w is not None:
                deps.add(w)
        for r in writes:
            w = self.last_writer.get(r)
            if w is not None:
                deps.add(w)
            for rd in self.readers.get(r, ()):
                deps.add(rd)
        o.deps = deps
        self.ops.append(o)
        for r in reads:
            self.readers.setdefault(r, []).append(o.idx)
        for r in writes:
            self.last_writer[r] = o.idx
            self.readers[r] = []
        return o

    def emit_phase(self, final=False):
        ops = self.ops
        if not ops:
            return
        nc = self.nc
        for o in ops:
            best = {}
            for d in o.deps:
                p = ops[d]
                if p.dma_ch is not None:
                    key = ("dma", p.dma_ch)
                elif p.eng != o.eng or o.eng in ("act", "dve", "pool"):
                    key = ("eng", p.eng)
                else:
                    continue
                if d > best.get(key, -1):
                    best[key] = d
            keep = set(best.values())
            o.deps = keep
            for d in keep:
                ops[d].has_consumer = True
        last = {}
        for o in ops:
            if o.dma_ch is None:
                last[o.eng] = o
        for o in last.values():
            o.has_consumer = True
        prev_cnt = dict(self.cnt)
        prev_dcnt = dict(self.dcnt)
        for o in ops:
            if o.dma_ch is not None:
                self.dcnt[o.dma_ch] = self.dcnt.get(o.dma_ch, 0) + 16
                o.sig = ("dma", o.dma_ch, self.dcnt[o.dma_ch])
            elif o.has_consumer:
                self.cnt[o.eng] += 1
                o.sig = ("eng", o.eng, self.cnt[o.eng])
        waited = {e: {} for e in ENGS}
        for e in ENGS:
            for e2 in ENGS:
                if prev_cnt[e2] > 0:
                    waited[e][("eng", e2)] = prev_cnt[e2]
            for ch, v in prev_dcnt.items():
                waited[e][("dma", ch)] = v
        for o in ops:
            need = {}
            for d in o.deps:
                s = ops[d].sig
                key = (s[0], s[1])
                if s[2] > need.get(key, 0):
                    need[key] = s[2]
            for key, v in need.items():
                if waited[o.eng].get(key, 0) >= v:
                    continue
                waited[o.eng][key] = v
                o.waits.append((key, v))
        by_eng = {e: [o for o in ops if o.eng == e] for e in ENGS}
        sems = self.sem
        final_d = dict(self.dcnt)
        final_c = dict(self.cnt)

        def run(eng_name, eng):
            for e2 in ENGS:
                if e2 != eng_name and prev_cnt[e2] > 0:
                    eng.wait_ge(sems(("eng", e2)), prev_cnt[e2])
            for ch, v in prev_dcnt.items():
                eng.wait_ge(sems(("dma", ch)), v)
            for o in by_eng[eng_name]:
                for key, v in o.waits:
                    eng.wait_ge(sems(key), v)
                ins = o.fn(eng)
                if o.dma_ch is not None:
                    ins.then_inc(sems(("dma", o.dma_ch)), 16)
                elif o.sig is not None:
                    ins.then_inc(sems(("eng", eng_name)), 1)
            if final:
                for ch, v in final_d.items():
                    eng.wait_ge(sems(("dma", ch)), v)
                for e2 in ENGS:
                    if e2 != eng_name and final_c[e2] > 0:
                        eng.wait_ge(sems(("eng", e2)), final_c[e2])

        for e in ENGS:
            sems(("eng", e))
        for ch in self.dcnt:
            sems(("dma", ch))
        with nc.Block() as block:
            @block.sync
            def _(e):
                run("sp", e)

            @block.tensor
            def _(e):
                run("pe", e)

            @block.scalar
            def _(e):
                run("act", e)

            @block.vector
            def _(e):
                run("dve", e)

            @block.gpsimd
            def _(e):
                run("pool", e)
        self.nphase += 1
        self.reset_phase()


class Pipe:
    def __init__(self, look):
        self.look = look
        self.q = []

    def step(self, qk, pv):
        qk()
        self.q.append(pv)
        if len(self.q) > self.look:
            self.q.pop(0)()

    def flush(self):
        while self.q:
            self.q.pop(0)()


def _rot_tables(rot_dim, theta=500000.0):
    pos = np.arange(S, dtype=np.float32)
    inv = (np.float32(theta) ** (-np.arange(0, rot_dim, 2, dtype=np.float32) / np.float32(rot_dim))).astype(np.float32)
    ang = (pos[:, None] * inv[None, :]).astype(np.float32)
    return np.cos(ang).astype(np.float32), np.sin(ang).astype(np.float32)


def _rope_tables_A():
    cos, sin = _rot_tables(16)
    C = np.ones((128, S), np.float32)
    Sg = np.zeros((128, S), np.float32)
    for base in (0, 64):
        for d in range(8):
            C[base + d] = cos[:, d]
            Sg[base + d] = -sin[:, d]
            C[base + 8 + d] = cos[:, d]
            Sg[base + 8 + d] = sin[:, d]
    return C, Sg


def _rope_tables_M():
    cos, sin = _rot_tables(32)
    C = np.ones((128, S), np.float32)
    Sg = np.zeros((128, S), np.float32)
    for d in range(16):
        C[64 + d] = cos[:, d]
        Sg[64 + d] = -sin[:, d]
        C[64 + 16 + d] = cos[:, d]
        Sg[64 + 16 + d] = sin[:, d]
    return C, Sg


def _swap_perm(n, head, rot_off, half):
    idx = np.arange(n)
    for h0 in range(0, n, head):
        for d in range(half):
            idx[h0 + rot_off + d] = h0 + rot_off + half + d
            idx[h0 + rot_off + half + d] = h0 + rot_off + d
    return idx


def _nbr_plan():
    rows, W, wr, wc = 32, 64, 8, 16
    gids = {}
    plan = []
    for m in range(16):
        need = {}
        for b in range(2):
            qr = 2 * m + b
            r0 = min(max(qr - wr // 2, 0), rows - wr)
            for kr in range(r0, r0 + wr):
                t, a = kr // 2, kr % 2
                need.setdefault(t, set()).add((a, b))
        lst = []
        for t in sorted(need):
            key = (2 * t - 2 * m, tuple(sorted(need[t])))
            if key not in gids:
                gids[key] = len(gids)
            lst.append((t, gids[key]))
        plan.append(lst)
    ng = len(gids)
    ridx = np.zeros((ng, 128, 128), np.int64)
    cidx = np.zeros((ng, 128, 128), np.int64)
    valid = np.zeros((ng, 128, 128), bool)
    c = np.arange(W)
    cs = np.clip(c - wc // 2, 0, W - wc)
    colmask = (c[None, :] >= cs[:, None]) & (c[None, :] < cs[:, None] + wc)
    for (delta, pat), g in gids.items():
        for a in range(2):
            for b in range(2):
                dr = delta + a - b + 7
                ok = (a, b) in pat
                for kc in range(W):
                    k = a * 64 + kc
                    q = b * 64 + c
                    ridx[g, k, q] = min(max(dr, 0), 14)
                    cidx[g, k, q] = np.clip(kc - c, -15, 15) + 15
                    valid[g, k, q] = ok & colmask[c, kc]
    return plan, ng, ridx, cidx, valid


_PLAN, _NG, _RIDX, _CIDX, _VALID = _nbr_plan()
MASKV = -30000.0


def build(nseq=4, layers=(0, 1, 2, 3), do_mlp=True, do_mix=True, parts="AB", dbg=False):
    nc = bass.Bass("TRN2", target_bir_lowering=False)
    dt = nc.dram_tensor
    xT_d = dt("xT", [nseq, D, S], F32, kind="ExternalInput").ap()
    out_d = dt("outT", [nseq, D, S], F32, kind="ExternalOutput").ap()
    vecs_d = dt("vecs", [128, 80], F32, kind="ExternalInput").ap()
    lam_d = dt("lamv", [128, 2 * 4 * 64], F32, kind="ExternalInput").ap()
    w_up_d = dt("w_up", [4, D, 4096], F32, kind="ExternalInput").ap()
    w_dn_d = dt("w_down", [4, 4096, D], F32, kind="ExternalInput").ap()
    ev_in_d = dt("ev_w_in", [2, D, 3072], F32, kind="ExternalInput").ap()
    ev_sw_d = dt("ev_w_sw", [2, D, 1024], F32, kind="ExternalInput").ap()
    ev_out_d = dt("ev_w_out", [2, D, D], F32, kind="ExternalInput").ap()
    od_in_d = dt("od_w_in", [2, D, 416], F32, kind="ExternalInput").ap()
    od_insw_d = dt("od_w_in_sw", [2, D, 96], F32, kind="ExternalInput").ap()
    od_uq_d = dt("od_w_uq", [2, 256, 1536], F32, kind="ExternalInput").ap()
    od_uqsw_d = dt("od_w_uq_sw", [2, 256, 1536], F32, kind="ExternalInput").ap()
    od_ukv_d = dt("od_w_ukv", [2, 128, 2048], F32, kind="ExternalInput").ap()
    od_out_d = dt("od_w_out", [2, D, D], F32, kind="ExternalInput").ap()
    ropeA_d = dt("ropeA", [2, 128, S], BF16, kind="ExternalInput").ap()
    ropeM_d = dt("ropeM", [2, 128, S], BF16, kind="ExternalInput").ap()
    gb_d = dt("gb", [2, 128, 8 * _NG * 128], F32, kind="ExternalInput").ap()
    gmask_d = dt("gmask", [128, _NG * 128], BF16, kind="ExternalInput").ap()
    ident_d = dt("ident", [128, 128], BF16, kind="ExternalInput").ap()

    if dbg:
        dbg_d = {nm: dt("dbg_" + nm, [128, S], BF16, kind="ExternalOutput").ap() for nm in ("q", "k", "v", "o")}
    es = ExitStack()
    with es:
        uniq = [0]

        def sb(name, shape, dty, stack=es):
            uniq[0] += 1
            return stack.enter_context(nc.sbuf_tensor("%s_u%d" % (name, uniq[0]), shape, dty))

        P = Prog(nc, es)
        xT = sb("xT_s", [128, NCH * S], F32)
        hT = sb("hT_s", [128, NCH * S], BF16)
        vecs = sb("vecs_s", [128, 80], F32)
        lams = sb("lams", [128, 16], F32)
        ones = sb("ones", [128, 128], BF16)
        ident = sb("ident_s", [128, 128], BF16)
        NSTG = 2
        STGW = 1536
        stage = [sb("stage%d" % i, [128, STGW], F32) for i in range(NSTG)]
        NTMP = 5
        tmp = [sb("tmp%d" % i, [128, TG], F32) for i in range(NTMP)]
        sq = [sb("sq%d" % i, [128, TG], BF16) for i in range(2)]
        psS = es.enter_context(nc.psum_tensor("psS", [128, 2048], F32))
        psX0 = es.enter_context(nc.psum_tensor("psX0", [128, TG], F32))
        psX1 = es.enter_context(nc.psum_tensor("psX1", [128, TG], F32))
        psO = es.enter_context(nc.psum_tensor("psO", [128, TG], F32))
        psD = es.enter_context(nc.psum_tensor("psD", [128, TG], F32))
        psX = [psX0[:, :], psX1[:, :], psD[:, :], psO[:, :]] + [psS[:, k * TG:(k + 1) * TG] for k in range(4)]

        st = {"stg": 0, "tmp": 0, "sq": 0, "px": 0, "npx": 2}

        def xs(c, g):
            return xT[:, c * S + g * TG: c * S + (g + 1) * TG]

        def hs(c, g):
            return hT[:, c * S + g * TG: c * S + (g + 1) * TG]

        def next_tmp():
            i = st["tmp"]
            st["tmp"] = (i + 1) % NTMP
            return i

        def next_px():
            i = st["px"] % st["npx"]
            st["px"] = (i + 1) % st["npx"]
            return i

        def next_sq():
            i = st["sq"]
            st["sq"] = (i + 1) % 2
            return i

        def wload(dst, dst_off, src3, kc, n, dst_res):
            per = max(1, STGW // n)
            k0 = 0
            while k0 < kc:
                k1 = min(kc, k0 + per)
                tot = (k1 - k0) * n
                slot = st["stg"]
                st["stg"] = (slot + 1) % NSTG
                stg = stage[slot]
                src = src3[:, k0:k1, :]
                dview = stg[:, 0:tot].rearrange("p (k n) -> p k n", k=k1 - k0)
                P.op("sp", lambda e, dview=dview, src=src: e.dma_start(out=dview, in_=src),
                     writes=[("stg", slot)], dma_ch=("stg", slot))
                o0 = dst_off + k0 * n
                P.op("pool", lambda e, stg=stg, o0=o0, tot=tot: e.tensor_copy(out=dst[:, o0:o0 + tot], in_=stg[:, 0:tot]),
                     reads=[("stg", slot)], writes=[dst_res])
                k0 = k1

        def wview(w2d, r0, nk, c0, n):
            return w2d[r0:r0 + nk * 128, c0:c0 + n].rearrange("(k p) n -> p k n", p=128)

        def rmsnorm_x(gcol0, dst_fn, dst_res_fn, out_f32=False):
            for g in range(NTG):
                px = next_px()
                ps = psX[px]
                for c in range(NCH):
                    P.op("act", lambda e, c=c, g=g: e.activation(out=hs(c, g), in_=xs(c, g), func=AF.Square),
                         reads=[("x", c, g)], writes=[("h", c, g)])
                    P.op("pe", lambda e, c=c, g=g, ps=ps: e.matmul(ps[:], lhsT=ones[:], rhs=hs(c, g), start=(c == 0), stop=(c == NCH - 1)),
                         reads=[("h", c, g), "ones"], writes=[("px", px)])
                ti = next_tmp()
                P.op("act", lambda e, ti=ti, ps=ps: e.activation(out=tmp[ti][:], in_=ps[:], func=AF.Sqrt, scale=1.0 / D, bias=EPS),
                     reads=[("px", px)], writes=[("tmp", ti)])
                P.op("dve", lambda e, ti=ti: e.reciprocal(out=tmp[ti][:], in_=tmp[ti][:]),
                     reads=[("tmp", ti)], writes=[("tmp", ti)])
                for c in range(NCH):
                    P.op("dve", lambda e, ti=ti, c=c, g=g: e.scalar_tensor_tensor(
                        out=dst_fn(c, g), in0=xs(c, g), scalar=vecs[:, gcol0 + c:gcol0 + c + 1], in1=tmp[ti][:],
                        op0=ALU.mult, op1=ALU.mult),
                        reads=[("x", c, g), ("tmp", ti), "vecs"], writes=[dst_res_fn(c, g)])

        init_es = ExitStack()
        lamt = sb("lamt", [128, 512], F32, init_es)
        lamw = sb("lamw", [128, 512], F32, init_es)
        P.op("sp", lambda e: e.dma_start(out=vecs[:], in_=vecs_d[:, :]), writes=["vecs"], dma_ch="vecs")
        P.op("sp", lambda e: e.dma_start(out=lamt[:], in_=lam_d[:, :]), writes=["lamt"], dma_ch="lamt")
        P.op("sp", lambda e: e.dma_start(out=ident[:], in_=ident_d[:, :]), writes=["ident"], dma_ch="ident")
        P.op("dve", lambda e: e.memset(ones[:], 1.0), writes=["ones"])
        for i in range(2):
            b = i * 256
            for j in range(2):
                P.op("dve", lambda e, b=b, j=j: e.tensor_tensor(out=lamw[:, b + 64 * j: b + 64 * j + 64], in0=lamt[:, b + 128 * j: b + 128 * j + 64],
                                                             in1=lamt[:, b + 128 * j + 64: b + 128 * j + 128], op=ALU.mult),
                     reads=["lamt"], writes=[("lamw", i, j)])
                P.op("dve", lambda e, b=b, j=j, i=i: e.reduce_sum(out=lams[:, 4 * i + j: 4 * i + j + 1], in_=lamw[:, b + 64 * j: b + 64 * j + 64], axis=AX.X),
                     reads=[("lamw", i, j)], writes=[("lams", i, j)])
                P.op("act", lambda e, j=j, i=i: e.activation(out=lams[:, 4 * i + j: 4 * i + j + 1], in_=lams[:, 4 * i + j: 4 * i + j + 1], func=AF.Exp),
                     reads=[("lams", i, j)], writes=[("lams", i, j)])
            lam_init = 0.8 - 0.6 * math.exp(-0.3 * (2 * i))
            P.op("dve", lambda e, i=i, lam_init=lam_init: e.scalar_tensor_tensor(
                out=lams[:, 4 * i + 2: 4 * i + 3], in0=lams[:, 4 * i + 1: 4 * i + 2], scalar=-lam_init, in1=lams[:, 4 * i: 4 * i + 1],
                op0=ALU.add, op1=ALU.subtract),
                reads=[("lams", i, 0), ("lams", i, 1)], writes=[("lams", i, 2)])
        P.emit_phase()
        init_es.close()

        def mlp_phase(l):
            with ExitStack() as ps_:
                st["npx"] = 8
                wm = [sb("wmlp%d" % i, [128, 8192], BF16, ps_) for i in range(2)]
                uT = sb("uT", [128, 4 * S], BF16, ps_)
                rmsnorm_x(32 + 8 * l, hs, lambda c, g: ("h", c, g))
                for part in range(8):
                    wb = part % 2
                    w = wm[wb]
                    wload(w, 0, wview(w_up_d[l], 0, 8, part * 512, 512), 8, 512, ("wm", wb, 0))
                    wload(w, 4096, wview(w_dn_d[l], part * 512, 4, 0, 1024), 4, 1024, ("wm", wb, 1))
                    for g in range(NTG):
                        for j in range(4):
                            px = next_px()
                            ps = psX[px]
                            for c in range(NCH):
                                P.op("pe", lambda e, w=w, c=c, j=j, g=g, ps=ps: e.matmul(
                                    ps[:], lhsT=w[:, c * 512 + j * 128: c * 512 + j * 128 + 128], rhs=hs(c, g),
                                    start=(c == 0), stop=(c == NCH - 1)),
                                    reads=[("wm", wb, 0), ("h", c, g)], writes=[("px", px)])
                            ti = next_tmp()
                            P.op("act", lambda e, ti=ti, ps=ps: e.activation(out=tmp[ti][:], in_=ps[:], func=AF.Relu),
                                 reads=[("px", px)], writes=[("tmp", ti)])
                            P.op("pool", lambda e, ti=ti, j=j, g=g: e.tensor_tensor(
                                out=uT[:, j * S + g * TG: j * S + (g + 1) * TG], in0=tmp[ti][:], in1=tmp[ti][:], op=ALU.mult),
                                reads=[("tmp", ti)], writes=[("u", j, g)])
                    for g in range(NTG):
                        for c in range(NCH):
                            px = next_px()
                            ps = psX[px]
                            for j in range(4):
                                P.op("pe", lambda e, w=w, c=c, j=j, g=g, ps=ps: e.matmul(
                                    ps[:], lhsT=w[:, 4096 + j * 1024 + c * 128: 4096 + j * 1024 + c * 128 + 128],
                                    rhs=uT[:, j * S + g * TG: j * S + (g + 1) * TG], start=(j == 0), stop=(j == 3)),
                                    reads=[("wm", wb, 1), ("u", j, g)], writes=[("px", px)])
                            P.op("dve", lambda e, c=c, g=g, ps=ps: e.tensor_tensor(out=xs(c, g), in0=ps[:], in1=xs(c, g), op=ALU.add),
                                 reads=[("px", px), ("x", c, g)], writes=[("x", c, g)])
                P.emit_phase()

        def proj_fm(ps, wt, woff, wstride, m, nk, rhs_fn, g, wres, rres_fn, pxres, prow0=0):
            for k in range(nk):
                P.op("pe", lambda e, k=k: e.matmul(ps[prow0:prow0 + m, :], lhsT=wt[:, woff + k * wstride: woff + k * wstride + m],
                                                  rhs=rhs_fn(k, g), start=(k == 0), stop=(k == nk - 1)),
                     reads=[wres, rres_fn(k, g)], writes=[pxres])

        def outproj_chunk(wout_d2, kc, oH, ores, watt, wb, wo=5120, load=True, compute=True):
            if load:
                wload(watt[wb], wo, wview(wout_d2, kc * 128, 1, 0, 1024), 1, 1024, ("wa", wb, 5))
            if not compute:
                return
            for g in range(NTG):
                for c in range(NCH):
                    px = next_px()
                    ps = psX[px]
                    P.op("pe", lambda e, c=c, g=g, ps=ps: e.matmul(ps[:], lhsT=watt[wb][:, wo + c * 128: wo + c * 128 + 128],
                                                                   rhs=oH[:, g * TG:(g + 1) * TG], start=True, stop=True),
                         reads=[("wa", wb, 5), (ores, g)], writes=[("px", px)])
                    P.op("dve", lambda e, c=c, g=g, ps=ps: e.tensor_tensor(out=xs(c, g), in0=ps[:], in1=xs(c, g), op=ALU.add),
                         reads=[("px", px), ("x", c, g)], writes=[("x", c, g)])

        def even_phase(l, part, do_norm):
            i = l // 2
            lam_init = 0.8 - 0.6 * math.exp(-0.3 * l)
            isA = part == "A"
            with ExitStack() as ps_:
                st["npx"] = 2
                st["px"] = 0
                watt = [sb("watt%d" % k, [128, 6144], BF16, ps_) for k in range(2)]
                qTm = [sb("qTm%d" % k, [128, S], BF16, ps_) for k in range(2)]
                kT = sb("kT", [128, S], BF16, ps_)
                vS = sb("vS", [128, 16 * 128], BF16, ps_)
                oH = [sb("oH%d" % k, [128, S], BF16, ps_) for k in range(2)]
                NPT = 6
                PT = [sb("PT%d" % k, [128, 512], BF16, ps_) for k in range(NPT)]
                unit_b = [0]
                unit_a = [0]
                for k_ in range(2):
                    P.op("pool", lambda e, k_=k_: e.memset(qTm[k_][:], 0.0), writes=[("qT", k_, g) for g in range(NTG)])
                pt_i = [0]
                sbuf_i = [0]
                if isA:
                    ropeC = sb("ropeC", [128, S], BF16, ps_)
                    ropeS = sb("ropeS", [128, S], BF16, ps_)
                    NAT = 8
                    atmp = [sb("atmp%d" % k, [128, TG], F32, ps_) for k in range(NAT)]
                    at_i = [0]
                    P.op("sp", lambda e: e.dma_start(out=ropeC[:], in_=ropeA_d[0]), writes=["ropeC"], dma_ch="ropeC")
                    P.op("sp", lambda e: e.dma_start(out=ropeS[:], in_=ropeA_d[1]), writes=["ropeS"], dma_ch="ropeS")
                else:
                    G = sb("G", [128, 8 * _NG * 128], BF16, ps_)
                    gmask = sb("gmask", [128, _NG * 128], BF16, ps_)
                    P.op("sp", lambda e: e.dma_start(out=gmask[:], in_=gmask_d[:, :]), writes=["gmask"], dma_ch="gmask")
                    GW = _NG * 128
                    for h in range(8):
                        slot = st["stg"]
                        st["stg"] = (slot + 1) % NSTG
                        stg = stage[slot]
                        P.op("sp", lambda e, stg=stg, h=h: e.dma_start(out=stg[:, 0:GW], in_=gb_d[i][:, h * GW:(h + 1) * GW]),
                             writes=[("stg", slot)], dma_ch=("stg", slot))
                        P.op("dve", lambda e, stg=stg, h=h: e.scalar_tensor_tensor(
                            out=G[:, h * GW:(h + 1) * GW], in0=stg[:, 0:GW], scalar=8.0, in1=gmask[:], op0=ALU.mult, op1=ALU.add),
                            reads=[("stg", slot), "gmask"], writes=[("G", h)])

                if do_norm:
                    rmsnorm_x(8 * l, hs, lambda c, g: ("h", c, g))

                def rope_evac(ps_a, ps_b, pxa, pxb, dst, g, dres, rows=128):
                    t1 = next_tmp()
                    t2 = next_tmp()
                    P.op("dve", lambda e: e.tensor_tensor(out=tmp[t1][0:rows, :], in0=ps_a[0:rows, :], in1=ropeC[0:rows, g * TG:(g + 1) * TG], op=ALU.mult),
                         reads=[("px", pxa), "ropeC"], writes=[("tmp", t1)])
                    P.op("dve", lambda e: e.tensor_tensor(out=tmp[t2][0:rows, :], in0=ps_b[0:rows, :], in1=ropeS[0:rows, g * TG:(g + 1) * TG], op=ALU.mult),
                         reads=[("px", pxb), "ropeS"], writes=[("tmp", t2)])
                    if dst is None:
                        for hf in range(2):
                            P.op("pool", lambda e, hf=hf: e.tensor_tensor(out=qTm[hf][64 * hf:64 * hf + 64, g * TG:(g + 1) * TG], in0=tmp[t1][64 * hf:64 * hf + 64, :],
                                                                      in1=tmp[t2][64 * hf:64 * hf + 64, :], op=ALU.add),
                                 reads=[("tmp", t1), ("tmp", t2)], writes=[("qT", hf, g)])
                    else:
                        P.op("pool", lambda e: e.tensor_tensor(out=dst[0:rows, g * TG:(g + 1) * TG], in0=tmp[t1][0:rows, :], in1=tmp[t2][0:rows, :], op=ALU.add),
                             reads=[("tmp", t1), ("tmp", t2)], writes=[(dres, g)])

                hfn = lambda k, g: hs(k, g)
                hres = lambda k, g: ("h", k, g)

                def v_tokmajor(w, woff, wres, ncols):
                    for t4 in range(4):
                        px = next_px()
                        ps = psX[px]
                        for tt in range(4):
                            tok = (t4 * 4 + tt) * 128
                            for k in range(NCH):
                                P.op("pe", lambda e, k=k, tt=tt, tok=tok, ps=ps: e.matmul(
                                    ps[:, tt * 128: tt * 128 + ncols], lhsT=hT[:, k * S + tok: k * S + tok + 128],
                                    rhs=w[:, woff + k * 128: woff + k * 128 + ncols], start=(k == 0), stop=(k == NCH - 1)),
                                    reads=[wres, ("h", k, tok // TG)], writes=[("px", px)])
                        P.op("act", lambda e, t4=t4, ps=ps: e.activation(out=vS[:, t4 * 512:(t4 + 1) * 512], in_=ps[:], func=AF.Copy),
                             reads=[("px", px)], writes=[("vS", t4)])

                pend_out = None
                for hd in (range(4) if isA else ()):
                    wb = hd % 2
                    w = watt[wb]
                    wload(w, 0, wview(ev_in_d[i], 0, 8, 128 * hd, 128), 8, 128, ("wa", wb, 0))
                    wload(w, 1024, wview(ev_sw_d[i], 0, 8, 128 * hd, 128), 8, 128, ("wa", wb, 1))
                    wload(w, 2048, wview(ev_in_d[i], 0, 8, 512 + 128 * hd, 128), 8, 128, ("wa", wb, 2))
                    wload(w, 3072, wview(ev_sw_d[i], 0, 8, 512 + 128 * hd, 128), 8, 128, ("wa", wb, 3))
                    wload(w, 4096, wview(ev_in_d[i], 0, 8, 1024 + 128 * hd, 128), 8, 128, ("wa", wb, 4))
                    for (dst, dres, o0) in ((None, "qT", 0), (kT, "kT", 2048)):
                        for g in range(NTG):
                            pa = next_px()
                            proj_fm(psX[pa], w, o0, 128, 128, 8, hfn, g, ("wa", wb, o0 // 1024), hres, ("px", pa))
                            pb = next_px()
                            proj_fm(psX[pb], w, o0 + 1024, 128, 128, 8, hfn, g, ("wa", wb, o0 // 1024 + 1), hres, ("px", pb))
                            rope_evac(psX[pa], psX[pb], pa, pb, dst, g, dres)
                    v_tokmajor(w, 4096, ("wa", wb, 4), 128)
                    if pend_out is not None:
                        outproj_chunk(*pend_out, load=False)
                        pend_out = None
                    ob = hd % 2
                    outproj_chunk(ev_out_d[i], hd, oH[ob], ("oH", ob), watt, wb, compute=False)
                    pipe = Pipe(3)
                    hold = {}
                    later = []

                    def fin_a(g, half, aset, ob=ob, hold=hold):
                        aO, nO, aD, nD = aset
                        tO = at_i[0]
                        tD = (tO + 1) % NAT
                        at_i[0] = (tO + 2) % NAT
                        P.op("dve", lambda e: e.reciprocal(out=atmp[tD][:], in_=aD[:, :]), reads=[nD], writes=[("at", tD)])
                        P.op("dve", lambda e: e.tensor_tensor(out=atmp[tO][:], in0=aO[:, :], in1=atmp[tD][:], op=ALU.mult),
                             reads=[nO, ("at", tD)], writes=[("at", tO)])
                        if half == 0:
                            hold[g] = tO
                            return
                        a = hold[g]
                        b = tO
                        P.op("dve", lambda e: e.scalar_tensor_tensor(
                            out=atmp[a][:], in0=atmp[b][:], scalar=lams[:, 4 * i + 2: 4 * i + 3], in1=atmp[a][:], op0=ALU.mult, op1=ALU.add),
                            reads=[("at", a), ("at", b), ("lams", i, 2)], writes=[("at", a)])
                        later.append(lambda: subln_tail(g, a, tD, ob))

                    def subln_tail(g, a, tD, ob):
                        si = next_sq()
                        P.op("act", lambda e: e.activation(out=sq[si][:], in_=atmp[a][:], func=AF.Square), reads=[("at", a)], writes=[("sq", si)])
                        sb2 = sbuf_i[0]
                        sbuf_i[0] = (sb2 + 1) % 4
                        sreg2 = psS[:, sb2 * TG:(sb2 + 1) * TG]
                        P.op("pe", lambda e: e.matmul(sreg2, lhsT=ones[:], rhs=sq[si][:], start=True, stop=True),
                             reads=[("sq", si), "ones"], writes=[("psS", sb2)])
                        k2 = 1.0 / ((1.0 - lam_init) ** 2)
                        P.op("act", lambda e: e.activation(out=atmp[tD][:], in_=sreg2, func=AF.Sqrt, scale=k2 / 128.0, bias=EPS * k2),
                             reads=[("psS", sb2)], writes=[("at", tD)])
                        P.op("dve", lambda e: e.reciprocal(out=atmp[tD][:], in_=atmp[tD][:]), reads=[("at", tD)], writes=[("at", tD)])
                        P.op("dve", lambda e: e.scalar_tensor_tensor(
                            out=oH[ob][:, g * TG:(g + 1) * TG], in0=atmp[a][:], scalar=vecs[:, 78 + i: 79 + i], in1=atmp[tD][:], op0=ALU.mult, op1=ALU.mult),
                            reads=[("at", a), ("at", tD), "vecs"], writes=[(("oH", ob), g)])

                    def a_step(g, half, kt, aset):
                        aO, nO, aD, nD = aset
                        sbi = sbuf_i[0]
                        sbuf_i[0] = (sbi + 1) % 4
                        pti = pt_i[0]
                        pt_i[0] = (pti + 1) % NPT
                        sreg = psS[:, sbi * TG:(sbi + 1) * TG]

                        def qk():
                            P.op("pe", lambda e: e.matmul(sreg, lhsT=kT[:, kt * 128:(kt + 1) * 128], rhs=qTm[half][:, g * TG:(g + 1) * TG], start=True, stop=True),
                                 reads=[("kT", kt // 4), ("qT", half, g)], writes=[("psS", sbi)])
                            P.op("act", lambda e: e.activation(out=PT[pti][:, 0:TG], in_=sreg, func=AF.Exp, scale=0.125),
                                 reads=[("psS", sbi)], writes=[("PT", pti)])

                        def pv():
                            P.op("pe", lambda e: e.matmul(aO[:, :], lhsT=vS[:, kt * 128:(kt + 1) * 128], rhs=PT[pti][:, 0:TG], start=(kt == 0), stop=(kt == 15)),
                                 reads=[("PT", pti), ("vS", kt // 4)], writes=[nO])
                            P.op("pe", lambda e: e.matmul(aD[:, :], lhsT=ones[:], rhs=PT[pti][:, 0:TG], start=(kt == 0), stop=(kt == 15)),
                                 reads=[("PT", pti), "ones"], writes=[nD])
                            if kt == 15:
                                fin_a(g, half, aset)
                            if kt == 7:
                                while later:
                                    later.pop(0)()
                        pipe.step(qk, pv)

                    accs_a = [(psO, ("px", 3), psD, ("px", 2)), (psX[0], ("px", 0), psX[1], ("px", 1))]
                    for g in range(NTG):
                        for half in range(2):
                            aset = accs_a[unit_a[0] % 2]
                            unit_a[0] += 1
                            for kt in range(16):
                                a_step(g, half, kt, aset)
                    pipe.flush()
                    while later:
                        later.pop(0)()
                    pend_out = (ev_out_d[i], hd, oH[ob], ("oH", ob), watt, wb)
                if pend_out is not None:
                    outproj_chunk(*pend_out, load=False)
                    pend_out = None

                for cpair in (() if isA else range(4)):
                    wb = cpair % 2
                    w = watt[wb]
                    wload(w, 0, wview(ev_in_d[i], 0, 8, 1536 + 128 * cpair, 128), 8, 128, ("wa", wb, 0))
                    wload(w, 2048, wview(ev_in_d[i], 0, 8, 2048 + 128 * cpair, 128), 8, 128, ("wa", wb, 2))
                    wload(w, 4096, wview(ev_in_d[i], 0, 8, 2560 + 128 * cpair, 128), 8, 128, ("wa", wb, 4))
                    for (dst, dres, o0) in ((None, "qT", 0), (kT, "kT", 2048)):
                        for g in range(NTG):
                            pa = next_px()
                            proj_fm(psX[pa], w, o0, 128, 128, 8, hfn, g, ("wa", wb, o0 // 1024), hres, ("px", pa))
                            if dst is None:
                                for hf in range(2):
                                    P.op("act", lambda e, pa=pa, hf=hf, g=g: e.activation(out=qTm[hf][64 * hf:64 * hf + 64, g * TG:(g + 1) * TG],
                                                                                      in_=psX[pa][64 * hf:64 * hf + 64, :], func=AF.Copy),
                                         reads=[("px", pa)], writes=[("qT", hf, g)])
                            else:
                                P.op("act", lambda e, pa=pa, dst=dst, g=g: e.activation(out=dst[:, g * TG:(g + 1) * TG], in_=psX[pa][:], func=AF.Copy),
                                     reads=[("px", pa)], writes=[(dres, g)])
                    v_tokmajor(w, 4096, ("wa", wb, 4), 128)
                    if pend_out is not None:
                        outproj_chunk(*pend_out, load=False)
                        pend_out = None
                    ob = cpair % 2
                    outproj_chunk(ev_out_d[i], 4 + cpair, oH[ob], ("oH", ob), watt, wb, compute=False)
                    accs_b = [(psO, ("px", 3), psD, ("px", 2)), (psX[0], ("px", 0), psX[1], ("px", 1))]
                    tl = []
                    for hh in range(2):
                        for m4 in range(4):
                            aset = accs_b[unit_b[0] % 2]
                            unit_b[0] += 1
                            for mm in range(4):
                                m = m4 * 4 + mm
                                tiles = _PLAN[m]
                                for ti_, (t, gid) in enumerate(tiles):
                                    tl.append(dict(hh=hh, h=2 * cpair + hh, m=m, mm=mm, m4=m4, t=t, gid=gid, first=(ti_ == 0),
                                                   last=(ti_ == len(tiles) - 1), aset=aset, fin=(mm == 3 and ti_ == len(tiles) - 1)))
                    pipe = Pipe(3)

                    def fin_b(d, ob=ob):
                        aO, nO, aD, nD = d["aset"]
                        r0 = 64 * d["hh"]
                        m4 = d["m4"]
                        tr = next_tmp()
                        P.op("dve", lambda e: e.reciprocal(out=tmp[tr][r0:r0 + 64, :], in_=aD[r0:r0 + 64, :]),
                             reads=[nD], writes=[("tmp", tr)])
                        P.op("dve", lambda e: e.tensor_tensor(
                            out=oH[ob][r0:r0 + 64, m4 * TG:(m4 + 1) * TG], in0=aO[r0:r0 + 64, :], in1=tmp[tr][r0:r0 + 64, :], op=ALU.mult),
                            reads=[nO, ("tmp", tr)], writes=[(("oH", ob), m4)])

                    def b_chunk(ch):
                        sbi = sbuf_i[0]
                        sbuf_i[0] = (sbi + 1) % 4
                        pti = pt_i[0]
                        pt_i[0] = (pti + 1) % NPT
                        n = len(ch)

                        def qk():
                            for j, d in enumerate(ch):
                                sreg = psS[:, sbi * TG + j * 128: sbi * TG + (j + 1) * 128]
                                goff = (d["h"] * _NG + d["gid"]) * 128
                                P.op("pe", lambda e, sreg=sreg, goff=goff: e.matmul(sreg, lhsT=ident[:], rhs=G[:, goff:goff + 128], start=True, stop=False),
                                     reads=[("G", d["h"]), "ident"], writes=[("psS", sbi)])
                                P.op("pe", lambda e, sreg=sreg, d=d: e.matmul(
                                    sreg, lhsT=kT[:, d["t"] * 128:(d["t"] + 1) * 128], rhs=qTm[d["hh"]][:, d["m"] * 128:(d["m"] + 1) * 128], start=False, stop=True),
                                    reads=[("kT", d["t"] // 4), ("qT", d["hh"], d["m"] // 4)], writes=[("psS", sbi)])
                            P.op("act", lambda e: e.activation(out=PT[pti][:, 0:n * 128], in_=psS[:, sbi * TG: sbi * TG + n * 128], func=AF.Exp, scale=0.125),
                                 reads=[("psS", sbi)], writes=[("PT", pti)])

                        def pv():
                            for j, d in enumerate(ch):
                                aO, nO, aD, nD = d["aset"]
                                mm = d["mm"]
                                P.op("pe", lambda e, j=j, d=d, aO=aO, mm=mm: e.matmul(
                                    aO[:, mm * 128:(mm + 1) * 128], lhsT=vS[:, d["t"] * 128:(d["t"] + 1) * 128], rhs=PT[pti][:, j * 128:(j + 1) * 128],
                                    start=d["first"], stop=d["last"]),
                                    reads=[("PT", pti), ("vS", d["t"] // 4)], writes=[nO])
                                P.op("pe", lambda e, j=j, d=d, aD=aD, mm=mm: e.matmul(
                                    aD[:, mm * 128:(mm + 1) * 128], lhsT=ones[:], rhs=PT[pti][:, j * 128:(j + 1) * 128],
                                    start=d["first"], stop=d["last"]),
                                    reads=[("PT", pti), "ones"], writes=[nD])
                                if d["fin"]:
                                    fin_b(d)
                        pipe.step(qk, pv)

                    for c0 in range(0, len(tl), 4):
                        b_chunk(tl[c0:c0 + 4])
                    pipe.flush()
                    pend_out = (ev_out_d[i], 4 + cpair, oH[ob], ("oH", ob), watt, wb)
                if pend_out is not None:
                    outproj_chunk(*pend_out, load=False)
                    pend_out = None
                P.emit_phase()

        def odd_phase(l):
            i = l // 2
            scale = 1.0 / math.sqrt(96.0)
            with ExitStack() as ps_:
                st["npx"] = 4
                st["px"] = 0
                watt = [sb("watt%d" % k, [128, 2048], BF16, ps_) for k in range(2)]
                win = sb("win", [128, 8 * 416], BF16, ps_)
                winsw = sb("winsw", [128, 8 * 96], BF16, ps_)
                qTt = [sb("qT%d" % k, [128, S], BF16, ps_) for k in range(2)]
                kTt = [sb("kT%d" % k, [128, S], BF16, ps_) for k in range(2)]
                vSt = [sb("vS%d" % k, [128, 16 * 128], BF16, ps_) for k in range(2)]
                oH = [sb("oH%d" % k, [128, S], BF16, ps_) for k in range(2)]
                NPT = 6
                PT = [sb("PT%d" % k, [128, TG], BF16, ps_) for k in range(NPT)]
                ropeC = sb("ropeC", [128, S], BF16, ps_)
                ropeS = sb("ropeS", [128, S], BF16, ps_)
                for k_ in range(2):
                    P.op("pool", lambda e, k_=k_: e.memset(qTt[k_][:], 0.0), writes=[(("qT", k_), g) for g in range(NTG)])
                for k_ in range(2):
                    P.op("pool", lambda e, k_=k_: e.memset(kTt[k_][:], 0.0),
                         writes=[(("kTn", k_), g) for g in range(NTG)] + [(("kTr", k_), g) for g in range(NTG)])
                accs = [(psX[3], ("px", 3)), (psX[2], ("px", 2))]
                unit = [0]
                cqn = sb("cqn", [128, 2 * S], BF16, ps_)
                ckvn = sb("ckvn", [128, S], BF16, ps_)
                pt_i = [0]
                sbuf_i = [0]
                P.op("sp", lambda e: e.dma_start(out=ropeC[:], in_=ropeM_d[0]), writes=["ropeC"], dma_ch="ropeC")
                P.op("sp", lambda e: e.dma_start(out=ropeS[:], in_=ropeM_d[1]), writes=["ropeS"], dma_ch="ropeS")
                for k in range(2):
                    P.op("dve", lambda e, k=k: e.memset(vSt[k][:], 1.0), writes=[("vS", k, 0), ("vS", k, 1)])
                rmsnorm_x(8 * l, hs, lambda c, g: ("h", c, g))
                wload(win, 0, wview(od_in_d[i], 0, 8, 0, 416), 8, 416, "win")
                wload(winsw, 0, wview(od_insw_d[i], 0, 8, 0, 96), 8, 96, "winsw")
                hfn = lambda k, g: hs(k, g)
                hres = lambda k, g: ("h", k, g)
                for g in range(NTG):
                    pq = [next_px(), next_px()]
                    for c2 in range(2):
                        proj_fm(psX[pq[c2]], win, 128 * c2, 416, 128, 8, hfn, g, "win", hres, ("px", pq[c2]))
                    pss = next_px()
                    for c2 in range(2):
                        si = next_sq()
                        P.op("act", lambda e, si=si, p_=pq[c2]: e.activation(out=sq[si][:], in_=psX[p_][:], func=AF.Square),
                             reads=[("px", pq[c2])], writes=[("sq", si)])
                        P.op("pe", lambda e, si=si, c2=c2, pss=pss: e.matmul(psX[pss][:], lhsT=ones[:], rhs=sq[si][:], start=(c2 == 0), stop=(c2 == 1)),
                             reads=[("sq", si), "ones"], writes=[("px", pss)])
                    tr = next_tmp()
                    P.op("act", lambda e, tr=tr, pss=pss: e.activation(out=tmp[tr][:], in_=psX[pss][:], func=AF.Sqrt, scale=1.0 / 256.0, bias=EPS),
                         reads=[("px", pss)], writes=[("tmp", tr)])
                    P.op("dve", lambda e, tr=tr: e.reciprocal(out=tmp[tr][:], in_=tmp[tr][:]), reads=[("tmp", tr)], writes=[("tmp", tr)])
                    for c2 in range(2):
                        P.op("dve", lambda e, tr=tr, c2=c2, g=g, p_=pq[c2]: e.scalar_tensor_tensor(
                            out=cqn[:, c2 * S + g * TG: c2 * S + (g + 1) * TG], in0=psX[p_][:], scalar=vecs[:, 72 + 2 * i + c2: 73 + 2 * i + c2],
                            in1=tmp[tr][:], op0=ALU.mult, op1=ALU.mult),
                            reads=[("px", pq[c2]), ("tmp", tr), "vecs"], writes=[("cqn", c2, g)])
                    pk = next_px()
                    proj_fm(psX[pk], win, 256, 416, 128, 8, hfn, g, "win", hres, ("px", pk))
                    pss = next_px()
                    si = next_sq()
                    P.op("act", lambda e, si=si, pk=pk: e.activation(out=sq[si][:], in_=psX[pk][:], func=AF.Square),
                         reads=[("px", pk)], writes=[("sq", si)])
                    P.op("pe", lambda e, si=si, pss=pss: e.matmul(psX[pss][:], lhsT=ones[:], rhs=sq[si][:], start=True, stop=True),
                         reads=[("sq", si), "ones"], writes=[("px", pss)])
                    tr = next_tmp()
                    P.op("act", lambda e, tr=tr, pss=pss: e.activation(out=tmp[tr][:], in_=psX[pss][:], func=AF.Sqrt, scale=1.0 / 128.0, bias=EPS),
                         reads=[("px", pss)], writes=[("tmp", tr)])
                    P.op("dve", lambda e, tr=tr: e.reciprocal(out=tmp[tr][:], in_=tmp[tr][:]), reads=[("tmp", tr)], writes=[("tmp", tr)])
                    P.op("dve", lambda e, tr=tr, g=g, pk=pk: e.scalar_tensor_tensor(
                        out=ckvn[:, g * TG:(g + 1) * TG], in0=psX[pk][:], scalar=vecs[:, 76 + i: 77 + i], in1=tmp[tr][:], op0=ALU.mult, op1=ALU.mult),
                        reads=[("px", pk), ("tmp", tr), "vecs"], writes=[("ckvn", g)])
                    pa = next_px()
                    proj_fm(psX[pa], win, 320, 416, 96, 8, hfn, g, "win", hres, ("px", pa))
                    pb = next_px()
                    proj_fm(psX[pb], winsw, 0, 96, 96, 8, hfn, g, "winsw", hres, ("px", pb))
                    t1 = next_tmp()
                    t2 = next_tmp()
                    P.op("dve", lambda e, t1=t1, pa=pa, g=g: e.tensor_tensor(out=tmp[t1][64:96, :], in0=psX[pa][64:96, :], in1=ropeC[64:96, g * TG:(g + 1) * TG], op=ALU.mult),
                         reads=[("px", pa), "ropeC"], writes=[("tmp", t1)])
                    P.op("dve", lambda e, t2=t2, pb=pb, g=g: e.tensor_tensor(out=tmp[t2][64:96, :], in0=psX[pb][64:96, :], in1=ropeS[64:96, g * TG:(g + 1) * TG], op=ALU.mult),
                         reads=[("px", pb), "ropeS"], writes=[("tmp", t2)])
                    for k in range(2):
                        P.op("pool", lambda e, t1=t1, t2=t2, g=g, k=k: e.tensor_tensor(
                            out=kTt[k][64:96, g * TG:(g + 1) * TG], in0=tmp[t1][64:96, :], in1=tmp[t2][64:96, :], op=ALU.add),
                            reads=[("tmp", t1), ("tmp", t2)], writes=[(("kTr", k), g)])

                cfn = lambda k, g: cqn[:, k * S + g * TG: k * S + (g + 1) * TG]
                cres = lambda k, g: ("cqn", k, g)
                st["npx"] = 2
                st["px"] = 0

                def proj_head(h):
                    wb = h % 2
                    w = watt[wb]
                    kT = kTt[wb]
                    vS = vSt[wb]
                    qT = qTt[wb]
                    vc0 = 0 if wb == 0 else 64
                    wload(w, 0, wview(od_uq_d[i], 0, 2, 96 * h, 96), 2, 96, ("wa", wb, 0))
                    wload(w, 256, wview(od_uqsw_d[i], 0, 2, 96 * h, 96), 2, 96, ("wa", wb, 1))
                    wload(w, 512, wview(od_ukv_d[i], 0, 1, 128 * h, 128), 1, 128, ("wa", wb, 2))
                    for g in range(NTG):
                        pa = next_px()
                        proj_fm(psX[pa], w, 0, 96, 128, 2, cfn, g, ("wa", wb, 0), cres, ("px", pa))
                        pb = next_px()
                        proj_fm(psX[pb], w, 256, 96, 128, 2, cfn, g, ("wa", wb, 1), cres, ("px", pb))
                        t1 = next_tmp()
                        t2 = next_tmp()
                        P.op("dve", lambda e, t1=t1, pa=pa, g=g: e.tensor_tensor(out=tmp[t1][0:96, :], in0=psX[pa][0:96, :], in1=ropeC[0:96, g * TG:(g + 1) * TG], op=ALU.mult),
                             reads=[("px", pa), "ropeC"], writes=[("tmp", t1)])
                        P.op("dve", lambda e, t2=t2, pb=pb, g=g: e.tensor_tensor(out=tmp[t2][0:96, :], in0=psX[pb][0:96, :], in1=ropeS[0:96, g * TG:(g + 1) * TG], op=ALU.mult),
                             reads=[("px", pb), "ropeS"], writes=[("tmp", t2)])
                        P.op("pool", lambda e, t1=t1, t2=t2, g=g: e.tensor_tensor(out=qT[0:96, g * TG:(g + 1) * TG], in0=tmp[t1][0:96, :], in1=tmp[t2][0:96, :], op=ALU.add),
                             reads=[("tmp", t1), ("tmp", t2)], writes=[(("qT", wb), g)])
                        pk = next_px()
                        P.op("pe", lambda e, pk=pk, g=g: e.matmul(psX[pk][:, :], lhsT=w[:, 512:640], rhs=ckvn[:, g * TG:(g + 1) * TG], start=True, stop=True),
                             reads=[("wa", wb, 2), ("ckvn", g)], writes=[("px", pk)])
                        P.op("act", lambda e, pk=pk, g=g: e.activation(out=kT[0:64, g * TG:(g + 1) * TG], in_=psX[pk][0:64, :], func=AF.Copy),
                             reads=[("px", pk)], writes=[(("kTn", wb), g)])
                    for t8 in range(2):
                        px = next_px()
                        ps = psX[px]
                        for tt in range(8):
                            tok = (t8 * 8 + tt) * 128
                            P.op("pe", lambda e, tt=tt, tok=tok, ps=ps: e.matmul(
                                ps[:, tt * 64:(tt + 1) * 64], lhsT=ckvn[:, tok:tok + 128], rhs=w[:, 576:640], start=True, stop=True),
                                reads=[("wa", wb, 2), ("ckvn", tok // TG)], writes=[("px", px)])
                        dstv = vS[:, t8 * 1024:(t8 + 1) * 1024].rearrange("p (t c) -> p t c", c=128)[:, :, vc0:vc0 + 64]
                        srcv = ps[:, :].rearrange("p (t c) -> p t c", c=64)
                        P.op("act", lambda e, dstv=dstv, srcv=srcv: e.activation(out=dstv, in_=srcv, func=AF.Copy),
                             reads=[("px", px)], writes=[("vS", wb, t8)])

                def attn_head(h):
                    wb = h % 2
                    kT = kTt[wb]
                    vS = vSt[wb]
                    qT = qTt[wb]
                    ob = (h // 2) % 2
                    orow = 64 * (h % 2)
                    drow = 64 - orow
                    pipe = Pipe(3)

                    def fin_m(g, aO, nO):
                        tr = next_tmp()
                        P.op("dve", lambda e: e.reciprocal(out=tmp[tr][drow:drow + 64, :], in_=aO[drow:drow + 64, :]),
                             reads=[nO], writes=[("tmp", tr)])
                        P.op("dve", lambda e: e.tensor_tensor(
                            out=oH[ob][orow:orow + 64, g * TG:(g + 1) * TG], in0=aO[orow:orow + 64, :], in1=tmp[tr][drow:drow + 64, :], op=ALU.mult),
                            reads=[nO, ("tmp", tr)], writes=[(("oH", ob), g)])

                    def m_step(g, kt, aO, nO):
                        sbi = sbuf_i[0]
                        sbuf_i[0] = (sbi + 1) % 4
                        pti = pt_i[0]
                        pt_i[0] = (pti + 1) % NPT
                        sreg = psS[:, sbi * TG:(sbi + 1) * TG]

                        def qk():
                            P.op("pe", lambda e: e.matmul(sreg, lhsT=kT[:, kt * 128:(kt + 1) * 128], rhs=qT[:, g * TG:(g + 1) * TG], start=True, stop=True),
                                 reads=[(("kTn", wb), kt // 4), (("kTr", wb), kt // 4), (("qT", wb), g)], writes=[("psS", sbi)])
                            P.op("act", lambda e: e.activation(out=PT[pti][:, 0:TG], in_=sreg, func=AF.Exp, scale=scale),
                                 reads=[("psS", sbi)], writes=[("PT", pti)])

                        def pv():
                            P.op("pe", lambda e: e.matmul(aO, lhsT=vS[:, kt * 128:(kt + 1) * 128], rhs=PT[pti][:, 0:TG], start=(kt == 0), stop=(kt == 15)),
                                 reads=[("PT", pti), ("vS", wb, kt // 8)], writes=[nO])
                            if kt == 15:
                                fin_m(g, aO, nO)
                        pipe.step(qk, pv)

                    for g in range(NTG):
                        aO, nO = accs[unit[0] % 2]
                        unit[0] += 1
                        for kt in range(16):
                            m_step(g, kt, aO, nO)
                    pipe.flush()
                    if h % 2 == 1:
                        outproj_chunk(od_out_d[i], h // 2, oH[ob], ("oH", ob), watt, wb, wo=1024)

                proj_head(0)
                for h in range(16):
                    if h + 1 < 16:
                        proj_head(h + 1)
                    attn_head(h)
                P.emit_phase()

        for s_ in range(nseq):
            for c in range(NCH):
                P.op("sp", lambda e, c=c, s_=s_: e.dma_start(out=xT[:, c * S:(c + 1) * S], in_=xT_d[s_][c * 128:(c + 1) * 128, :]),
                     writes=[("x", c, g) for g in range(NTG)], dma_ch=("xin", c))
            P.emit_phase()
            for l in layers:
                if do_mix:
                    if l % 2 == 0:
                        first = True
                        for part in parts:
                            even_phase(l, part, first)
                            first = False
                    else:
                        odd_phase(l)
                if do_mlp:
                    mlp_phase(l)
            with ExitStack() as ps_:
                st["npx"] = 2
                ob = [sb("outb%d" % k, [128, S], F32, ps_) for k in range(2)]
                rs = [sb("rs%d" % k, [128, TG], F32, ps_) for k in range(NTG)]
                for g in range(NTG):
                    px = next_px()
                    ps = psX[px]
                    for c in range(NCH):
                        P.op("act", lambda e, c=c, g=g: e.activation(out=hs(c, g), in_=xs(c, g), func=AF.Square),
                             reads=[("x", c, g)], writes=[("h", c, g)])
                        P.op("pe", lambda e, c=c, g=g, ps=ps: e.matmul(ps[:], lhsT=ones[:], rhs=hs(c, g), start=(c == 0), stop=(c == NCH - 1)),
                             reads=[("h", c, g), "ones"], writes=[("px", px)])
                    P.op("act", lambda e, g=g, ps=ps: e.activation(out=rs[g][:], in_=ps[:], func=AF.Sqrt, scale=1.0 / D, bias=EPS),
                         reads=[("px", px)], writes=[("rs", g)])
                    P.op("dve", lambda e, g=g: e.reciprocal(out=rs[g][:], in_=rs[g][:]), reads=[("rs", g)], writes=[("rs", g)])
                for c in range(NCH):
                    k = c % 2
                    for g in range(NTG):
                        P.op("dve", lambda e, c=c, g=g, k=k: e.scalar_tensor_tensor(
                            out=ob[k][:, g * TG:(g + 1) * TG], in0=xs(c, g), scalar=vecs[:, 64 + c:65 + c], in1=rs[g][:], op0=ALU.mult, op1=ALU.mult),
                            reads=[("x", c, g), ("rs", g), "vecs"], writes=[("ob", k)])
                    P.op("sp", lambda e, c=c, k=k, s_=s_: e.dma_start(out=out_d[s_][c * 128:(c + 1) * 128, :], in_=ob[k][:]),
                         reads=[("ob", k)], dma_ch=("out", k))
                P.emit_phase(final=(s_ == nseq - 1))
    return nc


def prep_shared(inp):
    f = lambda a: np.ascontiguousarray(np.asarray(a, dtype=np.float32))
    vecs = np.zeros((128, 80), np.float32)
    for l in range(4):
        vecs[:, 8 * l:8 * l + 8] = f(inp["ln_mix_g"])[l].reshape(8, 128).T
        vecs[:, 32 + 8 * l:32 + 8 * l + 8] = f(inp["ln_mlp_g"])[l].reshape(8, 128).T
    vecs[:, 64:72] = f(inp["ln_f_g"]).reshape(8, 128).T
    for i in range(2):
        vecs[:, 72 + 2 * i:74 + 2 * i] = f(inp["od_q_norm_g"])[i].reshape(2, 128).T
        vecs[:, 76 + i] = f(inp["od_kv_norm_g"])[i]
        vecs[:, 78 + i] = f(inp["ev_subln_g"])[i]
    lamv = np.zeros((128, 512), np.float32)
    for i in range(2):
        for j, nm in enumerate(("ev_lambda_q1", "ev_lambda_k1", "ev_lambda_q2", "ev_lambda_k2")):
            lamv[:, i * 256 + j * 64: i * 256 + (j + 1) * 64] = f(inp[nm])[i][None, :]
    ev_in = f(inp["ev_w_in"])
    permA = _swap_perm(1024, 64, 0, 8)
    ev_sw = np.ascontiguousarray(ev_in[:, :, :1024][:, :, permA])
    od_in = f(inp["od_w_in"])
    perm_kr = _swap_perm(32, 32, 0, 16)
    od_in_sw = np.ascontiguousarray(od_in[:, :, 320:416].copy())
    od_in_sw[:, :, 64:96] = od_in[:, :, 384:416][:, :, perm_kr]
    od_uq = f(inp["od_w_uq"])
    perm_q = _swap_perm(1536, 96, 64, 16)
    od_uq_sw = np.ascontiguousarray(od_uq[:, :, perm_q])
    CA, SA = _rope_tables_A()
    CM, SM = _rope_tables_M()
    rpb = f(inp["ev_rpb"])
    gb = rpb[:, :, _RIDX, _CIDX]
    gb = np.ascontiguousarray(gb.transpose(0, 3, 1, 2, 4)).reshape(2, 128, 8 * _NG * 128)
    gmask = np.where(_VALID, 0.0, MASKV).astype(np.float32)
    gmask = np.ascontiguousarray(gmask.transpose(1, 0, 2)).reshape(128, _NG * 128).astype(ml_dtypes.bfloat16)
    return {
        "vecs": vecs, "lamv": lamv,
        "w_up": f(inp["w_up"]), "w_down": f(inp["w_down"]),
        "ev_w_in": ev_in, "ev_w_sw": ev_sw, "ev_w_out": f(inp["ev_w_out"]),
        "od_w_in": od_in, "od_w_in_sw": od_in_sw, "od_w_uq": od_uq, "od_w_uq_sw": od_uq_sw,
        "od_w_ukv": f(inp["od_w_ukv"]), "od_w_out": f(inp["od_w_out"]),
        "ropeA": np.stack([CA, SA]).astype(ml_dtypes.bfloat16), "ropeM": np.stack([CM, SM]).astype(ml_dtypes.bfloat16),
        "gb": gb, "gmask": gmask,
        "ident": np.eye(128, dtype=np.float32).astype(ml_dtypes.bfloat16),
    }


_NC_CACHE = {}


def kernel(**inputs):
    x = np.asarray(inputs["x"], dtype=np.float32)
    B = x.shape[0]
    nseq = B // NCORES
    shared = prep_shared(inputs)
    key = nseq
    if key not in _NC_CACHE:
        _NC_CACHE[key] = build(nseq=nseq)
    nc = _NC_CACHE[key]
    in_maps = []
    for c in range(NCORES):
        xs_ = x[c * nseq:(c + 1) * nseq]
        m = dict(shared)
        m["xT"] = np.ascontiguousarray(xs_.transpose(0, 2, 1))
        in_maps.append(m)
    res = run_bass_kernel_spmd(nc, in_maps, core_ids=list(range(NCORES)))
    out = np.empty((B, S, D), np.float32)
    for c in range(NCORES):
        o = np.asarray(res.results[c]["outT"])
        out[c * nseq:(c + 1) * nseq] = o.transpose(0, 2, 1)
    return out
```

```python
import math
from contextlib import ExitStack

import numpy as np
import ml_dtypes
import concourse.bass as bass
import concourse.mybir as mybir
from concourse.bass_utils import run_bass_kernel_spmd

F32 = mybir.dt.float32
BF16 = mybir.dt.bfloat16
AF = mybir.ActivationFunctionType
ALU = mybir.AluOpType
AX = mybir.AxisListType

S = 2048
D = 1024
NCH = 8
NTG = 4
TG = 512
EPS = 1e-5
NCORES = 8
ENGS = ("pe", "act", "dve", "pool", "sp")


class Op:
    __slots__ = ("eng", "fn", "deps", "dma_ch", "sig", "idx", "has_consumer", "waits")

    def __init__(self, eng, fn, dma_ch=None):
        self.eng = eng
        self.fn = fn
        self.deps = set()
        self.dma_ch = dma_ch
        self.sig = None
        self.has_consumer = False
        self.waits = []


class Prog:
    def __init__(self, nc, es):
        self.nc = nc
        self.es = es
        self.sems = {}
        self.cnt = {e: 0 for e in ENGS}
        self.dcnt = {}
        self.reset_phase()
        self.nphase = 0

    def reset_phase(self):
        self.ops = []
        self.last_writer = {}
        self.readers = {}

    def sem(self, key):
        s = self.sems.get(key)
        if s is None:
            name = "s_" + "_".join(str(k) for k in key)
            s = self.es.enter_context(self.nc.semaphore(name))
            self.sems[key] = s
        return s

    def op(self, eng, fn, reads=(), writes=(), dma_ch=None):
        o = Op(eng, fn, dma_ch)
        o.idx = len(self.ops)
        deps = set()
        for r in reads:
            w = self.last_writer.get(r)
            if w is not None:
                deps.add(w)
        for r in writes:
            w = self.last_writer.get(r)
            if w is not None:
                deps.add(w)
            for rd in self.readers.get(r, ()):
                deps.add(rd)
        o.deps = deps
        self.ops.append(o)
        for r in reads:
            self.readers.setdefault(r, []).append(o.idx)
        for r in writes:
            self.last_writer[r] = o.idx
            self.readers[r] = []
        return o

    def emit_phase(self, final=False):
        ops = self.ops
        if not ops:
            return
        nc = self.nc
        for o in ops:
            best = {}
            for d in o.deps:
                p = ops[d]
                if p.dma_ch is not None:
                    key = ("dma", p.dma_ch)
                elif p.eng != o.eng or o.eng in ("act", "dve", "pool"):
                    key = ("eng", p.eng)
                else:
                    continue
                if d > best.get(key, -1):
                    best[key] = d
            keep = set(best.values())
            o.deps = keep
            for d in keep:
                ops[d].has_consumer = True
        last = {}
        for o in ops:
            if o.dma_ch is None:
                last[o.eng] = o
        for o in last.values():
            o.has_consumer = True
        prev_cnt = dict(self.cnt)
        prev_dcnt = dict(self.dcnt)
        for o in ops:
            if o.dma_ch is not None:
                self.dcnt[o.dma_ch] = self.dcnt.get(o.dma_ch, 0) + 16
                o.sig = ("dma", o.dma_ch, self.dcnt[o.dma_ch])
            elif o.has_consumer:
                self.cnt[o.eng] += 1
                o.sig = ("eng", o.eng, self.cnt[o.eng])
        waited = {e: {} for e in ENGS}
        for e in ENGS:
            for e2 in ENGS:
                if prev_cnt[e2] > 0:
                    waited[e][("eng", e2)] = prev_cnt[e2]
            for ch, v in prev_dcnt.items():
                waited[e][("dma", ch)] = v
        for o in ops:
            need = {}
            for d in o.deps:
                s = ops[d].sig
                key = (s[0], s[1])
                if s[2] > need.get(key, 0):
                    need[key] = s[2]
            for key, v in need.items():
                if waited[o.eng].get(key, 0) >= v:
                    continue
                waited[o.eng][key] = v
                o.waits.append((key, v))
        by_eng = {e: [o for o in ops if o.eng == e] for e in ENGS}
        sems = self.sem
        final_d = dict(self.dcnt)
        final_c = dict(self.cnt)

        def run(eng_name, eng):
            for e2 in ENGS:
                if e2 != eng_name and prev_cnt[e2] > 0:
                    eng.wait_ge(sems(("eng", e2)), prev_cnt[e2])
            for ch, v in prev_dcnt.items():
                eng.wait_ge(sems(("dma", ch)), v)
            for o in by_eng[eng_name]:
                for key, v in o.waits:
                    eng.wait_ge(sems(key), v)
                ins = o.fn(eng)
                if o.dma_ch is not None:
                    ins.then_inc(sems(("dma", o.dma_ch)), 16)
                elif o.sig is not None:
                    ins.then_inc(sems(("eng", eng_name)), 1)
            if final:
                for ch, v in final_d.items():
                    eng.wait_ge(sems(("dma", ch)), v)
                for e2 in ENGS:
                    if e2 != eng_name and final_c[e2] > 0:
                        eng.wait_ge(sems(("eng", e2)), final_c[e2])

        for e in ENGS:
            sems(("eng", e))
        for ch in self.dcnt:
            sems(("dma", ch))
        with nc.Block() as block:
            @block.sync
            def _(e):
                run("sp", e)

            @block.tensor
            def _(e):
                run("pe", e)

            @block.scalar
            def _(e):
                run("act", e)

            @block.vector
            def _(e):
                run("dve", e)

            @block.gpsimd
            def _(e):
                run("pool", e)
        self.nphase += 1
        self.reset_phase()


class Pipe:
    def __init__(self, look):
        self.look = look
        self.q = []

    def step(self, qk, pv):
        qk()
        self.q.append(pv)
        if len(self.q) > self.look:
            self.q.pop(0)()

    def flush(self):
        while self.q:
            self.q.pop(0)()


def _rot_tables(rot_dim, theta=500000.0):
    pos = np.arange(S, dtype=np.float32)
    inv = (np.float32(theta) ** (-np.arange(0, rot_dim, 2, dtype=np.float32) / np.float32(rot_dim))).astype(np.float32)
    ang = (pos[:, None] * inv[None, :]).astype(np.float32)
    return np.cos(ang).astype(np.float32), np.sin(ang).astype(np.float32)


def _rope_tables_A():
    cos, sin = _rot_tables(16)
    C = np.ones((128, S), np.float32)
    Sg = np.zeros((128, S), np.float32)
    for base in (0, 64):
        for d in range(8):
            C[base + d] = cos[:, d]
            Sg[base + d] = -sin[:, d]
            C[base + 8 + d] = cos[:, d]
            Sg[base + 8 + d] = sin[:, d]
    return C, Sg


def _rope_tables_M():
    cos, sin = _rot_tables(32)
    C = np.ones((128, S), np.float32)
    Sg = np.zeros((128, S), np.float32)
    for d in range(16):
        C[64 + d] = cos[:, d]
        Sg[64 + d] = -sin[:, d]
        C[64 + 16 + d] = cos[:, d]
        Sg[64 + 16 + d] = sin[:, d]
    return C, Sg


def _swap_perm(n, head, rot_off, half):
    idx = np.arange(n)
    for h0 in range(0, n, head):
        for d in range(half):
            idx[h0 + rot_off + d] = h0 + rot_off + half + d
            idx[h0 + rot_off + half + d] = h0 + rot_off + d
    return idx


def _nbr_plan():
    rows, W, wr, wc = 32, 64, 8, 16
    gids = {}
    plan = []
    for m in range(16):
        need = {}
        for b in range(2):
            qr = 2 * m + b
            r0 = min(max(qr - wr // 2, 0), rows - wr)
            for kr in range(r0, r0 + wr):
                t, a = kr // 2, kr % 2
                need.setdefault(t, set()).add((a, b))
        lst = []
        for t in sorted(need):
            key = (2 * t - 2 * m, tuple(sorted(need[t])))
            if key not in gids:
                gids[key] = len(gids)
            lst.append((t, gids[key]))
        plan.append(lst)
    ng = len(gids)
    ridx = np.zeros((ng, 128, 128), np.int64)
    cidx = np.zeros((ng, 128, 128), np.int64)
    valid = np.zeros((ng, 128, 128), bool)
    c = np.arange(W)
    cs = np.clip(c - wc // 2, 0, W - wc)
    colmask = (c[None, :] >= cs[:, None]) & (c[None, :] < cs[:, None] + wc)
    for (delta, pat), g in gids.items():
        for a in range(2):
            for b in range(2):
                dr = delta + a - b + 7
                ok = (a, b) in pat
                for kc in range(W):
                    k = a * 64 + kc
                    q = b * 64 + c
                    ridx[g, k, q] = min(max(dr, 0), 14)
                    cidx[g, k, q] = np.clip(kc - c, -15, 15) + 15
                    valid[g, k, q] = ok & colmask[c, kc]
    return plan, ng, ridx, cidx, valid


_PLAN, _NG, _RIDX, _CIDX, _VALID = _nbr_plan()
MASKV = -30000.0


def build(nseq=4, layers=(0, 1, 2, 3), do_mlp=True, do_mix=True, parts="AB", dbg=False):
    nc = bass.Bass("TRN2", target_bir_lowering=False)
    dt = nc.dram_tensor
    xT_d = dt("xT", [nseq, D, S], F32, kind="ExternalInput").ap()
    out_d = dt("outT", [nseq, D, S], F32, kind="ExternalOutput").ap()
    vecs_d = dt("vecs", [128, 80], F32, kind="ExternalInput").ap()
    lam_d = dt("lamv", [128, 2 * 4 * 64], F32, kind="ExternalInput").ap()
    w_up_d = dt("w_up", [4, D, 4096], F32, kind="ExternalInput").ap()
    w_dn_d = dt("w_down", [4, 4096, D], F32, kind="ExternalInput").ap()
    ev_in_d = dt("ev_w_in", [2, D, 3072], F32, kind="ExternalInput").ap()
    ev_sw_d = dt("ev_w_sw", [2, D, 1024], F32, kind="ExternalInput").ap()
    ev_out_d = dt("ev_w_out", [2, D, D], F32, kind="ExternalInput").ap()
    od_in_d = dt("od_w_in", [2, D, 416], F32, kind="ExternalInput").ap()
    od_insw_d = dt("od_w_in_sw", [2, D, 96], F32, kind="ExternalInput").ap()
    od_uq_d = dt("od_w_uq", [2, 256, 1536], F32, kind="ExternalInput").ap()
    od_uqsw_d = dt("od_w_uq_sw", [2, 256, 1536], F32, kind="ExternalInput").ap()
    od_ukv_d = dt("od_w_ukv", [2, 128, 2048], F32, kind="ExternalInput").ap()
    od_out_d = dt("od_w_out", [2, D, D], F32, kind="ExternalInput").ap()
    ropeA_d = dt("ropeA", [2, 128, S], BF16, kind="ExternalInput").ap()
    ropeM_d = dt("ropeM", [2, 128, S], BF16, kind="ExternalInput").ap()
    gb_d = dt("gb", [2, 128, 8 * _NG * 128], F32, kind="ExternalInput").ap()
    gmask_d = dt("gmask", [128, _NG * 128], BF16, kind="ExternalInput").ap()
    ident_d = dt("ident", [128, 128], BF16, kind="ExternalInput").ap()

    if dbg:
        dbg_d = {nm: dt("dbg_" + nm, [128, S], BF16, kind="ExternalOutput").ap() for nm in ("q", "k", "v", "o")}
    es = ExitStack()
    with es:
        uniq = [0]

        def sb(name, shape, dty, stack=es):
            uniq[0] += 1
            return stack.enter_context(nc.sbuf_tensor("%s_u%d" % (name, uniq[0]), shape, dty))

        P = Prog(nc, es)
        xT = sb("xT_s", [128, NCH * S], F32)
        hT = sb("hT_s", [128, NCH * S], BF16)
        vecs = sb("vecs_s", [128, 80], F32)
        lams = sb("lams", [128, 16], F32)
        ones = sb("ones", [128, 128], BF16)
        ident = sb("ident_s", [128, 128], BF16)
        NSTG = 2
        STGW = 1536
        stage = [sb("stage%d" % i, [128, STGW], F32) for i in range(NSTG)]
        NTMP = 5
        tmp = [sb("tmp%d" % i, [128, TG], F32) for i in range(NTMP)]
        sq = [sb("sq%d" % i, [128, TG], BF16) for i in range(2)]
        psS = es.enter_context(nc.psum_tensor("psS", [128, 2048], F32))
        psX0 = es.enter_context(nc.psum_tensor("psX0", [128, TG], F32))
        psX1 = es.enter_context(nc.psum_tensor("psX1", [128, TG], F32))
        psO = es.enter_context(nc.psum_tensor("psO", [128, TG], F32))
        psD = es.enter_context(nc.psum_tensor("psD", [128, TG], F32))
        psX = [psX0[:, :], psX1[:, :], psD[:, :], psO[:, :]] + [psS[:, k * TG:(k + 1) * TG] for k in range(4)]

        st = {"stg": 0, "tmp": 0, "sq": 0, "px": 0, "npx": 2}

        def xs(c, g):
            return xT[:, c * S + g * TG: c * S + (g + 1) * TG]

        def hs(c, g):
            return hT[:, c * S + g * TG: c * S + (g + 1) * TG]

        def next_tmp():
            i = st["tmp"]
            st["tmp"] = (i + 1) % NTMP
            return i

        def next_px():
            i = st["px"] % st["npx"]
            st["px"] = (i + 1) % st["npx"]
            return i

        def next_sq():
            i = st["sq"]
            st["sq"] = (i + 1) % 2
            return i

        def wload(dst, dst_off, src3, kc, n, dst_res):
            per = max(1, STGW // n)
            k0 = 0
            while k0 < kc:
                k1 = min(kc, k0 + per)
                tot = (k1 - k0) * n
                slot = st["stg"]
                st["stg"] = (slot + 1) % NSTG
                stg = stage[slot]
                src = src3[:, k0:k1, :]
                dview = stg[:, 0:tot].rearrange("p (k n) -> p k n", k=k1 - k0)
                P.op("sp", lambda e, dview=dview, src=src: e.dma_start(out=dview, in_=src),
                     writes=[("stg", slot)], dma_ch=("stg", slot))
                o0 = dst_off + k0 * n
                P.op("pool", lambda e, stg=stg, o0=o0, tot=tot: e.tensor_copy(out=dst[:, o0:o0 + tot], in_=stg[:, 0:tot]),
                     reads=[("stg", slot)], writes=(dst_res if isinstance(dst_res, list) else [dst_res]))
                k0 = k1

        def wview(w2d, r0, nk, c0, n):
            return w2d[r0:r0 + nk * 128, c0:c0 + n].rearrange("(k p) n -> p k n", p=128)

        def rmsnorm_x(gcol0, dst_fn, dst_res_fn, out_f32=False):
            for g in range(NTG):
                px = next_px()
                ps = psX[px]
                for c in range(NCH):
                    P.op("act", lambda e, c=c, g=g: e.activation(out=hs(c, g), in_=xs(c, g), func=AF.Square),
                         reads=[("x", c, g)], writes=[("h", c, g)])
                    P.op("pe", lambda e, c=c, g=g, ps=ps: e.matmul(ps[:], lhsT=ones[:], rhs=hs(c, g), start=(c == 0), stop=(c == NCH - 1)),
                         reads=[("h", c, g), "ones"], writes=[("px", px)])
                ti = next_tmp()
                P.op("act", lambda e, ti=ti, ps=ps: e.activation(out=tmp[ti][:], in_=ps[:], func=AF.Sqrt, scale=1.0 / D, bias=EPS),
                     reads=[("px", px)], writes=[("tmp", ti)])
                P.op("dve", lambda e, ti=ti: e.reciprocal(out=tmp[ti][:], in_=tmp[ti][:]),
                     reads=[("tmp", ti)], writes=[("tmp", ti)])
                for c in range(NCH):
                    P.op("dve", lambda e, ti=ti, c=c, g=g: e.scalar_tensor_tensor(
                        out=dst_fn(c, g), in0=xs(c, g), scalar=vecs[:, gcol0 + c:gcol0 + c + 1], in1=tmp[ti][:],
                        op0=ALU.mult, op1=ALU.mult),
                        reads=[("x", c, g), ("tmp", ti), "vecs"], writes=[dst_res_fn(c, g)])

        init_es = ExitStack()
        lamt = sb("lamt", [128, 512], F32, init_es)
        lamw = sb("lamw", [128, 512], F32, init_es)
        P.op("sp", lambda e: e.dma_start(out=vecs[:], in_=vecs_d[:, :]), writes=["vecs"], dma_ch="vecs")
        P.op("sp", lambda e: e.dma_start(out=lamt[:], in_=lam_d[:, :]), writes=["lamt"], dma_ch="lamt")
        P.op("sp", lambda e: e.dma_start(out=ident[:], in_=ident_d[:, :]), writes=["ident"], dma_ch="ident")
        P.op("dve", lambda e: e.memset(ones[:], 1.0), writes=["ones"])
        for i in range(2):
            b = i * 256
            for j in range(2):
                P.op("dve", lambda e, b=b, j=j: e.tensor_tensor(out=lamw[:, b + 64 * j: b + 64 * j + 64], in0=lamt[:, b + 128 * j: b + 128 * j + 64],
                                                             in1=lamt[:, b + 128 * j + 64: b + 128 * j + 128], op=ALU.mult),
                     reads=["lamt"], writes=[("lamw", i, j)])
                P.op("dve", lambda e, b=b, j=j, i=i: e.reduce_sum(out=lams[:, 4 * i + j: 4 * i + j + 1], in_=lamw[:, b + 64 * j: b + 64 * j + 64], axis=AX.X),
                     reads=[("lamw", i, j)], writes=[("lams", i, j)])
                P.op("act", lambda e, j=j, i=i: e.activation(out=lams[:, 4 * i + j: 4 * i + j + 1], in_=lams[:, 4 * i + j: 4 * i + j + 1], func=AF.Exp),
                     reads=[("lams", i, j)], writes=[("lams", i, j)])
            lam_init = 0.8 - 0.6 * math.exp(-0.3 * (2 * i))
            P.op("dve", lambda e, i=i, lam_init=lam_init: e.scalar_tensor_tensor(
                out=lams[:, 4 * i + 2: 4 * i + 3], in0=lams[:, 4 * i + 1: 4 * i + 2], scalar=-lam_init, in1=lams[:, 4 * i: 4 * i + 1],
                op0=ALU.add, op1=ALU.subtract),
                reads=[("lams", i, 0), ("lams", i, 1)], writes=[("lams", i, 2)])
        P.emit_phase()
        init_es.close()

        def mlp_phase(l):
            with ExitStack() as ps_:
                st["npx"] = 8
                wm = [sb("wmlp%d" % i, [128, 8192], BF16, ps_) for i in range(2)]
                uT = sb("uT", [128, 4 * S], BF16, ps_)
                rmsnorm_x(32 + 8 * l, hs, lambda c, g: ("h", c, g))
                for part in range(8):
                    wb = part % 2
                    w = wm[wb]
                    wload(w, 0, wview(w_up_d[l], 0, 8, part * 512, 512), 8, 512, ("wm", wb, 0))
                    wload(w, 4096, wview(w_dn_d[l], part * 512, 4, 0, 1024), 4, 1024, ("wm", wb, 1))
                    for g in range(NTG):
                        for j in range(4):
                            px = next_px()
                            ps = psX[px]
                            for c in range(NCH):
                                P.op("pe", lambda e, w=w, c=c, j=j, g=g, ps=ps: e.matmul(
                                    ps[:], lhsT=w[:, c * 512 + j * 128: c * 512 + j * 128 + 128], rhs=hs(c, g),
                                    start=(c == 0), stop=(c == NCH - 1)),
                                    reads=[("wm", wb, 0), ("h", c, g)], writes=[("px", px)])
                            ti = next_tmp()
                            P.op("act", lambda e, ti=ti, ps=ps: e.activation(out=tmp[ti][:], in_=ps[:], func=AF.Relu),
                                 reads=[("px", px)], writes=[("tmp", ti)])
                            P.op("pool", lambda e, ti=ti, j=j, g=g: e.tensor_tensor(
                                out=uT[:, j * S + g * TG: j * S + (g + 1) * TG], in0=tmp[ti][:], in1=tmp[ti][:], op=ALU.mult),
                                reads=[("tmp", ti)], writes=[("u", j, g)])
                    for g in range(NTG):
                        for c in range(NCH):
                            px = next_px()
                            ps = psX[px]
                            for j in range(4):
                                P.op("pe", lambda e, w=w, c=c, j=j, g=g, ps=ps: e.matmul(
                                    ps[:], lhsT=w[:, 4096 + j * 1024 + c * 128: 4096 + j * 1024 + c * 128 + 128],
                                    rhs=uT[:, j * S + g * TG: j * S + (g + 1) * TG], start=(j == 0), stop=(j == 3)),
                                    reads=[("wm", wb, 1), ("u", j, g)], writes=[("px", px)])
                            P.op("dve", lambda e, c=c, g=g, ps=ps: e.tensor_tensor(out=xs(c, g), in0=ps[:], in1=xs(c, g), op=ALU.add),
                                 reads=[("px", px), ("x", c, g)], writes=[("x", c, g)])
                P.emit_phase()

        def proj_fm(ps, wt, woff, wstride, m, nk, rhs_fn, g, wres, rres_fn, pxres, prow0=0):
            for k in range(nk):
                P.op("pe", lambda e, k=k: e.matmul(ps[prow0:prow0 + m, :], lhsT=wt[:, woff + k * wstride: woff + k * wstride + m],
                                                  rhs=rhs_fn(k, g), start=(k == 0), stop=(k == nk - 1)),
                     reads=[wres, rres_fn(k, g)], writes=[pxres])

        def outproj_chunk(wout_d2, kc, oH, ores, watt, wb, wo=5120, load=True, compute=True):
            if load:
                wload(watt[wb], wo, wview(wout_d2, kc * 128, 1, 0, 1024), 1, 1024, ("wa", wb, 5))
            if not compute:
                return
            for g in range(NTG):
                for c in range(NCH):
                    px = next_px()
                    ps = psX[px]
                    P.op("pe", lambda e, c=c, g=g, ps=ps: e.matmul(ps[:], lhsT=watt[wb][:, wo + c * 128: wo + c * 128 + 128],
                                                                   rhs=oH[:, g * TG:(g + 1) * TG], start=True, stop=True),
                         reads=[("wa", wb, 5), (ores, g)], writes=[("px", px)])
                    P.op("dve", lambda e, c=c, g=g, ps=ps: e.tensor_tensor(out=xs(c, g), in0=ps[:], in1=xs(c, g), op=ALU.add),
                         reads=[("px", px), ("x", c, g)], writes=[("x", c, g)])

        def even_phase(l, part, do_norm):
            i = l // 2
            lam_init = 0.8 - 0.6 * math.exp(-0.3 * l)
            isA = part == "A"
            with ExitStack() as ps_:
                st["npx"] = 2
                st["px"] = 0
                watt = [sb("watt%d" % k, [128, 6144], BF16, ps_) for k in range(2)]
                qTm = [sb("qTm%d" % k, [128, S], BF16, ps_) for k in range(2)]
                kT = sb("kT", [128, S], BF16, ps_)
                vS = sb("vS", [128, 16 * 128], BF16, ps_)
                oH = [sb("oH%d" % k, [128, S], BF16, ps_) for k in range(2)]
                NPT = 6
                PT = [sb("PT%d" % k, [128, 512], BF16, ps_) for k in range(NPT)]
                unit_b = [0]
                unit_a = [0]
                for k_ in range(2):
                    P.op("pool", lambda e, k_=k_: e.memset(qTm[k_][:], 0.0), writes=[("qT", k_, g) for g in range(NTG)])
                pt_i = [0]
                sbuf_i = [0]
                if isA:
                    ropeC = sb("ropeC", [128, S], BF16, ps_)
                    ropeS = sb("ropeS", [128, S], BF16, ps_)
                    NAT = 8
                    atmp = [sb("atmp%d" % k, [128, TG], F32, ps_) for k in range(NAT)]
                    at_i = [0]
                    P.op("sp", lambda e: e.dma_start(out=ropeC[:], in_=ropeA_d[0]), writes=["ropeC"], dma_ch="ropeC")
                    P.op("sp", lambda e: e.dma_start(out=ropeS[:], in_=ropeA_d[1]), writes=["ropeS"], dma_ch="ropeS")
                else:
                    G = sb("G", [128, 8 * _NG * 128], BF16, ps_)
                    gmask = sb("gmask", [128, _NG * 128], BF16, ps_)
                    P.op("sp", lambda e: e.dma_start(out=gmask[:], in_=gmask_d[:, :]), writes=["gmask"], dma_ch="gmask")
                    GW = _NG * 128
                    for h in range(8):
                        slot = st["stg"]
                        st["stg"] = (slot + 1) % NSTG
                        stg = stage[slot]
                        P.op("sp", lambda e, stg=stg, h=h: e.dma_start(out=stg[:, 0:GW], in_=gb_d[i][:, h * GW:(h + 1) * GW]),
                             writes=[("stg", slot)], dma_ch=("stg", slot))
                        P.op("dve", lambda e, stg=stg, h=h: e.scalar_tensor_tensor(
                            out=G[:, h * GW:(h + 1) * GW], in0=stg[:, 0:GW], scalar=8.0, in1=gmask[:], op0=ALU.mult, op1=ALU.add),
                            reads=[("stg", slot), "gmask"], writes=[("G", h)])

                if do_norm:
                    rmsnorm_x(8 * l, hs, lambda c, g: ("h", c, g))

                def rope_evac(ps_a, ps_b, pxa, pxb, dst, g, dres, rows=128):
                    t1 = next_tmp()
                    t2 = next_tmp()
                    P.op("dve", lambda e: e.tensor_tensor(out=tmp[t1][0:rows, :], in0=ps_a[0:rows, :], in1=ropeC[0:rows, g * TG:(g + 1) * TG], op=ALU.mult),
                         reads=[("px", pxa), "ropeC"], writes=[("tmp", t1)])
                    P.op("dve", lambda e: e.tensor_tensor(out=tmp[t2][0:rows, :], in0=ps_b[0:rows, :], in1=ropeS[0:rows, g * TG:(g + 1) * TG], op=ALU.mult),
                         reads=[("px", pxb), "ropeS"], writes=[("tmp", t2)])
                    if dst is None:
                        for hf in range(2):
                            P.op("pool", lambda e, hf=hf: e.tensor_tensor(out=qTm[hf][64 * hf:64 * hf + 64, g * TG:(g + 1) * TG], in0=tmp[t1][64 * hf:64 * hf + 64, :],
                                                                      in1=tmp[t2][64 * hf:64 * hf + 64, :], op=ALU.add),
                                 reads=[("tmp", t1), ("tmp", t2)], writes=[("qT", hf, g)])
                    else:
                        P.op("pool", lambda e: e.tensor_tensor(out=dst[0:rows, g * TG:(g + 1) * TG], in0=tmp[t1][0:rows, :], in1=tmp[t2][0:rows, :], op=ALU.add),
                             reads=[("tmp", t1), ("tmp", t2)], writes=[(dres, g)])

                hfn = lambda k, g: hs(k, g)
                hres = lambda k, g: ("h", k, g)

                def v_tokmajor(w, woff, wres, ncols):
                    for t4 in range(4):
                        px = next_px()
                        ps = psX[px]
                        for tt in range(4):
                            tok = (t4 * 4 + tt) * 128
                            for k in range(NCH):
                                P.op("pe", lambda e, k=k, tt=tt, tok=tok, ps=ps: e.matmul(
                                    ps[:, tt * 128: tt * 128 + ncols], lhsT=hT[:, k * S + tok: k * S + tok + 128],
                                    rhs=w[:, woff + k * 128: woff + k * 128 + ncols], start=(k == 0), stop=(k == NCH - 1)),
                                    reads=[wres, ("h", k, tok // TG)], writes=[("px", px)])
                        P.op("act", lambda e, t4=t4, ps=ps: e.activation(out=vS[:, t4 * 512:(t4 + 1) * 512], in_=ps[:], func=AF.Copy),
                             reads=[("px", px)], writes=[("vS", t4)])

                pend_out = None
                for hd in (range(4) if isA else ()):
                    wb = hd % 2
                    w = watt[wb]
                    wload(w, 0, wview(ev_in_d[i], 0, 8, 128 * hd, 128), 8, 128, ("wa", wb, 0))
                    wload(w, 1024, wview(ev_sw_d[i], 0, 8, 128 * hd, 128), 8, 128, ("wa", wb, 1))
                    wload(w, 2048, wview(ev_in_d[i], 0, 8, 512 + 128 * hd, 128), 8, 128, ("wa", wb, 2))
                    wload(w, 3072, wview(ev_sw_d[i], 0, 8, 512 + 128 * hd, 128), 8, 128, ("wa", wb, 3))
                    wload(w, 4096, wview(ev_in_d[i], 0, 8, 1024 + 128 * hd, 128), 8, 128, ("wa", wb, 4))
                    for (dst, dres, o0) in ((None, "qT", 0), (kT, "kT", 2048)):
                        for g in range(NTG):
                            pa = next_px()
                            proj_fm(psX[pa], w, o0, 128, 128, 8, hfn, g, ("wa", wb, o0 // 1024), hres, ("px", pa))
                            pb = next_px()
                            proj_fm(psX[pb], w, o0 + 1024, 128, 128, 8, hfn, g, ("wa", wb, o0 // 1024 + 1), hres, ("px", pb))
                            rope_evac(psX[pa], psX[pb], pa, pb, dst, g, dres)
                    v_tokmajor(w, 4096, ("wa", wb, 4), 128)
                    if pend_out is not None:
                        outproj_chunk(*pend_out, load=False)
                        pend_out = None
                    ob = hd % 2
                    outproj_chunk(ev_out_d[i], hd, oH[ob], ("oH", ob), watt, wb, compute=False)
                    pipe = Pipe(3)
                    hold = {}
                    later = []

                    def fin_a(g, half, aset, ob=ob, hold=hold):
                        aO, nO, aD, nD = aset
                        tO = at_i[0]
                        tD = (tO + 1) % NAT
                        at_i[0] = (tO + 2) % NAT
                        P.op("dve", lambda e: e.reciprocal(out=atmp[tD][:], in_=aD[:, :]), reads=[nD], writes=[("at", tD)])
                        P.op("dve", lambda e: e.tensor_tensor(out=atmp[tO][:], in0=aO[:, :], in1=atmp[tD][:], op=ALU.mult),
                             reads=[nO, ("at", tD)], writes=[("at", tO)])
                        if half == 0:
                            hold[g] = tO
                            return
                        a = hold[g]
                        b = tO
                        P.op("dve", lambda e: e.scalar_tensor_tensor(
                            out=atmp[a][:], in0=atmp[b][:], scalar=lams[:, 4 * i + 2: 4 * i + 3], in1=atmp[a][:], op0=ALU.mult, op1=ALU.add),
                            reads=[("at", a), ("at", b), ("lams", i, 2)], writes=[("at", a)])
                        later.append(lambda: subln_tail(g, a, tD, ob))

                    def subln_tail(g, a, tD, ob):
                        si = next_sq()
                        P.op("act", lambda e: e.activation(out=sq[si][:], in_=atmp[a][:], func=AF.Square), reads=[("at", a)], writes=[("sq", si)])
                        sb2 = sbuf_i[0]
                        sbuf_i[0] = (sb2 + 1) % 4
                        sreg2 = psS[:, sb2 * TG:(sb2 + 1) * TG]
                        P.op("pe", lambda e: e.matmul(sreg2, lhsT=ones[:], rhs=sq[si][:], start=True, stop=True),
                             reads=[("sq", si), "ones"], writes=[("psS", sb2)])
                        k2 = 1.0 / ((1.0 - lam_init) ** 2)
                        P.op("act", lambda e: e.activation(out=atmp[tD][:], in_=sreg2, func=AF.Sqrt, scale=k2 / 128.0, bias=EPS * k2),
                             reads=[("psS", sb2)], writes=[("at", tD)])
                        P.op("dve", lambda e: e.reciprocal(out=atmp[tD][:], in_=atmp[tD][:]), reads=[("at", tD)], writes=[("at", tD)])
                        P.op("dve", lambda e: e.scalar_tensor_tensor(
                            out=oH[ob][:, g * TG:(g + 1) * TG], in0=atmp[a][:], scalar=vecs[:, 78 + i: 79 + i], in1=atmp[tD][:], op0=ALU.mult, op1=ALU.mult),
                            reads=[("at", a), ("at", tD), "vecs"], writes=[(("oH", ob), g)])

                    def a_step(g, half, kt, aset):
                        aO, nO, aD, nD = aset
                        sbi = sbuf_i[0]
                        sbuf_i[0] = (sbi + 1) % 4
                        pti = pt_i[0]
                        pt_i[0] = (pti + 1) % NPT
                        sreg = psS[:, sbi * TG:(sbi + 1) * TG]

                        def qk():
                            P.op("pe", lambda e: e.matmul(sreg, lhsT=kT[:, kt * 128:(kt + 1) * 128], rhs=qTm[half][:, g * TG:(g + 1) * TG], start=True, stop=True),
                                 reads=[("kT", kt // 4), ("qT", half, g)], writes=[("psS", sbi)])
                            P.op("act", lambda e: e.activation(out=PT[pti][:, 0:TG], in_=sreg, func=AF.Exp, scale=0.125),
                                 reads=[("psS", sbi)], writes=[("PT", pti)])

                        def pv():
                            P.op("pe", lambda e: e.matmul(aO[:, :], lhsT=vS[:, kt * 128:(kt + 1) * 128], rhs=PT[pti][:, 0:TG], start=(kt == 0), stop=(kt == 15)),
                                 reads=[("PT", pti), ("vS", kt // 4)], writes=[nO])
                            P.op("pe", lambda e: e.matmul(aD[:, :], lhsT=ones[:], rhs=PT[pti][:, 0:TG], start=(kt == 0), stop=(kt == 15)),
                                 reads=[("PT", pti), "ones"], writes=[nD])
                            if kt == 15:
                                fin_a(g, half, aset)
                            if kt == 7:
                                while later:
                                    later.pop(0)()
                        pipe.step(qk, pv)

                    accs_a = [(psO, ("px", 3), psD, ("px", 2)), (psX[0], ("px", 0), psX[1], ("px", 1))]
                    for g in range(NTG):
                        for half in range(2):
                            aset = accs_a[unit_a[0] % 2]
                            unit_a[0] += 1
                            for kt in range(16):
                                a_step(g, half, kt, aset)
                    pipe.flush()
                    while later:
                        later.pop(0)()
                    pend_out = (ev_out_d[i], hd, oH[ob], ("oH", ob), watt, wb)
                if pend_out is not None:
                    outproj_chunk(*pend_out, load=False)
                    pend_out = None

                for cpair in (() if isA else range(4)):
                    wb = cpair % 2
                    w = watt[wb]
                    wload(w, 0, wview(ev_in_d[i], 0, 8, 1536 + 128 * cpair, 128), 8, 128, ("wa", wb, 0))
                    wload(w, 2048, wview(ev_in_d[i], 0, 8, 2048 + 128 * cpair, 128), 8, 128, ("wa", wb, 2))
                    wload(w, 4096, wview(ev_in_d[i], 0, 8, 2560 + 128 * cpair, 128), 8, 128, ("wa", wb, 4))
                    for (dst, dres, o0) in ((None, "qT", 0), (kT, "kT", 2048)):
                        for g in range(NTG):
                            pa = next_px()
                            proj_fm(psX[pa], w, o0, 128, 128, 8, hfn, g, ("wa", wb, o0 // 1024), hres, ("px", pa))
                            if dst is None:
                                for hf in range(2):
                                    P.op("act", lambda e, pa=pa, hf=hf, g=g: e.activation(out=qTm[hf][64 * hf:64 * hf + 64, g * TG:(g + 1) * TG],
                                                                                      in_=psX[pa][64 * hf:64 * hf + 64, :], func=AF.Copy),
                                         reads=[("px", pa)], writes=[("qT", hf, g)])
                            else:
                                P.op("act", lambda e, pa=pa, dst=dst, g=g: e.activation(out=dst[:, g * TG:(g + 1) * TG], in_=psX[pa][:], func=AF.Copy),
                                     reads=[("px", pa)], writes=[(dres, g)])
                    v_tokmajor(w, 4096, ("wa", wb, 4), 128)
                    if pend_out is not None:
                        outproj_chunk(*pend_out, load=False)
                        pend_out = None
                    ob = cpair % 2
                    outproj_chunk(ev_out_d[i], 4 + cpair, oH[ob], ("oH", ob), watt, wb, compute=False)
                    accs_b = [(psO, ("px", 3), psD, ("px", 2)), (psX[0], ("px", 0), psX[1], ("px", 1))]
                    tl = []
                    for hh in range(2):
                        for m4 in range(4):
                            aset = accs_b[unit_b[0] % 2]
                            unit_b[0] += 1
                            for mm in range(4):
                                m = m4 * 4 + mm
                                tiles = _PLAN[m]
                                for ti_, (t, gid) in enumerate(tiles):
                                    tl.append(dict(hh=hh, h=2 * cpair + hh, m=m, mm=mm, m4=m4, t=t, gid=gid, first=(ti_ == 0),
                                                   last=(ti_ == len(tiles) - 1), aset=aset, fin=(mm == 3 and ti_ == len(tiles) - 1)))
                    pipe = Pipe(3)

                    def fin_b(d, ob=ob):
                        aO, nO, aD, nD = d["aset"]
                        r0 = 64 * d["hh"]
                        m4 = d["m4"]
                        tr = next_tmp()
                        P.op("dve", lambda e: e.reciprocal(out=tmp[tr][r0:r0 + 64, :], in_=aD[r0:r0 + 64, :]),
                             reads=[nD], writes=[("tmp", tr)])
                        P.op("dve", lambda e: e.tensor_tensor(
                            out=oH[ob][r0:r0 + 64, m4 * TG:(m4 + 1) * TG], in0=aO[r0:r0 + 64, :], in1=tmp[tr][r0:r0 + 64, :], op=ALU.mult),
                            reads=[nO, ("tmp", tr)], writes=[(("oH", ob), m4)])

                    def b_chunk(ch):
                        sbi = sbuf_i[0]
                        sbuf_i[0] = (sbi + 1) % 4
                        pti = pt_i[0]
                        pt_i[0] = (pti + 1) % NPT
                        n = len(ch)

                        def qk():
                            for j, d in enumerate(ch):
                                sreg = psS[:, sbi * TG + j * 128: sbi * TG + (j + 1) * 128]
                                goff = (d["h"] * _NG + d["gid"]) * 128
                                P.op("pe", lambda e, sreg=sreg, goff=goff: e.matmul(sreg, lhsT=ident[:], rhs=G[:, goff:goff + 128], start=True, stop=False),
                                     reads=[("G", d["h"]), "ident"], writes=[("psS", sbi)])
                                P.op("pe", lambda e, sreg=sreg, d=d: e.matmul(
                                    sreg, lhsT=kT[:, d["t"] * 128:(d["t"] + 1) * 128], rhs=qTm[d["hh"]][:, d["m"] * 128:(d["m"] + 1) * 128], start=False, stop=True),
                                    reads=[("kT", d["t"] // 4), ("qT", d["hh"], d["m"] // 4)], writes=[("psS", sbi)])
                            P.op("act", lambda e: e.activation(out=PT[pti][:, 0:n * 128], in_=psS[:, sbi * TG: sbi * TG + n * 128], func=AF.Exp, scale=0.125),
                                 reads=[("psS", sbi)], writes=[("PT", pti)])

                        def pv():
                            for j, d in enumerate(ch):
                                aO, nO, aD, nD = d["aset"]
                                mm = d["mm"]
                                P.op("pe", lambda e, j=j, d=d, aO=aO, mm=mm: e.matmul(
                                    aO[:, mm * 128:(mm + 1) * 128], lhsT=vS[:, d["t"] * 128:(d["t"] + 1) * 128], rhs=PT[pti][:, j * 128:(j + 1) * 128],
                                    start=d["first"], stop=d["last"]),
                                    reads=[("PT", pti), ("vS", d["t"] // 4)], writes=[nO])
                                P.op("pe", lambda e, j=j, d=d, aD=aD, mm=mm: e.matmul(
                                    aD[:, mm * 128:(mm + 1) * 128], lhsT=ones[:], rhs=PT[pti][:, j * 128:(j + 1) * 128],
                                    start=d["first"], stop=d["last"]),
                                    reads=[("PT", pti), "ones"], writes=[nD])
                                if d["fin"]:
                                    fin_b(d)
                        pipe.step(qk, pv)

                    for c0 in range(0, len(tl), 4):
                        b_chunk(tl[c0:c0 + 4])
                    pipe.flush()
                    pend_out = (ev_out_d[i], 4 + cpair, oH[ob], ("oH", ob), watt, wb)
                if pend_out is not None:
                    outproj_chunk(*pend_out, load=False)
                    pend_out = None
                P.emit_phase()

        def odd_phase(l):
            i = l // 2
            scale = 1.0 / math.sqrt(96.0)
            with ExitStack() as ps_:
                st["npx"] = 4
                st["px"] = 0
                watt = [sb("watt%d" % k, [128, 2048], BF16, ps_) for k in range(2)]
                win = sb("win", [128, 8 * 416], BF16, ps_)
                winsw = sb("winsw", [128, 8 * 96], BF16, ps_)
                qTt = [sb("qT%d" % k, [128, S], BF16, ps_) for k in range(2)]
                kTt = [sb("kT%d" % k, [128, S], BF16, ps_) for k in range(2)]
                vSt = [sb("vS%d" % k, [128, 16 * 128], BF16, ps_) for k in range(2)]
                NPT = 6
                PT = [sb("PT%d" % k, [128, TG], BF16, ps_) for k in range(NPT)]
                ropeC = sb("ropeC", [128, S], BF16, ps_)
                ropeS = sb("ropeS", [128, S], BF16, ps_)
                for k_ in range(2):
                    P.op("pool", lambda e, k_=k_: e.memset(qTt[k_][:], 0.0), writes=[(("qT", k_), g) for g in range(NTG)])
                for k_ in range(2):
                    P.op("pool", lambda e, k_=k_: e.memset(kTt[k_][:], 0.0),
                         writes=[(("kTn", k_), g) for g in range(NTG)] + [(("kTr", k_), g) for g in range(NTG)])
                accs = [(psX[3], ("px", 3)), (psX[2], ("px", 2))]
                unit = [0]
                cqn = sb("cqn", [128, 2 * S], BF16, ps_)
                ckvn = sb("ckvn", [128, S], BF16, ps_)
                pt_i = [0]
                sbuf_i = [0]
                P.op("sp", lambda e: e.dma_start(out=ropeC[:], in_=ropeM_d[0]), writes=["ropeC"], dma_ch="ropeC")
                P.op("sp", lambda e: e.dma_start(out=ropeS[:], in_=ropeM_d[1]), writes=["ropeS"], dma_ch="ropeS")
                for k in range(2):
                    P.op("dve", lambda e, k=k: e.memset(vSt[k][:], 1.0), writes=[("vS", k, 0), ("vS", k, 1)])
                rmsnorm_x(8 * l, hs, lambda c, g: ("h", c, g))
                wload(win, 0, wview(od_in_d[i], 0, 8, 0, 416), 8, 416, "win")
                wload(winsw, 0, wview(od_insw_d[i], 0, 8, 0, 96), 8, 96, "winsw")
                hfn = lambda k, g: hs(k, g)
                hres = lambda k, g: ("h", k, g)
                for g in range(NTG):
                    pq = [next_px(), next_px()]
                    for c2 in range(2):
                        proj_fm(psX[pq[c2]], win, 128 * c2, 416, 128, 8, hfn, g, "win", hres, ("px", pq[c2]))
                    pss = next_px()
                    for c2 in range(2):
                        si = next_sq()
                        P.op("act", lambda e, si=si, p_=pq[c2]: e.activation(out=sq[si][:], in_=psX[p_][:], func=AF.Square),
                             reads=[("px", pq[c2])], writes=[("sq", si)])
                        P.op("pe", lambda e, si=si, c2=c2, pss=pss: e.matmul(psX[pss][:], lhsT=ones[:], rhs=sq[si][:], start=(c2 == 0), stop=(c2 == 1)),
                             reads=[("sq", si), "ones"], writes=[("px", pss)])
                    tr = next_tmp()
                    P.op("act", lambda e, tr=tr, pss=pss: e.activation(out=tmp[tr][:], in_=psX[pss][:], func=AF.Sqrt, scale=1.0 / 256.0, bias=EPS),
                         reads=[("px", pss)], writes=[("tmp", tr)])
                    P.op("dve", lambda e, tr=tr: e.reciprocal(out=tmp[tr][:], in_=tmp[tr][:]), reads=[("tmp", tr)], writes=[("tmp", tr)])
                    for c2 in range(2):
                        P.op("dve", lambda e, tr=tr, c2=c2, g=g, p_=pq[c2]: e.scalar_tensor_tensor(
                            out=cqn[:, c2 * S + g * TG: c2 * S + (g + 1) * TG], in0=psX[p_][:], scalar=vecs[:, 72 + 2 * i + c2: 73 + 2 * i + c2],
                            in1=tmp[tr][:], op0=ALU.mult, op1=ALU.mult),
                            reads=[("px", pq[c2]), ("tmp", tr), "vecs"], writes=[("cqn", c2, g)])
                    pk = next_px()
                    proj_fm(psX[pk], win, 256, 416, 128, 8, hfn, g, "win", hres, ("px", pk))
                    pss = next_px()
                    si = next_sq()
                    P.op("act", lambda e, si=si, pk=pk: e.activation(out=sq[si][:], in_=psX[pk][:], func=AF.Square),
                         reads=[("px", pk)], writes=[("sq", si)])
                    P.op("pe", lambda e, si=si, pss=pss: e.matmul(psX[pss][:], lhsT=ones[:], rhs=sq[si][:], start=True, stop=True),
                         reads=[("sq", si), "ones"], writes=[("px", pss)])
                    tr = next_tmp()
                    P.op("act", lambda e, tr=tr, pss=pss: e.activation(out=tmp[tr][:], in_=psX[pss][:], func=AF.Sqrt, scale=1.0 / 128.0, bias=EPS),
                         reads=[("px", pss)], writes=[("tmp", tr)])
                    P.op("dve", lambda e, tr=tr: e.reciprocal(out=tmp[tr][:], in_=tmp[tr][:]), reads=[("tmp", tr)], writes=[("tmp", tr)])
                    P.op("dve", lambda e, tr=tr, g=g, pk=pk: e.scalar_tensor_tensor(
                        out=ckvn[:, g * TG:(g + 1) * TG], in0=psX[pk][:], scalar=vecs[:, 76 + i: 77 + i], in1=tmp[tr][:], op0=ALU.mult, op1=ALU.mult),
                        reads=[("px", pk), ("tmp", tr), "vecs"], writes=[("ckvn", g)])
                    pa = next_px()
                    proj_fm(psX[pa], win, 320, 416, 96, 8, hfn, g, "win", hres, ("px", pa))
                    pb = next_px()
                    proj_fm(psX[pb], winsw, 0, 96, 96, 8, hfn, g, "winsw", hres, ("px", pb))
                    t1 = next_tmp()
                    t2 = next_tmp()
                    P.op("dve", lambda e, t1=t1, pa=pa, g=g: e.tensor_tensor(out=tmp[t1][64:96, :], in0=psX[pa][64:96, :], in1=ropeC[64:96, g * TG:(g + 1) * TG], op=ALU.mult),
                         reads=[("px", pa), "ropeC"], writes=[("tmp", t1)])
                    P.op("dve", lambda e, t2=t2, pb=pb, g=g: e.tensor_tensor(out=tmp[t2][64:96, :], in0=psX[pb][64:96, :], in1=ropeS[64:96, g * TG:(g + 1) * TG], op=ALU.mult),
                         reads=[("px", pb), "ropeS"], writes=[("tmp", t2)])
                    for k in range(2):
                        P.op("pool", lambda e, t1=t1, t2=t2, g=g, k=k: e.tensor_tensor(
                            out=kTt[k][64:96, g * TG:(g + 1) * TG], in0=tmp[t1][64:96, :], in1=tmp[t2][64:96, :], op=ALU.add),
                            reads=[("tmp", t1), ("tmp", t2)], writes=[(("kTr", k), g)])

                cfn = lambda k, g: cqn[:, k * S + g * TG: k * S + (g + 1) * TG]
                cres = lambda k, g: ("cqn", k, g)
                st["npx"] = 2
                st["px"] = 0

                def proj_head(h):
                    wb = h % 2
                    w = watt[wb]
                    kT = kTt[wb]
                    vS = vSt[wb]
                    qT = qTt[wb]
                    vc0 = 0 if wb == 0 else 64
                    wload(w, 0, wview(od_uq_d[i], 0, 2, 96 * h, 96), 2, 96, ("wa", wb, 0))
                    wload(w, 256, wview(od_uqsw_d[i], 0, 2, 96 * h, 96), 2, 96, ("wa", wb, 1))
                    wload(w, 512, wview(od_ukv_d[i], 0, 1, 128 * h, 128), 1, 128, ("wa", wb, 2))
                    for g in range(NTG):
                        pa = next_px()
                        proj_fm(psX[pa], w, 0, 96, 128, 2, cfn, g, ("wa", wb, 0), cres, ("px", pa))
                        pb = next_px()
                        proj_fm(psX[pb], w, 256, 96, 128, 2, cfn, g, ("wa", wb, 1), cres, ("px", pb))
                        t1 = next_tmp()
                        t2 = next_tmp()
                        P.op("dve", lambda e, t1=t1, pa=pa, g=g: e.tensor_tensor(out=tmp[t1][0:96, :], in0=psX[pa][0:96, :], in1=ropeC[0:96, g * TG:(g + 1) * TG], op=ALU.mult),
                             reads=[("px", pa), "ropeC"], writes=[("tmp", t1)])
                        P.op("dve", lambda e, t2=t2, pb=pb, g=g: e.tensor_tensor(out=tmp[t2][0:96, :], in0=psX[pb][0:96, :], in1=ropeS[0:96, g * TG:(g + 1) * TG], op=ALU.mult),
                             reads=[("px", pb), "ropeS"], writes=[("tmp", t2)])
                        P.op("pool", lambda e, t1=t1, t2=t2, g=g: e.tensor_tensor(out=qT[0:96, g * TG:(g + 1) * TG], in0=tmp[t1][0:96, :], in1=tmp[t2][0:96, :], op=ALU.add),
                             reads=[("tmp", t1), ("tmp", t2)], writes=[(("qT", wb), g)])
                        pk = next_px()
                        P.op("pe", lambda e, pk=pk, g=g: e.matmul(psX[pk][:, :], lhsT=w[:, 512:640], rhs=ckvn[:, g * TG:(g + 1) * TG], start=True, stop=True),
                             reads=[("wa", wb, 2), ("ckvn", g)], writes=[("px", pk)])
                        P.op("dve", lambda e, pk=pk, g=g: e.tensor_copy(out=kT[0:64, g * TG:(g + 1) * TG], in_=psX[pk][0:64, :]),
                             reads=[("px", pk)], writes=[(("kTn", wb), g)])
                    for t8 in range(2):
                        px = next_px()
                        ps = psX[px]
                        for tt in range(8):
                            tok = (t8 * 8 + tt) * 128
                            P.op("pe", lambda e, tt=tt, tok=tok, ps=ps: e.matmul(
                                ps[:, tt * 64:(tt + 1) * 64], lhsT=ckvn[:, tok:tok + 128], rhs=w[:, 576:640], start=True, stop=True),
                                reads=[("wa", wb, 2), ("ckvn", tok // TG)], writes=[("px", px)])
                        dstv = vS[:, t8 * 1024:(t8 + 1) * 1024].rearrange("p (t c) -> p t c", c=128)[:, :, vc0:vc0 + 64]
                        srcv = ps[:, :].rearrange("p (t c) -> p t c", c=64)
                        P.op("dve", lambda e, dstv=dstv, srcv=srcv: e.tensor_copy(out=dstv, in_=srcv),
                             reads=[("px", px)], writes=[("vS", wb, t8)])

                def attn_head(h):
                    wb = h % 2
                    kT = kTt[wb]
                    vS = vSt[wb]
                    qT = qTt[wb]
                    ob = (h // 2) % 2
                    orow = 64 * (h % 2)
                    drow = 64 - orow
                    pipe = Pipe(3)

                    def fin_m(g, aO, nO):
                        tr = next_tmp()
                        P.op("dve", lambda e: e.reciprocal(out=tmp[tr][drow:drow + 64, :], in_=aO[drow:drow + 64, :]),
                             reads=[nO], writes=[("tmp", tr)])
                        P.op("dve", lambda e: e.tensor_tensor(
                            out=hT[orow:orow + 64, (h // 2) * S + g * TG:(h // 2) * S + (g + 1) * TG], in0=aO[orow:orow + 64, :], in1=tmp[tr][drow:drow + 64, :], op=ALU.mult),
                            reads=[nO, ("tmp", tr)], writes=[("h", h // 2, g)])

                    def m_step(g, kt, aO, nO):
                        sbi = sbuf_i[0]
                        sbuf_i[0] = (sbi + 1) % 4
                        pti = pt_i[0]
                        pt_i[0] = (pti + 1) % NPT
                        sreg = psS[:, sbi * TG:(sbi + 1) * TG]

                        def qk():
                            P.op("pe", lambda e: e.matmul(sreg, lhsT=kT[:, kt * 128:(kt + 1) * 128], rhs=qT[:, g * TG:(g + 1) * TG], start=True, stop=True),
                                 reads=[(("kTn", wb), kt // 4), (("kTr", wb), kt // 4), (("qT", wb), g)], writes=[("psS", sbi)])
                            P.op("act", lambda e: e.activation(out=PT[pti][:, 0:TG], in_=sreg, func=AF.Exp, scale=scale),
                                 reads=[("psS", sbi)], writes=[("PT", pti)])

                        def pv():
                            P.op("pe", lambda e: e.matmul(aO, lhsT=vS[:, kt * 128:(kt + 1) * 128], rhs=PT[pti][:, 0:TG], start=(kt == 0), stop=(kt == 15)),
                                 reads=[("PT", pti), ("vS", wb, kt // 8)], writes=[nO])
                            if kt == 15:
                                fin_m(g, aO, nO)
                        pipe.step(qk, pv)

                    for g in range(NTG):
                        aO, nO = accs[unit[0] % 2]
                        unit[0] += 1
                        for kt in range(16):
                            m_step(g, kt, aO, nO)
                    pipe.flush()

                proj_head(0)
                for h in range(16):
                    if h + 1 < 16:
                        proj_head(h + 1)
                    attn_head(h)
                for c in range(NCH):
                    wb = c % 2
                    wload(watt[wb], 0, wview(od_out_d[i], 0, 8, c * 128, 128), 8, 128, [("wa", wb, 0), ("wa", wb, 1), ("wa", wb, 2)])
                    for g in range(NTG):
                        px = next_px()
                        ps = psX[px]
                        for k in range(NCH):
                            P.op("pe", lambda e, k=k, g=g, ps=ps, wb=wb: e.matmul(ps[:], lhsT=watt[wb][:, k * 128:(k + 1) * 128], rhs=hs(k, g),
                                                                              start=(k == 0), stop=(k == NCH - 1)),
                                 reads=[("wa", wb, 0), ("h", k, g)], writes=[("px", px)])
                        P.op("dve", lambda e, c=c, g=g, ps=ps: e.tensor_tensor(out=xs(c, g), in0=ps[:], in1=xs(c, g), op=ALU.add),
                             reads=[("px", px), ("x", c, g)], writes=[("x", c, g)])
                P.emit_phase()

        for s_ in range(nseq):
            for c in range(NCH):
                P.op("sp", lambda e, c=c, s_=s_: e.dma_start(out=xT[:, c * S:(c + 1) * S], in_=xT_d[s_][c * 128:(c + 1) * 128, :]),
                     writes=[("x", c, g) for g in range(NTG)], dma_ch=("xin", c))
            P.emit_phase()
            for l in layers:
                if do_mix:
                    if l % 2 == 0:
                        first = True
                        for part in parts:
                            even_phase(l, part, first)
                            first = False
                    else:
                        odd_phase(l)
                if do_mlp:
                    mlp_phase(l)
            with ExitStack() as ps_:
                st["npx"] = 2
                ob = [sb("outb%d" % k, [128, S], F32, ps_) for k in range(2)]
                rs = [sb("rs%d" % k, [128, TG], F32, ps_) for k in range(NTG)]
                for g in range(NTG):
                    px = next_px()
                    ps = psX[px]
                    for c in range(NCH):
                        P.op("act", lambda e, c=c, g=g: e.activation(out=hs(c, g), in_=xs(c, g), func=AF.Square),
                             reads=[("x", c, g)], writes=[("h", c, g)])
                        P.op("pe", lambda e, c=c, g=g, ps=ps: e.matmul(ps[:], lhsT=ones[:], rhs=hs(c, g), start=(c == 0), stop=(c == NCH - 1)),
                             reads=[("h", c, g), "ones"], writes=[("px", px)])
                    P.op("act", lambda e, g=g, ps=ps: e.activation(out=rs[g][:], in_=ps[:], func=AF.Sqrt, scale=1.0 / D, bias=EPS),
                         reads=[("px", px)], writes=[("rs", g)])
                    P.op("dve", lambda e, g=g: e.reciprocal(out=rs[g][:], in_=rs[g][:]), reads=[("rs", g)], writes=[("rs", g)])
                for c in range(NCH):
                    k = c % 2
                    for g in range(NTG):
                        P.op("dve", lambda e, c=c, g=g, k=k: e.scalar_tensor_tensor(
                            out=ob[k][:, g * TG:(g + 1) * TG], in0=xs(c, g), scalar=vecs[:, 64 + c:65 + c], in1=rs[g][:], op0=ALU.mult, op1=ALU.mult),
                            reads=[("x", c, g), ("rs", g), "vecs"], writes=[("ob", k)])
                    P.op("sp", lambda e, c=c, k=k, s_=s_: e.dma_start(out=out_d[s_][c * 128:(c + 1) * 128, :], in_=ob[k][:]),
                         reads=[("ob", k)], dma_ch=("out", k))
                P.emit_phase(final=(s_ == nseq - 1))
    return nc


def prep_shared(inp):
    f = lambda a: np.ascontiguousarray(np.asarray(a, dtype=np.float32))
    vecs = np.zeros((128, 80), np.float32)
    for l in range(4):
        vecs[:, 8 * l:8 * l + 8] = f(inp["ln_mix_g"])[l].reshape(8, 128).T
        vecs[:, 32 + 8 * l:32 + 8 * l + 8] = f(inp["ln_mlp_g"])[l].reshape(8, 128).T
    vecs[:, 64:72] = f(inp["ln_f_g"]).reshape(8, 128).T
    for i in range(2):
        vecs[:, 72 + 2 * i:74 + 2 * i] = f(inp["od_q_norm_g"])[i].reshape(2, 128).T
        vecs[:, 76 + i] = f(inp["od_kv_norm_g"])[i]
        vecs[:, 78 + i] = f(inp["ev_subln_g"])[i]
    lamv = np.zeros((128, 512), np.float32)
    for i in range(2):
        for j, nm in enumerate(("ev_lambda_q1", "ev_lambda_k1", "ev_lambda_q2", "ev_lambda_k2")):
            lamv[:, i * 256 + j * 64: i * 256 + (j + 1) * 64] = f(inp[nm])[i][None, :]
    ev_in = f(inp["ev_w_in"])
    permA = _swap_perm(1024, 64, 0, 8)
    ev_sw = np.ascontiguousarray(ev_in[:, :, :1024][:, :, permA])
    od_in = f(inp["od_w_in"])
    perm_kr = _swap_perm(32, 32, 0, 16)
    od_in_sw = np.ascontiguousarray(od_in[:, :, 320:416].copy())
    od_in_sw[:, :, 64:96] = od_in[:, :, 384:416][:, :, perm_kr]
    od_uq = f(inp["od_w_uq"])
    perm_q = _swap_perm(1536, 96, 64, 16)
    od_uq_sw = np.ascontiguousarray(od_uq[:, :, perm_q])
    CA, SA = _rope_tables_A()
    CM, SM = _rope_tables_M()
    rpb = f(inp["ev_rpb"])
    gb = rpb[:, :, _RIDX, _CIDX]
    gb = np.ascontiguousarray(gb.transpose(0, 3, 1, 2, 4)).reshape(2, 128, 8 * _NG * 128)
    gmask = np.where(_VALID, 0.0, MASKV).astype(np.float32)
    gmask = np.ascontiguousarray(gmask.transpose(1, 0, 2)).reshape(128, _NG * 128).astype(ml_dtypes.bfloat16)
    return {
        "vecs": vecs, "lamv": lamv,
        "w_up": f(inp["w_up"]), "w_down": f(inp["w_down"]),
        "ev_w_in": ev_in, "ev_w_sw": ev_sw, "ev_w_out": f(inp["ev_w_out"]),
        "od_w_in": od_in, "od_w_in_sw": od_in_sw, "od_w_uq": od_uq, "od_w_uq_sw": od_uq_sw,
        "od_w_ukv": f(inp["od_w_ukv"]), "od_w_out": f(inp["od_w_out"]),
        "ropeA": np.stack([CA, SA]).astype(ml_dtypes.bfloat16), "ropeM": np.stack([CM, SM]).astype(ml_dtypes.bfloat16),
        "gb": gb, "gmask": gmask,
        "ident": np.eye(128, dtype=np.float32).astype(ml_dtypes.bfloat16),
    }


_NC_CACHE = {}


def kernel(**inputs):
    x = np.asarray(inputs["x"], dtype=np.float32)
    B = x.shape[0]
    nseq = B // NCORES
    shared = prep_shared(inputs)
    key = nseq
    if key not in _NC_CACHE:
        _NC_CACHE[key] = build(nseq=nseq)
    nc = _NC_CACHE[key]
    in_maps = []
    for c in range(NCORES):
        xs_ = x[c * nseq:(c + 1) * nseq]
        m = dict(shared)
        m["xT"] = np.ascontiguousarray(xs_.transpose(0, 2, 1))
        in_maps.append(m)
    res = run_bass_kernel_spmd(nc, in_maps, core_ids=list(range(NCORES)))
    out = np.empty((B, S, D), np.float32)
    for c in range(NCORES):
        o = np.asarray(res.results[c]["outT"])
        out[c * nseq:(c + 1) * nseq] = o.transpose(0, 2, 1)
    return out
```

```python
import math
from contextlib import ExitStack

import numpy as np
import ml_dtypes
import concourse.bass as bass
import concourse.mybir as mybir
from concourse.bass_utils import run_bass_kernel_spmd

F32 = mybir.dt.float32
BF16 = mybir.dt.bfloat16
AF = mybir.ActivationFunctionType
ALU = mybir.AluOpType
AX = mybir.AxisListType

S = 2048
D = 1024
NCH = 8
NTG = 4
TG = 512
EPS = 1e-5
NCORES = 8
ENGS = ("pe", "act", "dve", "pool", "sp")


class Op:
    __slots__ = ("eng", "fn", "deps", "dma_ch", "sig", "idx", "has_consumer", "waits")

    def __init__(self, eng, fn, dma_ch=None):
        self.eng = eng
        self.fn = fn
        self.deps = set()
        self.dma_ch = dma_ch
        self.sig = None
        self.has_consumer = False
        self.waits = []


class Prog:
    def __init__(self, nc, es):
        self.nc = nc
        self.es = es
        self.sems = {}
        self.cnt = {e: 0 for e in ENGS}
        self.dcnt = {}
        self.reset_phase()
        self.nphase = 0

    def reset_phase(self):
        self.ops = []
        self.last_writer = {}
        self.readers = {}

    def sem(self, key):
        s = self.sems.get(key)
        if s is None:
            name = "s_" + "_".join(str(k) for k in key)
            s = self.es.enter_context(self.nc.semaphore(name))
            self.sems[key] = s
        return s

    def op(self, eng, fn, reads=(), writes=(), dma_ch=None):
        o = Op(eng, fn, dma_ch)
        o.idx = len(self.ops)
        deps = set()
        for r in reads:
            w = self.last_writer.get(r)
            if w is not None:
                deps.add(w)
        for r in writes:
            w = self.last_writer.get(r)
            if w is not None:
                deps.add(w)
            for rd in self.readers.get(r, ()):
                deps.add(rd)
        o.deps = deps
        self.ops.append(o)
        for r in reads:
            self.readers.setdefault(r, []).append(o.idx)
        for r in writes:
            self.last_writer[r] = o.idx
            self.readers[r] = []
        return o

    def emit_phase(self, final=False):
        ops = self.ops
        if not ops:
            return
        nc = self.nc
        for o in ops:
            best = {}
            for d in o.deps:
                p = ops[d]
                if p.dma_ch is not None:
                    key = ("dma", p.dma_ch)
                elif p.eng != o.eng or o.eng in ("act", "dve", "pool"):
                    key = ("eng", p.eng)
                else:
                    continue
                if d > best.get(key, -1):
                    best[key] = d
            keep = set(best.values())
            o.deps = keep
            for d in keep:
                ops[d].has_consumer = True
        last = {}
        for o in ops:
            if o.dma_ch is None:
                last[o.eng] = o
        for o in last.values():
            o.has_consumer = True
        prev_cnt = dict(self.cnt)
        prev_dcnt = dict(self.dcnt)
        for o in ops:
            if o.dma_ch is not None:
                self.dcnt[o.dma_ch] = self.dcnt.get(o.dma_ch, 0) + 16
                o.sig = ("dma", o.dma_ch, self.dcnt[o.dma_ch])
            elif o.has_consumer:
                self.cnt[o.eng] += 1
                o.sig = ("eng", o.eng, self.cnt[o.eng])
        waited = {e: {} for e in ENGS}
        for e in ENGS:
            for e2 in ENGS:
                if prev_cnt[e2] > 0:
                    waited[e][("eng", e2)] = prev_cnt[e2]
            for ch, v in prev_dcnt.items():
                waited[e][("dma", ch)] = v
        for o in ops:
            need = {}
            for d in o.deps:
                s = ops[d].sig
                key = (s[0], s[1])
                if s[2] > need.get(key, 0):
                    need[key] = s[2]
            for key, v in need.items():
                if waited[o.eng].get(key, 0) >= v:
                    continue
                waited[o.eng][key] = v
                o.waits.append((key, v))
        by_eng = {e: [o for o in ops if o.eng == e] for e in ENGS}
        sems = self.sem
        final_d = dict(self.dcnt)
        final_c = dict(self.cnt)

        def run(eng_name, eng):
            for e2 in ENGS:
                if e2 != eng_name and prev_cnt[e2] > 0:
                    eng.wait_ge(sems(("eng", e2)), prev_cnt[e2])
            for ch, v in prev_dcnt.items():
                eng.wait_ge(sems(("dma", ch)), v)
            for o in by_eng[eng_name]:
                for key, v in o.waits:
                    eng.wait_ge(sems(key), v)
                ins = o.fn(eng)
                if o.dma_ch is not None:
                    ins.then_inc(sems(("dma", o.dma_ch)), 16)
                elif o.sig is not None:
                    ins.then_inc(sems(("eng", eng_name)), 1)
            if final:
                for ch, v in final_d.items():
                    eng.wait_ge(sems(("dma", ch)), v)
                for e2 in ENGS:
                    if e2 != eng_name and final_c[e2] > 0:
                        eng.wait_ge(sems(("eng", e2)), final_c[e2])

        for e in ENGS:
            sems(("eng", e))
        for ch in self.dcnt:
            sems(("dma", ch))
        with nc.Block() as block:
            @block.sync
            def _(e):
                run("sp", e)

            @block.tensor
            def _(e):
                run("pe", e)

            @block.scalar
            def _(e):
                run("act", e)

            @block.vector
            def _(e):
                run("dve", e)

            @block.gpsimd
            def _(e):
                run("pool", e)
        self.nphase += 1
        self.reset_phase()


class Pipe:
    def __init__(self, look):
        self.look = look
        self.q = []

    def step(self, qk, pv):
        qk()
        self.q.append(pv)
        if len(self.q) > self.look:
            self.q.pop(0)()

    def flush(self):
        while self.q:
            self.q.pop(0)()


def _rot_tables(rot_dim, theta=500000.0):
    pos = np.arange(S, dtype=np.float32)
    inv = (np.float32(theta) ** (-np.arange(0, rot_dim, 2, dtype=np.float32) / np.float32(rot_dim))).astype(np.float32)
    ang = (pos[:, None] * inv[None, :]).astype(np.float32)
    return np.cos(ang).astype(np.float32), np.sin(ang).astype(np.float32)


def _rope_tables_A():
    cos, sin = _rot_tables(16)
    C = np.ones((128, S), np.float32)
    Sg = np.zeros((128, S), np.float32)
    for base in (0, 64):
        for d in range(8):
            C[base + d] = cos[:, d]
            Sg[base + d] = -sin[:, d]
            C[base + 8 + d] = cos[:, d]
            Sg[base + 8 + d] = sin[:, d]
    return C, Sg


def _rope_tables_M():
    cos, sin = _rot_tables(32)
    C = np.ones((128, S), np.float32)
    Sg = np.zeros((128, S), np.float32)
    for d in range(16):
        C[64 + d] = cos[:, d]
        Sg[64 + d] = -sin[:, d]
        C[64 + 16 + d] = cos[:, d]
        Sg[64 + 16 + d] = sin[:, d]
    return C, Sg


def _swap_perm(n, head, rot_off, half):
    idx = np.arange(n)
    for h0 in range(0, n, head):
        for d in range(half):
            idx[h0 + rot_off + d] = h0 + rot_off + half + d
            idx[h0 + rot_off + half + d] = h0 + rot_off + d
    return idx


def _nbr_plan():
    rows, W, wr, wc = 32, 64, 8, 16
    gids = {}
    plan = []
    for m in range(16):
        need = {}
        for b in range(2):
            qr = 2 * m + b
            r0 = min(max(qr - wr // 2, 0), rows - wr)
            for kr in range(r0, r0 + wr):
                t, a = kr // 2, kr % 2
                need.setdefault(t, set()).add((a, b))
        lst = []
        for t in sorted(need):
            key = (2 * t - 2 * m, tuple(sorted(need[t])))
            if key not in gids:
                gids[key] = len(gids)
            lst.append((t, gids[key]))
        plan.append(lst)
    ng = len(gids)
    ridx = np.zeros((ng, 128, 128), np.int64)
    cidx = np.zeros((ng, 128, 128), np.int64)
    valid = np.zeros((ng, 128, 128), bool)
    c = np.arange(W)
    cs = np.clip(c - wc // 2, 0, W - wc)
    colmask = (c[None, :] >= cs[:, None]) & (c[None, :] < cs[:, None] + wc)
    for (delta, pat), g in gids.items():
        for a in range(2):
            for b in range(2):
                dr = delta + a - b + 7
                ok = (a, b) in pat
                for kc in range(W):
                    k = a * 64 + kc
                    q = b * 64 + c
                    ridx[g, k, q] = min(max(dr, 0), 14)
                    cidx[g, k, q] = np.clip(kc - c, -15, 15) + 15
                    valid[g, k, q] = ok & colmask[c, kc]
    return plan, ng, ridx, cidx, valid


_PLAN, _NG, _RIDX, _CIDX, _VALID = _nbr_plan()
MASKV = -30000.0


def build(nseq=4, layers=(0, 1, 2, 3), do_mlp=True, do_mix=True, parts="AB", dbg=False):
    nc = bass.Bass("TRN2", target_bir_lowering=False)
    dt = nc.dram_tensor
    xT_d = dt("xT", [nseq, D, S], F32, kind="ExternalInput").ap()
    out_d = dt("outT", [nseq, D, S], F32, kind="ExternalOutput").ap()
    vecs_d = dt("vecs", [128, 80], F32, kind="ExternalInput").ap()
    lam_d = dt("lamv", [128, 2 * 4 * 64], F32, kind="ExternalInput").ap()
    w_up_d = dt("w_up", [4, D, 4096], F32, kind="ExternalInput").ap()
    w_dn_d = dt("w_down", [4, 4096, D], F32, kind="ExternalInput").ap()
    ev_in_d = dt("ev_w_in", [2, D, 3072], F32, kind="ExternalInput").ap()
    ev_sw_d = dt("ev_w_sw", [2, D, 1024], F32, kind="ExternalInput").ap()
    ev_out_d = dt("ev_w_out", [2, D, D], F32, kind="ExternalInput").ap()
    od_in_d = dt("od_w_in", [2, D, 416], F32, kind="ExternalInput").ap()
    od_insw_d = dt("od_w_in_sw", [2, D, 96], F32, kind="ExternalInput").ap()
    od_uq_d = dt("od_w_uq", [2, 256, 1536], F32, kind="ExternalInput").ap()
    od_uqsw_d = dt("od_w_uq_sw", [2, 256, 1536], F32, kind="ExternalInput").ap()
    od_ukv_d = dt("od_w_ukv", [2, 128, 2048], F32, kind="ExternalInput").ap()
    od_out_d = dt("od_w_out", [2, D, D], F32, kind="ExternalInput").ap()
    ropeA_d = dt("ropeA", [2, 128, S], BF16, kind="ExternalInput").ap()
    ropeM_d = dt("ropeM", [2, 128, S], BF16, kind="ExternalInput").ap()
    gb_d = dt("gb", [2, 128, 8 * _NG * 128], F32, kind="ExternalInput").ap()
    gmask_d = dt("gmask", [128, _NG * 128], BF16, kind="ExternalInput").ap()
    ident_d = dt("ident", [128, 128], BF16, kind="ExternalInput").ap()

    if dbg:
        dbg_d = {nm: dt("dbg_" + nm, [128, S], BF16, kind="ExternalOutput").ap() for nm in ("q", "k", "v", "o")}
    es = ExitStack()
    with es:
        uniq = [0]

        def sb(name, shape, dty, stack=es):
            uniq[0] += 1
            return stack.enter_context(nc.sbuf_tensor("%s_u%d" % (name, uniq[0]), shape, dty))

        P = Prog(nc, es)
        xT = sb("xT_s", [128, NCH * S], F32)
        hT = sb("hT_s", [128, NCH * S], BF16)
        vecs = sb("vecs_s", [128, 80], F32)
        lams = sb("lams", [128, 16], F32)
        ones = sb("ones", [128, 128], BF16)
        ident = sb("ident_s", [128, 128], BF16)
        NSTG = 2
        STGW = 1536
        stage = [sb("stage%d" % i, [128, STGW], F32) for i in range(NSTG)]
        NTMP = 5
        tmp = [sb("tmp%d" % i, [128, TG], F32) for i in range(NTMP)]
        sq = [sb("sq%d" % i, [128, TG], BF16) for i in range(2)]
        psS = es.enter_context(nc.psum_tensor("psS", [128, 2048], F32))
        psX0 = es.enter_context(nc.psum_tensor("psX0", [128, TG], F32))
        psX1 = es.enter_context(nc.psum_tensor("psX1", [128, TG], F32))
        psO = es.enter_context(nc.psum_tensor("psO", [128, TG], F32))
        psD = es.enter_context(nc.psum_tensor("psD", [128, TG], F32))
        psX = [psX0[:, :], psX1[:, :], psD[:, :], psO[:, :]] + [psS[:, k * TG:(k + 1) * TG] for k in range(4)]

        st = {"stg": 0, "tmp": 0, "sq": 0, "px": 0, "npx": 2}

        def xs(c, g):
            return xT[:, c * S + g * TG: c * S + (g + 1) * TG]

        def hs(c, g):
            return hT[:, c * S + g * TG: c * S + (g + 1) * TG]

        def next_tmp():
            i = st["tmp"]
            st["tmp"] = (i + 1) % NTMP
            return i

        def next_px():
            i = st["px"] % st["npx"]
            st["px"] = (i + 1) % st["npx"]
            return i

        def next_sq():
            i = st["sq"]
            st["sq"] = (i + 1) % 2
            return i

        def wload(dst, dst_off, src3, kc, n, dst_res):
            per = max(1, STGW // n)
            k0 = 0
            while k0 < kc:
                k1 = min(kc, k0 + per)
                tot = (k1 - k0) * n
                slot = st["stg"]
                st["stg"] = (slot + 1) % NSTG
                stg = stage[slot]
                src = src3[:, k0:k1, :]
                dview = stg[:, 0:tot].rearrange("p (k n) -> p k n", k=k1 - k0)
                P.op("sp", lambda e, dview=dview, src=src: e.dma_start(out=dview, in_=src),
                     writes=[("stg", slot)], dma_ch=("stg", slot))
                o0 = dst_off + k0 * n
                P.op("pool", lambda e, stg=stg, o0=o0, tot=tot: e.tensor_copy(out=dst[:, o0:o0 + tot], in_=stg[:, 0:tot]),
                     reads=[("stg", slot)], writes=(dst_res if isinstance(dst_res, list) else [dst_res]))
                k0 = k1

        def wview(w2d, r0, nk, c0, n):
            return w2d[r0:r0 + nk * 128, c0:c0 + n].rearrange("(k p) n -> p k n", p=128)

        def rmsnorm_x(gcol0, dst_fn, dst_res_fn, out_f32=False):
            for g in range(NTG):
                px = next_px()
                ps = psX[px]
                for c in range(NCH):
                    P.op("act", lambda e, c=c, g=g: e.activation(out=hs(c, g), in_=xs(c, g), func=AF.Square),
                         reads=[("x", c, g)], writes=[("h", c, g)])
                    P.op("pe", lambda e, c=c, g=g, ps=ps: e.matmul(ps[:], lhsT=ones[:], rhs=hs(c, g), start=(c == 0), stop=(c == NCH - 1)),
                         reads=[("h", c, g), "ones"], writes=[("px", px)])
                ti = next_tmp()
                P.op("act", lambda e, ti=ti, ps=ps: e.activation(out=tmp[ti][:], in_=ps[:], func=AF.Sqrt, scale=1.0 / D, bias=EPS),
                     reads=[("px", px)], writes=[("tmp", ti)])
                P.op("dve", lambda e, ti=ti: e.reciprocal(out=tmp[ti][:], in_=tmp[ti][:]),
                     reads=[("tmp", ti)], writes=[("tmp", ti)])
                for c in range(NCH):
                    P.op("dve", lambda e, ti=ti, c=c, g=g: e.scalar_tensor_tensor(
                        out=dst_fn(c, g), in0=xs(c, g), scalar=vecs[:, gcol0 + c:gcol0 + c + 1], in1=tmp[ti][:],
                        op0=ALU.mult, op1=ALU.mult),
                        reads=[("x", c, g), ("tmp", ti), "vecs"], writes=[dst_res_fn(c, g)])

        init_es = ExitStack()
        lamt = sb("lamt", [128, 512], F32, init_es)
        lamw = sb("lamw", [128, 512], F32, init_es)
        P.op("sp", lambda e: e.dma_start(out=vecs[:], in_=vecs_d[:, :]), writes=["vecs"], dma_ch="vecs")
        P.op("sp", lambda e: e.dma_start(out=lamt[:], in_=lam_d[:, :]), writes=["lamt"], dma_ch="lamt")
        P.op("sp", lambda e: e.dma_start(out=ident[:], in_=ident_d[:, :]), writes=["ident"], dma_ch="ident")
        P.op("dve", lambda e: e.memset(ones[:], 1.0), writes=["ones"])
        for i in range(2):
            b = i * 256
            for j in range(2):
                P.op("dve", lambda e, b=b, j=j: e.tensor_tensor(out=lamw[:, b + 64 * j: b + 64 * j + 64], in0=lamt[:, b + 128 * j: b + 128 * j + 64],
                                                             in1=lamt[:, b + 128 * j + 64: b + 128 * j + 128], op=ALU.mult),
                     reads=["lamt"], writes=[("lamw", i, j)])
                P.op("dve", lambda e, b=b, j=j, i=i: e.reduce_sum(out=lams[:, 4 * i + j: 4 * i + j + 1], in_=lamw[:, b + 64 * j: b + 64 * j + 64], axis=AX.X),
                     reads=[("lamw", i, j)], writes=[("lams", i, j)])
                P.op("act", lambda e, j=j, i=i: e.activation(out=lams[:, 4 * i + j: 4 * i + j + 1], in_=lams[:, 4 * i + j: 4 * i + j + 1], func=AF.Exp),
                     reads=[("lams", i, j)], writes=[("lams", i, j)])
            lam_init = 0.8 - 0.6 * math.exp(-0.3 * (2 * i))
            P.op("dve", lambda e, i=i, lam_init=lam_init: e.scalar_tensor_tensor(
                out=lams[:, 4 * i + 2: 4 * i + 3], in0=lams[:, 4 * i + 1: 4 * i + 2], scalar=-lam_init, in1=lams[:, 4 * i: 4 * i + 1],
                op0=ALU.add, op1=ALU.subtract),
                reads=[("lams", i, 0), ("lams", i, 1)], writes=[("lams", i, 2)])
        P.emit_phase()
        init_es.close()

        def mlp_phase(l):
            with ExitStack() as ps_:
                st["npx"] = 8
                wm = [sb("wmlp%d" % i, [128, 8192], BF16, ps_) for i in range(2)]
                uT = sb("uT", [128, 4 * S], BF16, ps_)
                rmsnorm_x(32 + 8 * l, hs, lambda c, g: ("h", c, g))
                for part in range(8):
                    wb = part % 2
                    w = wm[wb]
                    wload(w, 0, wview(w_up_d[l], 0, 8, part * 512, 512), 8, 512, ("wm", wb, 0))
                    wload(w, 4096, wview(w_dn_d[l], part * 512, 4, 0, 1024), 4, 1024, ("wm", wb, 1))
                    for g in range(NTG):
                        for j in range(4):
                            px = next_px()
                            ps = psX[px]
                            for c in range(NCH):
                                P.op("pe", lambda e, w=w, c=c, j=j, g=g, ps=ps: e.matmul(
                                    ps[:], lhsT=w[:, c * 512 + j * 128: c * 512 + j * 128 + 128], rhs=hs(c, g),
                                    start=(c == 0), stop=(c == NCH - 1)),
                                    reads=[("wm", wb, 0), ("h", c, g)], writes=[("px", px)])
                            ti = next_tmp()
                            P.op("act", lambda e, ti=ti, ps=ps: e.activation(out=tmp[ti][:], in_=ps[:], func=AF.Relu),
                                 reads=[("px", px)], writes=[("tmp", ti)])
                            P.op("pool", lambda e, ti=ti, j=j, g=g: e.tensor_tensor(
                                out=uT[:, j * S + g * TG: j * S + (g + 1) * TG], in0=tmp[ti][:], in1=tmp[ti][:], op=ALU.mult),
                                reads=[("tmp", ti)], writes=[("u", j, g)])
                    for g in range(NTG):
                        for c in range(NCH):
                            px = next_px()
                            ps = psX[px]
                            for j in range(4):
                                P.op("pe", lambda e, w=w, c=c, j=j, g=g, ps=ps: e.matmul(
                                    ps[:], lhsT=w[:, 4096 + j * 1024 + c * 128: 4096 + j * 1024 + c * 128 + 128],
                                    rhs=uT[:, j * S + g * TG: j * S + (g + 1) * TG], start=(j == 0), stop=(j == 3)),
                                    reads=[("wm", wb, 1), ("u", j, g)], writes=[("px", px)])
                            P.op("dve", lambda e, c=c, g=g, ps=ps: e.tensor_tensor(out=xs(c, g), in0=ps[:], in1=xs(c, g), op=ALU.add),
                                 reads=[("px", px), ("x", c, g)], writes=[("x", c, g)])
                P.emit_phase()

        def proj_fm(ps, wt, woff, wstride, m, nk, rhs_fn, g, wres, rres_fn, pxres, prow0=0):
            for k in range(nk):
                P.op("pe", lambda e, k=k: e.matmul(ps[prow0:prow0 + m, :], lhsT=wt[:, woff + k * wstride: woff + k * wstride + m],
                                                  rhs=rhs_fn(k, g), start=(k == 0), stop=(k == nk - 1)),
                     reads=[wres, rres_fn(k, g)], writes=[pxres])

        def outproj_chunk(wout_d2, kc, oH, ores, watt, wb, wo=5120, load=True, compute=True):
            if load:
                wload(watt[wb], wo, wview(wout_d2, kc * 128, 1, 0, 1024), 1, 1024, ("wa", wb, 5))
            if not compute:
                return
            for g in range(NTG):
                for c in range(NCH):
                    px = next_px()
                    ps = psX[px]
                    P.op("pe", lambda e, c=c, g=g, ps=ps: e.matmul(ps[:], lhsT=watt[wb][:, wo + c * 128: wo + c * 128 + 128],
                                                                   rhs=oH[:, g * TG:(g + 1) * TG], start=True, stop=True),
                         reads=[("wa", wb, 5), (ores, g)], writes=[("px", px)])
                    P.op("dve", lambda e, c=c, g=g, ps=ps: e.tensor_tensor(out=xs(c, g), in0=ps[:], in1=xs(c, g), op=ALU.add),
                         reads=[("px", px), ("x", c, g)], writes=[("x", c, g)])

        def even_phase(l, part, do_norm):
            i = l // 2
            lam_init = 0.8 - 0.6 * math.exp(-0.3 * l)
            isA = part == "A"
            with ExitStack() as ps_:
                st["npx"] = 2
                st["px"] = 0
                watt = [sb("watt%d" % k, [128, 6144], BF16, ps_) for k in range(2)]
                qTm = [sb("qTm%d" % k, [128, S], BF16, ps_) for k in range(2)]
                kT = sb("kT", [128, S], BF16, ps_)
                vS = sb("vS", [128, 16 * 128], BF16, ps_)
                oH = [sb("oH%d" % k, [128, S], BF16, ps_) for k in range(4)]
                NPT = 6
                PT = [sb("PT%d" % k, [128, 512], BF16, ps_) for k in range(NPT)]
                unit_b = [0]
                unit_a = [0]
                for k_ in range(2):
                    P.op("pool", lambda e, k_=k_: e.memset(qTm[k_][:], 0.0), writes=[("qT", k_, g) for g in range(NTG)])
                pt_i = [0]
                sbuf_i = [0]
                if isA:
                    ropeC = sb("ropeC", [128, S], BF16, ps_)
                    ropeS = sb("ropeS", [128, S], BF16, ps_)
                    NAT = 6
                    atmp = [sb("atmp%d" % k, [128, TG], F32, ps_) for k in range(NAT)]
                    at_i = [0]
                    P.op("sp", lambda e: e.dma_start(out=ropeC[:], in_=ropeA_d[0]), writes=["ropeC"], dma_ch="ropeC")
                    P.op("sp", lambda e: e.dma_start(out=ropeS[:], in_=ropeA_d[1]), writes=["ropeS"], dma_ch="ropeS")
                else:
                    G = sb("G", [128, 8 * _NG * 128], BF16, ps_)
                    gmask = sb("gmask", [128, _NG * 128], BF16, ps_)
                    P.op("sp", lambda e: e.dma_start(out=gmask[:], in_=gmask_d[:, :]), writes=["gmask"], dma_ch="gmask")
                    GW = _NG * 128
                    for h in range(8):
                        slot = st["stg"]
                        st["stg"] = (slot + 1) % NSTG
                        stg = stage[slot]
                        P.op("sp", lambda e, stg=stg, h=h: e.dma_start(out=stg[:, 0:GW], in_=gb_d[i][:, h * GW:(h + 1) * GW]),
                             writes=[("stg", slot)], dma_ch=("stg", slot))
                        P.op("dve", lambda e, stg=stg, h=h: e.scalar_tensor_tensor(
                            out=G[:, h * GW:(h + 1) * GW], in0=stg[:, 0:GW], scalar=8.0, in1=gmask[:], op0=ALU.mult, op1=ALU.add),
                            reads=[("stg", slot), "gmask"], writes=[("G", h)])

                if do_norm:
                    rmsnorm_x(8 * l, hs, lambda c, g: ("h", c, g))

                def rope_evac(ps_a, ps_b, pxa, pxb, dst, g, dres, rows=128):
                    t1 = next_tmp()
                    t2 = next_tmp()
                    P.op("dve", lambda e: e.tensor_tensor(out=tmp[t1][0:rows, :], in0=ps_a[0:rows, :], in1=ropeC[0:rows, g * TG:(g + 1) * TG], op=ALU.mult),
                         reads=[("px", pxa), "ropeC"], writes=[("tmp", t1)])
                    P.op("dve", lambda e: e.tensor_tensor(out=tmp[t2][0:rows, :], in0=ps_b[0:rows, :], in1=ropeS[0:rows, g * TG:(g + 1) * TG], op=ALU.mult),
                         reads=[("px", pxb), "ropeS"], writes=[("tmp", t2)])
                    if dst is None:
                        for hf in range(2):
                            P.op("pool", lambda e, hf=hf: e.tensor_tensor(out=qTm[hf][64 * hf:64 * hf + 64, g * TG:(g + 1) * TG], in0=tmp[t1][64 * hf:64 * hf + 64, :],
                                                                      in1=tmp[t2][64 * hf:64 * hf + 64, :], op=ALU.add),
                                 reads=[("tmp", t1), ("tmp", t2)], writes=[("qT", hf, g)])
                    else:
                        P.op("pool", lambda e: e.tensor_tensor(out=dst[0:rows, g * TG:(g + 1) * TG], in0=tmp[t1][0:rows, :], in1=tmp[t2][0:rows, :], op=ALU.add),
                             reads=[("tmp", t1), ("tmp", t2)], writes=[(dres, g)])

                hfn = lambda k, g: hs(k, g)
                hres = lambda k, g: ("h", k, g)

                def v_tokmajor(w, woff, wres, ncols):
                    for t4 in range(4):
                        px = next_px()
                        ps = psX[px]
                        for tt in range(4):
                            tok = (t4 * 4 + tt) * 128
                            for k in range(NCH):
                                P.op("pe", lambda e, k=k, tt=tt, tok=tok, ps=ps: e.matmul(
                                    ps[:, tt * 128: tt * 128 + ncols], lhsT=hT[:, k * S + tok: k * S + tok + 128],
                                    rhs=w[:, woff + k * 128: woff + k * 128 + ncols], start=(k == 0), stop=(k == NCH - 1)),
                                    reads=[wres, ("h", k, tok // TG)], writes=[("px", px)])
                        P.op("act", lambda e, t4=t4, ps=ps: e.activation(out=vS[:, t4 * 512:(t4 + 1) * 512], in_=ps[:], func=AF.Copy),
                             reads=[("px", px)], writes=[("vS", t4)])

                def outproj_acc(row0):
                    for c in range(NCH):
                        wb_ = c % 2
                        wload(watt[wb_], 5120, wview(ev_out_d[i], row0, 4, c * 128, 128), 4, 128, ("wa", wb_, 5))
                        for g in range(NTG):
                            px = next_px()
                            ps = psX[px]
                            for k in range(4):
                                P.op("pe", lambda e, k=k, g=g, ps=ps, wb_=wb_: e.matmul(ps[:], lhsT=watt[wb_][:, 5120 + k * 128: 5120 + (k + 1) * 128],
                                                                                  rhs=oH[k][:, g * TG:(g + 1) * TG], start=(k == 0), stop=(k == 3)),
                                     reads=[("wa", wb_, 5), (("oH", k), g)], writes=[("px", px)])
                            P.op("dve", lambda e, c=c, g=g, ps=ps: e.tensor_tensor(out=xs(c, g), in0=ps[:], in1=xs(c, g), op=ALU.add),
                                 reads=[("px", px), ("x", c, g)], writes=[("x", c, g)])

                for hd in (range(4) if isA else ()):
                    wb = hd % 2
                    w = watt[wb]
                    wload(w, 0, wview(ev_in_d[i], 0, 8, 128 * hd, 128), 8, 128, ("wa", wb, 0))
                    wload(w, 1024, wview(ev_sw_d[i], 0, 8, 128 * hd, 128), 8, 128, ("wa", wb, 1))
                    wload(w, 2048, wview(ev_in_d[i], 0, 8, 512 + 128 * hd, 128), 8, 128, ("wa", wb, 2))
                    wload(w, 3072, wview(ev_sw_d[i], 0, 8, 512 + 128 * hd, 128), 8, 128, ("wa", wb, 3))
                    wload(w, 4096, wview(ev_in_d[i], 0, 8, 1024 + 128 * hd, 128), 8, 128, ("wa", wb, 4))
                    for (dst, dres, o0) in ((None, "qT", 0), (kT, "kT", 2048)):
                        for g in range(NTG):
                            pa = next_px()
                            proj_fm(psX[pa], w, o0, 128, 128, 8, hfn, g, ("wa", wb, o0 // 1024), hres, ("px", pa))
                            pb = next_px()
                            proj_fm(psX[pb], w, o0 + 1024, 128, 128, 8, hfn, g, ("wa", wb, o0 // 1024 + 1), hres, ("px", pb))
                            rope_evac(psX[pa], psX[pb], pa, pb, dst, g, dres)
                    v_tokmajor(w, 4096, ("wa", wb, 4), 128)
                    ob = hd
                    pipe = Pipe(3)
                    hold = {}
                    later = []

                    def fin_a(g, half, aset, ob=ob, hold=hold):
                        aO, nO, aD, nD = aset
                        tO = at_i[0]
                        tD = (tO + 1) % NAT
                        at_i[0] = (tO + 2) % NAT
                        P.op("dve", lambda e: e.reciprocal(out=atmp[tD][:], in_=aD[:, :]), reads=[nD], writes=[("at", tD)])
                        P.op("dve", lambda e: e.tensor_tensor(out=atmp[tO][:], in0=aO[:, :], in1=atmp[tD][:], op=ALU.mult),
                             reads=[nO, ("at", tD)], writes=[("at", tO)])
                        if half == 0:
                            hold[g] = tO
                            return
                        a = hold[g]
                        b = tO
                        P.op("dve", lambda e: e.scalar_tensor_tensor(
                            out=atmp[a][:], in0=atmp[b][:], scalar=lams[:, 4 * i + 2: 4 * i + 3], in1=atmp[a][:], op0=ALU.mult, op1=ALU.add),
                            reads=[("at", a), ("at", b), ("lams", i, 2)], writes=[("at", a)])
                        later.append(lambda: subln_tail(g, a, tD, ob))

                    def subln_tail(g, a, tD, ob):
                        si = next_sq()
                        P.op("act", lambda e: e.activation(out=sq[si][:], in_=atmp[a][:], func=AF.Square), reads=[("at", a)], writes=[("sq", si)])
                        sb2 = sbuf_i[0]
                        sbuf_i[0] = (sb2 + 1) % 4
                        sreg2 = psS[:, sb2 * TG:(sb2 + 1) * TG]
                        P.op("pe", lambda e: e.matmul(sreg2, lhsT=ones[:], rhs=sq[si][:], start=True, stop=True),
                             reads=[("sq", si), "ones"], writes=[("psS", sb2)])
                        k2 = 1.0 / ((1.0 - lam_init) ** 2)
                        P.op("act", lambda e: e.activation(out=atmp[tD][:], in_=sreg2, func=AF.Sqrt, scale=k2 / 128.0, bias=EPS * k2),
                             reads=[("psS", sb2)], writes=[("at", tD)])
                        P.op("dve", lambda e: e.reciprocal(out=atmp[tD][:], in_=atmp[tD][:]), reads=[("at", tD)], writes=[("at", tD)])
                        P.op("dve", lambda e: e.scalar_tensor_tensor(
                            out=oH[ob][:, g * TG:(g + 1) * TG], in0=atmp[a][:], scalar=vecs[:, 78 + i: 79 + i], in1=atmp[tD][:], op0=ALU.mult, op1=ALU.mult),
                            reads=[("at", a), ("at", tD), "vecs"], writes=[(("oH", ob), g)])

                    def a_step(g, half, kt, aset):
                        aO, nO, aD, nD = aset
                        sbi = sbuf_i[0]
                        sbuf_i[0] = (sbi + 1) % 4
                        pti = pt_i[0]
                        pt_i[0] = (pti + 1) % NPT
                        sreg = psS[:, sbi * TG:(sbi + 1) * TG]

                        def qk():
                            P.op("pe", lambda e: e.matmul(sreg, lhsT=kT[:, kt * 128:(kt + 1) * 128], rhs=qTm[half][:, g * TG:(g + 1) * TG], start=True, stop=True),
                                 reads=[("kT", kt // 4), ("qT", half, g)], writes=[("psS", sbi)])
                            P.op("act", lambda e: e.activation(out=PT[pti][:, 0:TG], in_=sreg, func=AF.Exp, scale=0.125),
                                 reads=[("psS", sbi)], writes=[("PT", pti)])

                        def pv():
                            P.op("pe", lambda e: e.matmul(aO[:, :], lhsT=vS[:, kt * 128:(kt + 1) * 128], rhs=PT[pti][:, 0:TG], start=(kt == 0), stop=(kt == 15)),
                                 reads=[("PT", pti), ("vS", kt // 4)], writes=[nO])
                            P.op("pe", lambda e: e.matmul(aD[:, :], lhsT=ones[:], rhs=PT[pti][:, 0:TG], start=(kt == 0), stop=(kt == 15)),
                                 reads=[("PT", pti), "ones"], writes=[nD])
                            if kt == 15:
                                fin_a(g, half, aset)
                            if kt == 7:
                                while later:
                                    later.pop(0)()
                        pipe.step(qk, pv)

                    accs_a = [(psO, ("px", 3), psD, ("px", 2)), (psX[0], ("px", 0), psX[1], ("px", 1))]
                    for g in range(NTG):
                        for half in range(2):
                            aset = accs_a[unit_a[0] % 2]
                            unit_a[0] += 1
                            for kt in range(16):
                                a_step(g, half, kt, aset)
                    pipe.flush()
                    while later:
                        later.pop(0)()
                if isA:
                    outproj_acc(0)

                for cpair in (() if isA else range(4)):
                    wb = cpair % 2
                    w = watt[wb]
                    wload(w, 0, wview(ev_in_d[i], 0, 8, 1536 + 128 * cpair, 128), 8, 128, ("wa", wb, 0))
                    wload(w, 2048, wview(ev_in_d[i], 0, 8, 2048 + 128 * cpair, 128), 8, 128, ("wa", wb, 2))
                    wload(w, 4096, wview(ev_in_d[i], 0, 8, 2560 + 128 * cpair, 128), 8, 128, ("wa", wb, 4))
                    for (dst, dres, o0) in ((None, "qT", 0), (kT, "kT", 2048)):
                        for g in range(NTG):
                            pa = next_px()
                            proj_fm(psX[pa], w, o0, 128, 128, 8, hfn, g, ("wa", wb, o0 // 1024), hres, ("px", pa))
                            if dst is None:
                                for hf in range(2):
                                    P.op("act", lambda e, pa=pa, hf=hf, g=g: e.activation(out=qTm[hf][64 * hf:64 * hf + 64, g * TG:(g + 1) * TG],
                                                                                      in_=psX[pa][64 * hf:64 * hf + 64, :], func=AF.Copy),
                                         reads=[("px", pa)], writes=[("qT", hf, g)])
                            else:
                                P.op("act", lambda e, pa=pa, dst=dst, g=g: e.activation(out=dst[:, g * TG:(g + 1) * TG], in_=psX[pa][:], func=AF.Copy),
                                     reads=[("px", pa)], writes=[(dres, g)])
                    v_tokmajor(w, 4096, ("wa", wb, 4), 128)
                    ob = cpair
                    accs_b = [(psO, ("px", 3), psD, ("px", 2)), (psX[0], ("px", 0), psX[1], ("px", 1))]
                    tl = []
                    for hh in range(2):
                        for m4 in range(4):
                            aset = accs_b[unit_b[0] % 2]
                            unit_b[0] += 1
                            for mm in range(4):
                                m = m4 * 4 + mm
                                tiles = _PLAN[m]
                                for ti_, (t, gid) in enumerate(tiles):
                                    tl.append(dict(hh=hh, h=2 * cpair + hh, m=m, mm=mm, m4=m4, t=t, gid=gid, first=(ti_ == 0),
                                                   last=(ti_ == len(tiles) - 1), aset=aset, fin=(mm == 3 and ti_ == len(tiles) - 1)))
                    pipe = Pipe(3)

                    def fin_b(d, ob=ob):
                        aO, nO, aD, nD = d["aset"]
                        r0 = 64 * d["hh"]
                        m4 = d["m4"]
                        tr = next_tmp()
                        P.op("dve", lambda e: e.reciprocal(out=tmp[tr][r0:r0 + 64, :], in_=aD[r0:r0 + 64, :]),
                             reads=[nD], writes=[("tmp", tr)])
                        P.op("dve", lambda e: e.tensor_tensor(
                            out=oH[ob][r0:r0 + 64, m4 * TG:(m4 + 1) * TG], in0=aO[r0:r0 + 64, :], in1=tmp[tr][r0:r0 + 64, :], op=ALU.mult),
                            reads=[nO, ("tmp", tr)], writes=[(("oH", ob), m4)])

                    def b_chunk(ch):
                        sbi = sbuf_i[0]
                        sbuf_i[0] = (sbi + 1) % 4
                        pti = pt_i[0]
                        pt_i[0] = (pti + 1) % NPT
                        n = len(ch)

                        def qk():
                            for j, d in enumerate(ch):
                                sreg = psS[:, sbi * TG + j * 128: sbi * TG + (j + 1) * 128]
                                goff = (d["h"] * _NG + d["gid"]) * 128
                                P.op("pe", lambda e, sreg=sreg, goff=goff: e.matmul(sreg, lhsT=ident[:], rhs=G[:, goff:goff + 128], start=True, stop=False),
                                     reads=[("G", d["h"]), "ident"], writes=[("psS", sbi)])
                                P.op("pe", lambda e, sreg=sreg, d=d: e.matmul(
                                    sreg, lhsT=kT[:, d["t"] * 128:(d["t"] + 1) * 128], rhs=qTm[d["hh"]][:, d["m"] * 128:(d["m"] + 1) * 128], start=False, stop=True),
                                    reads=[("kT", d["t"] // 4), ("qT", d["hh"], d["m"] // 4)], writes=[("psS", sbi)])
                            P.op("act", lambda e: e.activation(out=PT[pti][:, 0:n * 128], in_=psS[:, sbi * TG: sbi * TG + n * 128], func=AF.Exp, scale=0.125),
                                 reads=[("psS", sbi)], writes=[("PT", pti)])

                        def pv():
                            for j, d in enumerate(ch):
                                aO, nO, aD, nD = d["aset"]
                                mm = d["mm"]
                                P.op("pe", lambda e, j=j, d=d, aO=aO, mm=mm: e.matmul(
                                    aO[:, mm * 128:(mm + 1) * 128], lhsT=vS[:, d["t"] * 128:(d["t"] + 1) * 128], rhs=PT[pti][:, j * 128:(j + 1) * 128],
                                    start=d["first"], stop=d["last"]),
                                    reads=[("PT", pti), ("vS", d["t"] // 4)], writes=[nO])
                                P.op("pe", lambda e, j=j, d=d, aD=aD, mm=mm: e.matmul(
                                    aD[:, mm * 128:(mm + 1) * 128], lhsT=ones[:], rhs=PT[pti][:, j * 128:(j + 1) * 128],
                                    start=d["first"], stop=d["last"]),
                                    reads=[("PT", pti), "ones"], writes=[nD])
                                if d["fin"]:
                                    fin_b(d)
                        pipe.step(qk, pv)

                    for c0 in range(0, len(tl), 4):
                        b_chunk(tl[c0:c0 + 4])
                    pipe.flush()
                if not isA:
                    outproj_acc(512)
                P.emit_phase()

        def odd_phase(l):
            i = l // 2
            scale = 1.0 / math.sqrt(96.0)
            with ExitStack() as ps_:
                st["npx"] = 4
                st["px"] = 0
                watt = [sb("watt%d" % k, [128, 2048], BF16, ps_) for k in range(2)]
                win = sb("win", [128, 8 * 416], BF16, ps_)
                winsw = sb("winsw", [128, 8 * 96], BF16, ps_)
                qTt = [sb("qT%d" % k, [128, S], BF16, ps_) for k in range(2)]
                kTt = [sb("kT%d" % k, [128, S], BF16, ps_) for k in range(2)]
                vSt = [sb("vS%d" % k, [128, 16 * 128], BF16, ps_) for k in range(2)]
                NPT = 6
                PT = [sb("PT%d" % k, [128, TG], BF16, ps_) for k in range(NPT)]
                ropeC = sb("ropeC", [128, S], BF16, ps_)
                ropeS = sb("ropeS", [128, S], BF16, ps_)
                for k_ in range(2):
                    P.op("pool", lambda e, k_=k_: e.memset(qTt[k_][:], 0.0), writes=[(("qT", k_), g) for g in range(NTG)])
                for k_ in range(2):
                    P.op("pool", lambda e, k_=k_: e.memset(kTt[k_][:], 0.0),
                         writes=[(("kTn", k_), g) for g in range(NTG)] + [(("kTr", k_), g) for g in range(NTG)])
                accs = [(psX[3], ("px", 3)), (psX[2], ("px", 2))]
                unit = [0]
                cqn = sb("cqn", [128, 2 * S], BF16, ps_)
                ckvn = sb("ckvn", [128, S], BF16, ps_)
                pt_i = [0]
                sbuf_i = [0]
                P.op("sp", lambda e: e.dma_start(out=ropeC[:], in_=ropeM_d[0]), writes=["ropeC"], dma_ch="ropeC")
                P.op("sp", lambda e: e.dma_start(out=ropeS[:], in_=ropeM_d[1]), writes=["ropeS"], dma_ch="ropeS")
                for k in range(2):
                    P.op("dve", lambda e, k=k: e.memset(vSt[k][:], 1.0), writes=[("vS", k, 0), ("vS", k, 1)])
                rmsnorm_x(8 * l, hs, lambda c, g: ("h", c, g))
                wload(win, 0, wview(od_in_d[i], 0, 8, 0, 416), 8, 416, "win")
                wload(winsw, 0, wview(od_insw_d[i], 0, 8, 0, 96), 8, 96, "winsw")
                hfn = lambda k, g: hs(k, g)
                hres = lambda k, g: ("h", k, g)
                for g in range(NTG):
                    pq = [next_px(), next_px()]
                    for c2 in range(2):
                        proj_fm(psX[pq[c2]], win, 128 * c2, 416, 128, 8, hfn, g, "win", hres, ("px", pq[c2]))
                    pss = next_px()
                    for c2 in range(2):
                        si = next_sq()
                        P.op("act", lambda e, si=si, p_=pq[c2]: e.activation(out=sq[si][:], in_=psX[p_][:], func=AF.Square),
                             reads=[("px", pq[c2])], writes=[("sq", si)])
                        P.op("pe", lambda e, si=si, c2=c2, pss=pss: e.matmul(psX[pss][:], lhsT=ones[:], rhs=sq[si][:], start=(c2 == 0), stop=(c2 == 1)),
                             reads=[("sq", si), "ones"], writes=[("px", pss)])
                    tr = next_tmp()
                    P.op("act", lambda e, tr=tr, pss=pss: e.activation(out=tmp[tr][:], in_=psX[pss][:], func=AF.Sqrt, scale=1.0 / 256.0, bias=EPS),
                         reads=[("px", pss)], writes=[("tmp", tr)])
                    P.op("dve", lambda e, tr=tr: e.reciprocal(out=tmp[tr][:], in_=tmp[tr][:]), reads=[("tmp", tr)], writes=[("tmp", tr)])
                    for c2 in range(2):
                        P.op("dve", lambda e, tr=tr, c2=c2, g=g, p_=pq[c2]: e.scalar_tensor_tensor(
                            out=cqn[:, c2 * S + g * TG: c2 * S + (g + 1) * TG], in0=psX[p_][:], scalar=vecs[:, 72 + 2 * i + c2: 73 + 2 * i + c2],
                            in1=tmp[tr][:], op0=ALU.mult, op1=ALU.mult),
                            reads=[("px", pq[c2]), ("tmp", tr), "vecs"], writes=[("cqn", c2, g)])
                    pk = next_px()
                    proj_fm(psX[pk], win, 256, 416, 128, 8, hfn, g, "win", hres, ("px", pk))
                    pss = next_px()
                    si = next_sq()
                    P.op("act", lambda e, si=si, pk=pk: e.activation(out=sq[si][:], in_=psX[pk][:], func=AF.Square),
                         reads=[("px", pk)], writes=[("sq", si)])
                    P.op("pe", lambda e, si=si, pss=pss: e.matmul(psX[pss][:], lhsT=ones[:], rhs=sq[si][:], start=True, stop=True),
                         reads=[("sq", si), "ones"], writes=[("px", pss)])
                    tr = next_tmp()
                    P.op("act", lambda e, tr=tr, pss=pss: e.activation(out=tmp[tr][:], in_=psX[pss][:], func=AF.Sqrt, scale=1.0 / 128.0, bias=EPS),
                         reads=[("px", pss)], writes=[("tmp", tr)])
                    P.op("dve", lambda e, tr=tr: e.reciprocal(out=tmp[tr][:], in_=tmp[tr][:]), reads=[("tmp", tr)], writes=[("tmp", tr)])
                    P.op("dve", lambda e, tr=tr, g=g, pk=pk: e.scalar_tensor_tensor(
                        out=ckvn[:, g * TG:(g + 1) * TG], in0=psX[pk][:], scalar=vecs[:, 76 + i: 77 + i], in1=tmp[tr][:], op0=ALU.mult, op1=ALU.mult),
                        reads=[("px", pk), ("tmp", tr), "vecs"], writes=[("ckvn", g)])
                    pa = next_px()
                    proj_fm(psX[pa], win, 320, 416, 96, 8, hfn, g, "win", hres, ("px", pa))
                    pb = next_px()
                    proj_fm(psX[pb], winsw, 0, 96, 96, 8, hfn, g, "winsw", hres, ("px", pb))
                    t1 = next_tmp()
                    t2 = next_tmp()
                    P.op("dve", lambda e, t1=t1, pa=pa, g=g: e.tensor_tensor(out=tmp[t1][64:96, :], in0=psX[pa][64:96, :], in1=ropeC[64:96, g * TG:(g + 1) * TG], op=ALU.mult),
                         reads=[("px", pa), "ropeC"], writes=[("tmp", t1)])
                    P.op("dve", lambda e, t2=t2, pb=pb, g=g: e.tensor_tensor(out=tmp[t2][64:96, :], in0=psX[pb][64:96, :], in1=ropeS[64:96, g * TG:(g + 1) * TG], op=ALU.mult),
                         reads=[("px", pb), "ropeS"], writes=[("tmp", t2)])
                    for k in range(2):
                        P.op("pool", lambda e, t1=t1, t2=t2, g=g, k=k: e.tensor_tensor(
                            out=kTt[k][64:96, g * TG:(g + 1) * TG], in0=tmp[t1][64:96, :], in1=tmp[t2][64:96, :], op=ALU.add),
                            reads=[("tmp", t1), ("tmp", t2)], writes=[(("kTr", k), g)])

                cfn = lambda k, g: cqn[:, k * S + g * TG: k * S + (g + 1) * TG]
                cres = lambda k, g: ("cqn", k, g)
                st["npx"] = 2
                st["px"] = 0

                def proj_head(h):
                    wb = h % 2
                    w = watt[wb]
                    kT = kTt[wb]
                    vS = vSt[wb]
                    qT = qTt[wb]
                    vc0 = 0 if wb == 0 else 64
                    wload(w, 0, wview(od_uq_d[i], 0, 2, 96 * h, 96), 2, 96, ("wa", wb, 0))
                    wload(w, 256, wview(od_uqsw_d[i], 0, 2, 96 * h, 96), 2, 96, ("wa", wb, 1))
                    wload(w, 512, wview(od_ukv_d[i], 0, 1, 128 * h, 128), 1, 128, ("wa", wb, 2))
                    for g in range(NTG):
                        pa = next_px()
                        proj_fm(psX[pa], w, 0, 96, 128, 2, cfn, g, ("wa", wb, 0), cres, ("px", pa))
                        pb = next_px()
                        proj_fm(psX[pb], w, 256, 96, 128, 2, cfn, g, ("wa", wb, 1), cres, ("px", pb))
                        t1 = next_tmp()
                        t2 = next_tmp()
                        P.op("dve", lambda e, t1=t1, pa=pa, g=g: e.tensor_tensor(out=tmp[t1][0:96, :], in0=psX[pa][0:96, :], in1=ropeC[0:96, g * TG:(g + 1) * TG], op=ALU.mult),
                             reads=[("px", pa), "ropeC"], writes=[("tmp", t1)])
                        P.op("dve", lambda e, t2=t2, pb=pb, g=g: e.tensor_tensor(out=tmp[t2][0:96, :], in0=psX[pb][0:96, :], in1=ropeS[0:96, g * TG:(g + 1) * TG], op=ALU.mult),
                             reads=[("px", pb), "ropeS"], writes=[("tmp", t2)])
                        P.op("pool", lambda e, t1=t1, t2=t2, g=g: e.tensor_tensor(out=qT[0:96, g * TG:(g + 1) * TG], in0=tmp[t1][0:96, :], in1=tmp[t2][0:96, :], op=ALU.add),
                             reads=[("tmp", t1), ("tmp", t2)], writes=[(("qT", wb), g)])
                        pk = next_px()
                        P.op("pe", lambda e, pk=pk, g=g: e.matmul(psX[pk][:, :], lhsT=w[:, 512:640], rhs=ckvn[:, g * TG:(g + 1) * TG], start=True, stop=True),
                             reads=[("wa", wb, 2), ("ckvn", g)], writes=[("px", pk)])
                        P.op("dve", lambda e, pk=pk, g=g: e.tensor_copy(out=kT[0:64, g * TG:(g + 1) * TG], in_=psX[pk][0:64, :]),
                             reads=[("px", pk)], writes=[(("kTn", wb), g)])
                    for t8 in range(2):
                        px = next_px()
                        ps = psX[px]
                        for tt in range(8):
                            tok = (t8 * 8 + tt) * 128
                            P.op("pe", lambda e, tt=tt, tok=tok, ps=ps: e.matmul(
                                ps[:, tt * 64:(tt + 1) * 64], lhsT=ckvn[:, tok:tok + 128], rhs=w[:, 576:640], start=True, stop=True),
                                reads=[("wa", wb, 2), ("ckvn", tok // TG)], writes=[("px", px)])
                        dstv = vS[:, t8 * 1024:(t8 + 1) * 1024].rearrange("p (t c) -> p t c", c=128)[:, :, vc0:vc0 + 64]
                        srcv = ps[:, :].rearrange("p (t c) -> p t c", c=64)
                        P.op("dve", lambda e, dstv=dstv, srcv=srcv: e.tensor_copy(out=dstv, in_=srcv),
                             reads=[("px", px)], writes=[("vS", wb, t8)])

                def attn_head(h):
                    wb = h % 2
                    kT = kTt[wb]
                    vS = vSt[wb]
                    qT = qTt[wb]
                    ob = (h // 2) % 2
                    orow = 64 * (h % 2)
                    drow = 64 - orow
                    pipe = Pipe(3)

                    def fin_m(g, aO, nO):
                        tr = next_tmp()
                        P.op("dve", lambda e: e.reciprocal(out=tmp[tr][drow:drow + 64, :], in_=aO[drow:drow + 64, :]),
                             reads=[nO], writes=[("tmp", tr)])
                        P.op("dve", lambda e: e.tensor_tensor(
                            out=hT[orow:orow + 64, (h // 2) * S + g * TG:(h // 2) * S + (g + 1) * TG], in0=aO[orow:orow + 64, :], in1=tmp[tr][drow:drow + 64, :], op=ALU.mult),
                            reads=[nO, ("tmp", tr)], writes=[("h", h // 2, g)])

                    def m_step(g, kt, aO, nO):
                        sbi = sbuf_i[0]
                        sbuf_i[0] = (sbi + 1) % 4
                        pti = pt_i[0]
                        pt_i[0] = (pti + 1) % NPT
                        sreg = psS[:, sbi * TG:(sbi + 1) * TG]

                        def qk():
                            P.op("pe", lambda e: e.matmul(sreg, lhsT=kT[:, kt * 128:(kt + 1) * 128], rhs=qT[:, g * TG:(g + 1) * TG], start=True, stop=True),
                                 reads=[(("kTn", wb), kt // 4), (("kTr", wb), kt // 4), (("qT", wb), g)], writes=[("psS", sbi)])
                            P.op("act", lambda e: e.activation(out=PT[pti][:, 0:TG], in_=sreg, func=AF.Exp, scale=scale),
                                 reads=[("psS", sbi)], writes=[("PT", pti)])

                        def pv():
                            P.op("pe", lambda e: e.matmul(aO, lhsT=vS[:, kt * 128:(kt + 1) * 128], rhs=PT[pti][:, 0:TG], start=(kt == 0), stop=(kt == 15)),
                                 reads=[("PT", pti), ("vS", wb, kt // 8)], writes=[nO])
                            if kt == 15:
                                fin_m(g, aO, nO)
                        pipe.step(qk, pv)

                    for g in range(NTG):
                        aO, nO = accs[unit[0] % 2]
                        unit[0] += 1
                        for kt in range(16):
                            m_step(g, kt, aO, nO)
                    pipe.flush()

                proj_head(0)
                for h in range(16):
                    if h + 1 < 16:
                        proj_head(h + 1)
                    attn_head(h)
                for c in range(NCH):
                    wb = c % 2
                    wload(watt[wb], 0, wview(od_out_d[i], 0, 8, c * 128, 128), 8, 128, [("wa", wb, 0), ("wa", wb, 1), ("wa", wb, 2)])
                    for g in range(NTG):
                        px = next_px()
                        ps = psX[px]
                        for k in range(NCH):
                            P.op("pe", lambda e, k=k, g=g, ps=ps, wb=wb: e.matmul(ps[:], lhsT=watt[wb][:, k * 128:(k + 1) * 128], rhs=hs(k, g),
                                                                              start=(k == 0), stop=(k == NCH - 1)),
                                 reads=[("wa", wb, 0), ("h", k, g)], writes=[("px", px)])
                        P.op("dve", lambda e, c=c, g=g, ps=ps: e.tensor_tensor(out=xs(c, g), in0=ps[:], in1=xs(c, g), op=ALU.add),
                             reads=[("px", px), ("x", c, g)], writes=[("x", c, g)])
                P.emit_phase()

        for s_ in range(nseq):
            for c in range(NCH):
                P.op("sp", lambda e, c=c, s_=s_: e.dma_start(out=xT[:, c * S:(c + 1) * S], in_=xT_d[s_][c * 128:(c + 1) * 128, :]),
                     writes=[("x", c, g) for g in range(NTG)], dma_ch=("xin", c))
            P.emit_phase()
            for l in layers:
                if do_mix:
                    if l % 2 == 0:
                        first = True
                        for part in parts:
                            even_phase(l, part, first)
                            first = False
                    else:
                        odd_phase(l)
                if do_mlp:
                    mlp_phase(l)
            with ExitStack() as ps_:
                st["npx"] = 2
                ob = [sb("outb%d" % k, [128, S], F32, ps_) for k in range(2)]
                rs = [sb("rs%d" % k, [128, TG], F32, ps_) for k in range(NTG)]
                for g in range(NTG):
                    px = next_px()
                    ps = psX[px]
                    for c in range(NCH):
                        P.op("act", lambda e, c=c, g=g: e.activation(out=hs(c, g), in_=xs(c, g), func=AF.Square),
                             reads=[("x", c, g)], writes=[("h", c, g)])
                        P.op("pe", lambda e, c=c, g=g, ps=ps: e.matmul(ps[:], lhsT=ones[:], rhs=hs(c, g), start=(c == 0), stop=(c == NCH - 1)),
                             reads=[("h", c, g), "ones"], writes=[("px", px)])
                    P.op("act", lambda e, g=g, ps=ps: e.activation(out=rs[g][:], in_=ps[:], func=AF.Sqrt, scale=1.0 / D, bias=EPS),
                         reads=[("px", px)], writes=[("rs", g)])
                    P.op("dve", lambda e, g=g: e.reciprocal(out=rs[g][:], in_=rs[g][:]), reads=[("rs", g)], writes=[("rs", g)])
                for c in range(NCH):
                    k = c % 2
                    for g in range(NTG):
                        P.op("dve", lambda e, c=c, g=g, k=k: e.scalar_tensor_tensor(
                            out=ob[k][:, g * TG:(g + 1) * TG], in0=xs(c, g), scalar=vecs[:, 64 + c:65 + c], in1=rs[g][:], op0=ALU.mult, op1=ALU.mult),
                            reads=[("x", c, g), ("rs", g), "vecs"], writes=[("ob", k)])
                    P.op("sp", lambda e, c=c, k=k, s_=s_: e.dma_start(out=out_d[s_][c * 128:(c + 1) * 128, :], in_=ob[k][:]),
                         reads=[("ob", k)], dma_ch=("out", k))
                P.emit_phase(final=(s_ == nseq - 1))
    return nc


def prep_shared(inp):
    f = lambda a: np.ascontiguousarray(np.asarray(a, dtype=np.float32))
    vecs = np.zeros((128, 80), np.float32)
    for l in range(4):
        vecs[:, 8 * l:8 * l + 8] = f(inp["ln_mix_g"])[l].reshape(8, 128).T
        vecs[:, 32 + 8 * l:32 + 8 * l + 8] = f(inp["ln_mlp_g"])[l].reshape(8, 128).T
    vecs[:, 64:72] = f(inp["ln_f_g"]).reshape(8, 128).T
    for i in range(2):
        vecs[:, 72 + 2 * i:74 + 2 * i] = f(inp["od_q_norm_g"])[i].reshape(2, 128).T
        vecs[:, 76 + i] = f(inp["od_kv_norm_g"])[i]
        vecs[:, 78 + i] = f(inp["ev_subln_g"])[i]
    lamv = np.zeros((128, 512), np.float32)
    for i in range(2):
        for j, nm in enumerate(("ev_lambda_q1", "ev_lambda_k1", "ev_lambda_q2", "ev_lambda_k2")):
            lamv[:, i * 256 + j * 64: i * 256 + (j + 1) * 64] = f(inp[nm])[i][None, :]
    ev_in = f(inp["ev_w_in"])
    permA = _swap_perm(1024, 64, 0, 8)
    ev_sw = np.ascontiguousarray(ev_in[:, :, :1024][:, :, permA])
    od_in = f(inp["od_w_in"])
    perm_kr = _swap_perm(32, 32, 0, 16)
    od_in_sw = np.ascontiguousarray(od_in[:, :, 320:416].copy())
    od_in_sw[:, :, 64:96] = od_in[:, :, 384:416][:, :, perm_kr]
    od_uq = f(inp["od_w_uq"])
    perm_q = _swap_perm(1536, 96, 64, 16)
    od_uq_sw = np.ascontiguousarray(od_uq[:, :, perm_q])
    CA, SA = _rope_tables_A()
    CM, SM = _rope_tables_M()
    rpb = f(inp["ev_rpb"])
    gb = rpb[:, :, _RIDX, _CIDX]
    gb = np.ascontiguousarray(gb.transpose(0, 3, 1, 2, 4)).reshape(2, 128, 8 * _NG * 128)
    gmask = np.where(_VALID, 0.0, MASKV).astype(np.float32)
    gmask = np.ascontiguousarray(gmask.transpose(1, 0, 2)).reshape(128, _NG * 128).astype(ml_dtypes.bfloat16)
    return {
        "vecs": vecs, "lamv": lamv,
        "w_up": f(inp["w_up"]), "w_down": f(inp["w_down"]),
        "ev_w_in": ev_in, "ev_w_sw": ev_sw, "ev_w_out": f(inp["ev_w_out"]),
        "od_w_in": od_in, "od_w_in_sw": od_in_sw, "od_w_uq": od_uq, "od_w_uq_sw": od_uq_sw,
        "od_w_ukv": f(inp["od_w_ukv"]), "od_w_out": f(inp["od_w_out"]),
        "ropeA": np.stack([CA, SA]).astype(ml_dtypes.bfloat16), "ropeM": np.stack([CM, SM]).astype(ml_dtypes.bfloat16),
        "gb": gb, "gmask": gmask,
        "ident": np.eye(128, dtype=np.float32).astype(ml_dtypes.bfloat16),
    }


_NC_CACHE = {}


def kernel(**inputs):
    x = np.asarray(inputs["x"], dtype=np.float32)
    B = x.shape[0]
    nseq = B // NCORES
    shared = prep_shared(inputs)
    key = nseq
    if key not in _NC_CACHE:
        _NC_CACHE[key] = build(nseq=nseq)
    nc = _NC_CACHE[key]
    in_maps = []
    for c in range(NCORES):
        xs_ = x[c * nseq:(c + 1) * nseq]
        m = dict(shared)
        m["xT"] = np.ascontiguousarray(xs_.transpose(0, 2, 1))
        in_maps.append(m)
    res = run_bass_kernel_spmd(nc, in_maps, core_ids=list(range(NCORES)))
    out = np.empty((B, S, D), np.float32)
    for c in range(NCORES):
        o = np.asarray(res.results[c]["outT"])
        out[c * nseq:(c + 1) * nseq] = o.transpose(0, 2, 1)
    return out
```

```python
import math
from contextlib import ExitStack

import numpy as np
import ml_dtypes
import concourse.bass as bass
import concourse.mybir as mybir
from concourse.bass_utils import run_bass_kernel_spmd

F32 = mybir.dt.float32
BF16 = mybir.dt.bfloat16
AF = mybir.ActivationFunctionType
ALU = mybir.AluOpType
AX = mybir.AxisListType

S = 2048
D = 1024
NCH = 8
NTG = 4
TG = 512
EPS = 1e-5
NCORES = 8
ENGS = ("pe", "act", "dve", "pool", "sp")


class Op:
    __slots__ = ("eng", "fn", "deps", "dma_ch", "sig", "idx", "has_consumer", "waits")

    def __init__(self, eng, fn, dma_ch=None):
        self.eng = eng
        self.fn = fn
        self.deps = set()
        self.dma_ch = dma_ch
        self.sig = None
        self.has_consumer = False
        self.waits = []


class Prog:
    def __init__(self, nc, es):
        self.nc = nc
        self.es = es
        self.sems = {}
        self.cnt = {e: 0 for e in ENGS}
        self.dcnt = {}
        self.reset_phase()
        self.nphase = 0

    def reset_phase(self):
        self.ops = []
        self.last_writer = {}
        self.readers = {}

    def sem(self, key):
        s = self.sems.get(key)
        if s is None:
            name = "s_" + "_".join(str(k) for k in key)
            s = self.es.enter_context(self.nc.semaphore(name))
            self.sems[key] = s
        return s

    def op(self, eng, fn, reads=(), writes=(), dma_ch=None):
        o = Op(eng, fn, dma_ch)
        o.idx = len(self.ops)
        deps = set()
        for r in reads:
            w = self.last_writer.get(r)
            if w is not None:
                deps.add(w)
        for r in writes:
            w = self.last_writer.get(r)
            if w is not None:
                deps.add(w)
            for rd in self.readers.get(r, ()):
                deps.add(rd)
        o.deps = deps
        self.ops.append(o)
        for r in reads:
            self.readers.setdefault(r, []).append(o.idx)
        for r in writes:
            self.last_writer[r] = o.idx
            self.readers[r] = []
        return o

    def emit_phase(self, final=False):
        ops = self.ops
        if not ops:
            return
        nc = self.nc
        for o in ops:
            best = {}
            for d in o.deps:
                p = ops[d]
                if p.dma_ch is not None:
                    key = ("dma", p.dma_ch)
                elif p.eng != o.eng or o.eng in ("act", "dve", "pool"):
                    key = ("eng", p.eng)
                else:
                    continue
                if d > best.get(key, -1):
                    best[key] = d
            keep = set(best.values())
            o.deps = keep
            for d in keep:
                ops[d].has_consumer = True
        last = {}
        for o in ops:
            if o.dma_ch is None:
                last[o.eng] = o
        for o in last.values():
            o.has_consumer = True
        prev_cnt = dict(self.cnt)
        prev_dcnt = dict(self.dcnt)
        for o in ops:
            if o.dma_ch is not None:
                self.dcnt[o.dma_ch] = self.dcnt.get(o.dma_ch, 0) + 16
                o.sig = ("dma", o.dma_ch, self.dcnt[o.dma_ch])
            elif o.has_consumer:
                self.cnt[o.eng] += 1
                o.sig = ("eng", o.eng, self.cnt[o.eng])
        waited = {e: {} for e in ENGS}
        for e in ENGS:
            for e2 in ENGS:
                if prev_cnt[e2] > 0:
                    waited[e][("eng", e2)] = prev_cnt[e2]
            for ch, v in prev_dcnt.items():
                waited[e][("dma", ch)] = v
        for o in ops:
            need = {}
            for d in o.deps:
                s = ops[d].sig
                key = (s[0], s[1])
                if s[2] > need.get(key, 0):
                    need[key] = s[2]
            for key, v in need.items():
                if waited[o.eng].get(key, 0) >= v:
                    continue
                waited[o.eng][key] = v
                o.waits.append((key, v))
        by_eng = {e: [o for o in ops if o.eng == e] for e in ENGS}
        sems = self.sem
        final_d = dict(self.dcnt)
        final_c = dict(self.cnt)

        def run(eng_name, eng):
            for e2 in ENGS:
                if e2 != eng_name and prev_cnt[e2] > 0:
                    eng.wait_ge(sems(("eng", e2)), prev_cnt[e2])
            for ch, v in prev_dcnt.items():
                eng.wait_ge(sems(("dma", ch)), v)
            for o in by_eng[eng_name]:
                for key, v in o.waits:
                    eng.wait_ge(sems(key), v)
                ins = o.fn(eng)
                if o.dma_ch is not None:
                    ins.then_inc(sems(("dma", o.dma_ch)), 16)
                elif o.sig is not None:
                    ins.then_inc(sems(("eng", eng_name)), 1)
            if final:
                for ch, v in final_d.items():
                    eng.wait_ge(sems(("dma", ch)), v)
                for e2 in ENGS:
                    if e2 != eng_name and final_c[e2] > 0:
                        eng.wait_ge(sems(("eng", e2)), final_c[e2])

        for e in ENGS:
            sems(("eng", e))
        for ch in self.dcnt:
            sems(("dma", ch))
        with nc.Block() as block:
            @block.sync
            def _(e):
                run("sp", e)

            @block.tensor
            def _(e):
                run("pe", e)

            @block.scalar
            def _(e):
                run("act", e)

            @block.vector
            def _(e):
                run("dve", e)

            @block.gpsimd
            def _(e):
                run("pool", e)
        self.nphase += 1
        self.reset_phase()


class Pipe:
    def __init__(self, look):
        self.look = look
        self.q = []

    def step(self, qk, pv):
        qk()
        self.q.append(pv)
        if len(self.q) > self.look:
            self.q.pop(0)()

    def flush(self):
        while self.q:
            self.q.pop(0)()


def _rot_tables(rot_dim, theta=500000.0):
    pos = np.arange(S, dtype=np.float32)
    inv = (np.float32(theta) ** (-np.arange(0, rot_dim, 2, dtype=np.float32) / np.float32(rot_dim))).astype(np.float32)
    ang = (pos[:, None] * inv[None, :]).astype(np.float32)
    return np.cos(ang).astype(np.float32), np.sin(ang).astype(np.float32)


def _rope_tables_A():
    cos, sin = _rot_tables(16)
    C = np.ones((128, S), np.float32)
    Sg = np.zeros((128, S), np.float32)
    for base in (0, 64):
        for d in range(8):
            C[base + d] = cos[:, d]
            Sg[base + d] = -sin[:, d]
            C[base + 8 + d] = cos[:, d]
            Sg[base + 8 + d] = sin[:, d]
    return C, Sg


def _rope_tables_M():
    cos, sin = _rot_tables(32)
    C = np.ones((128, S), np.float32)
    Sg = np.zeros((128, S), np.float32)
    for d in range(16):
        C[64 + d] = cos[:, d]
        Sg[64 + d] = -sin[:, d]
        C[64 + 16 + d] = cos[:, d]
        Sg[64 + 16 + d] = sin[:, d]
    return C, Sg


def _swap_perm(n, head, rot_off, half):
    idx = np.arange(n)
    for h0 in range(0, n, head):
        for d in range(half):
            idx[h0 + rot_off + d] = h0 + rot_off + half + d
            idx[h0 + rot_off + half + d] = h0 + rot_off + d
    return idx


def _nbr_plan():
    rows, W, wr, wc = 32, 64, 8, 16
    gids = {}
    plan = []
    for m in range(16):
        need = {}
        for b in range(2):
            qr = 2 * m + b
            r0 = min(max(qr - wr // 2, 0), rows - wr)
            for kr in range(r0, r0 + wr):
                t, a = kr // 2, kr % 2
                need.setdefault(t, set()).add((a, b))
        lst = []
        for t in sorted(need):
            key = (2 * t - 2 * m, tuple(sorted(need[t])))
            if key not in gids:
                gids[key] = len(gids)
            lst.append((t, gids[key]))
        plan.append(lst)
    ng = len(gids)
    ridx = np.zeros((ng, 128, 128), np.int64)
    cidx = np.zeros((ng, 128, 128), np.int64)
    valid = np.zeros((ng, 128, 128), bool)
    c = np.arange(W)
    cs = np.clip(c - wc // 2, 0, W - wc)
    colmask = (c[None, :] >= cs[:, None]) & (c[None, :] < cs[:, None] + wc)
    for (delta, pat), g in gids.items():
        for a in range(2):
            for b in range(2):
                dr = delta + a - b + 7
                ok = (a, b) in pat
                for kc in range(W):
                    k = a * 64 + kc
                    q = b * 64 + c
                    ridx[g, k, q] = min(max(dr, 0), 14)
                    cidx[g, k, q] = np.clip(kc - c, -15, 15) + 15
                    valid[g, k, q] = ok & colmask[c, kc]
    return plan, ng, ridx, cidx, valid


_PLAN, _NG, _RIDX, _CIDX, _VALID = _nbr_plan()
MASKV = -30000.0


def build(nseq=4, layers=(0, 1, 2, 3), do_mlp=True, do_mix=True, parts="AB", dbg=False):
    nc = bass.Bass("TRN2", target_bir_lowering=False)
    dt = nc.dram_tensor
    xT_d = dt("xT", [nseq, D, S], F32, kind="ExternalInput").ap()
    out_d = dt("outT", [nseq, D, S], F32, kind="ExternalOutput").ap()
    vecs_d = dt("vecs", [128, 80], F32, kind="ExternalInput").ap()
    lam_d = dt("lamv", [128, 2 * 4 * 64], F32, kind="ExternalInput").ap()
    w_up_d = dt("w_up", [4, D, 4096], F32, kind="ExternalInput").ap()
    w_dn_d = dt("w_down", [4, 4096, D], F32, kind="ExternalInput").ap()
    ev_in_d = dt("ev_w_in", [2, D, 3072], F32, kind="ExternalInput").ap()
    ev_sw_d = dt("ev_w_sw", [2, D, 1024], F32, kind="ExternalInput").ap()
    ev_out_d = dt("ev_w_out", [2, D, D], F32, kind="ExternalInput").ap()
    od_in_d = dt("od_w_in", [2, D, 416], F32, kind="ExternalInput").ap()
    od_insw_d = dt("od_w_in_sw", [2, D, 96], F32, kind="ExternalInput").ap()
    od_uq_d = dt("od_w_uq", [2, 256, 1536], F32, kind="ExternalInput").ap()
    od_uqsw_d = dt("od_w_uq_sw", [2, 256, 1536], F32, kind="ExternalInput").ap()
    od_ukv_d = dt("od_w_ukv", [2, 128, 2048], F32, kind="ExternalInput").ap()
    od_out_d = dt("od_w_out", [2, D, D], F32, kind="ExternalInput").ap()
    ropeA_d = dt("ropeA", [2, 128, S], BF16, kind="ExternalInput").ap()
    ropeM_d = dt("ropeM", [2, 128, S], BF16, kind="ExternalInput").ap()
    gb_d = dt("gb", [2, 128, 8 * _NG * 128], F32, kind="ExternalInput").ap()
    gmask_d = dt("gmask", [128, _NG * 128], BF16, kind="ExternalInput").ap()
    ident_d = dt("ident", [128, 128], BF16, kind="ExternalInput").ap()

    if dbg:
        dbg_d = {nm: dt("dbg_" + nm, [128, S], BF16, kind="ExternalOutput").ap() for nm in ("q", "k", "v", "o")}
    es = ExitStack()
    with es:
        uniq = [0]

        def sb(name, shape, dty, stack=es):
            uniq[0] += 1
            return stack.enter_context(nc.sbuf_tensor("%s_u%d" % (name, uniq[0]), shape, dty))

        P = Prog(nc, es)
        xT = sb("xT_s", [128, NCH * S], F32)
        hT = sb("hT_s", [128, NCH * S], BF16)
        vecs = sb("vecs_s", [128, 80], F32)
        lams = sb("lams", [128, 16], F32)
        ones = sb("ones", [128, 128], BF16)
        ident = sb("ident_s", [128, 128], BF16)
        NSTG = 2
        STGW = 1536
        stage = [sb("stage%d" % i, [128, STGW], F32) for i in range(NSTG)]
        NTMP = 5
        tmp = [sb("tmp%d" % i, [128, TG], F32) for i in range(NTMP)]
        sq = [sb("sq%d" % i, [128, TG], BF16) for i in range(2)]
        psS = es.enter_context(nc.psum_tensor("psS", [128, 2048], F32))
        psX0 = es.enter_context(nc.psum_tensor("psX0", [128, TG], F32))
        psX1 = es.enter_context(nc.psum_tensor("psX1", [128, TG], F32))
        psO = es.enter_context(nc.psum_tensor("psO", [128, TG], F32))
        psD = es.enter_context(nc.psum_tensor("psD", [128, TG], F32))
        psX = [psX0[:, :], psX1[:, :], psD[:, :], psO[:, :]] + [psS[:, k * TG:(k + 1) * TG] for k in range(4)]

        st = {"stg": 0, "tmp": 0, "sq": 0, "px": 0, "npx": 2}

        def xs(c, g):
            return xT[:, c * S + g * TG: c * S + (g + 1) * TG]

        def hs(c, g):
            return hT[:, c * S + g * TG: c * S + (g + 1) * TG]

        def next_tmp():
            i = st["tmp"]
            st["tmp"] = (i + 1) % NTMP
            return i

        def next_px():
            i = st["px"] % st["npx"]
            st["px"] = (i + 1) % st["npx"]
            return i

        def next_sq():
            i = st["sq"]
            st["sq"] = (i + 1) % 2
            return i

        def wload(dst, dst_off, src3, kc, n, dst_res):
            per = max(1, STGW // n)
            k0 = 0
            while k0 < kc:
                k1 = min(kc, k0 + per)
                tot = (k1 - k0) * n
                slot = st["stg"]
                st["stg"] = (slot + 1) % NSTG
                stg = stage[slot]
                src = src3[:, k0:k1, :]
                dview = stg[:, 0:tot].rearrange("p (k n) -> p k n", k=k1 - k0)
                P.op("sp", lambda e, dview=dview, src=src: e.dma_start(out=dview, in_=src),
                     writes=[("stg", slot)], dma_ch=("stg", slot))
                o0 = dst_off + k0 * n
                P.op("pool", lambda e, stg=stg, o0=o0, tot=tot: e.tensor_copy(out=dst[:, o0:o0 + tot], in_=stg[:, 0:tot]),
                     reads=[("stg", slot)], writes=(dst_res if isinstance(dst_res, list) else [dst_res]))
                k0 = k1

        def wview(w2d, r0, nk, c0, n):
            return w2d[r0:r0 + nk * 128, c0:c0 + n].rearrange("(k p) n -> p k n", p=128)

        def rmsnorm_x(gcol0, dst_fn, dst_res_fn, out_f32=False):
            for g in range(NTG):
                px = next_px()
                ps = psX[px]
                for c in range(NCH):
                    P.op("act", lambda e, c=c, g=g: e.activation(out=hs(c, g), in_=xs(c, g), func=AF.Square),
                         reads=[("x", c, g)], writes=[("h", c, g)])
                    P.op("pe", lambda e, c=c, g=g, ps=ps: e.matmul(ps[:], lhsT=ones[:], rhs=hs(c, g), start=(c == 0), stop=(c == NCH - 1)),
                         reads=[("h", c, g), "ones"], writes=[("px", px)])
                ti = next_tmp()
                P.op("act", lambda e, ti=ti, ps=ps: e.activation(out=tmp[ti][:], in_=ps[:], func=AF.Sqrt, scale=1.0 / D, bias=EPS),
                     reads=[("px", px)], writes=[("tmp", ti)])
                P.op("dve", lambda e, ti=ti: e.reciprocal(out=tmp[ti][:], in_=tmp[ti][:]),
                     reads=[("tmp", ti)], writes=[("tmp", ti)])
                for c in range(NCH):
                    P.op("dve", lambda e, ti=ti, c=c, g=g: e.scalar_tensor_tensor(
                        out=dst_fn(c, g), in0=xs(c, g), scalar=vecs[:, gcol0 + c:gcol0 + c + 1], in1=tmp[ti][:],
                        op0=ALU.mult, op1=ALU.mult),
                        reads=[("x", c, g), ("tmp", ti), "vecs"], writes=[dst_res_fn(c, g)])

        init_es = ExitStack()
        lamt = sb("lamt", [128, 512], F32, init_es)
        lamw = sb("lamw", [128, 512], F32, init_es)
        P.op("sp", lambda e: e.dma_start(out=vecs[:], in_=vecs_d[:, :]), writes=["vecs"], dma_ch="vecs")
        P.op("sp", lambda e: e.dma_start(out=lamt[:], in_=lam_d[:, :]), writes=["lamt"], dma_ch="lamt")
        P.op("sp", lambda e: e.dma_start(out=ident[:], in_=ident_d[:, :]), writes=["ident"], dma_ch="ident")
        P.op("dve", lambda e: e.memset(ones[:], 1.0), writes=["ones"])
        for i in range(2):
            b = i * 256
            for j in range(2):
                P.op("dve", lambda e, b=b, j=j: e.tensor_tensor(out=lamw[:, b + 64 * j: b + 64 * j + 64], in0=lamt[:, b + 128 * j: b + 128 * j + 64],
                                                             in1=lamt[:, b + 128 * j + 64: b + 128 * j + 128], op=ALU.mult),
                     reads=["lamt"], writes=[("lamw", i, j)])
                P.op("dve", lambda e, b=b, j=j, i=i: e.reduce_sum(out=lams[:, 4 * i + j: 4 * i + j + 1], in_=lamw[:, b + 64 * j: b + 64 * j + 64], axis=AX.X),
                     reads=[("lamw", i, j)], writes=[("lams", i, j)])
                P.op("act", lambda e, j=j, i=i: e.activation(out=lams[:, 4 * i + j: 4 * i + j + 1], in_=lams[:, 4 * i + j: 4 * i + j + 1], func=AF.Exp),
                     reads=[("lams", i, j)], writes=[("lams", i, j)])
            lam_init = 0.8 - 0.6 * math.exp(-0.3 * (2 * i))
            P.op("dve", lambda e, i=i, lam_init=lam_init: e.scalar_tensor_tensor(
                out=lams[:, 4 * i + 2: 4 * i + 3], in0=lams[:, 4 * i + 1: 4 * i + 2], scalar=-lam_init, in1=lams[:, 4 * i: 4 * i + 1],
                op0=ALU.add, op1=ALU.subtract),
                reads=[("lams", i, 0), ("lams", i, 1)], writes=[("lams", i, 2)])
        P.emit_phase()
        init_es.close()

        def mlp_phase(l):
            with ExitStack() as ps_:
                st["npx"] = 8
                wm = [sb("wmlp%d" % i, [128, 8192], BF16, ps_) for i in range(2)]
                uT = sb("uT", [128, 4 * S], BF16, ps_)
                rmsnorm_x(32 + 8 * l, hs, lambda c, g: ("h", c, g))
                for part in range(8):
                    wb = part % 2
                    w = wm[wb]
                    wload(w, 0, wview(w_up_d[l], 0, 8, part * 512, 512), 8, 512, ("wm", wb, 0))
                    wload(w, 4096, wview(w_dn_d[l], part * 512, 4, 0, 1024), 4, 1024, ("wm", wb, 1))
                    for g in range(NTG):
                        for j in range(4):
                            px = next_px()
                            ps = psX[px]
                            for c in range(NCH):
                                P.op("pe", lambda e, w=w, c=c, j=j, g=g, ps=ps: e.matmul(
                                    ps[:], lhsT=w[:, c * 512 + j * 128: c * 512 + j * 128 + 128], rhs=hs(c, g),
                                    start=(c == 0), stop=(c == NCH - 1)),
                                    reads=[("wm", wb, 0), ("h", c, g)], writes=[("px", px)])
                            ti = next_tmp()
                            P.op("act", lambda e, ti=ti, ps=ps: e.activation(out=tmp[ti][:], in_=ps[:], func=AF.Relu),
                                 reads=[("px", px)], writes=[("tmp", ti)])
                            P.op("pool", lambda e, ti=ti, j=j, g=g: e.tensor_tensor(
                                out=uT[:, j * S + g * TG: j * S + (g + 1) * TG], in0=tmp[ti][:], in1=tmp[ti][:], op=ALU.mult),
                                reads=[("tmp", ti)], writes=[("u", j, g)])
                    for g in range(NTG):
                        for c in range(NCH):
                            px = next_px()
                            ps = psX[px]
                            for j in range(4):
                                P.op("pe", lambda e, w=w, c=c, j=j, g=g, ps=ps: e.matmul(
                                    ps[:], lhsT=w[:, 4096 + j * 1024 + c * 128: 4096 + j * 1024 + c * 128 + 128],
                                    rhs=uT[:, j * S + g * TG: j * S + (g + 1) * TG], start=(j == 0), stop=(j == 3)),
                                    reads=[("wm", wb, 1), ("u", j, g)], writes=[("px", px)])
                            P.op("dve", lambda e, c=c, g=g, ps=ps: e.tensor_tensor(out=xs(c, g), in0=ps[:], in1=xs(c, g), op=ALU.add),
                                 reads=[("px", px), ("x", c, g)], writes=[("x", c, g)])
                P.emit_phase()

        def proj_fm(ps, wt, woff, wstride, m, nk, rhs_fn, g, wres, rres_fn, pxres, prow0=0):
            for k in range(nk):
                P.op("pe", lambda e, k=k: e.matmul(ps[prow0:prow0 + m, :], lhsT=wt[:, woff + k * wstride: woff + k * wstride + m],
                                                  rhs=rhs_fn(k, g), start=(k == 0), stop=(k == nk - 1)),
                     reads=[wres, rres_fn(k, g)], writes=[pxres])

        def outproj_chunk(wout_d2, kc, oH, ores, watt, wb, wo=5120, load=True, compute=True):
            if load:
                wload(watt[wb], wo, wview(wout_d2, kc * 128, 1, 0, 1024), 1, 1024, ("wa", wb, 5))
            if not compute:
                return
            for g in range(NTG):
                for c in range(NCH):
                    px = next_px()
                    ps = psX[px]
                    P.op("pe", lambda e, c=c, g=g, ps=ps: e.matmul(ps[:], lhsT=watt[wb][:, wo + c * 128: wo + c * 128 + 128],
                                                                   rhs=oH[:, g * TG:(g + 1) * TG], start=True, stop=True),
                         reads=[("wa", wb, 5), (ores, g)], writes=[("px", px)])
                    P.op("dve", lambda e, c=c, g=g, ps=ps: e.tensor_tensor(out=xs(c, g), in0=ps[:], in1=xs(c, g), op=ALU.add),
                         reads=[("px", px), ("x", c, g)], writes=[("x", c, g)])

        def even_phase(l, part, do_norm):
            i = l // 2
            lam_init = 0.8 - 0.6 * math.exp(-0.3 * l)
            isA = part == "A"
            with ExitStack() as ps_:
                st["npx"] = 2
                st["px"] = 0
                watt = [sb("watt%d" % k, [128, 6144], BF16, ps_) for k in range(2)]
                qTm = [sb("qTm%d" % k, [128, S], BF16, ps_) for k in range(2)]
                kT = sb("kT", [128, S], BF16, ps_)
                vS = sb("vS", [128, 16 * 128], BF16, ps_)
                oH = [sb("oH%d" % k, [128, S], BF16, ps_) for k in range(4)]
                NPT = 6
                PT = [sb("PT%d" % k, [128, 512], BF16, ps_) for k in range(NPT)]
                unit_b = [0]
                unit_a = [0]
                for k_ in range(2):
                    P.op("pool", lambda e, k_=k_: e.memset(qTm[k_][:], 0.0), writes=[("qT", k_, g) for g in range(NTG)])
                pt_i = [0]
                sbuf_i = [0]
                if isA:
                    ropeC = sb("ropeC", [128, S], BF16, ps_)
                    ropeS = sb("ropeS", [128, S], BF16, ps_)
                    NAT = 6
                    atmp = [sb("atmp%d" % k, [128, TG], F32, ps_) for k in range(NAT)]
                    at_i = [0]
                    P.op("sp", lambda e: e.dma_start(out=ropeC[:], in_=ropeA_d[0]), writes=["ropeC"], dma_ch="ropeC")
                    P.op("sp", lambda e: e.dma_start(out=ropeS[:], in_=ropeA_d[1]), writes=["ropeS"], dma_ch="ropeS")
                else:
                    G = sb("G", [128, 8 * _NG * 128], BF16, ps_)
                    gmask = sb("gmask", [128, _NG * 128], BF16, ps_)
                    P.op("sp", lambda e: e.dma_start(out=gmask[:], in_=gmask_d[:, :]), writes=["gmask"], dma_ch="gmask")
                    GW = _NG * 128

                    def build_G():
                        for h in range(8):
                            slot = st["stg"]
                            st["stg"] = (slot + 1) % NSTG
                            stg = stage[slot]
                            P.op("sp", lambda e, stg=stg, h=h: e.dma_start(out=stg[:, 0:GW], in_=gb_d[i][:, h * GW:(h + 1) * GW]),
                                 writes=[("stg", slot)], dma_ch=("stg", slot))
                            P.op("dve", lambda e, stg=stg, h=h: e.scalar_tensor_tensor(
                                out=G[:, h * GW:(h + 1) * GW], in0=stg[:, 0:GW], scalar=8.0, in1=gmask[:], op0=ALU.mult, op1=ALU.add),
                                reads=[("stg", slot), "gmask"], writes=[("G", h)])

                if do_norm:
                    rmsnorm_x(8 * l, hs, lambda c, g: ("h", c, g))

                def rope_evac(ps_a, ps_b, pxa, pxb, dst, g, dres, rows=128):
                    t1 = next_tmp()
                    t2 = next_tmp()
                    P.op("dve", lambda e: e.tensor_tensor(out=tmp[t1][0:rows, :], in0=ps_a[0:rows, :], in1=ropeC[0:rows, g * TG:(g + 1) * TG], op=ALU.mult),
                         reads=[("px", pxa), "ropeC"], writes=[("tmp", t1)])
                    P.op("dve", lambda e: e.tensor_tensor(out=tmp[t2][0:rows, :], in0=ps_b[0:rows, :], in1=ropeS[0:rows, g * TG:(g + 1) * TG], op=ALU.mult),
                         reads=[("px", pxb), "ropeS"], writes=[("tmp", t2)])
                    if dst is None:
                        for hf in range(2):
                            P.op("pool", lambda e, hf=hf: e.tensor_tensor(out=qTm[hf][64 * hf:64 * hf + 64, g * TG:(g + 1) * TG], in0=tmp[t1][64 * hf:64 * hf + 64, :],
                                                                      in1=tmp[t2][64 * hf:64 * hf + 64, :], op=ALU.add),
                                 reads=[("tmp", t1), ("tmp", t2)], writes=[("qT", hf, g)])
                    else:
                        P.op("pool", lambda e: e.tensor_tensor(out=dst[0:rows, g * TG:(g + 1) * TG], in0=tmp[t1][0:rows, :], in1=tmp[t2][0:rows, :], op=ALU.add),
                             reads=[("tmp", t1), ("tmp", t2)], writes=[(dres, g)])

                hfn = lambda k, g: hs(k, g)
                hres = lambda k, g: ("h", k, g)

                def v_tokmajor(w, woff, wres, ncols):
                    for t4 in range(4):
                        px = next_px()
                        ps = psX[px]
                        for tt in range(4):
                            tok = (t4 * 4 + tt) * 128
                            for k in range(NCH):
                                P.op("pe", lambda e, k=k, tt=tt, tok=tok, ps=ps: e.matmul(
                                    ps[:, tt * 128: tt * 128 + ncols], lhsT=hT[:, k * S + tok: k * S + tok + 128],
                                    rhs=w[:, woff + k * 128: woff + k * 128 + ncols], start=(k == 0), stop=(k == NCH - 1)),
                                    reads=[wres, ("h", k, tok // TG)], writes=[("px", px)])
                        P.op("act", lambda e, t4=t4, ps=ps: e.activation(out=vS[:, t4 * 512:(t4 + 1) * 512], in_=ps[:], func=AF.Copy),
                             reads=[("px", px)], writes=[("vS", t4)])

                def outproj_acc(row0):
                    for c in range(NCH):
                        wb_ = c % 2
                        wload(watt[wb_], 5120, wview(ev_out_d[i], row0, 4, c * 128, 128), 4, 128, ("wa", wb_, 5))
                        for g in range(NTG):
                            px = next_px()
                            ps = psX[px]
                            for k in range(4):
                                P.op("pe", lambda e, k=k, g=g, ps=ps, wb_=wb_: e.matmul(ps[:], lhsT=watt[wb_][:, 5120 + k * 128: 5120 + (k + 1) * 128],
                                                                                  rhs=oH[k][:, g * TG:(g + 1) * TG], start=(k == 0), stop=(k == 3)),
                                     reads=[("wa", wb_, 5), (("oH", k), g)], writes=[("px", px)])
                            P.op("dve", lambda e, c=c, g=g, ps=ps: e.tensor_tensor(out=xs(c, g), in0=ps[:], in1=xs(c, g), op=ALU.add),
                                 reads=[("px", px), ("x", c, g)], writes=[("x", c, g)])

                for hd in (range(4) if isA else ()):
                    wb = hd % 2
                    w = watt[wb]
                    wload(w, 0, wview(ev_in_d[i], 0, 8, 128 * hd, 128), 8, 128, ("wa", wb, 0))
                    wload(w, 1024, wview(ev_sw_d[i], 0, 8, 128 * hd, 128), 8, 128, ("wa", wb, 1))
                    wload(w, 2048, wview(ev_in_d[i], 0, 8, 512 + 128 * hd, 128), 8, 128, ("wa", wb, 2))
                    wload(w, 3072, wview(ev_sw_d[i], 0, 8, 512 + 128 * hd, 128), 8, 128, ("wa", wb, 3))
                    wload(w, 4096, wview(ev_in_d[i], 0, 8, 1024 + 128 * hd, 128), 8, 128, ("wa", wb, 4))
                    for (dst, dres, o0) in ((None, "qT", 0), (kT, "kT", 2048)):
                        for g in range(NTG):
                            pa = next_px()
                            proj_fm(psX[pa], w, o0, 128, 128, 8, hfn, g, ("wa", wb, o0 // 1024), hres, ("px", pa))
                            pb = next_px()
                            proj_fm(psX[pb], w, o0 + 1024, 128, 128, 8, hfn, g, ("wa", wb, o0 // 1024 + 1), hres, ("px", pb))
                            rope_evac(psX[pa], psX[pb], pa, pb, dst, g, dres)
                    v_tokmajor(w, 4096, ("wa", wb, 4), 128)
                    ob = hd
                    pipe = Pipe(3)
                    hold = {}
                    later = []

                    def fin_a(g, half, aset, ob=ob, hold=hold):
                        aO, nO, aD, nD = aset
                        tO = at_i[0]
                        tD = (tO + 1) % NAT
                        at_i[0] = (tO + 2) % NAT
                        P.op("dve", lambda e: e.reciprocal(out=atmp[tD][:], in_=aD[:, :]), reads=[nD], writes=[("at", tD)])
                        P.op("dve", lambda e: e.tensor_tensor(out=atmp[tO][:], in0=aO[:, :], in1=atmp[tD][:], op=ALU.mult),
                             reads=[nO, ("at", tD)], writes=[("at", tO)])
                        if half == 0:
                            hold[g] = tO
                            return
                        a = hold[g]
                        b = tO
                        P.op("dve", lambda e: e.scalar_tensor_tensor(
                            out=atmp[a][:], in0=atmp[b][:], scalar=lams[:, 4 * i + 2: 4 * i + 3], in1=atmp[a][:], op0=ALU.mult, op1=ALU.add),
                            reads=[("at", a), ("at", b), ("lams", i, 2)], writes=[("at", a)])
                        later.append(lambda: subln_tail(g, a, tD, ob))

                    def subln_tail(g, a, tD, ob):
                        si = next_sq()
                        P.op("act", lambda e: e.activation(out=sq[si][:], in_=atmp[a][:], func=AF.Square), reads=[("at", a)], writes=[("sq", si)])
                        sb2 = sbuf_i[0]
                        sbuf_i[0] = (sb2 + 1) % 4
                        sreg2 = psS[:, sb2 * TG:(sb2 + 1) * TG]
                        P.op("pe", lambda e: e.matmul(sreg2, lhsT=ones[:], rhs=sq[si][:], start=True, stop=True),
                             reads=[("sq", si), "ones"], writes=[("psS", sb2)])
                        k2 = 1.0 / ((1.0 - lam_init) ** 2)
                        P.op("act", lambda e: e.activation(out=atmp[tD][:], in_=sreg2, func=AF.Sqrt, scale=k2 / 128.0, bias=EPS * k2),
                             reads=[("psS", sb2)], writes=[("at", tD)])
                        P.op("dve", lambda e: e.reciprocal(out=atmp[tD][:], in_=atmp[tD][:]), reads=[("at", tD)], writes=[("at", tD)])
                        P.op("dve", lambda e: e.scalar_tensor_tensor(
                            out=oH[ob][:, g * TG:(g + 1) * TG], in0=atmp[a][:], scalar=vecs[:, 78 + i: 79 + i], in1=atmp[tD][:], op0=ALU.mult, op1=ALU.mult),
                            reads=[("at", a), ("at", tD), "vecs"], writes=[(("oH", ob), g)])

                    def a_step(g, half, kt, aset):
                        aO, nO, aD, nD = aset
                        sbi = sbuf_i[0]
                        sbuf_i[0] = (sbi + 1) % 4
                        pti = pt_i[0]
                        pt_i[0] = (pti + 1) % NPT
                        sreg = psS[:, sbi * TG:(sbi + 1) * TG]

                        def qk():
                            P.op("pe", lambda e: e.matmul(sreg, lhsT=kT[:, kt * 128:(kt + 1) * 128], rhs=qTm[half][:, g * TG:(g + 1) * TG], start=True, stop=True),
                                 reads=[("kT", kt // 4), ("qT", half, g)], writes=[("psS", sbi)])
                            P.op("act", lambda e: e.activation(out=PT[pti][:, 0:TG], in_=sreg, func=AF.Exp, scale=0.125),
                                 reads=[("psS", sbi)], writes=[("PT", pti)])

                        def pv():
                            P.op("pe", lambda e: e.matmul(aO[:, :], lhsT=vS[:, kt * 128:(kt + 1) * 128], rhs=PT[pti][:, 0:TG], start=(kt == 0), stop=(kt == 15)),
                                 reads=[("PT", pti), ("vS", kt // 4)], writes=[nO])
                            P.op("pe", lambda e: e.matmul(aD[:, :], lhsT=ones[:], rhs=PT[pti][:, 0:TG], start=(kt == 0), stop=(kt == 15)),
                                 reads=[("PT", pti), "ones"], writes=[nD])
                            if kt == 15:
                                fin_a(g, half, aset)
                            if kt == 7:
                                while later:
                                    later.pop(0)()
                        pipe.step(qk, pv)

                    accs_a = [(psO, ("px", 3), psD, ("px", 2)), (psX[0], ("px", 0), psX[1], ("px", 1))]
                    for g in range(NTG):
                        for half in range(2):
                            aset = accs_a[unit_a[0] % 2]
                            unit_a[0] += 1
                            for kt in range(16):
                                a_step(g, half, kt, aset)
                    pipe.flush()
                    while later:
                        later.pop(0)()
                if isA:
                    outproj_acc(0)

                for cpair in (() if isA else range(4)):
                    wb = cpair % 2
                    w = watt[wb]
                    wload(w, 0, wview(ev_in_d[i], 0, 8, 1536 + 128 * cpair, 128), 8, 128, ("wa", wb, 0))
                    wload(w, 2048, wview(ev_in_d[i], 0, 8, 2048 + 128 * cpair, 128), 8, 128, ("wa", wb, 2))
                    wload(w, 4096, wview(ev_in_d[i], 0, 8, 2560 + 128 * cpair, 128), 8, 128, ("wa", wb, 4))
                    if cpair == 0:
                        build_G()
                    for (dst, dres, o0) in ((None, "qT", 0), (kT, "kT", 2048)):
                        for g in range(NTG):
                            pa = next_px()
                            proj_fm(psX[pa], w, o0, 128, 128, 8, hfn, g, ("wa", wb, o0 // 1024), hres, ("px", pa))
                            if dst is None:
                                for hf in range(2):
                                    P.op("act", lambda e, pa=pa, hf=hf, g=g: e.activation(out=qTm[hf][64 * hf:64 * hf + 64, g * TG:(g + 1) * TG],
                                                                                      in_=psX[pa][64 * hf:64 * hf + 64, :], func=AF.Copy),
                                         reads=[("px", pa)], writes=[("qT", hf, g)])
                            else:
                                P.op("act", lambda e, pa=pa, dst=dst, g=g: e.activation(out=dst[:, g * TG:(g + 1) * TG], in_=psX[pa][:], func=AF.Copy),
                                     reads=[("px", pa)], writes=[(dres, g)])
                    v_tokmajor(w, 4096, ("wa", wb, 4), 128)
                    ob = cpair
                    accs_b = [(psO, ("px", 3), psD, ("px", 2)), (psX[0], ("px", 0), psX[1], ("px", 1))]
                    tl = []
                    for hh in range(2):
                        for m4 in range(4):
                            aset = accs_b[unit_b[0] % 2]
                            unit_b[0] += 1
                            for mm in range(4):
                                m = m4 * 4 + mm
                                tiles = _PLAN[m]
                                for ti_, (t, gid) in enumerate(tiles):
                                    tl.append(dict(hh=hh, h=2 * cpair + hh, m=m, mm=mm, m4=m4, t=t, gid=gid, first=(ti_ == 0),
                                                   last=(ti_ == len(tiles) - 1), aset=aset, fin=(mm == 3 and ti_ == len(tiles) - 1)))
                    pipe = Pipe(3)

                    def fin_b(d, ob=ob):
                        aO, nO, aD, nD = d["aset"]
                        r0 = 64 * d["hh"]
                        m4 = d["m4"]
                        tr = next_tmp()
                        P.op("dve", lambda e: e.reciprocal(out=tmp[tr][r0:r0 + 64, :], in_=aD[r0:r0 + 64, :]),
                             reads=[nD], writes=[("tmp", tr)])
                        P.op("dve", lambda e: e.tensor_tensor(
                            out=oH[ob][r0:r0 + 64, m4 * TG:(m4 + 1) * TG], in0=aO[r0:r0 + 64, :], in1=tmp[tr][r0:r0 + 64, :], op=ALU.mult),
                            reads=[nO, ("tmp", tr)], writes=[(("oH", ob), m4)])

                    def b_chunk(ch):
                        sbi = sbuf_i[0]
                        sbuf_i[0] = (sbi + 1) % 4
                        pti = pt_i[0]
                        pt_i[0] = (pti + 1) % NPT
                        n = len(ch)

                        def qk():
                            for j, d in enumerate(ch):
                                sreg = psS[:, sbi * TG + j * 128: sbi * TG + (j + 1) * 128]
                                goff = (d["h"] * _NG + d["gid"]) * 128
                                P.op("pe", lambda e, sreg=sreg, goff=goff: e.matmul(sreg, lhsT=ident[:], rhs=G[:, goff:goff + 128], start=True, stop=False),
                                     reads=[("G", d["h"]), "ident"], writes=[("psS", sbi)])
                                P.op("pe", lambda e, sreg=sreg, d=d: e.matmul(
                                    sreg, lhsT=kT[:, d["t"] * 128:(d["t"] + 1) * 128], rhs=qTm[d["hh"]][:, d["m"] * 128:(d["m"] + 1) * 128], start=False, stop=True),
                                    reads=[("kT", d["t"] // 4), ("qT", d["hh"], d["m"] // 4)], writes=[("psS", sbi)])
                            P.op("act", lambda e: e.activation(out=PT[pti][:, 0:n * 128], in_=psS[:, sbi * TG: sbi * TG + n * 128], func=AF.Exp, scale=0.125),
                                 reads=[("psS", sbi)], writes=[("PT", pti)])

                        def pv():
                            for j, d in enumerate(ch):
                                aO, nO, aD, nD = d["aset"]
                                mm = d["mm"]
                                P.op("pe", lambda e, j=j, d=d, aO=aO, mm=mm: e.matmul(
                                    aO[:, mm * 128:(mm + 1) * 128], lhsT=vS[:, d["t"] * 128:(d["t"] + 1) * 128], rhs=PT[pti][:, j * 128:(j + 1) * 128],
                                    start=d["first"], stop=d["last"]),
                                    reads=[("PT", pti), ("vS", d["t"] // 4)], writes=[nO])
                                P.op("pe", lambda e, j=j, d=d, aD=aD, mm=mm: e.matmul(
                                    aD[:, mm * 128:(mm + 1) * 128], lhsT=ones[:], rhs=PT[pti][:, j * 128:(j + 1) * 128],
                                    start=d["first"], stop=d["last"]),
                                    reads=[("PT", pti), "ones"], writes=[nD])
                                if d["fin"]:
                                    fin_b(d)
                        pipe.step(qk, pv)

                    for c0 in range(0, len(tl), 4):
                        b_chunk(tl[c0:c0 + 4])
                    pipe.flush()
                if not isA:
                    outproj_acc(512)
                P.emit_phase()

        def odd_phase(l):
            i = l // 2
            scale = 1.0 / math.sqrt(96.0)
            with ExitStack() as ps_:
                st["npx"] = 4
                st["px"] = 0
                watt = [sb("watt%d" % k, [128, 2048], BF16, ps_) for k in range(2)]
                win = sb("win", [128, 8 * 416], BF16, ps_)
                winsw = sb("winsw", [128, 8 * 96], BF16, ps_)
                qTt = [sb("qT%d" % k, [128, S], BF16, ps_) for k in range(2)]
                kTt = [sb("kT%d" % k, [128, S], BF16, ps_) for k in range(2)]
                vSt = [sb("vS%d" % k, [128, 16 * 128], BF16, ps_) for k in range(2)]
                NPT = 6
                PT = [sb("PT%d" % k, [128, TG], BF16, ps_) for k in range(NPT)]
                ropeC = sb("ropeC", [128, S], BF16, ps_)
                ropeS = sb("ropeS", [128, S], BF16, ps_)
                for k_ in range(2):
                    P.op("pool", lambda e, k_=k_: e.memset(qTt[k_][:], 0.0), writes=[(("qT", k_), g) for g in range(NTG)])
                for k_ in range(2):
                    P.op("pool", lambda e, k_=k_: e.memset(kTt[k_][:], 0.0),
                         writes=[(("kTn", k_), g) for g in range(NTG)] + [(("kTr", k_), g) for g in range(NTG)])
                accs = [(psX[3], ("px", 3)), (psX[2], ("px", 2))]
                unit = [0]
                cqn = sb("cqn", [128, 2 * S], BF16, ps_)
                ckvn = sb("ckvn", [128, S], BF16, ps_)
                pt_i = [0]
                sbuf_i = [0]
                P.op("sp", lambda e: e.dma_start(out=ropeC[:], in_=ropeM_d[0]), writes=["ropeC"], dma_ch="ropeC")
                P.op("sp", lambda e: e.dma_start(out=ropeS[:], in_=ropeM_d[1]), writes=["ropeS"], dma_ch="ropeS")
                for k in range(2):
                    P.op("dve", lambda e, k=k: e.memset(vSt[k][:], 1.0), writes=[("vS", k, 0), ("vS", k, 1)])
                rmsnorm_x(8 * l, hs, lambda c, g: ("h", c, g))
                wload(win, 0, wview(od_in_d[i], 0, 8, 0, 416), 8, 416, "win")
                wload(winsw, 0, wview(od_insw_d[i], 0, 8, 0, 96), 8, 96, "winsw")
                hfn = lambda k, g: hs(k, g)
                hres = lambda k, g: ("h", k, g)
                for g in range(NTG):
                    pq = [next_px(), next_px()]
                    for c2 in range(2):
                        proj_fm(psX[pq[c2]], win, 128 * c2, 416, 128, 8, hfn, g, "win", hres, ("px", pq[c2]))
                    pss = next_px()
                    for c2 in range(2):
                        si = next_sq()
                        P.op("act", lambda e, si=si, p_=pq[c2]: e.activation(out=sq[si][:], in_=psX[p_][:], func=AF.Square),
                             reads=[("px", pq[c2])], writes=[("sq", si)])
                        P.op("pe", lambda e, si=si, c2=c2, pss=pss: e.matmul(psX[pss][:], lhsT=ones[:], rhs=sq[si][:], start=(c2 == 0), stop=(c2 == 1)),
                             reads=[("sq", si), "ones"], writes=[("px", pss)])
                    tr = next_tmp()
                    P.op("act", lambda e, tr=tr, pss=pss: e.activation(out=tmp[tr][:], in_=psX[pss][:], func=AF.Sqrt, scale=1.0 / 256.0, bias=EPS),
                         reads=[("px", pss)], writes=[("tmp", tr)])
                    P.op("dve", lambda e, tr=tr: e.reciprocal(out=tmp[tr][:], in_=tmp[tr][:]), reads=[("tmp", tr)], writes=[("tmp", tr)])
                    for c2 in range(2):
                        P.op("dve", lambda e, tr=tr, c2=c2, g=g, p_=pq[c2]: e.scalar_tensor_tensor(
                            out=cqn[:, c2 * S + g * TG: c2 * S + (g + 1) * TG], in0=psX[p_][:], scalar=vecs[:, 72 + 2 * i + c2: 73 + 2 * i + c2],
                            in1=tmp[tr][:], op0=ALU.mult, op1=ALU.mult),
                            reads=[("px", pq[c2]), ("tmp", tr), "vecs"], writes=[("cqn", c2, g)])
                    pk = next_px()
                    proj_fm(psX[pk], win, 256, 416, 128, 8, hfn, g, "win", hres, ("px", pk))
                    pss = next_px()
                    si = next_sq()
                    P.op("act", lambda e, si=si, pk=pk: e.activation(out=sq[si][:], in_=psX[pk][:], func=AF.Square),
                         reads=[("px", pk)], writes=[("sq", si)])
                    P.op("pe", lambda e, si=si, pss=pss: e.matmul(psX[pss][:], lhsT=ones[:], rhs=sq[si][:], start=True, stop=True),
                         reads=[("sq", si), "ones"], writes=[("px", pss)])
                    tr = next_tmp()
                    P.op("act", lambda e, tr=tr, pss=pss: e.activation(out=tmp[tr][:], in_=psX[pss][:], func=AF.Sqrt, scale=1.0 / 128.0, bias=EPS),
                         reads=[("px", pss)], writes=[("tmp", tr)])
                    P.op("dve", lambda e, tr=tr: e.reciprocal(out=tmp[tr][:], in_=tmp[tr][:]), reads=[("tmp", tr)], writes=[("tmp", tr)])
                    P.op("dve", lambda e, tr=tr, g=g, pk=pk: e.scalar_tensor_tensor(
                        out=ckvn[:, g * TG:(g + 1) * TG], in0=psX[pk][:], scalar=vecs[:, 76 + i: 77 + i], in1=tmp[tr][:], op0=ALU.mult, op1=ALU.mult),
                        reads=[("px", pk), ("tmp", tr), "vecs"], writes=[("ckvn", g)])
                    pa = next_px()
                    proj_fm(psX[pa], win, 320, 416, 96, 8, hfn, g, "win", hres, ("px", pa))
                    pb = next_px()
                    proj_fm(psX[pb], winsw, 0, 96, 96, 8, hfn, g, "winsw", hres, ("px", pb))
                    t1 = next_tmp()
                    t2 = next_tmp()
                    P.op("dve", lambda e, t1=t1, pa=pa, g=g: e.tensor_tensor(out=tmp[t1][64:96, :], in0=psX[pa][64:96, :], in1=ropeC[64:96, g * TG:(g + 1) * TG], op=ALU.mult),
                         reads=[("px", pa), "ropeC"], writes=[("tmp", t1)])
                    P.op("dve", lambda e, t2=t2, pb=pb, g=g: e.tensor_tensor(out=tmp[t2][64:96, :], in0=psX[pb][64:96, :], in1=ropeS[64:96, g * TG:(g + 1) * TG], op=ALU.mult),
                         reads=[("px", pb), "ropeS"], writes=[("tmp", t2)])
                    for k in range(2):
                        P.op("pool", lambda e, t1=t1, t2=t2, g=g, k=k: e.tensor_tensor(
                            out=kTt[k][64:96, g * TG:(g + 1) * TG], in0=tmp[t1][64:96, :], in1=tmp[t2][64:96, :], op=ALU.add),
                            reads=[("tmp", t1), ("tmp", t2)], writes=[(("kTr", k), g)])

                cfn = lambda k, g: cqn[:, k * S + g * TG: k * S + (g + 1) * TG]
                cres = lambda k, g: ("cqn", k, g)
                st["npx"] = 2
                st["px"] = 0

                def proj_head(h):
                    wb = h % 2
                    w = watt[wb]
                    kT = kTt[wb]
                    vS = vSt[wb]
                    qT = qTt[wb]
                    vc0 = 0 if wb == 0 else 64
                    wload(w, 0, wview(od_uq_d[i], 0, 2, 96 * h, 96), 2, 96, ("wa", wb, 0))
                    wload(w, 256, wview(od_uqsw_d[i], 0, 2, 96 * h, 96), 2, 96, ("wa", wb, 1))
                    wload(w, 512, wview(od_ukv_d[i], 0, 1, 128 * h, 128), 1, 128, ("wa", wb, 2))
                    for g in range(NTG):
                        pa = next_px()
                        proj_fm(psX[pa], w, 0, 96, 128, 2, cfn, g, ("wa", wb, 0), cres, ("px", pa))
                        pb = next_px()
                        proj_fm(psX[pb], w, 256, 96, 128, 2, cfn, g, ("wa", wb, 1), cres, ("px", pb))
                        t1 = next_tmp()
                        t2 = next_tmp()
                        P.op("dve", lambda e, t1=t1, pa=pa, g=g: e.tensor_tensor(out=tmp[t1][0:96, :], in0=psX[pa][0:96, :], in1=ropeC[0:96, g * TG:(g + 1) * TG], op=ALU.mult),
                             reads=[("px", pa), "ropeC"], writes=[("tmp", t1)])
                        P.op("dve", lambda e, t2=t2, pb=pb, g=g: e.tensor_tensor(out=tmp[t2][0:96, :], in0=psX[pb][0:96, :], in1=ropeS[0:96, g * TG:(g + 1) * TG], op=ALU.mult),
                             reads=[("px", pb), "ropeS"], writes=[("tmp", t2)])
                        P.op("pool", lambda e, t1=t1, t2=t2, g=g: e.tensor_tensor(out=qT[0:96, g * TG:(g + 1) * TG], in0=tmp[t1][0:96, :], in1=tmp[t2][0:96, :], op=ALU.add),
                             reads=[("tmp", t1), ("tmp", t2)], writes=[(("qT", wb), g)])
                        pk = next_px()
                        P.op("pe", lambda e, pk=pk, g=g: e.matmul(psX[pk][:, :], lhsT=w[:, 512:640], rhs=ckvn[:, g * TG:(g + 1) * TG], start=True, stop=True),
                             reads=[("wa", wb, 2), ("ckvn", g)], writes=[("px", pk)])
                        P.op("dve", lambda e, pk=pk, g=g: e.tensor_copy(out=kT[0:64, g * TG:(g + 1) * TG], in_=psX[pk][0:64, :]),
                             reads=[("px", pk)], writes=[(("kTn", wb), g)])
                    for t8 in range(2):
                        px = next_px()
                        ps = psX[px]
                        for tt in range(8):
                            tok = (t8 * 8 + tt) * 128
                            P.op("pe", lambda e, tt=tt, tok=tok, ps=ps: e.matmul(
                                ps[:, tt * 64:(tt + 1) * 64], lhsT=ckvn[:, tok:tok + 128], rhs=w[:, 576:640], start=True, stop=True),
                                reads=[("wa", wb, 2), ("ckvn", tok // TG)], writes=[("px", px)])
                        dstv = vS[:, t8 * 1024:(t8 + 1) * 1024].rearrange("p (t c) -> p t c", c=128)[:, :, vc0:vc0 + 64]
                        srcv = ps[:, :].rearrange("p (t c) -> p t c", c=64)
                        P.op("dve", lambda e, dstv=dstv, srcv=srcv: e.tensor_copy(out=dstv, in_=srcv),
                             reads=[("px", px)], writes=[("vS", wb, t8)])

                def attn_head(h):
                    wb = h % 2
                    kT = kTt[wb]
                    vS = vSt[wb]
                    qT = qTt[wb]
                    ob = (h // 2) % 2
                    orow = 64 * (h % 2)
                    drow = 64 - orow
                    pipe = Pipe(3)

                    def fin_m(g, aO, nO):
                        tr = next_tmp()
                        P.op("dve", lambda e: e.reciprocal(out=tmp[tr][drow:drow + 64, :], in_=aO[drow:drow + 64, :]),
                             reads=[nO], writes=[("tmp", tr)])
                        P.op("dve", lambda e: e.tensor_tensor(
                            out=hT[orow:orow + 64, (h // 2) * S + g * TG:(h // 2) * S + (g + 1) * TG], in0=aO[orow:orow + 64, :], in1=tmp[tr][drow:drow + 64, :], op=ALU.mult),
                            reads=[nO, ("tmp", tr)], writes=[("h", h // 2, g)])

                    def m_step(g, kt, aO, nO):
                        sbi = sbuf_i[0]
                        sbuf_i[0] = (sbi + 1) % 4
                        pti = pt_i[0]
                        pt_i[0] = (pti + 1) % NPT
                        sreg = psS[:, sbi * TG:(sbi + 1) * TG]

                        def qk():
                            P.op("pe", lambda e: e.matmul(sreg, lhsT=kT[:, kt * 128:(kt + 1) * 128], rhs=qT[:, g * TG:(g + 1) * TG], start=True, stop=True),
                                 reads=[(("kTn", wb), kt // 4), (("kTr", wb), kt // 4), (("qT", wb), g)], writes=[("psS", sbi)])
                            P.op("act", lambda e: e.activation(out=PT[pti][:, 0:TG], in_=sreg, func=AF.Exp, scale=scale),
                                 reads=[("psS", sbi)], writes=[("PT", pti)])

                        def pv():
                            P.op("pe", lambda e: e.matmul(aO, lhsT=vS[:, kt * 128:(kt + 1) * 128], rhs=PT[pti][:, 0:TG], start=(kt == 0), stop=(kt == 15)),
                                 reads=[("PT", pti), ("vS", wb, kt // 8)], writes=[nO])
                            if kt == 15:
                                fin_m(g, aO, nO)
                        pipe.step(qk, pv)

                    for g in range(NTG):
                        aO, nO = accs[unit[0] % 2]
                        unit[0] += 1
                        for kt in range(16):
                            m_step(g, kt, aO, nO)
                    pipe.flush()

                proj_head(0)
                for h in range(16):
                    if h + 1 < 16:
                        proj_head(h + 1)
                    attn_head(h)
                for c in range(NCH):
                    wb = c % 2
                    wload(watt[wb], 0, wview(od_out_d[i], 0, 8, c * 128, 128), 8, 128, [("wa", wb, 0), ("wa", wb, 1), ("wa", wb, 2)])
                    for g in range(NTG):
                        px = next_px()
                        ps = psX[px]
                        for k in range(NCH):
                            P.op("pe", lambda e, k=k, g=g, ps=ps, wb=wb: e.matmul(ps[:], lhsT=watt[wb][:, k * 128:(k + 1) * 128], rhs=hs(k, g),
                                                                              start=(k == 0), stop=(k == NCH - 1)),
                                 reads=[("wa", wb, 0), ("h", k, g)], writes=[("px", px)])
                        P.op("dve", lambda e, c=c, g=g, ps=ps: e.tensor_tensor(out=xs(c, g), in0=ps[:], in1=xs(c, g), op=ALU.add),
                             reads=[("px", px), ("x", c, g)], writes=[("x", c, g)])
                P.emit_phase()

        for s_ in range(nseq):
            for c in range(NCH):
                P.op("sp", lambda e, c=c, s_=s_: e.dma_start(out=xT[:, c * S:(c + 1) * S], in_=xT_d[s_][c * 128:(c + 1) * 128, :]),
                     writes=[("x", c, g) for g in range(NTG)], dma_ch=("xin", c))
            P.emit_phase()
            for l in layers:
                if do_mix:
                    if l % 2 == 0:
                        first = True
                        for part in parts:
                            even_phase(l, part, first)
                            first = False
                    else:
                        odd_phase(l)
                if do_mlp:
                    mlp_phase(l)
            with ExitStack() as ps_:
                st["npx"] = 2
                ob = [sb("outb%d" % k, [128, S], F32, ps_) for k in range(2)]
                rs = [sb("rs%d" % k, [128, TG], F32, ps_) for k in range(NTG)]
                for g in range(NTG):
                    px = next_px()
                    ps = psX[px]
                    for c in range(NCH):
                        P.op("act", lambda e, c=c, g=g: e.activation(out=hs(c, g), in_=xs(c, g), func=AF.Square),
                             reads=[("x", c, g)], writes=[("h", c, g)])
                        P.op("pe", lambda e, c=c, g=g, ps=ps: e.matmul(ps[:], lhsT=ones[:], rhs=hs(c, g), start=(c == 0), stop=(c == NCH - 1)),
                             reads=[("h", c, g), "ones"], writes=[("px", px)])
                    P.op("act", lambda e, g=g, ps=ps: e.activation(out=rs[g][:], in_=ps[:], func=AF.Sqrt, scale=1.0 / D, bias=EPS),
                         reads=[("px", px)], writes=[("rs", g)])
                    P.op("dve", lambda e, g=g: e.reciprocal(out=rs[g][:], in_=rs[g][:]), reads=[("rs", g)], writes=[("rs", g)])
                for c in range(NCH):
                    k = c % 2
                    for g in range(NTG):
                        P.op("dve", lambda e, c=c, g=g, k=k: e.scalar_tensor_tensor(
                            out=ob[k][:, g * TG:(g + 1) * TG], in0=xs(c, g), scalar=vecs[:, 64 + c:65 + c], in1=rs[g][:], op0=ALU.mult, op1=ALU.mult),
                            reads=[("x", c, g), ("rs", g), "vecs"], writes=[("ob", k)])
                    P.op("sp", lambda e, c=c, k=k, s_=s_: e.dma_start(out=out_d[s_][c * 128:(c + 1) * 128, :], in_=ob[k][:]),
                         reads=[("ob", k)], dma_ch=("out", k))
                P.emit_phase(final=(s_ == nseq - 1))
    return nc


def prep_shared(inp):
    f = lambda a: np.ascontiguousarray(np.asarray(a, dtype=np.float32))
    vecs = np.zeros((128, 80), np.float32)
    for l in range(4):
        vecs[:, 8 * l:8 * l + 8] = f(inp["ln_mix_g"])[l].reshape(8, 128).T
        vecs[:, 32 + 8 * l:32 + 8 * l + 8] = f(inp["ln_mlp_g"])[l].reshape(8, 128).T
    vecs[:, 64:72] = f(inp["ln_f_g"]).reshape(8, 128).T
    for i in range(2):
        vecs[:, 72 + 2 * i:74 + 2 * i] = f(inp["od_q_norm_g"])[i].reshape(2, 128).T
        vecs[:, 76 + i] = f(inp["od_kv_norm_g"])[i]
        vecs[:, 78 + i] = f(inp["ev_subln_g"])[i]
    lamv = np.zeros((128, 512), np.float32)
    for i in range(2):
        for j, nm in enumerate(("ev_lambda_q1", "ev_lambda_k1", "ev_lambda_q2", "ev_lambda_k2")):
            lamv[:, i * 256 + j * 64: i * 256 + (j + 1) * 64] = f(inp[nm])[i][None, :]
    ev_in = f(inp["ev_w_in"])
    permA = _swap_perm(1024, 64, 0, 8)
    ev_sw = np.ascontiguousarray(ev_in[:, :, :1024][:, :, permA])
    od_in = f(inp["od_w_in"])
    perm_kr = _swap_perm(32, 32, 0, 16)
    od_in_sw = np.ascontiguousarray(od_in[:, :, 320:416].copy())
    od_in_sw[:, :, 64:96] = od_in[:, :, 384:416][:, :, perm_kr]
    od_uq = f(inp["od_w_uq"])
    perm_q = _swap_perm(1536, 96, 64, 16)
    od_uq_sw = np.ascontiguousarray(od_uq[:, :, perm_q])
    CA, SA = _rope_tables_A()
    CM, SM = _rope_tables_M()
    rpb = f(inp["ev_rpb"])
    gb = rpb[:, :, _RIDX, _CIDX]
    gb = np.ascontiguousarray(gb.transpose(0, 3, 1, 2, 4)).reshape(2, 128, 8 * _NG * 128)
    gmask = np.where(_VALID, 0.0, MASKV).astype(np.float32)
    gmask = np.ascontiguousarray(gmask.transpose(1, 0, 2)).reshape(128, _NG * 128).astype(ml_dtypes.bfloat16)
    return {
        "vecs": vecs, "lamv": lamv,
        "w_up": f(inp["w_up"]), "w_down": f(inp["w_down"]),
        "ev_w_in": ev_in, "ev_w_sw": ev_sw, "ev_w_out": f(inp["ev_w_out"]),
        "od_w_in": od_in, "od_w_in_sw": od_in_sw, "od_w_uq": od_uq, "od_w_uq_sw": od_uq_sw,
        "od_w_ukv": f(inp["od_w_ukv"]), "od_w_out": f(inp["od_w_out"]),
        "ropeA": np.stack([CA, SA]).astype(ml_dtypes.bfloat16), "ropeM": np.stack([CM, SM]).astype(ml_dtypes.bfloat16),
        "gb": gb, "gmask": gmask,
        "ident": np.eye(128, dtype=np.float32).astype(ml_dtypes.bfloat16),
    }


_NC_CACHE = {}


def kernel(**inputs):
    x = np.asarray(inputs["x"], dtype=np.float32)
    B = x.shape[0]
    nseq = B // NCORES
    shared = prep_shared(inputs)
    key = nseq
    if key not in _NC_CACHE:
        _NC_CACHE[key] = build(nseq=nseq)
    nc = _NC_CACHE[key]
    in_maps = []
    for c in range(NCORES):
        xs_ = x[c * nseq:(c + 1) * nseq]
        m = dict(shared)
        m["xT"] = np.ascontiguousarray(xs_.transpose(0, 2, 1))
        in_maps.append(m)
    res = run_bass_kernel_spmd(nc, in_maps, core_ids=list(range(NCORES)))
    out = np.empty((B, S, D), np.float32)
    for c in range(NCORES):
        o = np.asarray(res.results[c]["outT"])
        out[c * nseq:(c + 1) * nseq] = o.transpose(0, 2, 1)
    return out
```

```python
import math
from contextlib import ExitStack

import numpy as np
import ml_dtypes
import concourse.bass as bass
import concourse.mybir as mybir
from concourse.bass_utils import run_bass_kernel_spmd

F32 = mybir.dt.float32
BF16 = mybir.dt.bfloat16
AF = mybir.ActivationFunctionType
ALU = mybir.AluOpType
AX = mybir.AxisListType

S = 2048
D = 1024
NCH = 8
NTG = 4
TG = 512
EPS = 1e-5
NCORES = 8
ENGS = ("pe", "act", "dve", "pool", "sp")


class Op:
    __slots__ = ("eng", "fn", "deps", "dma_ch", "sig", "idx", "has_consumer", "waits")

    def __init__(self, eng, fn, dma_ch=None):
        self.eng = eng
        self.fn = fn
        self.deps = set()
        self.dma_ch = dma_ch
        self.sig = None
        self.has_consumer = False
        self.waits = []


class Prog:
    def __init__(self, nc, es):
        self.nc = nc
        self.es = es
        self.sems = {}
        self.cnt = {e: 0 for e in ENGS}
        self.dcnt = {}
        self.reset_phase()
        self.nphase = 0

    def reset_phase(self):
        self.ops = []
        self.last_writer = {}
        self.readers = {}

    def sem(self, key):
        s = self.sems.get(key)
        if s is None:
            name = "s_" + "_".join(str(k) for k in key)
            s = self.es.enter_context(self.nc.semaphore(name))
            self.sems[key] = s
        return s

    def op(self, eng, fn, reads=(), writes=(), dma_ch=None):
        o = Op(eng, fn, dma_ch)
        o.idx = len(self.ops)
        deps = set()
        for r in reads:
            w = self.last_writer.get(r)
            if w is not None:
                deps.add(w)
        for r in writes:
            w = self.last_writer.get(r)
            if w is not None:
                deps.add(w)
            for rd in self.readers.get(r, ()):
                deps.add(rd)
        o.deps = deps
        self.ops.append(o)
        for r in reads:
            self.readers.setdefault(r, []).append(o.idx)
        for r in writes:
            self.last_writer[r] = o.idx
            self.readers[r] = []
        return o

    def emit_phase(self, final=False):
        ops = self.ops
        if not ops:
            return
        nc = self.nc
        for o in ops:
            best = {}
            for d in o.deps:
                p = ops[d]
                if p.dma_ch is not None:
                    key = ("dma", p.dma_ch)
                elif p.eng != o.eng or o.eng in ("act", "dve", "pool"):
                    key = ("eng", p.eng)
                else:
                    continue
                if d > best.get(key, -1):
                    best[key] = d
            keep = set(best.values())
            o.deps = keep
            for d in keep:
                ops[d].has_consumer = True
        last = {}
        for o in ops:
            if o.dma_ch is None:
                last[o.eng] = o
        for o in last.values():
            o.has_consumer = True
        prev_cnt = dict(self.cnt)
        prev_dcnt = dict(self.dcnt)
        for o in ops:
            if o.dma_ch is not None:
                self.dcnt[o.dma_ch] = self.dcnt.get(o.dma_ch, 0) + 16
                o.sig = ("dma", o.dma_ch, self.dcnt[o.dma_ch])
            elif o.has_consumer:
                self.cnt[o.eng] += 1
                o.sig = ("eng", o.eng, self.cnt[o.eng])
        waited = {e: {} for e in ENGS}
        for e in ENGS:
            for e2 in ENGS:
                if prev_cnt[e2] > 0:
                    waited[e][("eng", e2)] = prev_cnt[e2]
            for ch, v in prev_dcnt.items():
                waited[e][("dma", ch)] = v
        for o in ops:
            need = {}
            for d in o.deps:
                s = ops[d].sig
                key = (s[0], s[1])
                if s[2] > need.get(key, 0):
                    need[key] = s[2]
            for key, v in need.items():
                if waited[o.eng].get(key, 0) >= v:
                    continue
                waited[o.eng][key] = v
                o.waits.append((key, v))
        by_eng = {e: [o for o in ops if o.eng == e] for e in ENGS}
        sems = self.sem
        final_d = dict(self.dcnt)
        final_c = dict(self.cnt)

        def run(eng_name, eng):
            for e2 in ENGS:
                if e2 != eng_name and prev_cnt[e2] > 0:
                    eng.wait_ge(sems(("eng", e2)), prev_cnt[e2])
            for ch, v in prev_dcnt.items():
                eng.wait_ge(sems(("dma", ch)), v)
            for o in by_eng[eng_name]:
                for key, v in o.waits:
                    eng.wait_ge(sems(key), v)
                ins = o.fn(eng)
                if o.dma_ch is not None:
                    ins.then_inc(sems(("dma", o.dma_ch)), 16)
                elif o.sig is not None:
                    ins.then_inc(sems(("eng", eng_name)), 1)
            if final:
                for ch, v in final_d.items():
                    eng.wait_ge(sems(("dma", ch)), v)
                for e2 in ENGS:
                    if e2 != eng_name and final_c[e2] > 0:
                        eng.wait_ge(sems(("eng", e2)), final_c[e2])

        for e in ENGS:
            sems(("eng", e))
        for ch in self.dcnt:
            sems(("dma", ch))
        with nc.Block() as block:
            @block.sync
            def _(e):
                run("sp", e)

            @block.tensor
            def _(e):
                run("pe", e)

            @block.scalar
            def _(e):
                run("act", e)

            @block.vector
            def _(e):
                run("dve", e)

            @block.gpsimd
            def _(e):
                run("pool", e)
        self.nphase += 1
        self.reset_phase()


class Pipe:
    def __init__(self, look):
        self.look = look
        self.q = []

    def step(self, qk, pv):
        qk()
        self.q.append(pv)
        if len(self.q) > self.look:
            self.q.pop(0)()

    def flush(self):
        while self.q:
            self.q.pop(0)()


def _rot_tables(rot_dim, theta=500000.0):
    pos = np.arange(S, dtype=np.float32)
    inv = (np.float32(theta) ** (-np.arange(0, rot_dim, 2, dtype=np.float32) / np.float32(rot_dim))).astype(np.float32)
    ang = (pos[:, None] * inv[None, :]).astype(np.float32)
    return np.cos(ang).astype(np.float32), np.sin(ang).astype(np.float32)


def _rope_tables_A():
    cos, sin = _rot_tables(16)
    C = np.ones((128, S), np.float32)
    Sg = np.zeros((128, S), np.float32)
    for base in (0, 64):
        for d in range(8):
            C[base + d] = cos[:, d]
            Sg[base + d] = -sin[:, d]
            C[base + 8 + d] = cos[:, d]
            Sg[base + 8 + d] = sin[:, d]
    return C, Sg


def _rope_tables_M():
    cos, sin = _rot_tables(32)
    C = np.ones((128, S), np.float32)
    Sg = np.zeros((128, S), np.float32)
    for d in range(16):
        C[64 + d] = cos[:, d]
        Sg[64 + d] = -sin[:, d]
        C[64 + 16 + d] = cos[:, d]
        Sg[64 + 16 + d] = sin[:, d]
    return C, Sg


def _swap_perm(n, head, rot_off, half):
    idx = np.arange(n)
    for h0 in range(0, n, head):
        for d in range(half):
            idx[h0 + rot_off + d] = h0 + rot_off + half + d
            idx[h0 + rot_off + half + d] = h0 + rot_off + d
    return idx


def _nbr_plan():
    rows, W, wr, wc = 32, 64, 8, 16
    gids = {}
    plan = []
    for m in range(16):
        need = {}
        for b in range(2):
            qr = 2 * m + b
            r0 = min(max(qr - wr // 2, 0), rows - wr)
            for kr in range(r0, r0 + wr):
                t, a = kr // 2, kr % 2
                need.setdefault(t, set()).add((a, b))
        lst = []
        for t in sorted(need):
            key = (2 * t - 2 * m, tuple(sorted(need[t])))
            if key not in gids:
                gids[key] = len(gids)
            lst.append((t, gids[key]))
        plan.append(lst)
    ng = len(gids)
    ridx = np.zeros((ng, 128, 128), np.int64)
    cidx = np.zeros((ng, 128, 128), np.int64)
    valid = np.zeros((ng, 128, 128), bool)
    c = np.arange(W)
    cs = np.clip(c - wc // 2, 0, W - wc)
    colmask = (c[None, :] >= cs[:, None]) & (c[None, :] < cs[:, None] + wc)
    for (delta, pat), g in gids.items():
        for a in range(2):
            for b in range(2):
                dr = delta + a - b + 7
                ok = (a, b) in pat
                for kc in range(W):
                    k = a * 64 + kc
                    q = b * 64 + c
                    ridx[g, k, q] = min(max(dr, 0), 14)
                    cidx[g, k, q] = np.clip(kc - c, -15, 15) + 15
                    valid[g, k, q] = ok & colmask[c, kc]
    return plan, ng, ridx, cidx, valid


_PLAN, _NG, _RIDX, _CIDX, _VALID = _nbr_plan()
MASKV = -30000.0


def build(nseq=4, layers=(0, 1, 2, 3), do_mlp=True, do_mix=True, parts="AB", dbg=False):
    nc = bass.Bass("TRN2", target_bir_lowering=False)
    dt = nc.dram_tensor
    xT_d = dt("xT", [nseq, D, S], F32, kind="ExternalInput").ap()
    out_d = dt("outT", [nseq, D, S], F32, kind="ExternalOutput").ap()
    vecs_d = dt("vecs", [128, 80], F32, kind="ExternalInput").ap()
    lam_d = dt("lamv", [128, 2 * 4 * 64], F32, kind="ExternalInput").ap()
    w_up_d = dt("w_up", [4, D, 4096], F32, kind="ExternalInput").ap()
    w_dn_d = dt("w_down", [4, 4096, D], F32, kind="ExternalInput").ap()
    ev_in_d = dt("ev_w_in", [2, D, 3072], F32, kind="ExternalInput").ap()
    ev_sw_d = dt("ev_w_sw", [2, D, 1024], F32, kind="ExternalInput").ap()
    ev_out_d = dt("ev_w_out", [2, D, D], F32, kind="ExternalInput").ap()
    od_in_d = dt("od_w_in", [2, D, 416], F32, kind="ExternalInput").ap()
    od_insw_d = dt("od_w_in_sw", [2, D, 96], F32, kind="ExternalInput").ap()
    od_uq_d = dt("od_w_uq", [2, 256, 1536], F32, kind="ExternalInput").ap()
    od_uqsw_d = dt("od_w_uq_sw", [2, 256, 1536], F32, kind="ExternalInput").ap()
    od_ukv_d = dt("od_w_ukv", [2, 128, 2048], F32, kind="ExternalInput").ap()
    od_out_d = dt("od_w_out", [2, D, D], F32, kind="ExternalInput").ap()
    ropeA_d = dt("ropeA", [2, 128, S], BF16, kind="ExternalInput").ap()
    ropeM_d = dt("ropeM", [2, 128, S], BF16, kind="ExternalInput").ap()
    gb_d = dt("gb", [2, 128, 8 * _NG * 128], F32, kind="ExternalInput").ap()
    gmask_d = dt("gmask", [128, _NG * 128], BF16, kind="ExternalInput").ap()
    ident_d = dt("ident", [128, 128], BF16, kind="ExternalInput").ap()

    if dbg:
        dbg_d = {nm: dt("dbg_" + nm, [128, S], BF16, kind="ExternalOutput").ap() for nm in ("q", "k", "v", "o")}
    es = ExitStack()
    with es:
        uniq = [0]

        def sb(name, shape, dty, stack=es):
            uniq[0] += 1
            return stack.enter_context(nc.sbuf_tensor("%s_u%d" % (name, uniq[0]), shape, dty))

        P = Prog(nc, es)
        xT = sb("xT_s", [128, NCH * S], F32)
        hT = sb("hT_s", [128, NCH * S], BF16)
        vecs = sb("vecs_s", [128, 80], F32)
        lams = sb("lams", [128, 16], F32)
        ones = sb("ones", [128, 128], BF16)
        ident = sb("ident_s", [128, 128], BF16)
        NSTG = 2
        STGW = 1536
        stage = [sb("stage%d" % i, [128, STGW], F32) for i in range(NSTG)]
        NTMP = 5
        tmp = [sb("tmp%d" % i, [128, TG], F32) for i in range(NTMP)]
        sq = [sb("sq%d" % i, [128, TG], BF16) for i in range(2)]
        psS = es.enter_context(nc.psum_tensor("psS", [128, 2048], F32))
        psX0 = es.enter_context(nc.psum_tensor("psX0", [128, TG], F32))
        psX1 = es.enter_context(nc.psum_tensor("psX1", [128, TG], F32))
        psO = es.enter_context(nc.psum_tensor("psO", [128, TG], F32))
        psD = es.enter_context(nc.psum_tensor("psD", [128, TG], F32))
        psX = [psX0[:, :], psX1[:, :], psD[:, :], psO[:, :]] + [psS[:, k * TG:(k + 1) * TG] for k in range(4)]

        st = {"stg": 0, "tmp": 0, "sq": 0, "px": 0, "npx": 2}

        def xs(c, g):
            return xT[:, c * S + g * TG: c * S + (g + 1) * TG]

        def hs(c, g):
            return hT[:, c * S + g * TG: c * S + (g + 1) * TG]

        def next_tmp():
            i = st["tmp"]
            st["tmp"] = (i + 1) % NTMP
            return i

        def next_px():
            i = st["px"] % st["npx"]
            st["px"] = (i + 1) % st["npx"]
            return i

        def next_sq():
            i = st["sq"]
            st["sq"] = (i + 1) % 2
            return i

        def wload(dst, dst_off, src3, kc, n, dst_res):
            per = max(1, STGW // n)
            k0 = 0
            while k0 < kc:
                k1 = min(kc, k0 + per)
                tot = (k1 - k0) * n
                slot = st["stg"]
                st["stg"] = (slot + 1) % NSTG
                stg = stage[slot]
                src = src3[:, k0:k1, :]
                dview = stg[:, 0:tot].rearrange("p (k n) -> p k n", k=k1 - k0)
                P.op("sp", lambda e, dview=dview, src=src: e.dma_start(out=dview, in_=src),
                     writes=[("stg", slot)], dma_ch=("stg", slot))
                o0 = dst_off + k0 * n
                P.op("pool", lambda e, stg=stg, o0=o0, tot=tot: e.tensor_copy(out=dst[:, o0:o0 + tot], in_=stg[:, 0:tot]),
                     reads=[("stg", slot)], writes=(dst_res if isinstance(dst_res, list) else [dst_res]))
                k0 = k1

        def wview(w2d, r0, nk, c0, n):
            return w2d[r0:r0 + nk * 128, c0:c0 + n].rearrange("(k p) n -> p k n", p=128)

        def rmsnorm_x(gcol0, dst_fn, dst_res_fn, out_f32=False):
            for g in range(NTG):
                px = next_px()
                ps = psX[px]
                for c in range(NCH):
                    P.op("act", lambda e, c=c, g=g: e.activation(out=hs(c, g), in_=xs(c, g), func=AF.Square),
                         reads=[("x", c, g)], writes=[("h", c, g)])
                    P.op("pe", lambda e, c=c, g=g, ps=ps: e.matmul(ps[:], lhsT=ones[:], rhs=hs(c, g), start=(c == 0), stop=(c == NCH - 1)),
                         reads=[("h", c, g), "ones"], writes=[("px", px)])
                ti = next_tmp()
                P.op("act", lambda e, ti=ti, ps=ps: e.activation(out=tmp[ti][:], in_=ps[:], func=AF.Sqrt, scale=1.0 / D, bias=EPS),
                     reads=[("px", px)], writes=[("tmp", ti)])
                P.op("dve", lambda e, ti=ti: e.reciprocal(out=tmp[ti][:], in_=tmp[ti][:]),
                     reads=[("tmp", ti)], writes=[("tmp", ti)])
                for c in range(NCH):
                    P.op("dve", lambda e, ti=ti, c=c, g=g: e.scalar_tensor_tensor(
                        out=dst_fn(c, g), in0=xs(c, g), scalar=vecs[:, gcol0 + c:gcol0 + c + 1], in1=tmp[ti][:],
                        op0=ALU.mult, op1=ALU.mult),
                        reads=[("x", c, g), ("tmp", ti), "vecs"], writes=[dst_res_fn(c, g)])

        init_es = ExitStack()
        lamt = sb("lamt", [128, 512], F32, init_es)
        lamw = sb("lamw", [128, 512], F32, init_es)
        P.op("sp", lambda e: e.dma_start(out=vecs[:], in_=vecs_d[:, :]), writes=["vecs"], dma_ch="vecs")
        P.op("sp", lambda e: e.dma_start(out=lamt[:], in_=lam_d[:, :]), writes=["lamt"], dma_ch="lamt")
        P.op("sp", lambda e: e.dma_start(out=ident[:], in_=ident_d[:, :]), writes=["ident"], dma_ch="ident")
        P.op("dve", lambda e: e.memset(ones[:], 1.0), writes=["ones"])
        for i in range(2):
            b = i * 256
            for j in range(2):
                P.op("dve", lambda e, b=b, j=j: e.tensor_tensor(out=lamw[:, b + 64 * j: b + 64 * j + 64], in0=lamt[:, b + 128 * j: b + 128 * j + 64],
                                                             in1=lamt[:, b + 128 * j + 64: b + 128 * j + 128], op=ALU.mult),
                     reads=["lamt"], writes=[("lamw", i, j)])
                P.op("dve", lambda e, b=b, j=j, i=i: e.reduce_sum(out=lams[:, 4 * i + j: 4 * i + j + 1], in_=lamw[:, b + 64 * j: b + 64 * j + 64], axis=AX.X),
                     reads=[("lamw", i, j)], writes=[("lams", i, j)])
                P.op("act", lambda e, j=j, i=i: e.activation(out=lams[:, 4 * i + j: 4 * i + j + 1], in_=lams[:, 4 * i + j: 4 * i + j + 1], func=AF.Exp),
                     reads=[("lams", i, j)], writes=[("lams", i, j)])
            lam_init = 0.8 - 0.6 * math.exp(-0.3 * (2 * i))
            P.op("dve", lambda e, i=i, lam_init=lam_init: e.scalar_tensor_tensor(
                out=lams[:, 4 * i + 2: 4 * i + 3], in0=lams[:, 4 * i + 1: 4 * i + 2], scalar=-lam_init, in1=lams[:, 4 * i: 4 * i + 1],
                op0=ALU.add, op1=ALU.subtract),
                reads=[("lams", i, 0), ("lams", i, 1)], writes=[("lams", i, 2)])
        P.emit_phase()
        init_es.close()

        def mlp_phase(l):
            with ExitStack() as ps_:
                st["npx"] = 8
                wm = [sb("wmlp%d" % i, [128, 8192], BF16, ps_) for i in range(2)]
                uT = sb("uT", [128, 4 * S], BF16, ps_)
                rmsnorm_x(32 + 8 * l, hs, lambda c, g: ("h", c, g))
                for part in range(8):
                    wb = part % 2
                    w = wm[wb]
                    wload(w, 0, wview(w_up_d[l], 0, 8, part * 512, 512), 8, 512, ("wm", wb, 0))
                    wload(w, 4096, wview(w_dn_d[l], part * 512, 4, 0, 1024), 4, 1024, ("wm", wb, 1))
                    for g in range(NTG):
                        for j in range(4):
                            px = next_px()
                            ps = psX[px]
                            for c in range(NCH):
                                P.op("pe", lambda e, w=w, c=c, j=j, g=g, ps=ps: e.matmul(
                                    ps[:], lhsT=w[:, c * 512 + j * 128: c * 512 + j * 128 + 128], rhs=hs(c, g),
                                    start=(c == 0), stop=(c == NCH - 1)),
                                    reads=[("wm", wb, 0), ("h", c, g)], writes=[("px", px)])
                            ti = next_tmp()
                            P.op("act", lambda e, ti=ti, ps=ps: e.activation(out=tmp[ti][:], in_=ps[:], func=AF.Relu),
                                 reads=[("px", px)], writes=[("tmp", ti)])
                            P.op("pool", lambda e, ti=ti, j=j, g=g: e.tensor_tensor(
                                out=uT[:, j * S + g * TG: j * S + (g + 1) * TG], in0=tmp[ti][:], in1=tmp[ti][:], op=ALU.mult),
                                reads=[("tmp", ti)], writes=[("u", j, g)])
                    for g in range(NTG):
                        for c in range(NCH):
                            px = next_px()
                            ps = psX[px]
                            for j in range(4):
                                P.op("pe", lambda e, w=w, c=c, j=j, g=g, ps=ps: e.matmul(
                                    ps[:], lhsT=w[:, 4096 + j * 1024 + c * 128: 4096 + j * 1024 + c * 128 + 128],
                                    rhs=uT[:, j * S + g * TG: j * S + (g + 1) * TG], start=(j == 0), stop=(j == 3)),
                                    reads=[("wm", wb, 1), ("u", j, g)], writes=[("px", px)])
                            P.op("dve", lambda e, c=c, g=g, ps=ps: e.tensor_tensor(out=xs(c, g), in0=ps[:], in1=xs(c, g), op=ALU.add),
                                 reads=[("px", px), ("x", c, g)], writes=[("x", c, g)])
                P.emit_phase()

        def proj_fm(ps, wt, woff, wstride, m, nk, rhs_fn, g, wres, rres_fn, pxres, prow0=0):
            for k in range(nk):
                P.op("pe", lambda e, k=k: e.matmul(ps[prow0:prow0 + m, :], lhsT=wt[:, woff + k * wstride: woff + k * wstride + m],
                                                  rhs=rhs_fn(k, g), start=(k == 0), stop=(k == nk - 1)),
                     reads=[wres, rres_fn(k, g)], writes=[pxres])

        def outproj_chunk(wout_d2, kc, oH, ores, watt, wb, wo=5120, load=True, compute=True):
            if load:
                wload(watt[wb], wo, wview(wout_d2, kc * 128, 1, 0, 1024), 1, 1024, ("wa", wb, 5))
            if not compute:
                return
            for g in range(NTG):
                for c in range(NCH):
                    px = next_px()
                    ps = psX[px]
                    P.op("pe", lambda e, c=c, g=g, ps=ps: e.matmul(ps[:], lhsT=watt[wb][:, wo + c * 128: wo + c * 128 + 128],
                                                                   rhs=oH[:, g * TG:(g + 1) * TG], start=True, stop=True),
                         reads=[("wa", wb, 5), (ores, g)], writes=[("px", px)])
                    P.op("dve", lambda e, c=c, g=g, ps=ps: e.tensor_tensor(out=xs(c, g), in0=ps[:], in1=xs(c, g), op=ALU.add),
                         reads=[("px", px), ("x", c, g)], writes=[("x", c, g)])

        def even_phase(l, part, do_norm):
            i = l // 2
            lam_init = 0.8 - 0.6 * math.exp(-0.3 * l)
            isA = part == "A"
            with ExitStack() as ps_:
                st["npx"] = 2
                st["px"] = 0
                watt = [sb("watt%d" % k, [128, 6144], BF16, ps_) for k in range(2)]
                qTm = [sb("qTm%d" % k, [128, S], BF16, ps_) for k in range(2)]
                kT = sb("kT", [128, S], BF16, ps_)
                vS = sb("vS", [128, 16 * 128], BF16, ps_)
                oH = [sb("oH%d" % k, [128, S], BF16, ps_) for k in range(4)]
                NPT = 6
                PT = [sb("PT%d" % k, [128, 512], BF16, ps_) for k in range(NPT)]
                unit_b = [0]
                unit_a = [0]
                for k_ in range(2):
                    P.op("pool", lambda e, k_=k_: e.memset(qTm[k_][:], 0.0), writes=[("qT", k_, g) for g in range(NTG)])
                pt_i = [0]
                sbuf_i = [0]
                if isA:
                    ropeC = sb("ropeC", [128, S], BF16, ps_)
                    ropeS = sb("ropeS", [128, S], BF16, ps_)
                    NAT = 6
                    atmp = [sb("atmp%d" % k, [128, TG], F32, ps_) for k in range(NAT)]
                    at_i = [0]
                    P.op("sp", lambda e: e.dma_start(out=ropeC[:], in_=ropeA_d[0]), writes=["ropeC"], dma_ch="ropeC")
                    P.op("sp", lambda e: e.dma_start(out=ropeS[:], in_=ropeA_d[1]), writes=["ropeS"], dma_ch="ropeS")
                else:
                    G = sb("G", [128, 8 * _NG * 128], BF16, ps_)
                    gmask = sb("gmask", [128, _NG * 128], BF16, ps_)
                    P.op("sp", lambda e: e.dma_start(out=gmask[:], in_=gmask_d[:, :]), writes=["gmask"], dma_ch="gmask")
                    GW = _NG * 128
                    for h in range(8):
                        slot = st["stg"]
                        st["stg"] = (slot + 1) % NSTG
                        stg = stage[slot]
                        P.op("sp", lambda e, stg=stg, h=h: e.dma_start(out=stg[:, 0:GW], in_=gb_d[i][:, h * GW:(h + 1) * GW]),
                             writes=[("stg", slot)], dma_ch=("stg", slot))
                        P.op("dve", lambda e, stg=stg, h=h: e.scalar_tensor_tensor(
                            out=G[:, h * GW:(h + 1) * GW], in0=stg[:, 0:GW], scalar=8.0, in1=gmask[:], op0=ALU.mult, op1=ALU.add),
                            reads=[("stg", slot), "gmask"], writes=[("G", h)])

                if do_norm:
                    rmsnorm_x(8 * l, hs, lambda c, g: ("h", c, g))

                def rope_evac(ps_a, ps_b, pxa, pxb, dst, g, dres, rows=128):
                    t1 = next_tmp()
                    t2 = next_tmp()
                    P.op("dve", lambda e: e.tensor_tensor(out=tmp[t1][0:rows, :], in0=ps_a[0:rows, :], in1=ropeC[0:rows, g * TG:(g + 1) * TG], op=ALU.mult),
                         reads=[("px", pxa), "ropeC"], writes=[("tmp", t1)])
                    P.op("dve", lambda e: e.tensor_tensor(out=tmp[t2][0:rows, :], in0=ps_b[0:rows, :], in1=ropeS[0:rows, g * TG:(g + 1) * TG], op=ALU.mult),
                         reads=[("px", pxb), "ropeS"], writes=[("tmp", t2)])
                    if dst is None:
                        for hf in range(2):
                            P.op("pool", lambda e, hf=hf: e.tensor_tensor(out=qTm[hf][64 * hf:64 * hf + 64, g * TG:(g + 1) * TG], in0=tmp[t1][64 * hf:64 * hf + 64, :],
                                                                      in1=tmp[t2][64 * hf:64 * hf + 64, :], op=ALU.add),
                                 reads=[("tmp", t1), ("tmp", t2)], writes=[("qT", hf, g)])
                    else:
                        P.op("pool", lambda e: e.tensor_tensor(out=dst[0:rows, g * TG:(g + 1) * TG], in0=tmp[t1][0:rows, :], in1=tmp[t2][0:rows, :], op=ALU.add),
                             reads=[("tmp", t1), ("tmp", t2)], writes=[(dres, g)])

                hfn = lambda k, g: hs(k, g)
                hres = lambda k, g: ("h", k, g)

                def v_tokmajor(w, woff, wres, ncols):
                    for t4 in range(4):
                        px = next_px()
                        ps = psX[px]
                        for tt in range(4):
                            tok = (t4 * 4 + tt) * 128
                            for k in range(NCH):
                                P.op("pe", lambda e, k=k, tt=tt, tok=tok, ps=ps: e.matmul(
                                    ps[:, tt * 128: tt * 128 + ncols], lhsT=hT[:, k * S + tok: k * S + tok + 128],
                                    rhs=w[:, woff + k * 128: woff + k * 128 + ncols], start=(k == 0), stop=(k == NCH - 1)),
                                    reads=[wres, ("h", k, tok // TG)], writes=[("px", px)])
                        P.op("act", lambda e, t4=t4, ps=ps: e.activation(out=vS[:, t4 * 512:(t4 + 1) * 512], in_=ps[:], func=AF.Copy),
                             reads=[("px", px)], writes=[("vS", t4)])

                def outproj_acc(row0):
                    for c in range(NCH):
                        wb_ = c % 2
                        wload(watt[wb_], 5120, wview(ev_out_d[i], row0, 4, c * 128, 128), 4, 128, ("wa", wb_, 5))
                        for g in range(NTG):
                            px = next_px()
                            ps = psX[px]
                            for k in range(4):
                                P.op("pe", lambda e, k=k, g=g, ps=ps, wb_=wb_: e.matmul(ps[:], lhsT=watt[wb_][:, 5120 + k * 128: 5120 + (k + 1) * 128],
                                                                                  rhs=oH[k][:, g * TG:(g + 1) * TG], start=(k == 0), stop=(k == 3)),
                                     reads=[("wa", wb_, 5), (("oH", k), g)], writes=[("px", px)])
                            P.op("dve", lambda e, c=c, g=g, ps=ps: e.tensor_tensor(out=xs(c, g), in0=ps[:], in1=xs(c, g), op=ALU.add),
                                 reads=[("px", px), ("x", c, g)], writes=[("x", c, g)])

                for hd in (range(4) if isA else ()):
                    wb = hd % 2
                    w = watt[wb]
                    wload(w, 0, wview(ev_in_d[i], 0, 8, 128 * hd, 128), 8, 128, ("wa", wb, 0))
                    wload(w, 1024, wview(ev_sw_d[i], 0, 8, 128 * hd, 128), 8, 128, ("wa", wb, 1))
                    wload(w, 2048, wview(ev_in_d[i], 0, 8, 512 + 128 * hd, 128), 8, 128, ("wa", wb, 2))
                    wload(w, 3072, wview(ev_sw_d[i], 0, 8, 512 + 128 * hd, 128), 8, 128, ("wa", wb, 3))
                    wload(w, 4096, wview(ev_in_d[i], 0, 8, 1024 + 128 * hd, 128), 8, 128, ("wa", wb, 4))
                    for (dst, dres, o0) in ((None, "qT", 0), (kT, "kT", 2048)):
                        for g in range(NTG):
                            pa = next_px()
                            proj_fm(psX[pa], w, o0, 128, 128, 8, hfn, g, ("wa", wb, o0 // 1024), hres, ("px", pa))
                            pb = next_px()
                            proj_fm(psX[pb], w, o0 + 1024, 128, 128, 8, hfn, g, ("wa", wb, o0 // 1024 + 1), hres, ("px", pb))
                            rope_evac(psX[pa], psX[pb], pa, pb, dst, g, dres)
                    v_tokmajor(w, 4096, ("wa", wb, 4), 128)
                    ob = hd
                    pipe = Pipe(3)
                    hold = {}
                    later = []

                    def fin_a(g, half, aset, ob=ob, hold=hold):
                        aO, nO, aD, nD = aset
                        tO = at_i[0]
                        tD = (tO + 1) % NAT
                        at_i[0] = (tO + 2) % NAT
                        P.op("dve", lambda e: e.reciprocal(out=atmp[tD][:], in_=aD[:, :]), reads=[nD], writes=[("at", tD)])
                        P.op("dve", lambda e: e.tensor_tensor(out=atmp[tO][:], in0=aO[:, :], in1=atmp[tD][:], op=ALU.mult),
                             reads=[nO, ("at", tD)], writes=[("at", tO)])
                        if half == 0:
                            hold[g] = tO
                            return
                        a = hold[g]
                        b = tO
                        P.op("dve", lambda e: e.scalar_tensor_tensor(
                            out=atmp[a][:], in0=atmp[b][:], scalar=lams[:, 4 * i + 2: 4 * i + 3], in1=atmp[a][:], op0=ALU.mult, op1=ALU.add),
                            reads=[("at", a), ("at", b), ("lams", i, 2)], writes=[("at", a)])
                        later.append(lambda: subln_tail(g, a, tD, ob))

                    def subln_tail(g, a, tD, ob):
                        si = next_sq()
                        P.op("act", lambda e: e.activation(out=sq[si][:], in_=atmp[a][:], func=AF.Square), reads=[("at", a)], writes=[("sq", si)])
                        sb2 = sbuf_i[0]
                        sbuf_i[0] = (sb2 + 1) % 4
                        sreg2 = psS[:, sb2 * TG:(sb2 + 1) * TG]
                        P.op("pe", lambda e: e.matmul(sreg2, lhsT=ones[:], rhs=sq[si][:], start=True, stop=True),
                             reads=[("sq", si), "ones"], writes=[("psS", sb2)])
                        k2 = 1.0 / ((1.0 - lam_init) ** 2)
                        P.op("act", lambda e: e.activation(out=atmp[tD][:], in_=sreg2, func=AF.Sqrt, scale=k2 / 128.0, bias=EPS * k2),
                             reads=[("psS", sb2)], writes=[("at", tD)])
                        P.op("dve", lambda e: e.reciprocal(out=atmp[tD][:], in_=atmp[tD][:]), reads=[("at", tD)], writes=[("at", tD)])
                        P.op("dve", lambda e: e.scalar_tensor_tensor(
                            out=oH[ob][:, g * TG:(g + 1) * TG], in0=atmp[a][:], scalar=vecs[:, 78 + i: 79 + i], in1=atmp[tD][:], op0=ALU.mult, op1=ALU.mult),
                            reads=[("at", a), ("at", tD), "vecs"], writes=[(("oH", ob), g)])

                    def a_step(g, half, kt, aset):
                        aO, nO, aD, nD = aset
                        sbi = sbuf_i[0]
                        sbuf_i[0] = (sbi + 1) % 4
                        pti = pt_i[0]
                        pt_i[0] = (pti + 1) % NPT
                        sreg = psS[:, sbi * TG:(sbi + 1) * TG]

                        def qk():
                            P.op("pe", lambda e: e.matmul(sreg, lhsT=kT[:, kt * 128:(kt + 1) * 128], rhs=qTm[half][:, g * TG:(g + 1) * TG], start=True, stop=True),
                                 reads=[("kT", kt // 4), ("qT", half, g)], writes=[("psS", sbi)])
                            P.op("act", lambda e: e.activation(out=PT[pti][:, 0:TG], in_=sreg, func=AF.Exp, scale=0.125),
                                 reads=[("psS", sbi)], writes=[("PT", pti)])

                        def pv():
                            P.op("pe", lambda e: e.matmul(aO[:, :], lhsT=vS[:, kt * 128:(kt + 1) * 128], rhs=PT[pti][:, 0:TG], start=(kt == 0), stop=(kt == 15)),
                                 reads=[("PT", pti), ("vS", kt // 4)], writes=[nO])
                            P.op("pe", lambda e: e.matmul(aD[:, :], lhsT=ones[:], rhs=PT[pti][:, 0:TG], start=(kt == 0), stop=(kt == 15)),
                                 reads=[("PT", pti), "ones"], writes=[nD])
                            if kt == 15:
                                fin_a(g, half, aset)
                            if kt == 7:
                                while later:
                                    later.pop(0)()
                        pipe.step(qk, pv)

                    accs_a = [(psO, ("px", 3), psD, ("px", 2)), (psX[0], ("px", 0), psX[1], ("px", 1))]
                    for g in range(NTG):
                        for half in range(2):
                            aset = accs_a[unit_a[0] % 2]
                            unit_a[0] += 1
                            for kt in range(16):
                                a_step(g, half, kt, aset)
                    pipe.flush()
                    while later:
                        later.pop(0)()
                if isA:
                    outproj_acc(0)

                for cpair in (() if isA else range(4)):
                    wb = cpair % 2
                    w = watt[wb]
                    wload(w, 0, wview(ev_in_d[i], 0, 8, 1536 + 128 * cpair, 128), 8, 128, ("wa", wb, 0))
                    wload(w, 2048, wview(ev_in_d[i], 0, 8, 2048 + 128 * cpair, 128), 8, 128, ("wa", wb, 2))
                    wload(w, 4096, wview(ev_in_d[i], 0, 8, 2560 + 128 * cpair, 128), 8, 128, ("wa", wb, 4))
                    for (dst, dres, o0) in ((None, "qT", 0), (kT, "kT", 2048)):
                        for g in range(NTG):
                            pa = next_px()
                            proj_fm(psX[pa], w, o0, 128, 128, 8, hfn, g, ("wa", wb, o0 // 1024), hres, ("px", pa))
                            if dst is None:
                                for hf in range(2):
                                    P.op("act", lambda e, pa=pa, hf=hf, g=g: e.activation(out=qTm[hf][64 * hf:64 * hf + 64, g * TG:(g + 1) * TG],
                                                                                      in_=psX[pa][64 * hf:64 * hf + 64, :], func=AF.Copy),
                                         reads=[("px", pa)], writes=[("qT", hf, g)])
                            else:
                                P.op("act", lambda e, pa=pa, dst=dst, g=g: e.activation(out=dst[:, g * TG:(g + 1) * TG], in_=psX[pa][:], func=AF.Copy),
                                     reads=[("px", pa)], writes=[(dres, g)])
                    v_tokmajor(w, 4096, ("wa", wb, 4), 128)
                    ob = cpair
                    accs_b = [(psO, ("px", 3), psD, ("px", 2)), (psX[0], ("px", 0), psX[1], ("px", 1))]
                    tl = []
                    for hh in range(2):
                        for m4 in range(4):
                            aset = accs_b[unit_b[0] % 2]
                            unit_b[0] += 1
                            for mm in range(4):
                                m = m4 * 4 + mm
                                tiles = _PLAN[m]
                                for ti_, (t, gid) in enumerate(tiles):
                                    tl.append(dict(hh=hh, h=2 * cpair + hh, m=m, mm=mm, m4=m4, t=t, gid=gid, first=(ti_ == 0),
                                                   last=(ti_ == len(tiles) - 1), aset=aset, fin=(mm == 3 and ti_ == len(tiles) - 1)))
                    pipe = Pipe(3)

                    def fin_b(d, ob=ob):
                        aO, nO, aD, nD = d["aset"]
                        r0 = 64 * d["hh"]
                        m4 = d["m4"]
                        tr = next_tmp()
                        P.op("dve", lambda e: e.reciprocal(out=tmp[tr][r0:r0 + 64, :], in_=aD[r0:r0 + 64, :]),
                             reads=[nD], writes=[("tmp", tr)])
                        P.op("dve", lambda e: e.tensor_tensor(
                            out=oH[ob][r0:r0 + 64, m4 * TG:(m4 + 1) * TG], in0=aO[r0:r0 + 64, :], in1=tmp[tr][r0:r0 + 64, :], op=ALU.mult),
                            reads=[nO, ("tmp", tr)], writes=[(("oH", ob), m4)])

                    def b_chunk(ch):
                        sbi = sbuf_i[0]
                        sbuf_i[0] = (sbi + 1) % 4
                        pti = pt_i[0]
                        pt_i[0] = (pti + 1) % NPT
                        n = len(ch)

                        def qk():
                            for j, d in enumerate(ch):
                                sreg = psS[:, sbi * TG + j * 128: sbi * TG + (j + 1) * 128]
                                goff = (d["h"] * _NG + d["gid"]) * 128
                                P.op("pe", lambda e, sreg=sreg, goff=goff: e.matmul(sreg, lhsT=ident[:], rhs=G[:, goff:goff + 128], start=True, stop=False),
                                     reads=[("G", d["h"]), "ident"], writes=[("psS", sbi)])
                                P.op("pe", lambda e, sreg=sreg, d=d: e.matmul(
                                    sreg, lhsT=kT[:, d["t"] * 128:(d["t"] + 1) * 128], rhs=qTm[d["hh"]][:, d["m"] * 128:(d["m"] + 1) * 128], start=False, stop=True),
                                    reads=[("kT", d["t"] // 4), ("qT", d["hh"], d["m"] // 4)], writes=[("psS", sbi)])
                            P.op("act", lambda e: e.activation(out=PT[pti][:, 0:n * 128], in_=psS[:, sbi * TG: sbi * TG + n * 128], func=AF.Exp, scale=0.125),
                                 reads=[("psS", sbi)], writes=[("PT", pti)])

                        def pv():
                            for j, d in enumerate(ch):
                                aO, nO, aD, nD = d["aset"]
                                mm = d["mm"]
                                P.op("pe", lambda e, j=j, d=d, aO=aO, mm=mm: e.matmul(
                                    aO[:, mm * 128:(mm + 1) * 128], lhsT=vS[:, d["t"] * 128:(d["t"] + 1) * 128], rhs=PT[pti][:, j * 128:(j + 1) * 128],
                                    start=d["first"], stop=d["last"]),
                                    reads=[("PT", pti), ("vS", d["t"] // 4)], writes=[nO])
                                P.op("pe", lambda e, j=j, d=d, aD=aD, mm=mm: e.matmul(
                                    aD[:, mm * 128:(mm + 1) * 128], lhsT=ones[:], rhs=PT[pti][:, j * 128:(j + 1) * 128],
                                    start=d["first"], stop=d["last"]),
                                    reads=[("PT", pti), "ones"], writes=[nD])
                                if d["fin"]:
                                    fin_b(d)
                        pipe.step(qk, pv)

                    for c0 in range(0, len(tl), 4):
                        b_chunk(tl[c0:c0 + 4])
                    pipe.flush()
                if not isA:
                    outproj_acc(512)
                P.emit_phase()

        def odd_phase(l):
            i = l // 2
            scale = 1.0 / math.sqrt(96.0)
            with ExitStack() as ps_:
                st["npx"] = 4
                st["px"] = 0
                watt = [sb("watt%d" % k, [128, 2048], BF16, ps_) for k in range(2)]
                win = sb("win", [128, 8 * 416], BF16, ps_)
                winsw = sb("winsw", [128, 8 * 96], BF16, ps_)
                qTt = [sb("qT%d" % k, [128, S], BF16, ps_) for k in range(2)]
                kTt = [sb("kT%d" % k, [128, S], BF16, ps_) for k in range(2)]
                vSt = [sb("vS%d" % k, [128, 16 * 128], BF16, ps_) for k in range(2)]
                NPT = 6
                PT = [sb("PT%d" % k, [128, TG], BF16, ps_) for k in range(NPT)]
                ropeC = sb("ropeC", [128, S], BF16, ps_)
                ropeS = sb("ropeS", [128, S], BF16, ps_)
                for k_ in range(2):
                    P.op("pool", lambda e, k_=k_: e.memset(qTt[k_][:], 0.0), writes=[(("qT", k_), g) for g in range(NTG)])
                for k_ in range(2):
                    P.op("pool", lambda e, k_=k_: e.memset(kTt[k_][:], 0.0),
                         writes=[(("kTn", k_), g) for g in range(NTG)] + [(("kTr", k_), g) for g in range(NTG)])
                accs = [(psX[3], ("px", 3)), (psX[2], ("px", 2))]
                unit = [0]
                cqn = sb("cqn", [128, 2 * S], BF16, ps_)
                ckvn = sb("ckvn", [128, S], BF16, ps_)
                pt_i = [0]
                sbuf_i = [0]
                P.op("sp", lambda e: e.dma_start(out=ropeC[:], in_=ropeM_d[0]), writes=["ropeC"], dma_ch="ropeC")
                P.op("sp", lambda e: e.dma_start(out=ropeS[:], in_=ropeM_d[1]), writes=["ropeS"], dma_ch="ropeS")
                for k in range(2):
                    P.op("dve", lambda e, k=k: e.memset(vSt[k][:], 1.0), writes=[("vS", k, 0), ("vS", k, 1)])
                rmsnorm_x(8 * l, hs, lambda c, g: ("h", c, g))
                wload(win, 0, wview(od_in_d[i], 0, 8, 0, 416), 8, 416, "win")
                wload(winsw, 0, wview(od_insw_d[i], 0, 8, 0, 96), 8, 96, "winsw")
                hfn = lambda k, g: hs(k, g)
                hres = lambda k, g: ("h", k, g)
                for g in range(NTG):
                    pq = [next_px(), next_px()]
                    for c2 in range(2):
                        proj_fm(psX[pq[c2]], win, 128 * c2, 416, 128, 8, hfn, g, "win", hres, ("px", pq[c2]))
                    pss = next_px()
                    for c2 in range(2):
                        si = next_sq()
                        P.op("act", lambda e, si=si, p_=pq[c2]: e.activation(out=sq[si][:], in_=psX[p_][:], func=AF.Square),
                             reads=[("px", pq[c2])], writes=[("sq", si)])
                        P.op("pe", lambda e, si=si, c2=c2, pss=pss: e.matmul(psX[pss][:], lhsT=ones[:], rhs=sq[si][:], start=(c2 == 0), stop=(c2 == 1)),
                             reads=[("sq", si), "ones"], writes=[("px", pss)])
                    tr = next_tmp()
                    P.op("act", lambda e, tr=tr, pss=pss: e.activation(out=tmp[tr][:], in_=psX[pss][:], func=AF.Sqrt, scale=1.0 / 256.0, bias=EPS),
                         reads=[("px", pss)], writes=[("tmp", tr)])
                    P.op("dve", lambda e, tr=tr: e.reciprocal(out=tmp[tr][:], in_=tmp[tr][:]), reads=[("tmp", tr)], writes=[("tmp", tr)])
                    for c2 in range(2):
                        P.op("dve", lambda e, tr=tr, c2=c2, g=g, p_=pq[c2]: e.scalar_tensor_tensor(
                            out=cqn[:, c2 * S + g * TG: c2 * S + (g + 1) * TG], in0=psX[p_][:], scalar=vecs[:, 72 + 2 * i + c2: 73 + 2 * i + c2],
                            in1=tmp[tr][:], op0=ALU.mult, op1=ALU.mult),
                            reads=[("px", pq[c2]), ("tmp", tr), "vecs"], writes=[("cqn", c2, g)])
                    pk = next_px()
                    proj_fm(psX[pk], win, 256, 416, 128, 8, hfn, g, "win", hres, ("px", pk))
                    pss = next_px()
                    si = next_sq()
                    P.op("act", lambda e, si=si, pk=pk: e.activation(out=sq[si][:], in_=psX[pk][:], func=AF.Square),
                         reads=[("px", pk)], writes=[("sq", si)])
                    P.op("pe", lambda e, si=si, pss=pss: e.matmul(psX[pss][:], lhsT=ones[:], rhs=sq[si][:], start=True, stop=True),
                         reads=[("sq", si), "ones"], writes=[("px", pss)])
                    tr = next_tmp()
                    P.op("act", lambda e, tr=tr, pss=pss: e.activation(out=tmp[tr][:], in_=psX[pss][:], func=AF.Sqrt, scale=1.0 / 128.0, bias=EPS),
                         reads=[("px", pss)], writes=[("tmp", tr)])
                    P.op("dve", lambda e, tr=tr: e.reciprocal(out=tmp[tr][:], in_=tmp[tr][:]), reads=[("tmp", tr)], writes=[("tmp", tr)])
                    P.op("dve", lambda e, tr=tr, g=g, pk=pk: e.scalar_tensor_tensor(
                        out=ckvn[:, g * TG:(g + 1) * TG], in0=psX[pk][:], scalar=vecs[:, 76 + i: 77 + i], in1=tmp[tr][:], op0=ALU.mult, op1=ALU.mult),
                        reads=[("px", pk), ("tmp", tr), "vecs"], writes=[("ckvn", g)])
                    pa = next_px()
                    proj_fm(psX[pa], win, 320, 416, 96, 8, hfn, g, "win", hres, ("px", pa))
                    pb = next_px()
                    proj_fm(psX[pb], winsw, 0, 96, 96, 8, hfn, g, "winsw", hres, ("px", pb))
                    t1 = next_tmp()
                    t2 = next_tmp()
                    P.op("dve", lambda e, t1=t1, pa=pa, g=g: e.tensor_tensor(out=tmp[t1][64:96, :], in0=psX[pa][64:96, :], in1=ropeC[64:96, g * TG:(g + 1) * TG], op=ALU.mult),
                         reads=[("px", pa), "ropeC"], writes=[("tmp", t1)])
                    P.op("dve", lambda e, t2=t2, pb=pb, g=g: e.tensor_tensor(out=tmp[t2][64:96, :], in0=psX[pb][64:96, :], in1=ropeS[64:96, g * TG:(g + 1) * TG], op=ALU.mult),
                         reads=[("px", pb), "ropeS"], writes=[("tmp", t2)])
                    for k in range(2):
                        P.op("pool", lambda e, t1=t1, t2=t2, g=g, k=k: e.tensor_tensor(
                            out=kTt[k][64:96, g * TG:(g + 1) * TG], in0=tmp[t1][64:96, :], in1=tmp[t2][64:96, :], op=ALU.add),
                            reads=[("tmp", t1), ("tmp", t2)], writes=[(("kTr", k), g)])

                cfn = lambda k, g: cqn[:, k * S + g * TG: k * S + (g + 1) * TG]
                cres = lambda k, g: ("cqn", k, g)
                st["npx"] = 2
                st["px"] = 0

                def proj_head(h):
                    wb = h % 2
                    w = watt[wb]
                    kT = kTt[wb]
                    vS = vSt[wb]
                    qT = qTt[wb]
                    vc0 = 0 if wb == 0 else 64
                    wload(w, 0, wview(od_uq_d[i], 0, 2, 96 * h, 96), 2, 96, ("wa", wb, 0))
                    wload(w, 256, wview(od_uqsw_d[i], 0, 2, 96 * h, 96), 2, 96, ("wa", wb, 1))
                    wload(w, 512, wview(od_ukv_d[i], 0, 1, 128 * h, 128), 1, 128, ("wa", wb, 2))
                    for g in range(NTG):
                        pa = next_px()
                        proj_fm(psX[pa], w, 0, 96, 128, 2, cfn, g, ("wa", wb, 0), cres, ("px", pa))
                        pb = next_px()
                        proj_fm(psX[pb], w, 256, 96, 128, 2, cfn, g, ("wa", wb, 1), cres, ("px", pb))
                        t1 = next_tmp()
                        t2 = next_tmp()
                        P.op("dve", lambda e, t1=t1, pa=pa, g=g: e.tensor_tensor(out=tmp[t1][0:96, :], in0=psX[pa][0:96, :], in1=ropeC[0:96, g * TG:(g + 1) * TG], op=ALU.mult),
                             reads=[("px", pa), "ropeC"], writes=[("tmp", t1)])
                        P.op("dve", lambda e, t2=t2, pb=pb, g=g: e.tensor_tensor(out=tmp[t2][0:96, :], in0=psX[pb][0:96, :], in1=ropeS[0:96, g * TG:(g + 1) * TG], op=ALU.mult),
                             reads=[("px", pb), "ropeS"], writes=[("tmp", t2)])
                        P.op("pool", lambda e, t1=t1, t2=t2, g=g: e.tensor_tensor(out=qT[0:96, g * TG:(g + 1) * TG], in0=tmp[t1][0:96, :], in1=tmp[t2][0:96, :], op=ALU.add),
                             reads=[("tmp", t1), ("tmp", t2)], writes=[(("qT", wb), g)])
                        pk = next_px()
                        P.op("pe", lambda e, pk=pk, g=g: e.matmul(psX[pk][:, :], lhsT=w[:, 512:640], rhs=ckvn[:, g * TG:(g + 1) * TG], start=True, stop=True),
                             reads=[("wa", wb, 2), ("ckvn", g)], writes=[("px", pk)])
                        P.op("dve", lambda e, pk=pk, g=g: e.tensor_copy(out=kT[0:64, g * TG:(g + 1) * TG], in_=psX[pk][0:64, :]),
                             reads=[("px", pk)], writes=[(("kTn", wb), g)])
                    for t8 in range(2):
                        px = next_px()
                        ps = psX[px]
                        for tt in range(8):
                            tok = (t8 * 8 + tt) * 128
                            P.op("pe", lambda e, tt=tt, tok=tok, ps=ps: e.matmul(
                                ps[:, tt * 64:(tt + 1) * 64], lhsT=ckvn[:, tok:tok + 128], rhs=w[:, 576:640], start=True, stop=True),
                                reads=[("wa", wb, 2), ("ckvn", tok // TG)], writes=[("px", px)])
                        dstv = vS[:, t8 * 1024:(t8 + 1) * 1024].rearrange("p (t c) -> p t c", c=128)[:, :, vc0:vc0 + 64]
                        srcv = ps[:, :].rearrange("p (t c) -> p t c", c=64)
                        P.op("dve", lambda e, dstv=dstv, srcv=srcv: e.tensor_copy(out=dstv, in_=srcv),
                             reads=[("px", px)], writes=[("vS", wb, t8)])

                def attn_head(h):
                    wb = h % 2
                    kT = kTt[wb]
                    vS = vSt[wb]
                    qT = qTt[wb]
                    ob = (h // 2) % 2
                    orow = 64 * (h % 2)
                    drow = 64 - orow
                    pipe = Pipe(3)

                    def fin_m(g, aO, nO):
                        tr = next_tmp()
                        P.op("dve", lambda e: e.reciprocal(out=tmp[tr][drow:drow + 64, :], in_=aO[drow:drow + 64, :]),
                             reads=[nO], writes=[("tmp", tr)])
                        P.op("dve", lambda e: e.tensor_tensor(
                            out=hT[orow:orow + 64, (h // 2) * S + g * TG:(h // 2) * S + (g + 1) * TG], in0=aO[orow:orow + 64, :], in1=tmp[tr][drow:drow + 64, :], op=ALU.mult),
                            reads=[nO, ("tmp", tr)], writes=[("h", h // 2, g)])

                    def m_step(g, kt, aO, nO):
                        sbi = sbuf_i[0]
                        sbuf_i[0] = (sbi + 1) % 4
                        pti = pt_i[0]
                        pt_i[0] = (pti + 1) % NPT
                        sreg = psS[:, sbi * TG:(sbi + 1) * TG]

                        def qk():
                            P.op("pe", lambda e: e.matmul(sreg, lhsT=kT[:, kt * 128:(kt + 1) * 128], rhs=qT[:, g * TG:(g + 1) * TG], start=True, stop=True),
                                 reads=[(("kTn", wb), kt // 4), (("kTr", wb), kt // 4), (("qT", wb), g)], writes=[("psS", sbi)])
                            P.op("act", lambda e: e.activation(out=PT[pti][:, 0:TG], in_=sreg, func=AF.Exp, scale=scale),
                                 reads=[("psS", sbi)], writes=[("PT", pti)])

                        def pv():
                            P.op("pe", lambda e: e.matmul(aO, lhsT=vS[:, kt * 128:(kt + 1) * 128], rhs=PT[pti][:, 0:TG], start=(kt == 0), stop=(kt == 15)),
                                 reads=[("PT", pti), ("vS", wb, kt // 8)], writes=[nO])
                            if kt == 15:
                                fin_m(g, aO, nO)
                        pipe.step(qk, pv)

                    for g in range(NTG):
                        aO, nO = accs[unit[0] % 2]
                        unit[0] += 1
                        for kt in range(16):
                            m_step(g, kt, aO, nO)
                    pipe.flush()

                proj_head(0)
                for h in range(16):
                    if h + 1 < 16:
                        proj_head(h + 1)
                    attn_head(h)
                for c in range(NCH):
                    wb = c % 2
                    wload(watt[wb], 0, wview(od_out_d[i], 0, 8, c * 128, 128), 8, 128, [("wa", wb, 0), ("wa", wb, 1), ("wa", wb, 2)])
                    for g in range(NTG):
                        px = next_px()
                        ps = psX[px]
                        for k in range(NCH):
                            P.op("pe", lambda e, k=k, g=g, ps=ps, wb=wb: e.matmul(ps[:], lhsT=watt[wb][:, k * 128:(k + 1) * 128], rhs=hs(k, g),
                                                                              start=(k == 0), stop=(k == NCH - 1)),
                                 reads=[("wa", wb, 0), ("h", k, g)], writes=[("px", px)])
                        P.op("dve", lambda e, c=c, g=g, ps=ps: e.tensor_tensor(out=xs(c, g), in0=ps[:], in1=xs(c, g), op=ALU.add),
                             reads=[("px", px), ("x", c, g)], writes=[("x", c, g)])
                P.emit_phase()

        def load_x(s_):
            for c in range(NCH):
                P.op("sp", lambda e, c=c, s_=s_: e.dma_start(out=xT[:, c * S:(c + 1) * S], in_=xT_d[s_][c * 128:(c + 1) * 128, :]),
                     writes=[("x", c, g) for g in range(NTG)], dma_ch=("xin", c))

        for s_ in range(nseq):
            if s_ == 0:
                load_x(0)
                P.emit_phase()
            for l in layers:
                if do_mix:
                    if l % 2 == 0:
                        first = True
                        for part in parts:
                            even_phase(l, part, first)
                            first = False
                    else:
                        odd_phase(l)
                if do_mlp:
                    mlp_phase(l)
            with ExitStack() as ps_:
                st["npx"] = 2
                ob = [sb("outb%d" % k, [128, S], F32, ps_) for k in range(2)]
                rs = [sb("rs%d" % k, [128, TG], F32, ps_) for k in range(NTG)]
                for g in range(NTG):
                    px = next_px()
                    ps = psX[px]
                    for c in range(NCH):
                        P.op("act", lambda e, c=c, g=g: e.activation(out=hs(c, g), in_=xs(c, g), func=AF.Square),
                             reads=[("x", c, g)], writes=[("h", c, g)])
                        P.op("pe", lambda e, c=c, g=g, ps=ps: e.matmul(ps[:], lhsT=ones[:], rhs=hs(c, g), start=(c == 0), stop=(c == NCH - 1)),
                             reads=[("h", c, g), "ones"], writes=[("px", px)])
                    P.op("act", lambda e, g=g, ps=ps: e.activation(out=rs[g][:], in_=ps[:], func=AF.Sqrt, scale=1.0 / D, bias=EPS),
                         reads=[("px", px)], writes=[("rs", g)])
                    P.op("dve", lambda e, g=g: e.reciprocal(out=rs[g][:], in_=rs[g][:]), reads=[("rs", g)], writes=[("rs", g)])
                for c in range(NCH):
                    k = c % 2
                    for g in range(NTG):
                        P.op("dve", lambda e, c=c, g=g, k=k: e.scalar_tensor_tensor(
                            out=ob[k][:, g * TG:(g + 1) * TG], in0=xs(c, g), scalar=vecs[:, 64 + c:65 + c], in1=rs[g][:], op0=ALU.mult, op1=ALU.mult),
                            reads=[("x", c, g), ("rs", g), "vecs"], writes=[("ob", k)])
                    P.op("sp", lambda e, c=c, k=k, s_=s_: e.dma_start(out=out_d[s_][c * 128:(c + 1) * 128, :], in_=ob[k][:]),
                         reads=[("ob", k)], dma_ch=("out", k))
                if s_ + 1 < nseq:
                    load_x(s_ + 1)
                P.emit_phase(final=(s_ == nseq - 1))
    return nc


def prep_shared(inp):
    f = lambda a: np.ascontiguousarray(np.asarray(a, dtype=np.float32))
    vecs = np.zeros((128, 80), np.float32)
    for l in range(4):
        vecs[:, 8 * l:8 * l + 8] = f(inp["ln_mix_g"])[l].reshape(8, 128).T
        vecs[:, 32 + 8 * l:32 + 8 * l + 8] = f(inp["ln_mlp_g"])[l].reshape(8, 128).T
    vecs[:, 64:72] = f(inp["ln_f_g"]).reshape(8, 128).T
    for i in range(2):
        vecs[:, 72 + 2 * i:74 + 2 * i] = f(inp["od_q_norm_g"])[i].reshape(2, 128).T
        vecs[:, 76 + i] = f(inp["od_kv_norm_g"])[i]
        vecs[:, 78 + i] = f(inp["ev_subln_g"])[i]
    lamv = np.zeros((128, 512), np.float32)
    for i in range(2):
        for j, nm in enumerate(("ev_lambda_q1", "ev_lambda_k1", "ev_lambda_q2", "ev_lambda_k2")):
            lamv[:, i * 256 + j * 64: i * 256 + (j + 1) * 64] = f(inp[nm])[i][None, :]
    ev_in = f(inp["ev_w_in"])
    permA = _swap_perm(1024, 64, 0, 8)
    ev_sw = np.ascontiguousarray(ev_in[:, :, :1024][:, :, permA])
    od_in = f(inp["od_w_in"])
    perm_kr = _swap_perm(32, 32, 0, 16)
    od_in_sw = np.ascontiguousarray(od_in[:, :, 320:416].copy())
    od_in_sw[:, :, 64:96] = od_in[:, :, 384:416][:, :, perm_kr]
    od_uq = f(inp["od_w_uq"])
    perm_q = _swap_perm(1536, 96, 64, 16)
    od_uq_sw = np.ascontiguousarray(od_uq[:, :, perm_q])
    CA, SA = _rope_tables_A()
    CM, SM = _rope_tables_M()
    rpb = f(inp["ev_rpb"])
    gb = rpb[:, :, _RIDX, _CIDX]
    gb = np.ascontiguousarray(gb.transpose(0, 3, 1, 2, 4)).reshape(2, 128, 8 * _NG * 128)
    gmask = np.where(_VALID, 0.0, MASKV).astype(np.float32)
    gmask = np.ascontiguousarray(gmask.transpose(1, 0, 2)).reshape(128, _NG * 128).astype(ml_dtypes.bfloat16)
    return {
        "vecs": vecs, "lamv": lamv,
        "w_up": f(inp["w_up"]), "w_down": f(inp["w_down"]),
        "ev_w_in": ev_in, "ev_w_sw": ev_sw, "ev_w_out": f(inp["ev_w_out"]),
        "od_w_in": od_in, "od_w_in_sw": od_in_sw, "od_w_uq": od_uq, "od_w_uq_sw": od_uq_sw,
        "od_w_ukv": f(inp["od_w_ukv"]), "od_w_out": f(inp["od_w_out"]),
        "ropeA": np.stack([CA, SA]).astype(ml_dtypes.bfloat16), "ropeM": np.stack([CM, SM]).astype(ml_dtypes.bfloat16),
        "gb": gb, "gmask": gmask,
        "ident": np.eye(128, dtype=np.float32).astype(ml_dtypes.bfloat16),
    }


_NC_CACHE = {}


def kernel(**inputs):
    x = np.asarray(inputs["x"], dtype=np.float32)
    B = x.shape[0]
    nseq = B // NCORES
    shared = prep_shared(inputs)
    key = nseq
    if key not in _NC_CACHE:
        _NC_CACHE[key] = build(nseq=nseq)
    nc = _NC_CACHE[key]
    in_maps = []
    for c in range(NCORES):
        xs_ = x[c * nseq:(c + 1) * nseq]
        m = dict(shared)
        m["xT"] = np.ascontiguousarray(xs_.transpose(0, 2, 1))
        in_maps.append(m)
    res = run_bass_kernel_spmd(nc, in_maps, core_ids=list(range(NCORES)))
    out = np.empty((B, S, D), np.float32)
    for c in range(NCORES):
        o = np.asarray(res.results[c]["outT"])
        out[c * nseq:(c + 1) * nseq] = o.transpose(0, 2, 1)
    return out
```
